# Optimizing a Trainium2 kernel written in Bass

```python
import math
import jax, jax.numpy as jnp
from jax import lax
import numpy as np

D_MODEL = 1024
BATCH = 2
SEQ = 8192
DEPTH = 1
DEC_BATCH = 32
DEC_SEQ = 64
PAST_LEN = 4096

CHUNK = 64
D_POOL = 512
POOL_WINDOWS = (2, 4, 8, 16)
N_POOL_GROUPS = 4
POOL_GROUP = D_POOL // N_POOL_GROUPS
POOL_STATE = max(POOL_WINDOWS) - 1
DN_HEADS = 4
DN_DK = 128
DN_DV = 128
DN_QK = DN_HEADS * DN_DK
DN_VW = DN_HEADS * DN_DV
CONV_W = 4
CONV_CH = 2 * DN_QK + DN_VW
N_BRANCH = 2
OFF_U = 0
OFF_QKV = OFF_U + D_POOL
OFF_Z = OFF_QKV + CONV_CH
OFF_B = OFF_Z + DN_VW
OFF_A = OFF_B + DN_HEADS
OFF_G = OFF_A + DN_HEADS
D_IN = OFF_G + N_BRANCH * D_MODEL
PEER_HEADS = 8
PEER_KEYS = 128
PEER_N = PEER_KEYS * PEER_KEYS
PEER_DQ = 256
PEER_TOPK = 16
PEER_BLOCK = 128
EPS = 1e-6

kernel_name = "hybrid_pool_gdn_peer_stream_step"


def _rmsnorm(x, g):
    xf = x.astype(jnp.float32)
    y = xf * lax.rsqrt(jnp.mean(xf * xf, axis=-1, keepdims=True) + EPS)
    return (y * g.astype(jnp.float32)).astype(x.dtype)


def _l2norm(t):
    return t * lax.rsqrt(jnp.sum(t * t, axis=-1, keepdims=True) + EPS)


def _pool_mixer(u, prefix, pos0, w_grp, scale):
    B, L, _ = u.shape
    P = POOL_STATE
    ext = jnp.concatenate([prefix.astype(u.dtype), u], axis=1)
    cs = jnp.cumsum(ext.astype(jnp.float32), axis=1)
    cs = jnp.pad(cs, ((0, 0), (1, 0), (0, 0)))
    pos = pos0 + jnp.arange(L)
    uf = u.astype(jnp.float32)
    outs = []
    for gi, w in enumerate(POOL_WINDOWS):
        sl = slice(gi * POOL_GROUP, (gi + 1) * POOL_GROUP)
        s = cs[:, P + 1:P + 1 + L, sl] - cs[:, P + 1 - w:P + 1 - w + L, sl]
        cnt = jnp.minimum(pos + 1, w).astype(jnp.float32)[None, :, None]
        outs.append(s / cnt - uf[..., sl])
    pooled = jnp.stack(outs, axis=2)
    mixed = jnp.einsum('blgc,gcd->blgd', pooled, w_grp.astype(jnp.float32)).reshape(B, L, D_POOL)
    y = (mixed * scale.astype(jnp.float32)).astype(u.dtype)
    return y, ext[:, -P:]


def _causal_conv(xc, prefix, w):
    L = xc.shape[1]
    ext = jnp.concatenate([prefix.astype(xc.dtype), xc], axis=1)
    y = ext[:, 0:L] * w[0]
    for j in range(1, CONV_W):
        y = y + ext[:, j:j + L] * w[j]
    return jax.nn.silu(y), ext[:, -(CONV_W - 1):]


def _gated_delta(q, k, v, g, beta, S0):
    B, L, H, DK = q.shape
    DV = v.shape[-1]
    NC = -(-L // CHUNK)
    pad = NC * CHUNK - L

    def prep(t):
        t = jnp.pad(t, [(0, 0), (0, pad)] + [(0, 0)] * (t.ndim - 2))
        t = t.reshape((B, NC, CHUNK) + t.shape[2:])
        return jnp.moveaxis(t, 3, 1)

    q, k, v, g, beta = prep(q), prep(k), prep(v), prep(g), prep(beta)
    q = q * (DK ** -0.5)
    gc = jnp.cumsum(g, axis=-1)
    idx = jnp.arange(CHUNK)
    incl = idx[:, None] >= idx[None, :]
    strict = idx[:, None] > idx[None, :]
    decay = jnp.exp(jnp.where(incl, gc[..., :, None] - gc[..., None, :], -jnp.inf))
    kb = k * beta[..., None]
    A = jnp.where(strict, jnp.einsum('bhnid,bhnjd->bhnij', kb, k) * decay, 0.0)
    T = A + jnp.eye(CHUNK, dtype=jnp.float32)
    u_v = lax.linalg.triangular_solve(T, v * beta[..., None], left_side=True, lower=True, unit_diagonal=True)
    w_k = lax.linalg.triangular_solve(T, kb * jnp.exp(gc)[..., None], left_side=True, lower=True, unit_diagonal=True)
    qk = jnp.einsum('bhnid,bhnjd->bhnij', q, k) * decay
    gl = gc[..., -1]
    q_dec = q * jnp.exp(gc)[..., None]
    k_dec = k * jnp.exp(gl[..., None] - gc)[..., None]

    def step(S, xs):
        u_c, w_c, q_c, qk_c, k_c, gl_c = xs
        v_new = u_c - jnp.einsum('bhck,bhkv->bhcv', w_c, S)
        o = jnp.einsum('bhck,bhkv->bhcv', q_c, S) + jnp.einsum('bhij,bhjv->bhiv', qk_c, v_new)
        S = S * jnp.exp(gl_c)[..., None, None] + jnp.einsum('bhck,bhcv->bhkv', k_c, v_new)
        return S, o

    xs = tuple(jnp.moveaxis(t, 2, 0) for t in (u_v, w_k, q_dec, qk, k_dec, gl))
    S, o = lax.scan(step, S0, xs)
    o = jnp.transpose(o, (1, 0, 3, 2, 4)).reshape(B, NC * CHUNK, H, DV)[:, :L]
    return o, S


def _peer(xn, w_q, sub_keys, u_tab, v_tab):
    B, L, D = xn.shape
    T = B * L
    x2 = xn.reshape(T, D)
    q = jnp.einsum('td,dhq->thq', x2, w_q)
    half = PEER_DQ // 2
    s1 = jnp.einsum('thq,hkq->thk', q[..., :half], sub_keys[0]).astype(jnp.float32)
    s2 = jnp.einsum('thq,hkq->thk', q[..., half:], sub_keys[1]).astype(jnp.float32)
    v1, i1 = lax.top_k(s1, PEER_TOPK)
    v2, i2 = lax.top_k(s2, PEER_TOPK)
    cand = (v1[..., :, None] + v2[..., None, :]).reshape(T, PEER_HEADS, PEER_TOPK * PEER_TOPK)
    cv, ci = lax.top_k(cand, PEER_TOPK)
    e1 = jnp.take_along_axis(i1, ci // PEER_TOPK, axis=-1)
    e2 = jnp.take_along_axis(i2, ci % PEER_TOPK, axis=-1)
    experts = e1 * PEER_KEYS + e2
    gates = jax.nn.softmax(cv, axis=-1)
    nb = -(-T // PEER_BLOCK)
    pad = nb * PEER_BLOCK - T
    xp = jnp.pad(x2, ((0, pad), (0, 0))).reshape(nb, PEER_BLOCK, D)
    ep = jnp.pad(experts, ((0, pad), (0, 0), (0, 0))).reshape(nb, PEER_BLOCK, PEER_HEADS, PEER_TOPK)
    gp = jnp.pad(gates, ((0, pad), (0, 0), (0, 0))).reshape(nb, PEER_BLOCK, PEER_HEADS, PEER_TOPK)

    def blk(args):
        xb, eb, gb = args
        ub = u_tab[eb]
        vb = v_tab[eb]
        act = jax.nn.gelu(jnp.einsum('thkd,td->thk', ub, xb).astype(jnp.float32), approximate=False)
        return jnp.einsum('thk,thkd->td', (gb * act).astype(xb.dtype), vb)

    y = lax.map(blk, (xp, ep, gp))
    return y.reshape(nb * PEER_BLOCK, D)[:T].reshape(B, L, D)


def _layer(x, pool_prev, conv_prev, S_prev, pos0, g_mix, w_in, w_pool_grp, pool_scale, w_conv,
           a_log, dt_bias, g_dn_out, w_up_pool, w_up_dn, w_out, g_ffn, w_peer_q,
           peer_sub_keys, peer_u, peer_v):
    B, L, _ = x.shape
    h = _rmsnorm(x, g_mix)
    zc = jnp.einsum('bld,de->ble', h, w_in)
    u = zc[..., OFF_U:OFF_QKV]
    qkv = zc[..., OFF_QKV:OFF_Z]
    zg = zc[..., OFF_Z:OFF_B].astype(jnp.float32).reshape(B, L, DN_HEADS, DN_DV)
    b_raw = zc[..., OFF_B:OFF_A].astype(jnp.float32)
    a_raw = zc[..., OFF_A:OFF_G].astype(jnp.float32)
    gate_raw = zc[..., OFF_G:]
    ya, pool_new = _pool_mixer(u, pool_prev, pos0, w_pool_grp, pool_scale)
    qkv_c, conv_new = _causal_conv(qkv, conv_prev, w_conv)
    qkv_c = qkv_c.astype(jnp.float32)
    q = _l2norm(qkv_c[..., :DN_QK].reshape(B, L, DN_HEADS, DN_DK))
    k = _l2norm(qkv_c[..., DN_QK:2 * DN_QK].reshape(B, L, DN_HEADS, DN_DK))
    v = qkv_c[..., 2 * DN_QK:].reshape(B, L, DN_HEADS, DN_DV)
    beta = jax.nn.sigmoid(b_raw)
    g = -jnp.exp(a_log.astype(jnp.float32)) * jax.nn.softplus(a_raw + dt_bias.astype(jnp.float32))
    o, S_new = _gated_delta(q, k, v, g, beta, S_prev.astype(jnp.float32))
    o = _rmsnorm(o, g_dn_out) * jax.nn.silu(zg)
    yb = o.reshape(B, L, DN_VW).astype(x.dtype)
    ga = jax.nn.sigmoid(gate_raw[..., :D_MODEL])
    gb = jax.nn.sigmoid(gate_raw[..., D_MODEL:])
    merged = ga * jnp.einsum('blc,cd->bld', ya, w_up_pool) + gb * jnp.einsum('blc,cd->bld', yb, w_up_dn)
    x = x + jnp.einsum('bld,de->ble', merged, w_out)
    x = x + _peer(_rmsnorm(x, g_ffn), w_peer_q, peer_sub_keys, peer_u, peer_v)
    return x, pool_new, conv_new, S_new.astype(S_prev.dtype)


def setup_inputs(seed: int = 0) -> dict:
    key = jax.random.key(seed)
    ks = jax.random.split(key, 24)
    f32 = jnp.float32

    def nrm(k, shape, s):
        return jax.random.normal(k, shape, f32) * s

    return {
        "x_prompt": nrm(ks[0], (BATCH, SEQ, D_MODEL), 1.0),
        "x_sample": nrm(ks[1], (DEC_BATCH, DEC_SEQ, D_MODEL), 1.0),
        "cache_pool": nrm(ks[2], (DEPTH, DEC_BATCH, POOL_STATE, D_POOL), 1.0),
        "state_dn_conv": nrm(ks[3], (DEPTH, DEC_BATCH, CONV_W - 1, CONV_CH), 1.0),
        "state_dn": nrm(ks[4], (DEPTH, DEC_BATCH, DN_HEADS, DN_DK, DN_DV), 0.1),
        "g_mix": 1.0 + nrm(ks[5], (DEPTH, D_MODEL), 0.02),
        "w_in": nrm(ks[6], (DEPTH, D_MODEL, D_IN), D_MODEL ** -0.5),
        "w_pool_grp": nrm(ks[7], (DEPTH, N_POOL_GROUPS, POOL_GROUP, POOL_GROUP), POOL_GROUP ** -0.5),
        "pool_scale": 1.0 + nrm(ks[8], (DEPTH, D_POOL), 0.02),
        "w_conv": nrm(ks[9], (DEPTH, CONV_W, CONV_CH), CONV_W ** -0.5),
        "a_log": jnp.log(jax.random.uniform(ks[10], (DEPTH, DN_HEADS), f32, 1.0, 16.0)),
        "dt_bias": jax.random.uniform(ks[11], (DEPTH, DN_HEADS), f32, -5.0, -2.0),
        "g_dn_out": 1.0 + nrm(ks[12], (DEPTH, DN_DV), 0.02),
        "w_up_pool": nrm(ks[13], (DEPTH, D_POOL, D_MODEL), D_POOL ** -0.5),
        "w_up_dn": nrm(ks[14], (DEPTH, DN_VW, D_MODEL), DN_VW ** -0.5),
        "w_out": nrm(ks[15], (DEPTH, D_MODEL, D_MODEL), D_MODEL ** -0.5),
        "g_ffn": 1.0 + nrm(ks[16], (DEPTH, D_MODEL), 0.02),
        "w_peer_q": nrm(ks[17], (DEPTH, D_MODEL, PEER_HEADS, PEER_DQ), D_MODEL ** -0.5),
        "peer_sub_keys": nrm(ks[18], (DEPTH, 2, PEER_HEADS, PEER_KEYS, PEER_DQ // 2), (PEER_DQ // 2) ** -0.5),
        "peer_u": nrm(ks[19], (DEPTH, PEER_N, D_MODEL), D_MODEL ** -0.5),
        "peer_v": nrm(ks[20], (DEPTH, PEER_N, D_MODEL), (PEER_HEADS * PEER_TOPK) ** -0.5),
        "g_final": 1.0 + nrm(ks[21], (D_MODEL,), 0.02),
    }


def reference(x_prompt, x_sample, cache_pool, state_dn_conv, state_dn, g_mix, w_in, w_pool_grp,
              pool_scale, w_conv, a_log, dt_bias, g_dn_out, w_up_pool, w_up_dn, w_out, g_ffn,
              w_peer_q, peer_sub_keys, peer_u, peer_v, g_final):
    xp, xs = x_prompt, x_sample
    Bp = xp.shape[0]
    pool_p, conv_p, dn_p, pool_s, conv_s, dn_s = [], [], [], [], [], []
    for l in range(DEPTH):
        params = (g_mix[l], w_in[l], w_pool_grp[l], pool_scale[l], w_conv[l], a_log[l], dt_bias[l],
                  g_dn_out[l], w_up_pool[l], w_up_dn[l], w_out[l], g_ffn[l], w_peer_q[l],
                  peer_sub_keys[l], peer_u[l], peer_v[l])
        xp, pp, cp, sp = _layer(xp,
                                jnp.zeros((Bp, POOL_STATE, D_POOL), xp.dtype),
                                jnp.zeros((Bp, CONV_W - 1, CONV_CH), xp.dtype),
                                jnp.zeros((Bp, DN_HEADS, DN_DK, DN_DV), state_dn.dtype),
                                0, *params)
        xs, ps, cs_, ss = _layer(xs, cache_pool[l], state_dn_conv[l], state_dn[l], PAST_LEN, *params)
        pool_p.append(pp); conv_p.append(cp); dn_p.append(sp)
        pool_s.append(ps); conv_s.append(cs_); dn_s.append(ss)
    y_prompt = _rmsnorm(xp, g_final)
    y_sample = _rmsnorm(xs, g_final)
    return (y_prompt, y_sample, jnp.stack(pool_p), jnp.stack(conv_p), jnp.stack(dn_p),
            jnp.stack(pool_s), jnp.stack(conv_s), jnp.stack(dn_s))
```

```python
import os
import numpy as np
from contextlib import ExitStack
import concourse.bass as bass
import concourse.mybir as mybir
from concourse.bass_utils import run_bass_kernel_spmd

F32 = mybir.dt.float32
F32R = mybir.dt.float32r
U32 = mybir.dt.uint32
I32 = mybir.dt.int32
AF = mybir.ActivationFunctionType
ALU = mybir.AluOpType
AX = mybir.AxisListType

D_MODEL = 1024
SEQ = 8192
NCORE = 8
QTOK = 2048
STOK = 256
OWN = QTOK + STOK
D_IN = 4616
OFF_QKV, OFF_Z, OFF_B, OFF_A, OFF_G = 512, 2048, 2560, 2564, 2568
EPS = 1e-6
NEG = -30000.0

STAGE = int(os.environ.get("MK_STAGE", "99"))
NGROUP_DBG = int(os.environ.get("MK_NGROUP", "32"))
SUB = int(os.environ.get("MK_SUB", "99"))
DEBUG_A = int(os.environ.get("MK_DEBUG_A", "0"))
TILES = [int(t) for t in os.environ.get("MK_TILES", "0,1,2,3,4,5,6,7,8").split(",")]


class Trk:
    ROT = int(os.environ.get("MK_ROT", "8000"))
    NDMA = 24

    def __init__(self, nc):
        self.nc = nc
        self.eng = dict(pe=nc.tensor, act=nc.scalar, dve=nc.vector, pool=nc.gpsimd, sp=nc.sync)
        self.sem = {k: nc.alloc_semaphore("prog_%s_0" % k) for k in self.eng}
        self.semgen = {k: 0 for k in self.eng}
        self.cnt = {k: 0 for k in self.eng}
        self.waited = {k: {} for k in self.eng}
        self.lastw = {}
        self.readers = {}
        self.dma_sems = [nc.alloc_semaphore("dmas_%d" % i) for i in range(self.NDMA)]
        self.dma_cnt = [0] * self.NDMA
        self.dma_rr = {'hw': 0, 'sw': 0}
        self.NSW = 8
        self.n_inst = 0

    def _wait(self, e, tok):
        owner, sem, val, semkey = tok
        if owner == e and e == 'pe':
            return
        w = self.waited[e]
        if w.get(semkey, 0) >= val:
            return
        self.eng[e].wait_ge(sem, val)
        w[semkey] = val

    def _deps(self, e, reads, writes):
        for b in reads:
            lw = self.lastw.get(b)
            if lw is not None:
                self._wait(e, lw)
            if b.startswith("ps"):
                for tok in self.readers.get(b, {}).values():
                    if tok[0] != e:
                        self._wait(e, tok)
        for b in writes:
            lw = self.lastw.get(b)
            if lw is not None:
                self._wait(e, lw)
            for tok in self.readers.get(b, {}).values():
                self._wait(e, tok)

    def _record(self, tok, reads, writes):
        for b in reads:
            self.readers.setdefault(b, {})[tok[0]] = tok
        for b in writes:
            self.lastw[b] = tok
            self.readers[b] = {}

    def op(self, e, fn, reads=(), writes=()):
        self._deps(e, reads, writes)
        inst = fn(self.eng[e])
        if self.cnt[e] >= self.ROT:
            self.semgen[e] += 1
            self.sem[e] = self.nc.alloc_semaphore("prog_%s_%d" % (e, self.semgen[e]))
            self.cnt[e] = 0
        self.cnt[e] += 1
        sem = self.sem[e]
        inst.then_inc(sem, 1)
        tok = (e, sem, self.cnt[e], (e, self.semgen[e]))
        self._record(tok, reads, writes)
        self.n_inst += 1
        return inst

    def dma(self, q, out, in_, reads=(), writes=()):
        self._deps(q, reads, writes)
        if q == 'pool':
            k = self.dma_rr['sw']
            self.dma_rr['sw'] = (k + 1) % self.NSW
        else:
            k = self.NSW + self.dma_rr['hw']
            self.dma_rr['hw'] = (self.dma_rr['hw'] + 1) % (self.NDMA - self.NSW)
        sem = self.dma_sems[k]
        if self.dma_cnt[k] > 0:
            self._wait(q, (('dma', k), sem, 16 * self.dma_cnt[k], ('dma', k)))
        inst = self.eng[q].dma_start(out=out, in_=in_)
        self.dma_cnt[k] += 1
        inst.then_inc(sem, 16)
        tok = (('dma', k), sem, 16 * self.dma_cnt[k], ('dma', k))
        self._record(tok, reads, writes)
        self.n_inst += 1
        return inst

    def wait_all_dma(self, e):
        for k in range(self.NDMA):
            if self.dma_cnt[k] > 0:
                self._wait(e, (('dma', k), self.dma_sems[k], 16 * self.dma_cnt[k], ('dma', k)))

    def barrier(self):
        toks = []
        for e2 in self.eng:
            if self.cnt[e2] > 0:
                toks.append((e2, self.sem[e2], self.cnt[e2], (e2, self.semgen[e2])))
        for e in self.eng:
            for tok in toks:
                if tok[0] != e:
                    self._wait(e, tok)
            self.wait_all_dma(e)

    def barrier_consts(self):
        for e in self.eng:
            self.wait_all_dma(e)


def _consts():
    idx = np.arange(128)
    ch = idx // 64
    same = (ch[:, None] == ch[None, :])
    c = {}
    c["ident"] = np.eye(128, dtype=np.float32)
    c["ones"] = np.ones((128, 128), np.float32)
    c["negones"] = -np.ones((128, 128), np.float32)
    c["umask"] = (same & (idx[:, None] <= idx[None, :])).astype(np.float32)
    c["bd"] = same.astype(np.float32)
    low_incl = same & (idx[None, :] <= idx[:, None])
    c["negL"] = np.where(low_incl, 0.0, NEG).astype(np.float32)
    c["negLT"] = np.ascontiguousarray(c["negL"].T)
    c["offdiag"] = (1.0 - np.eye(128)).astype(np.float32)
    cs = np.zeros((128, 2), np.float32)
    cs[:64, 0] = 1.0
    cs[64:, 1] = 1.0
    c["chunksel"] = cs
    c["iota"] = np.tile(np.arange(128, dtype=np.float32)[None, :], (128, 1))
    return c


def rep128(v):
    v = np.asarray(v, np.float32).reshape(1, -1)
    return np.ascontiguousarray(np.repeat(v, 128, axis=0))


class Prog:
    def __init__(self):
        self.nc = bass.Bass("TRN2", target_bir_lowering=False)
        self.T = Trk(self.nc)
        self.din = {}
        self.dout = {}

    def inp(self, name, shape, dt=F32):
        ap = self.nc.dram_tensor(name, list(shape), dt, kind="ExternalInput").ap()
        self.din[name] = ap
        return ap

    def outp(self, name, shape, dt=F32):
        ap = self.nc.dram_tensor(name, list(shape), dt, kind="ExternalOutput").ap()
        self.dout[name] = ap
        return ap

    def scratch(self, name, shape, dt=F32):
        return self.nc.dram_tensor(name, list(shape), dt).ap()


def sb(nc, es, name, shape, dt=F32):
    h = es.enter_context(nc.sbuf_tensor("s_" + name, list(shape), dt))
    return h.ap()


def build():
    P = Prog()
    nc, T = P.nc, P.T
    xseq = P.inp("xseq", [SEQ, D_MODEL])
    xs = P.inp("xs", [STOK, D_MODEL])
    S0 = P.inp("S0", [4, 4, 128, 128])
    conv0T = P.inp("conv0T", [4, 1536, 3])
    w_in = P.inp("w_in", [D_MODEL, D_IN])
    gmix = P.inp("gmix_bc", [128, D_MODEL])
    wconvT = P.inp("wconvT", [1536, 4])
    alog = P.inp("alog_bc", [128, 4])
    dtb = P.inp("dtb_bc", [128, 4])
    cn = {k: P.inp("c_" + k, list(v.shape)) for k, v in _consts().items()}

    dn_p = P.outp("dn_p", [4, 128, 128])
    dn_s = P.outp("dn_s", [4, 4, 128, 128])
    conv_pT = P.outp("conv_pT", [1536, 3])
    conv_sT = P.outp("conv_sT", [4, 1536, 3])
    o_all = P.outp("o_all", [SEQ + STOK, 512]) if DEBUG_A else P.scratch("o_all", [SEQ + STOK, 512])

    Bd = {}
    for nm, shp in [("xown", [OWN, D_MODEL]), ("xhalo", [16, D_MODEL]), ("pool0", [4, 15, 512]),
                    ("gmix_col", [128, 8]), ("gffn_col", [128, 8]), ("gfin_bc", [128, D_MODEL]), ("gdn_bc", [128, 512]),
                    ("pscale_col", [128, 4]), ("sel", [128, 4]), ("w_grp", [4, 128, 128]), ("keysT", [16, 128, 128]),
                    ("bmain", [128, 4, 128]), ("bprev", [128, 4, 128]), ("bfirst", [128, 4, 128]), ("bsamp", [128, 4, 128]),
                    ("bsprev", [128, 4, 128]), ("w_up_pool", [512, D_MODEL]), ("w_up_dn", [512, D_MODEL]),
                    ("w_out", [D_MODEL, D_MODEL]), ("w_q", [D_MODEL, 2048]), ("UTb", [128, 128, 8, 128]),
                    ("peer_v", [16384, D_MODEL])]:
        Bd[nm] = P.inp(nm, shp)
    Bd["w_in"] = w_in
    Bd["o_all"] = o_all
    Bd["y"] = P.outp("y", [OWN, D_MODEL])
    Bd["pool_p"] = P.outp("pool_p", [15, 512])
    Bd["pool_s"] = P.outp("pool_s", [4, 15, 512])

    ps = [nc.alloc_psum_tensor("psb%d" % i, [128, 512], F32).ap() for i in range(8)]

    with ExitStack() as es:
        C = {}
        for k, v in cn.items():
            C[k] = sb(nc, es, "k_" + k, list(v.shape))
            T.dma('sp', C[k], v)
        gmix_sb = sb(nc, es, "gmix", [128, D_MODEL])
        T.dma('sp', gmix_sb, gmix)
        wc_sb = sb(nc, es, "wconv", [128, 12, 4])
        T.dma('sp', wc_sb, wconvT.rearrange("(c p) j -> p c j", p=128))
        alog_sb = sb(nc, es, "alog", [128, 4])
        dtb_sb = sb(nc, es, "dtb", [128, 4])
        T.dma('sp', alog_sb, alog)
        T.dma('sp', dtb_sb, dtb)
        onesR = sb(nc, es, "onesR", [128, 128], F32R)
        T.dma('pool', onesR, cn["ones"])
        T.barrier_consts()
        nexpA = sb(nc, es, "nexpA", [128, 4])
        T.op('act', lambda e: e.activation(out=nexpA, in_=alog_sb, func=AF.Exp), writes=["nexpA"])
        T.op('dve', lambda e: e.tensor_scalar(out=nexpA, in0=nexpA, scalar1=-1.0, scalar2=None, op0=ALU.mult),
             reads=["nexpA"], writes=["nexpA"])

        phase_a(P, es, C, ps, dict(xseq=xseq, xs=xs, S0=S0, conv0T=conv0T, w_in=w_in, gmix=gmix_sb,
                                   wc=wc_sb, dtb=dtb_sb, nexpA=nexpA, onesR=onesR,
                                   dn_p=dn_p, dn_s=dn_s, conv_pT=conv_pT, conv_sT=conv_sT, o_all=o_all))
        T.barrier()
        phase_b(P, C, ps, Bd)
        T.barrier()
    return P


def phase_a(P, es0, C, ps, A):
    nc, T = P.nc, P.T
    ident, onesF, negones = C["ident"], C["ones"], C["negones"]
    with ExitStack() as es:
        w_a = sb(nc, es, "w_a", [128, 8, 1544], F32R)
        for dc in range(8):
            T.dma('pool', w_a[:, dc, 0:1536], A["w_in"][dc * 128:(dc + 1) * 128, OFF_QKV:OFF_Z], writes=["w_a"])
            T.dma('pool', w_a[:, dc, 1536:1544], A["w_in"][dc * 128:(dc + 1) * 128, OFF_B:OFF_G], writes=["w_a"])
        xt = [sb(nc, es, "xt%d" % i, [128, D_MODEL]) for i in range(2)]
        xn = sb(nc, es, "xn", [128, D_MODEL])
        ss = sb(nc, es, "ss", [128, 1])
        rs = sb(nc, es, "rs", [128, 1])
        rstd = sb(nc, es, "rstd", [128, 1])
        xnT = sb(nc, es, "xnT", [128, 8, 256], F32R)
        ext = sb(nc, es, "ext", [128, 12, 268])
        carry = sb(nc, es, "carry", [128, 12, 3])
        qc = sb(nc, es, "qc", [128, 12, 256])
        sq = sb(nc, es, "sq", [128, 8, 256], F32R)
        rinv = sb(nc, es, "rinv", [128, 256])
        ba = [sb(nc, es, "ba%d" % i, [128, 8]) for i in range(2)]
        small = {}
        for nm, w in [("beta", 4), ("negbeta", 4), ("g", 4), ("sp", 4), ("gc", 4), ("gam", 4), ("dlt", 4),
                      ("bgn", 4), ("gsel", 8), ("egl", 8), ("tmp4", 4)]:
            small[nm] = [sb(nc, es, "%s%d" % (nm, i), [128, w]) for i in range(2)]
        NH = 4
        def tiles(nm, n=1, shape=(128, 128)):
            return [[sb(nc, es, "%s_h%d_%d" % (nm, h, i), list(shape)) for i in range(n)] for h in range(NH)]
        Lg = tiles("Lg")
        dec = tiles("dec")
        decT = tiles("decT")
        dstr = tiles("dstr")
        Mp = tiles("Mp", 2)
        Np = tiles("Np", 2)
        Rr = tiles("Rr", 2)
        Kbgn = tiles("Kbgn")
        Vb = tiles("Vb")
        Qg = tiles("Qg")
        wkn = tiles("wkn")
        kdec = tiles("kdec", 2)
        uv = tiles("uv", 2)
        PT = tiles("PT", 2)
        qeT = tiles("qeT", 2)
        FT = tiles("FT", 4)
        Sst = [sb(nc, es, "Sst%d" % i, [128, 4, 128]) for i in range(2)]
        ostage = [sb(nc, es, "ost%d" % i, [64, 512]) for i in range(2)]

        T.op('dve', lambda e: e.memset(Sst[0], 0.0), writes=["S0buf"])
        T.op('dve', lambda e: e.memset(carry, 0.0), writes=["carry"])
        s_cur = 0
        chunk_ctr = 0

        groups = [("p", g) for g in range(min(32, NGROUP_DBG))] + [("s", 0)]
        for (kind, g) in groups:
            nseq, L = (1, 256) if kind == "p" else (4, 64)
            extv = ext[:, :, 0:nseq * (L + 3)].rearrange("p c (s l) -> p c s l", s=nseq)
            for blk in range(2):
                xb = xt[blk]
                src = A["xseq"][g * 256 + blk * 128: g * 256 + (blk + 1) * 128, :] if kind == "p" \
                    else A["xs"][blk * 128:(blk + 1) * 128, :]
                T.dma('sp', xb, src, writes=["xt%d" % blk])
                T.op('act', lambda e: e.activation(out=xn, in_=xb, func=AF.Square, accum_out=ss),
                     reads=["xt%d" % blk], writes=["xn", "ss"])
                T.op('act', lambda e: e.activation(out=rs, in_=ss, func=AF.Sqrt, bias=EPS, scale=1.0 / D_MODEL),
                     reads=["ss"], writes=["rs"])
                T.op('dve', lambda e: e.reciprocal(out=rstd, in_=rs), reads=["rs"], writes=["rstd"])
                T.op('dve', lambda e: e.scalar_tensor_tensor(out=xn, in0=xb, scalar=rstd, in1=A["gmix"],
                                                             op0=ALU.mult, op1=ALU.mult),
                     reads=["xt%d" % blk, "rstd"], writes=["xn"])
                for half in range(2):
                    for q in range(4):
                        dc = half * 4 + q
                        T.op('pe', lambda e: e.transpose(out=ps[half][:, q * 128:(q + 1) * 128],
                                                         in_=xn[:, dc * 128:(dc + 1) * 128], identity=ident),
                             reads=["xn"], writes=["ps%d" % half])
                    T.op('act', lambda e: e.activation(
                        out=xnT[:, half * 4:(half + 1) * 4, blk * 128:(blk + 1) * 128],
                        in_=ps[half].rearrange("p (q t) -> p q t", q=4), func=AF.Copy),
                        reads=["ps%d" % half], writes=["xnT"])
                for dc in range(8):
                    T.op('pe', lambda e: e.matmul(ps[1][:, 504:512], lhsT=xnT[:, dc, blk * 128:(blk + 1) * 128],
                                                  rhs=w_a[:, dc, 1536:1544], start=(dc == 0), stop=(dc == 7)),
                         reads=["xnT", "w_a"], writes=["ps1"])
                T.op('dve', lambda e: e.tensor_copy(out=ba[blk], in_=ps[1][:, 504:512]),
                     reads=["ps1"], writes=["ba%d" % blk])
            if STAGE < 2:
                continue
            if kind == "p":
                T.op('pool', lambda e: e.tensor_copy(out=extv[:, :, 0, 0:3], in_=carry), reads=["carry"], writes=["ext"])
            else:
                for s in range(4):
                    T.dma('sp', extv[:, :, s, 0:3], A["conv0T"][s].rearrange("(c p) j -> p c j", p=128), writes=["ext"])
            for cc in range(12):
                hf = cc % 2
                pq = ps[hf][:, 0:256]
                for dc in range(8):
                    T.op('pe', lambda e: e.matmul(pq, lhsT=w_a[:, dc, cc * 128:(cc + 1) * 128],
                                                  rhs=xnT[:, dc, :], start=(dc == 0), stop=(dc == 7)),
                         reads=["xnT", "w_a"], writes=["ps%d" % hf])
                T.op('act', lambda e: e.activation(out=extv[:, cc, :, 3:3 + L],
                                                   in_=pq.rearrange("p (s l) -> p s l", s=nseq),
                                                   func=AF.Copy),
                     reads=["ps%d" % hf], writes=["ext"])
            if kind == "p":
                T.op('pool', lambda e: e.tensor_copy(out=carry, in_=extv[:, :, 0, L:L + 3]), reads=["ext"], writes=["carry"])
                if g == 31:
                    T.dma('sp', A["conv_pT"].rearrange("(c p) j -> p c j", p=128), carry, reads=["carry"])
            else:
                for s in range(4):
                    T.dma('sp', A["conv_sT"][s].rearrange("(c p) j -> p c j", p=128), extv[:, :, s, L:L + 3], reads=["ext"])
            qcv = qc.rearrange("p c (s l) -> p c s l", s=nseq)
            for cc in range(12):
                T.op('dve', lambda e: e.tensor_scalar(out=qcv[:, cc], in0=extv[:, cc, :, 0:L], scalar1=A["wc"][:, cc, 0:1],
                                                      scalar2=None, op0=ALU.mult), reads=["ext"], writes=["qc"])
                for j in range(1, 4):
                    T.op('dve', lambda e: e.scalar_tensor_tensor(out=qcv[:, cc], in0=extv[:, cc, :, j:j + L],
                                                                 scalar=A["wc"][:, cc, j:j + 1], in1=qcv[:, cc],
                                                                 op0=ALU.mult, op1=ALU.add),
                         reads=["ext", "qc"], writes=["qc"])
            T.op('act', lambda e: e.activation(out=qc, in_=qc, func=AF.Silu), reads=["qc"], writes=["qc"])
            T.op('act', lambda e: e.activation(out=sq, in_=qc[:, 0:8, :], func=AF.Square), reads=["qc"], writes=["sq"])
            for cc in range(8):
                hf = cc % 2
                pq = ps[hf][:, 0:256]
                T.op('pe', lambda e: e.matmul(pq, lhsT=A["onesR"], rhs=sq[:, cc, :], start=True, stop=True),
                     reads=["sq"], writes=["ps%d" % hf])
                sc = 128.0 if cc < 4 else 1.0
                T.op('act', lambda e: e.activation(out=rinv, in_=pq, func=AF.Sqrt, bias=EPS * sc, scale=sc),
                     reads=["ps%d" % hf], writes=["rinv"])
                T.op('dve', lambda e: e.reciprocal(out=rinv, in_=rinv), reads=["rinv"], writes=["rinv"])
                T.op('dve', lambda e: e.tensor_tensor(out=qc[:, cc, :], in0=qc[:, cc, :], in1=rinv, op=ALU.mult),
                     reads=["qc", "rinv"], writes=["qc"])
            if STAGE < 3:
                continue
            for blk in range(2):
                bp = blk
                cols = slice(blk * 128, (blk + 1) * 128)
                sm = {k: v[bp] for k, v in small.items()}
                bab = ba[blk]
                T.op('act', lambda e: e.activation(out=sm["beta"], in_=bab[:, 0:4], func=AF.Sigmoid),
                     reads=["ba%d" % blk], writes=["beta%d" % bp])
                T.op('dve', lambda e: e.tensor_tensor(out=sm["sp"], in0=bab[:, 4:8], in1=A["dtb"], op=ALU.add),
                     reads=["ba%d" % blk], writes=["sp%d" % bp])
                T.op('act', lambda e: e.activation(out=sm["sp"], in_=sm["sp"], func=AF.Exp), reads=["sp%d" % bp], writes=["sp%d" % bp])
                T.op('act', lambda e: e.activation(out=sm["sp"], in_=sm["sp"], func=AF.Ln, bias=1.0),
                     reads=["sp%d" % bp], writes=["sp%d" % bp])
                T.op('dve', lambda e: e.tensor_tensor(out=sm["g"], in0=sm["sp"], in1=A["nexpA"], op=ALU.mult),
                     reads=["sp%d" % bp, "nexpA"], writes=["g%d" % bp])
                T.op('dve', lambda e: e.tensor_scalar(out=sm["negbeta"], in0=sm["beta"], scalar1=-1.0, scalar2=None, op0=ALU.mult),
                     reads=["beta%d" % bp], writes=["negbeta%d" % bp])
                T.op('pe', lambda e: e.matmul(ps[1][:, 0:4], lhsT=C["umask"], rhs=sm["g"], start=True, stop=True),
                     reads=["g%d" % bp], writes=["ps1"])
                T.op('pe', lambda e: e.matmul(ps[1][:, 8:12], lhsT=C["bd"], rhs=sm["g"], start=True, stop=True),
                     reads=["g%d" % bp], writes=["ps1"])
                T.op('act', lambda e: e.activation(out=sm["gc"], in_=ps[1][:, 0:4], func=AF.Copy), reads=["ps1"], writes=["gc%d" % bp])
                T.op('act', lambda e: e.activation(out=sm["gam"], in_=ps[1][:, 0:4], func=AF.Exp), reads=["ps1"], writes=["gam%d" % bp])
                T.op('dve', lambda e: e.tensor_tensor(out=sm["tmp4"], in0=ps[1][:, 8:12], in1=sm["gc"], op=ALU.subtract),
                     reads=["ps1", "gc%d" % bp], writes=["tmp4%d" % bp])
                T.op('act', lambda e: e.activation(out=sm["dlt"], in_=sm["tmp4"], func=AF.Exp), reads=["tmp4%d" % bp], writes=["dlt%d" % bp])
                T.op('dve', lambda e: e.tensor_tensor(out=sm["bgn"], in0=sm["negbeta"], in1=sm["gam"], op=ALU.mult),
                     reads=["negbeta%d" % bp, "gam%d" % bp], writes=["bgn%d" % bp])
                T.op('dve', lambda e: e.tensor_tensor(
                    out=sm["gsel"].rearrange("p (h c) -> p h c", c=2),
                    in0=sm["g"].unsqueeze(2).to_broadcast([128, 4, 2]),
                    in1=C["chunksel"].unsqueeze(1).to_broadcast([128, 4, 2]), op=ALU.mult),
                    reads=["g%d" % bp], writes=["gsel%d" % bp])
                T.op('pe', lambda e: e.matmul(ps[1][:, 16:24], lhsT=onesF, rhs=sm["gsel"], start=True, stop=True),
                     reads=["gsel%d" % bp], writes=["ps1"])
                T.op('act', lambda e: e.activation(out=sm["egl"], in_=ps[1][:, 16:24], func=AF.Exp), reads=["ps1"], writes=["egl%d" % bp])

                if SUB < 1:
                    continue

                def hk(nm, h, i=0):
                    return "%s_%d_%d" % (nm, h, i)

                def pslot(h, i):
                    if i == 0:
                        return ps[5][:, h * 128:(h + 1) * 128], "ps5_%d" % h
                    return ps[4][:, 128 + 0:128 + 0], None

                def slotA(h):
                    return ps[2 + h][:, 0:128], "ps%d" % (2 + h)

                def slotB(h):
                    return ps[2 + h][:, 128:256], "ps%d" % (2 + h)

                def slotC(h):
                    return ps[2 + h][:, 256:384], "ps%d" % (2 + h)

                for h in range(NH):
                    KT = qc[:, 4 + h, cols]
                    T.op('pool', lambda e: e.tensor_scalar(out=Lg[h][0], in0=C["umask"], scalar1=sm["g"][:, h:h + 1],
                                                           scalar2=None, op0=ALU.mult),
                         reads=["g%d" % bp], writes=[hk("Lg", h)])
                    pa, ka = slotA(h)
                    T.op('pe', lambda e: e.matmul(pa, lhsT=Lg[h][0], rhs=onesF, start=True, stop=False), reads=[hk("Lg", h)], writes=[ka])
                    T.op('pe', lambda e: e.matmul(pa, lhsT=negones, rhs=Lg[h][0], start=False, stop=False), reads=[hk("Lg", h)], writes=[ka])
                    T.op('pe', lambda e: e.matmul(pa, lhsT=ident, rhs=C["negL"], start=False, stop=True), writes=[ka])
                    T.op('act', lambda e: e.activation(out=dec[h][0], in_=pa, func=AF.Exp), reads=[ka], writes=[hk("dec", h)])
                    pb, kb = slotB(h)
                    T.op('pe', lambda e: e.matmul(pb, lhsT=onesF, rhs=Lg[h][0], start=True, stop=False), reads=[hk("Lg", h)], writes=[kb])
                    T.op('pe', lambda e: e.matmul(pb, lhsT=Lg[h][0], rhs=negones, start=False, stop=False), reads=[hk("Lg", h)], writes=[kb])
                    T.op('pe', lambda e: e.matmul(pb, lhsT=ident, rhs=C["negLT"], start=False, stop=True), writes=[kb])
                    T.op('act', lambda e: e.activation(out=decT[h][0], in_=pb, func=AF.Exp), reads=[kb], writes=[hk("decT", h)])
                    T.op('pool', lambda e: e.tensor_tensor(out=dstr[h][0], in0=dec[h][0], in1=C["offdiag"], op=ALU.mult),
                         reads=[hk("dec", h)], writes=[hk("dstr", h)])
                    pc, kc = slotC(h)
                    T.op('pe', lambda e: e.matmul(pc, lhsT=KT, rhs=KT, start=True, stop=True), reads=["qc"], writes=[kc])
                    T.op('dve', lambda e: e.scalar_tensor_tensor(out=Mp[h][0], in0=pc, scalar=sm["negbeta"][:, h:h + 1],
                                                                 in1=dstr[h][0], op0=ALU.mult, op1=ALU.mult),
                         reads=[kc, hk("dstr", h), "negbeta%d" % bp], writes=[hk("Mp", h, 0)])
                if SUB < 2:
                    continue
                for h in range(NH):
                    pa, ka = slotA(h)
                    T.op('pe', lambda e: e.transpose(out=pa, in_=Mp[h][0], identity=ident), reads=[hk("Mp", h, 0)], writes=[ka])
                    T.op('act', lambda e: e.activation(out=Np[h][0], in_=pa, func=AF.Copy), reads=[ka], writes=[hk("Np", h, 0)])
                    T.op('pool', lambda e: e.tensor_tensor(out=Rr[h][0], in0=Np[h][0], in1=ident, op=ALU.add),
                         reads=[hk("Np", h, 0)], writes=[hk("Rr", h, 0)])
                if SUB < 3:
                    continue
                for lev in range(5):
                    a, b2 = lev % 2, (lev + 1) % 2
                    for h in range(NH):
                        pb, kb = slotB(h)
                        T.op('pe', lambda e: e.matmul(pb, lhsT=Np[h][a], rhs=Mp[h][a], start=True, stop=True),
                             reads=[hk("Np", h, a), hk("Mp", h, a)], writes=[kb])
                        if lev < 4:
                            pc, kc = slotC(h)
                            T.op('pe', lambda e: e.matmul(pc, lhsT=Mp[h][a], rhs=Np[h][a], start=True, stop=True),
                                 reads=[hk("Np", h, a), hk("Mp", h, a)], writes=[kc])
                    for h in range(NH):
                        pb, kb = slotB(h)
                        T.op('dve', lambda e: e.tensor_copy(out=Mp[h][b2], in_=pb), reads=[kb], writes=[hk("Mp", h, b2)])
                        if lev < 4:
                            pc, kc = slotC(h)
                            T.op('act', lambda e: e.activation(out=Np[h][b2], in_=pc, func=AF.Copy), reads=[kc], writes=[hk("Np", h, b2)])
                    for h in range(NH):
                        pa, ka = slotA(h)
                        T.op('pe', lambda e: e.matmul(pa, lhsT=ident, rhs=Rr[h][a], start=True, stop=False),
                             reads=[hk("Rr", h, a)], writes=[ka])
                        T.op('pe', lambda e: e.matmul(pa, lhsT=Mp[h][b2], rhs=Rr[h][a], start=False, stop=True),
                             reads=[hk("Rr", h, a), hk("Mp", h, b2)], writes=[ka])
                    for h in range(NH):
                        pa, ka = slotA(h)
                        T.op('dve' if h % 2 else 'act',
                             (lambda e: e.tensor_copy(out=Rr[h][b2], in_=pa)) if h % 2 else
                             (lambda e: e.activation(out=Rr[h][b2], in_=pa, func=AF.Copy)),
                             reads=[ka], writes=[hk("Rr", h, b2)])
                TI = 1
                if SUB < 4:
                    continue
                for h in range(NH):
                    QT, KT, VT = qc[:, h, cols], qc[:, 4 + h, cols], qc[:, 8 + h, cols]
                    pa, ka = slotA(h)
                    T.op('pe', lambda e: e.transpose(out=pa, in_=KT, identity=ident), reads=["qc"], writes=[ka])
                    T.op('act', lambda e: e.activation(out=Kbgn[h][0], in_=pa, func=AF.Copy, scale=sm["bgn"][:, h:h + 1]),
                         reads=[ka, "bgn%d" % bp], writes=[hk("Kbgn", h)])
                    T.op('dve', lambda e: e.tensor_scalar(out=kdec[h][bp], in0=pa, scalar1=sm["dlt"][:, h:h + 1], scalar2=None, op0=ALU.mult),
                         reads=[ka, "dlt%d" % bp], writes=[hk("kdec", h, bp)])
                    pb, kb = slotB(h)
                    T.op('pe', lambda e: e.transpose(out=pb, in_=VT, identity=ident), reads=["qc"], writes=[kb])
                    T.op('act', lambda e: e.activation(out=Vb[h][0], in_=pb, func=AF.Copy, scale=sm["beta"][:, h:h + 1]),
                         reads=[kb, "beta%d" % bp], writes=[hk("Vb", h)])
                    pc, kc = slotC(h)
                    T.op('pe', lambda e: e.transpose(out=pc, in_=QT, identity=ident), reads=["qc"], writes=[kc])
                    T.op('dve', lambda e: e.tensor_scalar(out=Qg[h][0], in0=pc, scalar1=sm["gam"][:, h:h + 1], scalar2=None, op0=ALU.mult),
                         reads=[kc, "gam%d" % bp], writes=[hk("Qg", h)])
                if SUB < 5:
                    continue
                for h in range(NH):
                    QT, KT = qc[:, h, cols], qc[:, 4 + h, cols]
                    pa, ka = slotA(h)
                    T.op('pe', lambda e: e.matmul(pa, lhsT=Rr[h][TI], rhs=Vb[h][0], start=True, stop=True),
                         reads=[hk("Rr", h, TI), hk("Vb", h)], writes=[ka])
                    T.op('act', lambda e: e.activation(out=uv[h][bp], in_=pa, func=AF.Copy), reads=[ka], writes=[hk("uv", h, bp)])
                    pb, kb = slotB(h)
                    T.op('pe', lambda e: e.matmul(pb, lhsT=Rr[h][TI], rhs=Kbgn[h][0], start=True, stop=True),
                         reads=[hk("Rr", h, TI), hk("Kbgn", h)], writes=[kb])
                    T.op('dve', lambda e: e.tensor_copy(out=wkn[h][0], in_=pb), reads=[kb], writes=[hk("wkn", h)])
                    pc, kc = slotC(h)
                    T.op('pe', lambda e: e.matmul(pc, lhsT=KT, rhs=QT, start=True, stop=True), reads=["qc"], writes=[kc])
                    T.op('dve', lambda e: e.tensor_tensor(out=PT[h][bp], in0=pc, in1=decT[h][0], op=ALU.mult),
                         reads=[kc, hk("decT", h)], writes=[hk("PT", h, bp)])
                if SUB < 6:
                    continue
                for h in range(NH):
                    pa, ka = slotA(h)
                    T.op('pe', lambda e: e.matmul(pa, lhsT=Qg[h][0], rhs=ident, start=True, stop=False), reads=[hk("Qg", h)], writes=[ka])
                    T.op('pe', lambda e: e.matmul(pa, lhsT=wkn[h][0], rhs=PT[h][bp], start=False, stop=True),
                         reads=[hk("wkn", h), hk("PT", h, bp)], writes=[ka])
                    T.op('act', lambda e: e.activation(out=qeT[h][bp], in_=pa, func=AF.Copy), reads=[ka], writes=[hk("qeT", h, bp)])
                    for c in range(2):
                        pp, kp = (slotB(h) if c == 0 else slotC(h))
                        rows = slice(c * 64, (c + 1) * 64)
                        T.op('pe', lambda e: e.matmul(pp, lhsT=wkn[h][0][rows, :], rhs=kdec[h][bp][rows, :], start=True, stop=True),
                             reads=[hk("wkn", h), hk("kdec", h, bp)], writes=[kp])
                        T.op('dve', lambda e: e.scalar_tensor_tensor(out=FT[h][bp * 2 + c], in0=ident,
                                                                     scalar=sm["egl"][:, h * 2 + c:h * 2 + c + 1], in1=pp,
                                                                     op0=ALU.mult, op1=ALU.add),
                             reads=[kp, "egl%d" % bp], writes=[hk("FT", h, bp * 2 + c)])
                for c in range(2 if STAGE >= 4 else 0):
                    rows = slice(c * 64, (c + 1) * 64)
                    if kind == "s":
                        seq = blk * 2 + c
                        T.dma('sp', Sst[s_cur], A["S0"][seq].rearrange("h k v -> k h v"), writes=["S%dbuf" % s_cur])
                    Sc = Sst[s_cur]
                    skey = "S%dbuf" % s_cur
                    op_ = ostage[chunk_ctr % 2]
                    okey = "ost%d" % (chunk_ctr % 2)
                    for h in range(NH):
                        oo = ps[7][0:64, h * 128:(h + 1) * 128]
                        T.op('pe', lambda e: e.matmul(oo, lhsT=qeT[h][bp][:, rows], rhs=Sc[:, h, :], start=True, stop=False),
                             reads=[hk("qeT", h, bp), skey], writes=["ps7"])
                        T.op('pe', lambda e: e.matmul(oo, lhsT=PT[h][bp][rows, rows], rhs=uv[h][bp][rows, :], start=False, stop=True),
                             reads=[hk("PT", h, bp), hk("uv", h, bp)], writes=["ps7"])
                    T.op('act', lambda e: e.activation(out=op_, in_=ps[7][0:64, :], func=AF.Copy), reads=["ps7"], writes=[okey])
                    if kind == "p":
                        row0 = g * 256 + blk * 128 + c * 64
                    else:
                        row0 = SEQ + blk * 128 + c * 64
                    T.dma('sp', A["o_all"][row0:row0 + 64, :], op_, reads=[okey])
                    s_nxt = 1 - s_cur
                    for h in range(NH):
                        so = ps[6][:, h * 128:(h + 1) * 128]
                        T.op('pe', lambda e: e.matmul(so, lhsT=FT[h][bp * 2 + c], rhs=Sc[:, h, :], start=True, stop=False),
                             reads=[hk("FT", h, bp * 2 + c), skey], writes=["ps6"])
                        T.op('pe', lambda e: e.matmul(so, lhsT=kdec[h][bp][rows, :], rhs=uv[h][bp][rows, :], start=False, stop=True),
                             reads=[hk("kdec", h, bp), hk("uv", h, bp)], writes=["ps6"])
                    T.op('dve', lambda e: e.tensor_copy(out=Sst[s_nxt].rearrange("p h v -> p (h v)"), in_=ps[6]),
                         reads=["ps6"], writes=["S%dbuf" % s_nxt])
                    if kind == "s":
                        seq = blk * 2 + c
                        T.dma('sp', A["dn_s"][seq].rearrange("h k v -> k h v"), Sst[s_nxt], reads=["S%dbuf" % s_nxt])
                    s_cur = s_nxt
                    chunk_ctr += 1
            if kind == "p" and g == min(32, NGROUP_DBG) - 1:
                T.dma('sp', A["dn_p"].rearrange("h k v -> k h v"), Sst[s_cur], reads=["S%dbuf" % s_cur])


_CACHE = {}


def kernel(**inp):
    f = lambda a: np.ascontiguousarray(np.asarray(a, dtype=np.float32))
    x_prompt, x_sample = f(inp["x_prompt"]), f(inp["x_sample"])
    if "prog" not in _CACHE:
        _CACHE["prog"] = build()
    P = _CACHE["prog"]
    consts = _consts()
    pu = f(inp["peer_u"])[0]
    shared = {
        "gmix_col": np.ascontiguousarray(f(inp["g_mix"])[0].reshape(8, 128).T),
        "gffn_col": np.ascontiguousarray(f(inp["g_ffn"])[0].reshape(8, 128).T),
        "gfin_bc": rep128(f(inp["g_final"])),
        "gdn_bc": rep128(np.tile(f(inp["g_dn_out"])[0], 4)),
        "pscale_col": np.ascontiguousarray(f(inp["pool_scale"])[0].reshape(4, 128).T),
        "w_grp": f(inp["w_pool_grp"])[0],
        "keysT": np.ascontiguousarray(f(inp["peer_sub_keys"])[0].transpose(0, 1, 3, 2).reshape(16, 128, 128)),
        "w_up_pool": f(inp["w_up_pool"])[0],
        "w_up_dn": f(inp["w_up_dn"])[0],
        "w_out": f(inp["w_out"])[0],
        "w_q": np.ascontiguousarray(f(inp["w_peer_q"])[0].reshape(D_MODEL, 2048)),
        "UTb": np.ascontiguousarray(pu.reshape(128, 128, 8, 128).transpose(0, 3, 2, 1)),
        "peer_v": f(inp["peer_v"])[0],
    }
    in_maps = []
    for c in range(NCORE):
        b, j = c // 4, c % 4
        m = {
            "xseq": x_prompt[b],
            "xs": x_sample[4 * c:4 * c + 4].reshape(STOK, D_MODEL),
            "S0": f(inp["state_dn"])[0, 4 * c:4 * c + 4],
            "conv0T": np.ascontiguousarray(f(inp["state_dn_conv"])[0, 4 * c:4 * c + 4].transpose(0, 2, 1)),
            "w_in": f(inp["w_in"])[0],
            "gmix_bc": rep128(f(inp["g_mix"])[0]),
            "wconvT": np.ascontiguousarray(f(inp["w_conv"])[0].T),
            "alog_bc": rep128(f(inp["a_log"])[0]),
            "dtb_bc": rep128(f(inp["dt_bias"])[0]),
        }
        for k, v in consts.items():
            m["c_" + k] = v
        m.update(shared)
        m["xown"] = np.ascontiguousarray(np.concatenate([x_prompt[b, j * QTOK:(j + 1) * QTOK], m["xs"]], axis=0))
        m["xhalo"] = np.ascontiguousarray(x_prompt[b, j * QTOK - 16:j * QTOK]) if j > 0 else np.zeros((16, D_MODEL), np.float32)
        m["pool0"] = f(inp["cache_pool"])[0, 4 * c:4 * c + 4]
        selv = np.zeros((128, 4), np.float32)
        selv[:, j] = 1.0
        m["sel"] = selv
        m.update(_bands(j))
        in_maps.append(m)
    res = run_bass_kernel_spmd(P.nc, in_maps, core_ids=list(range(NCORE)))
    R = res.results
    _CACHE["last"] = R
    new_dn_p = np.stack([R[0]["dn_p"], R[4]["dn_p"]])[None]
    new_dn_s = np.concatenate([R[c]["dn_s"] for c in range(NCORE)])[None]
    new_conv_p = np.stack([R[0]["conv_pT"].T, R[4]["conv_pT"].T])[None]
    new_conv_s = np.concatenate([R[c]["conv_sT"].transpose(0, 2, 1) for c in range(NCORE)])[None]
    y_prompt = np.stack([np.concatenate([R[4 * b + j]["y"][:QTOK] for j in range(4)]) for b in range(2)])
    y_sample = np.concatenate([R[c]["y"][QTOK:].reshape(4, 64, D_MODEL) for c in range(NCORE)])
    new_pool_p = np.stack([R[3]["pool_p"], R[7]["pool_p"]])[None]
    new_pool_s = np.concatenate([R[c]["pool_s"] for c in range(NCORE)])[None]
    f32 = lambda a: np.ascontiguousarray(a, dtype=np.float32)
    return (f32(y_prompt), f32(y_sample), f32(new_pool_p), f32(new_conv_p), f32(new_dn_p),
            f32(new_pool_s), f32(new_conv_s), f32(new_dn_s))


TT = 256
NTILE = OWN // TT
POOL_W = (2, 4, 8, 16)


def _bands(j):
    t = np.arange(128)
    out = {}
    main, prev, first, samp, sprev = [], [], [], [], []
    for w in POOL_W:
        src, dst = t[:, None], t[None, :]
        inwin = (src <= dst) & (src > dst - w)
        m = inwin / float(w) - np.eye(128)
        main.append(m)
        pw = ((src - 128) > dst - w) & (src >= 112)
        prev.append(pw / float(w))
        if j == 0:
            cnt = np.minimum(dst + 1, w).astype(np.float64)
            first.append(inwin / cnt - np.eye(128))
        else:
            first.append(m)
        same = (src // 64) == (dst // 64)
        samp.append((inwin & same) / float(w) - np.eye(128))
        sp = np.zeros((128, 128))
        for (r0, c0) in ((97, 0), (113, 64)):
            for i in range(15):
                for d in range(64):
                    if (-15 + i) > d - w:
                        sp[r0 + i, c0 + d] = 1.0 / w
        sprev.append(sp)
    f = lambda l: np.ascontiguousarray(np.stack(l).transpose(1, 0, 2).astype(np.float32))
    return dict(bmain=f(main), bprev=f(prev), bfirst=f(first), bsamp=f(samp), bsprev=f(sprev))


def phase_b(P, C, ps, B):
    nc, T = P.nc, P.T
    ident, iota = C["ident"], C["iota"]
    with ExitStack() as es:
        big = sb(nc, es, "big", [128, 16384])
        def reg(off, shape, dt=F32):
            n = int(np.prod(shape[1:]))
            v = big[:, off:off + n]
            if dt != F32:
                v = v.bitcast(dt)
            if len(shape) == 3:
                v = v.rearrange("p (a b) -> p a b", a=shape[1])
            return v
        xnT = reg(0, [128, 8, TT], F32R)
        wb = [reg(2048 + 2048 * i, [128, 8, 256], F32R) for i in range(3)]
        yaT = reg(8192, [128, 4, TT], F32R)
        ztok, ztok_w = reg(9216, [128, 2, 512]), reg(9216, [128, 2, 512], F32R)
        ybT = reg(10240, [128, 4, TT], F32R)
        mtok, mtok_w = reg(11264, [128, 2, 1024]), reg(11264, [128, 2, 1024], F32R)
        mT = reg(13312, [128, 8, TT], F32R)
        xnTh = reg(15360, [128, 8, 128], F32R)
        utok, utok_w = reg(11264, [128, 512]), reg(11264, [128, 512], F32R)
        otok, otok_w = reg(11776, [128, 512]), reg(11776, [128, 512], F32R)
        ocand, ocand_w = reg(12288, [128, 4, 512]), reg(12288, [128, 4, 512], F32R)
        cand, cand_w = reg(11264, [128, 2048]), reg(11264, [128, 2048], F32R)
        eqb, eqb_w = reg(13312, [128, 2048]), reg(13312, [128, 2048], F32R)
        Wsb = big.rearrange("p (e t) -> p e t", e=64)
        Wsb_w = big.bitcast(F32R).rearrange("p (e t) -> p e t", e=64)
        xres = sb(nc, es, "xres", [128, 2, 1024])
        xn2T = sb(nc, es, "xn2T", [128, 8, TT], F32R)
        ublk = [sb(nc, es, "ublk%d" % i, [128, 8, 128], F32R) for i in range(3)]
        vblk = [sb(nc, es, "vblk%d" % i, [128, 1024], F32R) for i in range(3)]
        actb = [sb(nc, es, "actb%d" % i, [128, TT]) for i in range(2)]
        gab = [sb(nc, es, "gab%d" % i, [128, TT], F32R) for i in range(2)]
        e1T = sb(nc, es, "e1T", [128, TT])
        e2T = sb(nc, es, "e2T", [128, TT])
        gT = sb(nc, es, "gT", [128, TT])
        uprev = sb(nc, es, "uprev", [128, 512])
        xh = sb(nc, es, "xh", [128, 1024])
        xnb = sb(nc, es, "xnb", [128, 1024])
        ss = sb(nc, es, "ssb", [128, 1])
        rs = sb(nc, es, "rsb", [128, 1])
        rstd = sb(nc, es, "rstdb", [128, 1])
        pooledT = sb(nc, es, "pooledT", [128, 128])
        qTsb = sb(nc, es, "qTsb", [128, 2, TT])
        sga = sb(nc, es, "sga", [128, 256])
        sgb = sb(nc, es, "sgb", [128, 256])
        sq4 = sb(nc, es, "sq4", [128, 4])
        B_ = {"v1": [sb(nc, es, "v1_%d" % i, [128, 8, 16]) for i in range(2)],
              "v2": [sb(nc, es, "v2_%d" % i, [128, 8, 16]) for i in range(2)],
              "i1u": [sb(nc, es, "i1u_%d" % i, [128, 8, 16], U32) for i in range(2)],
              "i2u": [sb(nc, es, "i2u_%d" % i, [128, 8, 16], U32) for i in range(2)]}
        i1f = sb(nc, es, "i1f", [128, 8, 16])
        i2f = sb(nc, es, "i2f", [128, 8, 16])
        tmpk = sb(nc, es, "tmpk", [128, 128])
        tmp2 = sb(nc, es, "tmp2", [128, 256])
        cv = sb(nc, es, "cv", [128, 8, 16])
        ciu = sb(nc, es, "ciu", [128, 8, 16], U32)
        abu = sb(nc, es, "abu", [128, 8, 16], U32)
        af = sb(nc, es, "af", [128, 8, 16])
        bf = sb(nc, es, "bf", [128, 8, 16])
        ex = sb(nc, es, "ex", [128, 8, 16])
        s8 = sb(nc, es, "s8", [128, 8])
        e1f = sb(nc, es, "e1f", [128, 128])
        e2f = sb(nc, es, "e2f", [128, 128])
        gsl = sb(nc, es, "gsl", [128, 128])
        oha = [sb(nc, es, "oha%d" % i, [128, 64], F32R) for i in range(4)]
        ohb = [sb(nc, es, "ohb%d" % i, [128, 128], F32R) for i in range(4)]
        gmixc = sb(nc, es, "gmixc", [128, 8]); T.dma('sp', gmixc, B["gmix_col"])
        gffnc = sb(nc, es, "gffnc", [128, 8]); T.dma('sp', gffnc, B["gffn_col"])
        gfin = sb(nc, es, "gfin", [128, 1024]); T.dma('sp', gfin, B["gfin_bc"])
        gdn = sb(nc, es, "gdn", [128, 512]); T.dma('sp', gdn, B["gdn_bc"])
        pscale = sb(nc, es, "pscale", [128, 4]); T.dma('sp', pscale, B["pscale_col"])
        selb = sb(nc, es, "selb", [128, 4]); T.dma('sp', selb, B["sel"])
        wgrp = sb(nc, es, "wgrp", [128, 4, 128]); T.dma('sp', wgrp, B["w_grp"].rearrange("g c d -> c g d"))
        keysT = sb(nc, es, "keysT", [128, 16, 128]); T.dma('sp', keysT, B["keysT"].rearrange("q d k -> d q k"))
        bands = {}
        for k in ("bmain", "bprev", "bfirst", "bsamp", "bsprev"):
            bands[k] = sb(nc, es, k, [128, 4, 128]); T.dma('sp', bands[k], B[k])
        T.barrier()

        w_in, w_upp, w_upd, w_out, w_q = B["w_in"], B["w_up_pool"], B["w_up_dn"], B["w_out"], B["w_q"]
        wrr = [0]

        def load_w(src2d, c0, ncol, kdim):
            i = wrr[0] % 3
            wrr[0] += 1
            dst = wb[i][:, 0:kdim, 0:ncol]
            T.dma('pool', dst, src2d[:, c0:c0 + ncol].rearrange("(k p) c -> p k c", p=128), writes=["wb%d" % i])
            return wb[i], "wb%d" % i

        def norm_block(xsrc, xkey, gcol, dstT, dkey, tcols):
            T.op('act', lambda e: e.activation(out=xnb, in_=xsrc, func=AF.Square, accum_out=ss), reads=[xkey], writes=["xnb", "ssb"])
            T.op('act', lambda e: e.activation(out=rs, in_=ss, func=AF.Sqrt, bias=EPS, scale=1.0 / D_MODEL), reads=["ssb"], writes=["rsb"])
            T.op('dve', lambda e: e.reciprocal(out=rstd, in_=rs), reads=["rsb"], writes=["rstdb"])
            T.op('dve', lambda e: e.tensor_scalar(out=xnb, in0=xsrc, scalar1=rstd, scalar2=None, op0=ALU.mult),
                 reads=[xkey, "rstdb"], writes=["xnb"])
            for half in range(2):
                for q in range(4):
                    dc = half * 4 + q
                    T.op('pe', lambda e: e.transpose(out=ps[half][:, q * 128:(q + 1) * 128], in_=xnb[:, dc * 128:(dc + 1) * 128], identity=ident),
                         reads=["xnb"], writes=["ps%d" % half])
                for q in range(4):
                    dc = half * 4 + q
                    T.op('act' if q % 2 else 'dve',
                         (lambda e: e.activation(out=dstT[:, dc, tcols], in_=ps[half][:, q * 128:(q + 1) * 128], func=AF.Copy, scale=gcol[:, dc:dc + 1]))
                         if q % 2 else
                         (lambda e: e.tensor_scalar(out=dstT[:, dc, tcols], in0=ps[half][:, q * 128:(q + 1) * 128], scalar1=gcol[:, dc:dc + 1], scalar2=None, op0=ALU.mult)),
                         reads=["ps%d" % half], writes=[dkey])

        for ti in (TILES if STAGE >= 5 else []):
            samp = (ti == NTILE - 1)
            blks = [0, 1]
            if ti == 0:
                T.op('pool', lambda e: e.memset(xh, 0.0), writes=["xh"])
                T.dma('sp', xh[112:128, :], B["xhalo"], writes=["xh"])
                norm_block(xh, "xh", gmixc, xnTh, "xnTh", slice(0, 128))
            for blk in blks:
                row0 = ti * TT + blk * 128
                T.dma('sp', xres[:, blk, :], B["xown"][row0:row0 + 128, :], writes=["xres%d" % blk])
                norm_block(xres[:, blk, :], "xres%d" % blk, gmixc, xnT, "xnT", slice(blk * 128, (blk + 1) * 128))
            wu = [load_w(w_in, c * 256, 256, 8) for c in range(2)]
            if ti == 0:
                for c in range(2):
                    for dc in range(8):
                        T.op('pe', lambda e: e.matmul(ps[4][:, c * 256:(c + 1) * 256], lhsT=xnTh[:, dc, :], rhs=wu[c][0][:, dc, :],
                                                      start=(dc == 0), stop=(dc == 7)), reads=["xnTh", wu[c][1]], writes=["ps4"])
                T.op('act', lambda e: e.activation(out=uprev, in_=ps[4], func=AF.Copy), reads=["ps4"], writes=["uprev"])
            if samp:
                T.op('pool', lambda e: e.memset(uprev, 0.0), writes=["uprev"])
            for c in range(2):
                for blk in blks:
                    for dc in range(8):
                        T.op('pe', lambda e: e.matmul(ps[2 + blk][:, c * 256:(c + 1) * 256], lhsT=xnT[:, dc, blk * 128:(blk + 1) * 128],
                                                      rhs=wu[c][0][:, dc, :], start=(dc == 0), stop=(dc == 7)),
                             reads=["xnT", wu[c][1]], writes=["ps%d" % (2 + blk)])
            for blk in blks:
                bi = ti * 2 + blk
                T.op('act', lambda e: e.activation(out=utok_w, in_=ps[2 + blk], func=AF.Copy), reads=["ps%d" % (2 + blk)], writes=["utok"])
                if samp:
                    for sq_ in range(2):
                        seq = blk * 2 + sq_
                        r0 = 97 if sq_ == 0 else 113
                        T.dma('sp', uprev[r0:r0 + 15, :], B["pool0"][seq], writes=["uprev"])
                        T.dma('sp', B["pool_s"][seq], utok[sq_ * 64 + 49: sq_ * 64 + 64, :], reads=["utok"])
                    bm, bp = bands["bsamp"], bands["bsprev"]
                else:
                    bm, bp = (bands["bfirst"] if bi == 0 else bands["bmain"]), bands["bprev"]
                    if bi == 15:
                        T.dma('sp', B["pool_p"], utok[113:128, :], reads=["utok"])
                for gi in range(4):
                    gc_ = slice(gi * 128, (gi + 1) * 128)
                    T.op('pe', lambda e: e.matmul(ps[4][:, 0:128], lhsT=uprev[:, gc_], rhs=bp[:, gi, :], start=True, stop=False),
                         reads=["uprev"], writes=["ps4"])
                    T.op('pe', lambda e: e.matmul(ps[4][:, 0:128], lhsT=utok[:, gc_], rhs=bm[:, gi, :], start=False, stop=True),
                         reads=["utok"], writes=["ps4"])
                    T.op('dve', lambda e: e.tensor_copy(out=pooledT, in_=ps[4][:, 0:128]), reads=["ps4"], writes=["pooledT"])
                    T.op('pe', lambda e: e.matmul(ps[5][:, 0:128], lhsT=wgrp[:, gi, :], rhs=pooledT, start=True, stop=True),
                         reads=["pooledT"], writes=["ps5"])
                    T.op('act', lambda e: e.activation(out=yaT[:, gi, blk * 128:(blk + 1) * 128], in_=ps[5][:, 0:128], func=AF.Copy,
                                                       scale=pscale[:, gi:gi + 1]), reads=["ps5"], writes=["yaT"])
                if not samp:
                    T.op('pool', lambda e: e.tensor_copy(out=uprev, in_=utok), reads=["utok"], writes=["uprev"])
            wz = [load_w(w_in, OFF_Z + c * 256, 256, 8) for c in range(2)]
            for c in range(2):
                for blk in blks:
                    for dc in range(8):
                        T.op('pe', lambda e: e.matmul(ps[2 + blk][:, c * 256:(c + 1) * 256], lhsT=xnT[:, dc, blk * 128:(blk + 1) * 128],
                                                      rhs=wz[c][0][:, dc, :], start=(dc == 0), stop=(dc == 7)),
                             reads=["xnT", wz[c][1]], writes=["ps%d" % (2 + blk)])
            for blk in blks:
                bi = ti * 2 + blk
                T.op('act', lambda e: e.activation(out=ztok_w[:, blk, :], in_=ps[2 + blk], func=AF.Silu), reads=["ps%d" % (2 + blk)], writes=["ztok"])
                if samp:
                    T.dma('pool', otok_w, B["o_all"][SEQ + blk * 128: SEQ + (blk + 1) * 128, :], writes=["otok"])
                elif NGROUP_DBG < 32:
                    T.dma('pool', otok_w, B["o_all"][bi * 128:(bi + 1) * 128, :], writes=["otok"])
                else:
                    T.dma('pool', ocand_w, B["o_all"][0:SEQ, :].rearrange("(q r) c -> r q c", q=4)[bi * 128:(bi + 1) * 128], writes=["ocand"])
                    T.op('dve', lambda e: e.tensor_scalar(out=otok_w, in0=ocand[:, 0, :], scalar1=selb[:, 0:1], scalar2=None, op0=ALU.mult),
                         reads=["ocand"], writes=["otok"])
                    for q in range(1, 4):
                        T.op('dve', lambda e: e.scalar_tensor_tensor(out=otok_w, in0=ocand[:, q, :], scalar=selb[:, q:q + 1], in1=otok,
                                                                     op0=ALU.mult, op1=ALU.add), reads=["ocand", "otok"], writes=["otok"])
                T.op('act', lambda e: e.activation(out=xnb[:, 0:512], in_=otok, func=AF.Square), reads=["otok"], writes=["xnb"])
                T.op('dve', lambda e: e.tensor_reduce(out=sq4, in_=xnb[:, 0:512].rearrange("p (h v) -> p h v", h=4), axis=AX.X, op=ALU.add),
                     reads=["xnb"], writes=["sq4"])
                T.op('act', lambda e: e.activation(out=sq4, in_=sq4, func=AF.Sqrt, bias=EPS, scale=1.0 / 128), reads=["sq4"], writes=["sq4"])
                T.op('dve', lambda e: e.reciprocal(out=sq4, in_=sq4), reads=["sq4"], writes=["sq4"])
                o3 = otok.rearrange("p (h v) -> p h v", h=4)
                o3w = otok_w.rearrange("p (h v) -> p h v", h=4)
                T.op('dve', lambda e: e.tensor_tensor(out=o3w, in0=o3, in1=sq4.unsqueeze(2).to_broadcast([128, 4, 128]), op=ALU.mult),
                     reads=["otok", "sq4"], writes=["otok"])
                T.op('dve', lambda e: e.tensor_tensor(out=otok_w, in0=otok, in1=gdn, op=ALU.mult), reads=["otok"], writes=["otok"])
                T.op('dve', lambda e: e.tensor_tensor(out=otok_w, in0=otok, in1=ztok[:, blk, :], op=ALU.mult), reads=["otok", "ztok"], writes=["otok"])
                for cc in range(4):
                    T.op('pe', lambda e: e.transpose(out=ps[4][:, cc * 128:(cc + 1) * 128], in_=otok[:, cc * 128:(cc + 1) * 128], identity=ident),
                         reads=["otok"], writes=["ps4"])
                T.op('act', lambda e: e.activation(out=ybT[:, :, blk * 128:(blk + 1) * 128], in_=ps[4].rearrange("p (c t) -> p c t", c=4), func=AF.Copy),
                     reads=["ps4"], writes=["ybT"])
            for n in range(4):
                wga = load_w(w_in, OFF_G + n * 256, 256, 8)
                wgb = load_w(w_in, OFF_G + 1024 + n * 256, 256, 8)
                i = wrr[0] % 3
                wrr[0] += 1
                T.dma('pool', wb[i][:, 0:4, :], w_upp[:, n * 256:(n + 1) * 256].rearrange("(k p) c -> p k c", p=128), writes=["wb%d" % i])
                T.dma('pool', wb[i][:, 4:8, :], w_upd[:, n * 256:(n + 1) * 256].rearrange("(k p) c -> p k c", p=128), writes=["wb%d" % i])
                wup, wupk = wb[i], "wb%d" % i
                for blk in blks:
                    tcs = slice(blk * 128, (blk + 1) * 128)
                    pg, pu = ps[blk * 2], ps[blk * 2 + 1]
                    kg, ku = "ps%d" % (blk * 2), "ps%d" % (blk * 2 + 1)
                    for dc in range(8):
                        T.op('pe', lambda e: e.matmul(pg[:, 0:256], lhsT=xnT[:, dc, tcs], rhs=wga[0][:, dc, :], start=(dc == 0), stop=(dc == 7)),
                             reads=["xnT", wga[1]], writes=[kg])
                    for dc in range(8):
                        T.op('pe', lambda e: e.matmul(pg[:, 256:512], lhsT=xnT[:, dc, tcs], rhs=wgb[0][:, dc, :], start=(dc == 0), stop=(dc == 7)),
                             reads=["xnT", wgb[1]], writes=[kg])
                    for cc in range(4):
                        T.op('pe', lambda e: e.matmul(pu[:, 0:256], lhsT=yaT[:, cc, tcs], rhs=wup[:, cc, :], start=(cc == 0), stop=(cc == 3)),
                             reads=["yaT", wupk], writes=[ku])
                    for cc in range(4):
                        T.op('pe', lambda e: e.matmul(pu[:, 256:512], lhsT=ybT[:, cc, tcs], rhs=wup[:, 4 + cc, :], start=(cc == 0), stop=(cc == 3)),
                             reads=["ybT", wupk], writes=[ku])
                    T.op('act', lambda e: e.activation(out=sga, in_=pg[:, 0:256], func=AF.Sigmoid), reads=[kg], writes=["sga"])
                    T.op('act', lambda e: e.activation(out=sgb, in_=pg[:, 256:512], func=AF.Sigmoid), reads=[kg], writes=["sgb"])
                    T.op('dve', lambda e: e.tensor_tensor(out=sga, in0=pu[:, 0:256], in1=sga, op=ALU.mult), reads=[ku, "sga"], writes=["sga"])
                    T.op('dve', lambda e: e.tensor_tensor(out=sgb, in0=pu[:, 256:512], in1=sgb, op=ALU.mult), reads=[ku, "sgb"], writes=["sgb"])
                    T.op('pool', lambda e: e.tensor_tensor(out=mtok_w[:, blk, n * 256:(n + 1) * 256], in0=sga, in1=sgb, op=ALU.add),
                         reads=["sga", "sgb"], writes=["mtok"])
            for blk in blks:
                for half in range(2):
                    for q in range(4):
                        dc = half * 4 + q
                        T.op('pe', lambda e: e.transpose(out=ps[4 + half][:, q * 128:(q + 1) * 128], in_=mtok[:, blk, dc * 128:(dc + 1) * 128], identity=ident),
                             reads=["mtok"], writes=["ps%d" % (4 + half)])
                    T.op('act' if half else 'dve',
                         (lambda e: e.activation(out=mT[:, half * 4:(half + 1) * 4, blk * 128:(blk + 1) * 128],
                                                 in_=ps[4 + half].rearrange("p (q t) -> p q t", q=4), func=AF.Copy)) if half else
                         (lambda e: e.tensor_copy(out=mT[:, half * 4:(half + 1) * 4, blk * 128:(blk + 1) * 128],
                                                  in_=ps[4 + half].rearrange("p (q t) -> p q t", q=4))),
                         reads=["ps%d" % (4 + half)], writes=["mT"])
            for n in range(4):
                wo = load_w(w_out, n * 256, 256, 8)
                for blk in blks:
                    pk = 6 + blk
                    for dc in range(8):
                        T.op('pe', lambda e: e.matmul(ps[pk][:, 0:256], lhsT=mT[:, dc, blk * 128:(blk + 1) * 128], rhs=wo[0][:, dc, :],
                                                      start=(dc == 0), stop=(dc == 7)), reads=["mT", wo[1]], writes=["ps%d" % pk])
                    T.op('dve', lambda e: e.tensor_tensor(out=xres[:, blk, n * 256:(n + 1) * 256], in0=ps[pk][:, 0:256],
                                                          in1=xres[:, blk, n * 256:(n + 1) * 256], op=ALU.add),
                         reads=["ps%d" % pk, "xres%d" % blk], writes=["xres%d" % blk])
            for blk in blks:
                norm_block(xres[:, blk, :], "xres%d" % blk, gffnc, xn2T, "xn2T", slice(blk * 128, (blk + 1) * 128))
            if STAGE < 6:
                for blk in blks:
                    row0 = ti * TT + blk * 128
                    T.dma('sp', B["y"][row0:row0 + 128, :], xres[:, blk, :], reads=["xres%d" % blk])
                T.barrier()
                continue
            for qc_ in range(8):
                wq = load_w(w_q, qc_ * 256, 256, 8)
                for gq in range(2):
                    for dc in range(8):
                        T.op('pe', lambda e: e.matmul(ps[2][:, gq * 256:(gq + 1) * 256], lhsT=wq[0][:, dc, gq * 128:(gq + 1) * 128],
                                                      rhs=xn2T[:, dc, :], start=(dc == 0), stop=(dc == 7)),
                             reads=["xn2T", wq[1]], writes=["ps2"])
                T.op('act', lambda e: e.activation(out=qTsb, in_=ps[2].rearrange("p (g t) -> p g t", g=2), func=AF.Copy), reads=["ps2"], writes=["qTsb"])
                for gq in range(2):
                    cq = qc_ * 2 + gq
                    hh, half = cq // 2, cq % 2
                    for blk in blks:
                        bk = "B%d_" % blk
                        pk = 3 + blk
                        T.op('pe', lambda e: e.matmul(ps[pk][:, 0:128], lhsT=qTsb[:, gq, blk * 128:(blk + 1) * 128],
                                                      rhs=keysT[:, half * 8 + hh, :], start=True, stop=True),
                             reads=["qTsb"], writes=["ps%d" % pk])
                        vvb, iub = B_["v%d" % (half + 1)][blk], B_["i%du" % (half + 1)][blk]
                        T.op('dve', lambda e: e.max(out=vvb[:, hh, 0:8], in_=ps[pk][:, 0:128]), reads=["ps%d" % pk], writes=[bk + "v"])
                        T.op('dve', lambda e: e.match_replace(out=tmpk, in_to_replace=vvb[:, hh, 0:8], in_values=ps[pk][:, 0:128], imm_value=-1e30),
                             reads=["ps%d" % pk, bk + "v"], writes=["tmpk"])
                        T.op('dve', lambda e: e.max(out=vvb[:, hh, 8:16], in_=tmpk), reads=["tmpk"], writes=[bk + "v"])
                        T.op('dve', lambda e: e.max_index(out=iub[:, hh, 0:8], in_max=vvb[:, hh, 0:8], in_values=ps[pk][:, 0:128]),
                             reads=["ps%d" % pk, bk + "v"], writes=[bk + "i"])
                        T.op('dve', lambda e: e.max_index(out=iub[:, hh, 8:16], in_max=vvb[:, hh, 8:16], in_values=ps[pk][:, 0:128]),
                             reads=["ps%d" % pk, bk + "v"], writes=[bk + "i"])
            for blk in blks:
                bk = "B%d_" % blk
                v1b, v2b, i1b, i2b = B_["v1"][blk], B_["v2"][blk], B_["i1u"][blk], B_["i2u"][blk]
                tcs = slice(blk * 128, (blk + 1) * 128)
                T.op('dve', lambda e: e.tensor_copy(out=i1f, in_=i1b), reads=[bk + "i"], writes=["i1f"])
                T.op('dve', lambda e: e.tensor_copy(out=i2f, in_=i2b), reads=[bk + "i"], writes=["i2f"])
                c4 = cand_w.rearrange("p (h a b) -> p h a b", h=8, a=16)
                T.op('dve', lambda e: e.tensor_tensor(out=c4, in0=v1b.unsqueeze(3).to_broadcast([128, 8, 16, 16]),
                                                      in1=v2b.unsqueeze(2).to_broadcast([128, 8, 16, 16]), op=ALU.add),
                     reads=[bk + "v"], writes=["cand"])
                c3 = cand.rearrange("p (h x) -> p h x", h=8)
                for hh in range(8):
                    T.op('dve', lambda e: e.max(out=cv[:, hh, 0:8], in_=c3[:, hh, :]), reads=["cand"], writes=["cv"])
                    T.op('dve', lambda e: e.match_replace(out=tmp2, in_to_replace=cv[:, hh, 0:8], in_values=c3[:, hh, :], imm_value=-1e30),
                         reads=["cand", "cv"], writes=["tmp2"])
                    T.op('dve', lambda e: e.max(out=cv[:, hh, 8:16], in_=tmp2), reads=["tmp2"], writes=["cv"])
                    T.op('dve', lambda e: e.max_index(out=ciu[:, hh, 0:8], in_max=cv[:, hh, 0:8], in_values=c3[:, hh, :]), reads=["cand", "cv"], writes=["ciu"])
                    T.op('dve', lambda e: e.max_index(out=ciu[:, hh, 8:16], in_max=cv[:, hh, 8:16], in_values=c3[:, hh, :]), reads=["cand", "cv"], writes=["ciu"])
                T.op('dve', lambda e: e.tensor_tensor(out=ex, in0=cv, in1=cv[:, :, 0:1].to_broadcast([128, 8, 16]), op=ALU.subtract),
                     reads=["cv"], writes=["ex"])
                T.op('act', lambda e: e.activation(out=ex, in_=ex, func=AF.Exp), reads=["ex"], writes=["ex"])
                T.op('dve', lambda e: e.tensor_reduce(out=s8, in_=ex, axis=AX.X, op=ALU.add), reads=["ex"], writes=["s8"])
                T.op('dve', lambda e: e.reciprocal(out=s8, in_=s8), reads=["s8"], writes=["s8"])
                T.op('dve', lambda e: e.tensor_tensor(out=gsl.rearrange("p (h k) -> p h k", h=8), in0=ex,
                                                      in1=s8.unsqueeze(2).to_broadcast([128, 8, 16]), op=ALU.mult),
                     reads=["ex", "s8"], writes=["gsl"])
                T.op('dve', lambda e: e.tensor_scalar(out=abu, in0=ciu, scalar1=4, scalar2=None, op0=ALU.logical_shift_right), reads=["ciu"], writes=["abu"])
                T.op('dve', lambda e: e.tensor_copy(out=af, in_=abu), reads=["abu"], writes=["af"])
                T.op('dve', lambda e: e.tensor_scalar(out=abu, in0=ciu, scalar1=15, scalar2=None, op0=ALU.bitwise_and), reads=["ciu", "af"], writes=["abu"])
                T.op('dve', lambda e: e.tensor_copy(out=bf, in_=abu), reads=["abu"], writes=["bf"])
                for (sel_f, idx_f, dst) in ((af, i1f, e1f), (bf, i2f, e2f)):
                    e3 = eqb.rearrange("p (s a) -> p s a", a=16)
                    e3w = eqb_w.rearrange("p (s a) -> p s a", a=16)
                    T.op('dve', lambda e: e.tensor_tensor(out=e3w, in0=sel_f.rearrange("p h k -> p (h k)").unsqueeze(2).to_broadcast([128, 128, 16]),
                                                          in1=iota[:, 0:16].unsqueeze(1).to_broadcast([128, 128, 16]), op=ALU.is_equal),
                         reads=["af", "bf"], writes=["eqb"])
                    e4 = eqb.rearrange("p (h k a) -> p h k a", h=8, k=16)
                    e4w = eqb_w.rearrange("p (h k a) -> p h k a", h=8, k=16)
                    T.op('dve', lambda e: e.tensor_tensor(out=e4w, in0=e4, in1=idx_f.unsqueeze(2).to_broadcast([128, 8, 16, 16]), op=ALU.mult),
                         reads=["eqb", "i1f", "i2f"], writes=["eqb"])
                    T.op('dve', lambda e: e.tensor_reduce(out=dst, in_=e3, axis=AX.X, op=ALU.add), reads=["eqb"], writes=["e12f"])
                for (src, dstT, nm) in ((e1f, e1T, "e1T"), (e2f, e2T, "e2T"), (gsl, gT, "gT")):
                    T.op('pe', lambda e: e.transpose(out=ps[5][:, 0:128], in_=src, identity=ident), reads=["e12f", "gsl"], writes=["ps5"])
                    T.op('act', lambda e: e.activation(out=dstT[:, tcs], in_=ps[5][:, 0:128], func=AF.Copy), reads=["ps5"], writes=[nm])
            T.barrier()
            for hf in range(2):
                for t in range(TT):
                    a_ = oha[t % 4]
                    b_ = ohb[t % 4]
                    T.op('dve', lambda e: e.tensor_scalar(out=a_, in0=iota[:, hf * 64:(hf + 1) * 64], scalar1=e1T[:, t:t + 1], scalar2=gT[:, t:t + 1],
                                                          op0=ALU.is_equal, op1=ALU.mult), reads=["e1T", "gT"], writes=["oha%d" % (t % 4)])
                    T.op('dve', lambda e: e.tensor_scalar(out=b_, in0=iota, scalar1=e2T[:, t:t + 1], scalar2=None, op0=ALU.is_equal),
                         reads=["e2T"], writes=["ohb%d" % (t % 4)])
                    pk = 6 + (t // 8) % 2
                    T.op('pe', lambda e: e.matmul(ps[pk][:, (t % 8) * 64:(t % 8 + 1) * 64], lhsT=b_, rhs=a_, start=True, stop=True),
                         reads=["oha%d" % (t % 4), "ohb%d" % (t % 4)], writes=["ps%d" % pk])
                    if t % 8 == 7:
                        t0 = t - 7
                        T.op('act', lambda e: e.activation(out=Wsb_w[:, :, t0:t0 + 8].rearrange("p e t -> p t e"),
                                                           in_=ps[pk].rearrange("p (t e) -> p t e", t=8), func=AF.Copy),
                             reads=["ps%d" % pk], writes=["Wsb"])
                for i in range(64):
                    e1 = hf * 64 + i
                    ub, vb = ublk[e1 % 3], vblk[e1 % 3]
                    uk, vk = "ublk%d" % (e1 % 3), "vblk%d" % (e1 % 3)
                    T.dma('pool', ub, B["UTb"][e1], writes=[uk])
                    T.dma('pool', vb, B["peer_v"][e1 * 128:(e1 + 1) * 128, :], writes=[vk])
                    pk = 4 + e1 % 2
                    for dc in range(8):
                        T.op('pe', lambda e: e.matmul(ps[pk][:, 0:TT], lhsT=ub[:, dc, :], rhs=xn2T[:, dc, :], start=(dc == 0), stop=(dc == 7)),
                             reads=[uk, "xn2T"], writes=["ps%d" % pk])
                    ab, gb_ = actb[e1 % 2], gab[e1 % 2]
                    T.op('act', lambda e: e.activation(out=ab, in_=ps[pk][:, 0:TT], func=AF.Gelu), reads=["ps%d" % pk], writes=["actb%d" % (e1 % 2)])
                    T.op('dve', lambda e: e.tensor_tensor(out=gb_, in0=ab, in1=Wsb[:, i, :], op=ALU.mult),
                         reads=["actb%d" % (e1 % 2), "Wsb"], writes=["gab%d" % (e1 % 2)])
                    for blk in blks:
                        for half in range(2):
                            T.op('pe', lambda e: e.matmul(ps[blk * 2 + half], lhsT=gb_[:, blk * 128:(blk + 1) * 128], rhs=vb[:, half * 512:(half + 1) * 512],
                                                          start=(e1 == 0), stop=(e1 == 127)),
                                 reads=["gab%d" % (e1 % 2), vk], writes=["psy%d" % (blk * 2 + half)])
                T.barrier()
            for blk in blks:
                row0 = ti * TT + blk * 128
                for half in range(2):
                    T.op('dve', lambda e: e.tensor_tensor(out=xres[:, blk, half * 512:(half + 1) * 512], in0=ps[blk * 2 + half],
                                                          in1=xres[:, blk, half * 512:(half + 1) * 512], op=ALU.add),
                         reads=["psy%d" % (blk * 2 + half), "xres%d" % blk], writes=["xres%d" % blk])
                T.op('act', lambda e: e.activation(out=xnb, in_=xres[:, blk, :], func=AF.Square, accum_out=ss), reads=["xres%d" % blk], writes=["xnb", "ssb"])
                T.op('act', lambda e: e.activation(out=rs, in_=ss, func=AF.Sqrt, bias=EPS, scale=1.0 / D_MODEL), reads=["ssb"], writes=["rsb"])
                T.op('dve', lambda e: e.reciprocal(out=rstd, in_=rs), reads=["rsb"], writes=["rstdb"])
                T.op('dve', lambda e: e.scalar_tensor_tensor(out=xnb, in0=xres[:, blk, :], scalar=rstd, in1=gfin, op0=ALU.mult, op1=ALU.mult),
                     reads=["xres%d" % blk, "rstdb"], writes=["xnb"])
                T.dma('sp', B["y"][row0:row0 + 128, :], xnb, reads=["xnb"])
            T.barrier()
```

```python
import os
import numpy as np
from contextlib import ExitStack
import concourse.bass as bass
import concourse.mybir as mybir
from concourse.bass_utils import run_bass_kernel_spmd

F32 = mybir.dt.float32
F32R = mybir.dt.float32r
U32 = mybir.dt.uint32
I32 = mybir.dt.int32
AF = mybir.ActivationFunctionType
ALU = mybir.AluOpType
AX = mybir.AxisListType

D_MODEL = 1024
SEQ = 8192
NCORE = 8
QTOK = 2048
STOK = 256
OWN = QTOK + STOK
D_IN = 4616
OFF_QKV, OFF_Z, OFF_B, OFF_A, OFF_G = 512, 2048, 2560, 2564, 2568
EPS = 1e-6
NEG = -30000.0

STAGE = int(os.environ.get("MK_STAGE", "99"))
NGROUP_DBG = int(os.environ.get("MK_NGROUP", "32"))
SUB = int(os.environ.get("MK_SUB", "99"))
DEBUG_A = int(os.environ.get("MK_DEBUG_A", "0"))
TILES = [int(t) for t in os.environ.get("MK_TILES", "0,1,2,3,4,5,6,7,8").split(",")]


class Trk:
    ROT = int(os.environ.get("MK_ROT", "8000"))
    NDMA = 24

    def __init__(self, nc):
        self.nc = nc
        self.eng = dict(pe=nc.tensor, act=nc.scalar, dve=nc.vector, pool=nc.gpsimd, sp=nc.sync)
        self.sem = {k: nc.alloc_semaphore("prog_%s_0" % k) for k in self.eng}
        self.semgen = {k: 0 for k in self.eng}
        self.cnt = {k: 0 for k in self.eng}
        self.waited = {k: {} for k in self.eng}
        self.lastw = {}
        self.readers = {}
        self.dma_sems = [nc.alloc_semaphore("dmas_%d" % i) for i in range(self.NDMA)]
        self.dma_cnt = [0] * self.NDMA
        self.dma_rr = {'hw': 0, 'sw': 0}
        self.NSW = 8
        self.n_inst = 0

    def _wait(self, e, tok):
        owner, sem, val, semkey = tok
        if owner == e and e == 'pe':
            return
        w = self.waited[e]
        if w.get(semkey, 0) >= val:
            return
        self.eng[e].wait_ge(sem, val)
        w[semkey] = val

    def _deps(self, e, reads, writes):
        for b in reads:
            lw = self.lastw.get(b)
            if lw is not None:
                self._wait(e, lw)
            if b.startswith("ps"):
                for tok in self.readers.get(b, {}).values():
                    if tok[0] != e:
                        self._wait(e, tok)
        for b in writes:
            lw = self.lastw.get(b)
            if lw is not None:
                self._wait(e, lw)
            for tok in self.readers.get(b, {}).values():
                self._wait(e, tok)

    def _record(self, tok, reads, writes):
        for b in reads:
            self.readers.setdefault(b, {})[tok[0]] = tok
        for b in writes:
            self.lastw[b] = tok
            self.readers[b] = {}

    def op(self, e, fn, reads=(), writes=()):
        self._deps(e, reads, writes)
        inst = fn(self.eng[e])
        if self.cnt[e] >= self.ROT:
            self.semgen[e] += 1
            self.sem[e] = self.nc.alloc_semaphore("prog_%s_%d" % (e, self.semgen[e]))
            self.cnt[e] = 0
        self.cnt[e] += 1
        sem = self.sem[e]
        inst.then_inc(sem, 1)
        tok = (e, sem, self.cnt[e], (e, self.semgen[e]))
        self._record(tok, reads, writes)
        self.n_inst += 1
        return inst

    def dma(self, q, out, in_, reads=(), writes=()):
        self._deps(q, reads, writes)
        if q == 'pool':
            k = self.dma_rr['sw']
            self.dma_rr['sw'] = (k + 1) % self.NSW
        else:
            k = self.NSW + self.dma_rr['hw']
            self.dma_rr['hw'] = (self.dma_rr['hw'] + 1) % (self.NDMA - self.NSW)
        sem = self.dma_sems[k]
        if self.dma_cnt[k] > 0:
            self._wait(q, (('dma', k), sem, 16 * self.dma_cnt[k], ('dma', k)))
        inst = self.eng[q].dma_start(out=out, in_=in_)
        self.dma_cnt[k] += 1
        inst.then_inc(sem, 16)
        tok = (('dma', k), sem, 16 * self.dma_cnt[k], ('dma', k))
        self._record(tok, reads, writes)
        self.n_inst += 1
        return inst

    def wait_all_dma(self, e):
        for k in range(self.NDMA):
            if self.dma_cnt[k] > 0:
                self._wait(e, (('dma', k), self.dma_sems[k], 16 * self.dma_cnt[k], ('dma', k)))

    def barrier(self):
        toks = []
        for e2 in self.eng:
            if self.cnt[e2] > 0:
                toks.append((e2, self.sem[e2], self.cnt[e2], (e2, self.semgen[e2])))
        for e in self.eng:
            for tok in toks:
                if tok[0] != e:
                    self._wait(e, tok)
            self.wait_all_dma(e)

    def barrier_consts(self):
        for e in self.eng:
            self.wait_all_dma(e)


def _consts():
    idx = np.arange(128)
    ch = idx // 64
    same = (ch[:, None] == ch[None, :])
    c = {}
    c["ident"] = np.eye(128, dtype=np.float32)
    c["ones"] = np.ones((128, 128), np.float32)
    c["umask"] = (same & (idx[:, None] <= idx[None, :])).astype(np.float32)
    c["bd"] = same.astype(np.float32)
    low_incl = same & (idx[None, :] <= idx[:, None])
    c["lstrict"] = (same & (idx[None, :] < idx[:, None])).astype(np.float32)
    cs = np.zeros((128, 2), np.float32)
    cs[:64, 0] = 1.0
    cs[64:, 1] = 1.0
    c["chunksel"] = cs
    c["iota"] = np.tile(np.arange(128, dtype=np.float32)[None, :], (128, 1))
    return c


def rep128(v):
    v = np.asarray(v, np.float32).reshape(1, -1)
    return np.ascontiguousarray(np.repeat(v, 128, axis=0))


class Prog:
    def __init__(self):
        self.nc = bass.Bass("TRN2", target_bir_lowering=False)
        self.T = Trk(self.nc)
        self.din = {}
        self.dout = {}

    def inp(self, name, shape, dt=F32):
        ap = self.nc.dram_tensor(name, list(shape), dt, kind="ExternalInput").ap()
        self.din[name] = ap
        return ap

    def outp(self, name, shape, dt=F32):
        ap = self.nc.dram_tensor(name, list(shape), dt, kind="ExternalOutput").ap()
        self.dout[name] = ap
        return ap

    def scratch(self, name, shape, dt=F32):
        return self.nc.dram_tensor(name, list(shape), dt).ap()


def sb(nc, es, name, shape, dt=F32):
    h = es.enter_context(nc.sbuf_tensor("s_" + name, list(shape), dt))
    return h.ap()


def build():
    P = Prog()
    nc, T = P.nc, P.T
    xseq = P.inp("xseq", [SEQ, D_MODEL])
    xs = P.inp("xs", [STOK, D_MODEL])
    S0 = P.inp("S0", [4, 4, 128, 128])
    conv0T = P.inp("conv0T", [4, 1536, 3])
    w_in = P.inp("w_in", [D_MODEL, D_IN])
    gmix = P.inp("gmix_bc", [128, D_MODEL])
    wconvT = P.inp("wconvT", [1536, 4])
    alog = P.inp("alog_bc", [128, 4])
    dtb = P.inp("dtb_bc", [128, 4])
    cn = {k: P.inp("c_" + k, list(v.shape)) for k, v in _consts().items()}

    dn_p = P.outp("dn_p", [4, 128, 128])
    dn_s = P.outp("dn_s", [4, 4, 128, 128])
    conv_pT = P.outp("conv_pT", [1536, 3])
    conv_sT = P.outp("conv_sT", [4, 1536, 3])
    o_all = P.outp("o_all", [SEQ + STOK, 512]) if DEBUG_A else P.scratch("o_all", [SEQ + STOK, 512])

    Bd = {}
    for nm, shp in [("xown", [OWN, D_MODEL]), ("xhalo", [16, D_MODEL]), ("pool0", [4, 15, 512]),
                    ("gmix_col", [128, 8]), ("gffn_col", [128, 8]), ("gfin_bc", [128, D_MODEL]), ("gdn_bc", [128, 512]),
                    ("pscale_col", [128, 4]), ("sel", [128, 4]), ("w_grp", [4, 128, 128]), ("keysT", [16, 128, 128]),
                    ("bmain", [128, 4, 128]), ("bprev", [128, 4, 128]), ("bfirst", [128, 4, 128]), ("bsamp", [128, 4, 128]),
                    ("bsprev", [128, 4, 128]), ("w_up_pool", [512, D_MODEL]), ("w_up_dn", [512, D_MODEL]),
                    ("w_out", [D_MODEL, D_MODEL]), ("w_q", [D_MODEL, 2048]), ("UTb", [128, 128, 8, 128]),
                    ("peer_v", [16384, D_MODEL])]:
        Bd[nm] = P.inp(nm, shp)
    Bd["w_in"] = w_in
    Bd["o_all"] = o_all
    Bd["y"] = P.outp("y", [OWN, D_MODEL])
    Bd["pool_p"] = P.outp("pool_p", [15, 512])
    Bd["pool_s"] = P.outp("pool_s", [4, 15, 512])

    ps = [nc.alloc_psum_tensor("psb%d" % i, [128, 512], F32).ap() for i in range(8)]

    with ExitStack() as es:
        C = {}
        for k, v in cn.items():
            C[k] = sb(nc, es, "k_" + k, list(v.shape))
            T.dma('sp', C[k], v)
        gmix_sb = sb(nc, es, "gmix", [128, D_MODEL])
        T.dma('sp', gmix_sb, gmix)
        wc_sb = sb(nc, es, "wconv", [128, 12, 4])
        T.dma('sp', wc_sb, wconvT.rearrange("(c p) j -> p c j", p=128))
        alog_sb = sb(nc, es, "alog", [128, 4])
        dtb_sb = sb(nc, es, "dtb", [128, 4])
        T.dma('sp', alog_sb, alog)
        T.dma('sp', dtb_sb, dtb)
        onesR = sb(nc, es, "onesR", [128, 128], F32R)
        T.dma('pool', onesR, cn["ones"])
        T.barrier_consts()
        nexpA = sb(nc, es, "nexpA", [128, 4])
        T.op('act', lambda e: e.activation(out=nexpA, in_=alog_sb, func=AF.Exp), writes=["nexpA"])
        T.op('dve', lambda e: e.tensor_scalar(out=nexpA, in0=nexpA, scalar1=-1.0, scalar2=None, op0=ALU.mult),
             reads=["nexpA"], writes=["nexpA"])

        phase_a(P, es, C, ps, dict(xseq=xseq, xs=xs, S0=S0, conv0T=conv0T, w_in=w_in, gmix=gmix_sb,
                                   wc=wc_sb, dtb=dtb_sb, nexpA=nexpA, onesR=onesR,
                                   dn_p=dn_p, dn_s=dn_s, conv_pT=conv_pT, conv_sT=conv_sT, o_all=o_all))
        T.barrier()
        phase_b(P, C, ps, Bd)
        T.barrier()
    return P


def phase_a(P, es0, C, ps, A):
    nc, T = P.nc, P.T
    ident, onesF = C["ident"], C["ones"]
    with ExitStack() as es:
        w_a = sb(nc, es, "w_a", [128, 8, 1544], F32R)
        for dc in range(8):
            T.dma('pool', w_a[:, dc, 0:1536], A["w_in"][dc * 128:(dc + 1) * 128, OFF_QKV:OFF_Z], writes=["w_a"])
            T.dma('pool', w_a[:, dc, 1536:1544], A["w_in"][dc * 128:(dc + 1) * 128, OFF_B:OFF_G], writes=["w_a"])
        xt = [sb(nc, es, "xt%d" % i, [128, D_MODEL]) for i in range(2)]
        xn = sb(nc, es, "xn", [128, D_MODEL])
        ss = sb(nc, es, "ss", [128, 1])
        rs = sb(nc, es, "rs", [128, 1])
        rstd = sb(nc, es, "rstd", [128, 1])
        xnT = sb(nc, es, "xnT", [128, 8, 256], F32R)
        ext_w = sb(nc, es, "ext", [128, 12, 268], F32R)
        ext = ext_w.bitcast(F32)
        diagW = sb(nc, es, "diagW", [128, 48, 128], F32R)
        for cc in range(12):
            for j in range(4):
                T.op('pool', lambda e: e.tensor_scalar(out=diagW[:, cc * 4 + j, :], in0=ident, scalar1=A["wc"][:, cc, j:j + 1],
                                                       scalar2=None, op0=ALU.mult), writes=["diagW"])
        carry = sb(nc, es, "carry", [128, 12, 3])
        qcs = [sb(nc, es, "qc%d" % i, [128, 12, 256], F32R) for i in range(2)]
        sq = sb(nc, es, "sq", [128, 8, 256], F32R)
        rinv = sb(nc, es, "rinv", [128, 256])
        bas = [[sb(nc, es, "ba%d_%d" % (p_, i), [128, 8]) for i in range(2)] for p_ in range(2)]
        small = {}
        for nm, w in [("beta", 4), ("negbeta", 4), ("g", 4), ("sp", 4), ("gc", 4), ("gam", 4), ("dlt", 4),
                      ("bgn", 4), ("gsel", 8), ("egl", 8), ("tmp4", 4)]:
            small[nm] = [sb(nc, es, "%s%d" % (nm, i), [128, w]) for i in range(2)]
        NH = 4
        def tiles(nm, n=1, shape=(128, 128)):
            return [[sb(nc, es, "%s_h%d_%d" % (nm, h, i), list(shape)) for i in range(n)] for h in range(NH)]
        Lg_all = sb(nc, es, "Lg_all", [128, 4, 128])
        dec = tiles("dec")
        decT = tiles("decT")
        def tilesR(nm, n=1, shape=(128, 128)):
            return [[sb(nc, es, "%s_h%d_%d" % (nm, h, i), list(shape), F32R) for i in range(n)] for h in range(NH)]
        Mp = tilesR("Mp", 2)
        NR = tilesR("NR", 2, (128, 256))
        Rfin = tilesR("Rfin")
        VK = tilesR("VK", 1, (128, 256))
        Qg = tiles("Qg")
        wkn = tiles("wkn")
        kdec = tiles("kdec", 2)
        uv = tiles("uv", 2)
        PT = tiles("PT", 2)
        qeT = tiles("qeT", 2)
        FT = tiles("FT", 4)
        Sst = [sb(nc, es, "Sst%d" % i, [128, 4, 128]) for i in range(2)]
        ostage = [sb(nc, es, "ost%d" % i, [64, 512]) for i in range(2)]

        T.op('dve', lambda e: e.memset(Sst[0], 0.0), writes=["S0buf"])
        T.op('dve', lambda e: e.memset(carry, 0.0), writes=["carry"])
        s_cur = 0
        chunk_ctr = 0

        groups = [("p", g) for g in range(min(32, NGROUP_DBG))] + [("s", 0)]
        def s1a(kind, g, par):
            nseq, L = (1, 256) if kind == "p" else (4, 64)
            ba = bas[par]
            extv = ext[:, :, 0:nseq * (L + 3)].rearrange("p c (s l) -> p c s l", s=nseq)
            extvw = ext_w[:, :, 0:nseq * (L + 3)].rearrange("p c (s l) -> p c s l", s=nseq)
            for blk in range(2):
                xb = xt[blk]
                src = A["xseq"][g * 256 + blk * 128: g * 256 + (blk + 1) * 128, :] if kind == "p" \
                    else A["xs"][blk * 128:(blk + 1) * 128, :]
                T.dma('sp', xb, src, writes=["xt%d" % blk])
                T.op('act', lambda e: e.activation(out=xn, in_=xb, func=AF.Square, accum_out=ss),
                     reads=["xt%d" % blk], writes=["xn", "ss"])
                T.op('act', lambda e: e.activation(out=rs, in_=ss, func=AF.Sqrt, bias=EPS, scale=1.0 / D_MODEL),
                     reads=["ss"], writes=["rs"])
                T.op('dve', lambda e: e.reciprocal(out=rstd, in_=rs), reads=["rs"], writes=["rstd"])
                T.op('dve', lambda e: e.scalar_tensor_tensor(out=xn, in0=xb, scalar=rstd, in1=A["gmix"],
                                                             op0=ALU.mult, op1=ALU.mult),
                     reads=["xt%d" % blk, "rstd"], writes=["xn"])
                for half in range(2):
                    for q in range(4):
                        dc = half * 4 + q
                        T.op('pe', lambda e: e.transpose(out=ps[half][:, q * 128:(q + 1) * 128],
                                                         in_=xn[:, dc * 128:(dc + 1) * 128], identity=ident),
                             reads=["xn"], writes=["ps%d" % half])
                    T.op('act', lambda e: e.activation(
                        out=xnT[:, half * 4:(half + 1) * 4, blk * 128:(blk + 1) * 128],
                        in_=ps[half].rearrange("p (q t) -> p q t", q=4), func=AF.Copy),
                        reads=["ps%d" % half], writes=["xnT"])
                for dc in range(8):
                    T.op('pe', lambda e: e.matmul(ps[1][:, 504:512], lhsT=xnT[:, dc, blk * 128:(blk + 1) * 128],
                                                  rhs=w_a[:, dc, 1536:1544], start=(dc == 0), stop=(dc == 7)),
                         reads=["xnT", "w_a"], writes=["ps1"])
                T.op('dve', lambda e: e.tensor_copy(out=ba[blk], in_=ps[1][:, 504:512]),
                     reads=["ps1"], writes=["ba%d_%d" % (par, blk)])
        def s1b(kind, g, par):
            if STAGE < 2:
                return
            nseq, L = (1, 256) if kind == "p" else (4, 64)
            qc_w = qcs[par]
            qc = qc_w.bitcast(F32)
            extv = ext[:, :, 0:nseq * (L + 3)].rearrange("p c (s l) -> p c s l", s=nseq)
            extvw = ext_w[:, :, 0:nseq * (L + 3)].rearrange("p c (s l) -> p c s l", s=nseq)
            if kind == "p":
                T.op('pool', lambda e: e.tensor_copy(out=extvw[:, :, 0, 0:3], in_=carry), reads=["carry"], writes=["ext"])
            else:
                for s in range(4):
                    T.dma('pool', extvw[:, :, s, 0:3], A["conv0T"][s].rearrange("(c p) j -> p c j", p=128), writes=["ext"])
            for cc in range(12):
                hf = cc % 2
                pq = ps[hf][:, 0:256]
                for dc in range(8):
                    T.op('pe', lambda e: e.matmul(pq, lhsT=w_a[:, dc, cc * 128:(cc + 1) * 128],
                                                  rhs=xnT[:, dc, :], start=(dc == 0), stop=(dc == 7)),
                         reads=["xnT", "w_a"], writes=["ps%d" % hf])
                T.op('act', lambda e: e.activation(out=extvw[:, cc, :, 3:3 + L],
                                                   in_=pq.rearrange("p (s l) -> p s l", s=nseq),
                                                   func=AF.Copy),
                     reads=["ps%d" % hf], writes=["ext"])
            if kind == "p":
                T.op('pool', lambda e: e.tensor_copy(out=carry, in_=extv[:, :, 0, L:L + 3]), reads=["ext"], writes=["carry"])
                if g == 31:
                    T.dma('sp', A["conv_pT"].rearrange("(c p) j -> p c j", p=128), carry, reads=["carry"])
            else:
                for s in range(4):
                    T.dma('sp', A["conv_sT"][s].rearrange("(c p) j -> p c j", p=128), extv[:, :, s, L:L + 3], reads=["ext"])
            for cc in range(12):
                hf = cc % 2
                pq = ps[hf][:, 0:256]
                for j in range(4):
                    T.op('pe', lambda e: e.matmul(pq, lhsT=diagW[:, cc * 4 + j, :], rhs=extvw[:, cc, :, j:j + L], start=(j == 0), stop=(j == 3)),
                         reads=["ext"], writes=["ps%d" % hf])
                T.op('act', lambda e: e.activation(out=qc_w[:, cc, :], in_=pq, func=AF.Silu), reads=["ps%d" % hf], writes=["qc%d" % par])
            T.op('act', lambda e: e.activation(out=sq, in_=qc[:, 0:8, :], func=AF.Square), reads=["qc%d" % par], writes=["sq"])
            for cc in range(8):
                hf = cc % 2
                pq = ps[hf][:, 0:256]
                T.op('pe', lambda e: e.matmul(pq, lhsT=A["onesR"], rhs=sq[:, cc, :], start=True, stop=True),
                     reads=["sq"], writes=["ps%d" % hf])
                sc = 128.0 if cc < 4 else 1.0
                T.op('act', lambda e: e.activation(out=rinv, in_=pq, func=AF.Sqrt, bias=EPS * sc, scale=sc),
                     reads=["ps%d" % hf], writes=["rinv"])
                T.op('dve', lambda e: e.reciprocal(out=rinv, in_=rinv), reads=["rinv"], writes=["rinv"])
                T.op('dve', lambda e: e.tensor_tensor(out=qc_w[:, cc, :], in0=qc[:, cc, :], in1=rinv, op=ALU.mult),
                     reads=["qc%d" % par, "rinv"], writes=["qc%d" % par])
        def s2(kind, g, par, blk):
            nonlocal s_cur, chunk_ctr
            if STAGE < 3:
                return
            qc_w = qcs[par]
            qc = qc_w.bitcast(F32)
            ba = bas[par]
            if True:
                bp = blk
                cols = slice(blk * 128, (blk + 1) * 128)
                sm = {k: v[bp] for k, v in small.items()}
                bab = ba[blk]
                T.op('act', lambda e: e.activation(out=sm["beta"], in_=bab[:, 0:4], func=AF.Sigmoid),
                     reads=["ba%d_%d" % (par, blk)], writes=["beta%d" % bp])
                T.op('dve', lambda e: e.tensor_tensor(out=sm["sp"], in0=bab[:, 4:8], in1=A["dtb"], op=ALU.add),
                     reads=["ba%d_%d" % (par, blk)], writes=["sp%d" % bp])
                T.op('act', lambda e: e.activation(out=sm["sp"], in_=sm["sp"], func=AF.Exp), reads=["sp%d" % bp], writes=["sp%d" % bp])
                T.op('act', lambda e: e.activation(out=sm["sp"], in_=sm["sp"], func=AF.Ln, bias=1.0),
                     reads=["sp%d" % bp], writes=["sp%d" % bp])
                T.op('dve', lambda e: e.tensor_tensor(out=sm["g"], in0=sm["sp"], in1=A["nexpA"], op=ALU.mult),
                     reads=["sp%d" % bp, "nexpA"], writes=["g%d" % bp])
                T.op('dve', lambda e: e.tensor_scalar(out=sm["negbeta"], in0=sm["beta"], scalar1=-1.0, scalar2=None, op0=ALU.mult),
                     reads=["beta%d" % bp], writes=["negbeta%d" % bp])
                T.op('pe', lambda e: e.matmul(ps[7][:, 0:4], lhsT=C["umask"], rhs=sm["g"], start=True, stop=True),
                     reads=["g%d" % bp], writes=["ps7"])
                T.op('pe', lambda e: e.matmul(ps[7][:, 8:12], lhsT=C["bd"], rhs=sm["g"], start=True, stop=True),
                     reads=["g%d" % bp], writes=["ps7"])
                T.op('act', lambda e: e.activation(out=sm["gc"], in_=ps[7][:, 0:4], func=AF.Copy), reads=["ps7"], writes=["gc%d" % bp])
                T.op('act', lambda e: e.activation(out=sm["gam"], in_=ps[7][:, 0:4], func=AF.Exp), reads=["ps7"], writes=["gam%d" % bp])
                T.op('dve', lambda e: e.tensor_tensor(out=sm["tmp4"], in0=ps[7][:, 8:12], in1=sm["gc"], op=ALU.subtract),
                     reads=["ps7", "gc%d" % bp], writes=["tmp4%d" % bp])
                T.op('act', lambda e: e.activation(out=sm["dlt"], in_=sm["tmp4"], func=AF.Exp), reads=["tmp4%d" % bp], writes=["dlt%d" % bp])
                T.op('dve', lambda e: e.tensor_tensor(out=sm["bgn"], in0=sm["negbeta"], in1=sm["gam"], op=ALU.mult),
                     reads=["negbeta%d" % bp, "gam%d" % bp], writes=["bgn%d" % bp])
                T.op('dve', lambda e: e.tensor_tensor(
                    out=sm["gsel"].rearrange("p (h c) -> p h c", c=2),
                    in0=sm["g"].unsqueeze(2).to_broadcast([128, 4, 2]),
                    in1=C["chunksel"].unsqueeze(1).to_broadcast([128, 4, 2]), op=ALU.mult),
                    reads=["g%d" % bp], writes=["gsel%d" % bp])
                T.op('pe', lambda e: e.matmul(ps[7][:, 16:24], lhsT=onesF, rhs=sm["gsel"], start=True, stop=True),
                     reads=["gsel%d" % bp], writes=["ps7"])
                T.op('act', lambda e: e.activation(out=sm["egl"], in_=ps[7][:, 16:24], func=AF.Exp), reads=["ps7"], writes=["egl%d" % bp])

                if SUB < 1:
                    return

                def hk(nm, h, i=0):
                    return "%s_%d_%d" % (nm, h, i)

                def pslot(h, i):
                    if i == 0:
                        return ps[5][:, h * 128:(h + 1) * 128], "ps5_%d" % h
                    return ps[4][:, 128 + 0:128 + 0], None

                def slotA(h):
                    return ps[2 + h][:, 0:128], "ps%d" % (2 + h)

                def slotB(h):
                    return ps[2 + h][:, 128:256], "ps%d" % (2 + h)

                def slotC(h):
                    return ps[2 + h][:, 256:384], "ps%d" % (2 + h)

                T.op('dve', lambda e: e.tensor_tensor(out=Lg_all, in0=C["umask"].unsqueeze(1).to_broadcast([128, 4, 128]),
                                                      in1=sm["g"].unsqueeze(2).to_broadcast([128, 4, 128]), op=ALU.mult),
                     reads=["g%d" % bp], writes=["Lg_all"])
                T.op('pe', lambda e: e.matmul(ps[7], lhsT=onesF, rhs=Lg_all.rearrange("p h t -> p (h t)"), start=True, stop=True),
                     reads=["Lg_all"], writes=["ps7"])
                for h in range(NH):
                    gcb = ps[7][:, h * 128:(h + 1) * 128]
                    T.op('dve', lambda e: e.tensor_scalar(out=dec[h][0], in0=gcb, scalar1=sm["gc"][:, h:h + 1], scalar2=0.0,
                                                          op0=ALU.subtract, op1=ALU.max), reads=["ps7", "gc%d" % bp], writes=[hk("dec", h)])
                    T.op('dve', lambda e: e.tensor_scalar(out=decT[h][0], in0=gcb, scalar1=sm["gc"][:, h:h + 1], scalar2=0.0,
                                                          op0=ALU.subtract, op1=ALU.min), reads=["ps7", "gc%d" % bp], writes=[hk("decT", h)])
                    T.op('act', lambda e: e.activation(out=dec[h][0], in_=dec[h][0], func=AF.Exp, scale=-1.0), reads=[hk("dec", h)], writes=[hk("dec", h)])
                    T.op('act', lambda e: e.activation(out=decT[h][0], in_=decT[h][0], func=AF.Exp), reads=[hk("decT", h)], writes=[hk("decT", h)])
                    T.op('pool', lambda e: e.tensor_tensor(out=dec[h][0], in0=dec[h][0], in1=C["lstrict"], op=ALU.mult),
                         reads=[hk("dec", h)], writes=[hk("dec", h)])
                    T.op('pool', lambda e: e.tensor_tensor(out=decT[h][0], in0=decT[h][0], in1=C["umask"], op=ALU.mult),
                         reads=[hk("decT", h)], writes=[hk("decT", h)])
                for h in range(NH):
                    pc, kc = ps[2 + h][:, 0:256], "ps%d" % (2 + h)
                    T.op('pe', lambda e: e.matmul(pc, lhsT=qc_w[:, 4 + h, cols], rhs=qc_w[:, h:h + 5:4, cols], start=True, stop=True),
                         reads=["qc%d" % par], writes=[kc])
                    T.op('dve', lambda e: e.scalar_tensor_tensor(out=Mp[h][0], in0=pc[:, 128:256], scalar=sm["negbeta"][:, h:h + 1],
                                                                 in1=dec[h][0], op0=ALU.mult, op1=ALU.mult),
                         reads=[kc, hk("dec", h), "negbeta%d" % bp], writes=[hk("Mp", h, 0)])
                    T.op('dve', lambda e: e.tensor_tensor(out=PT[h][bp], in0=pc[:, 0:128], in1=decT[h][0], op=ALU.mult),
                         reads=[kc, hk("decT", h)], writes=[hk("PT", h, bp)])
                for h in range(NH):
                    pa, ka = ps[2 + h][:, 256:384], "ps%d" % (2 + h)
                    T.op('pe', lambda e: e.transpose(out=pa, in_=Mp[h][0].bitcast(F32), identity=ident), reads=[hk("Mp", h, 0)], writes=[ka])
                    T.op('act', lambda e: e.activation(out=NR[h][0][:, 0:128], in_=pa, func=AF.Copy), reads=[ka], writes=[hk("NR", h, 0)])
                    T.op('pool', lambda e: e.tensor_tensor(out=NR[h][1][:, 128:256], in0=NR[h][0][:, 0:128].bitcast(F32), in1=ident, op=ALU.add),
                         reads=[hk("NR", h, 0)], writes=[hk("NR", h, 1)])
                for h in range(NH):
                    pb, kb = ps[2 + h], "ps%d" % (2 + h)
                    T.op('pe', lambda e: e.matmul(pb[:, 0:128], lhsT=NR[h][0][:, 0:128], rhs=Mp[h][0], start=True, stop=True),
                         reads=[hk("NR", h, 0), hk("Mp", h, 0)], writes=[kb])
                    T.op('pe', lambda e: e.matmul(pb[:, 128:256], lhsT=Mp[h][0], rhs=NR[h][0][:, 0:128], start=True, stop=True),
                         reads=[hk("NR", h, 0), hk("Mp", h, 0)], writes=[kb])
                for h in range(NH):
                    pb, kb = ps[2 + h], "ps%d" % (2 + h)
                    T.op('dve', lambda e: e.tensor_copy(out=Mp[h][1], in_=pb[:, 0:128]), reads=[kb], writes=[hk("Mp", h, 1)])
                    T.op('act', lambda e: e.activation(out=NR[h][1][:, 0:128], in_=pb[:, 128:256], func=AF.Copy), reads=[kb], writes=[hk("NR", h, 1)])
                for lev in range(1, 5):
                    a, b2 = lev % 2, (lev + 1) % 2
                    for h in range(NH):
                        pb, kb = ps[2 + h], "ps%d" % (2 + h)
                        T.op('pe', lambda e: e.matmul(pb[:, 0:256], lhsT=Mp[h][a], rhs=NR[h][a], start=True, stop=True),
                             reads=[hk("NR", h, a), hk("Mp", h, a)], writes=[kb])
                        T.op('pe', lambda e: e.matmul(pb[:, 256:384], lhsT=NR[h][a][:, 0:128], rhs=Mp[h][a], start=True, stop=True),
                             reads=[hk("NR", h, a), hk("Mp", h, a)], writes=[kb])
                    for h in range(NH):
                        pb, kb = ps[2 + h], "ps%d" % (2 + h)
                        T.op('act', lambda e: e.activation(out=NR[h][b2][:, 0:128], in_=pb[:, 0:128], func=AF.Copy), reads=[kb], writes=[hk("NR", h, b2)])
                        T.op('dve', lambda e: e.tensor_tensor(out=NR[h][b2][:, 128:256], in0=pb[:, 128:256], in1=NR[h][a][:, 128:256].bitcast(F32), op=ALU.add),
                             reads=[kb, hk("NR", h, a)], writes=[hk("NR", h, b2)])
                        T.op('dve', lambda e: e.tensor_copy(out=Mp[h][b2], in_=pb[:, 256:384]), reads=[kb], writes=[hk("Mp", h, b2)])
                for h in range(NH):
                    pb, kb = ps[2 + h], "ps%d" % (2 + h)
                    T.op('pe', lambda e: e.matmul(pb[:, 0:128], lhsT=Mp[h][1], rhs=NR[h][1][:, 128:256], start=True, stop=True),
                         reads=[hk("NR", h, 1), hk("Mp", h, 1)], writes=[kb])
                    T.op('dve', lambda e: e.tensor_tensor(out=Rfin[h][0], in0=pb[:, 0:128], in1=NR[h][1][:, 128:256].bitcast(F32), op=ALU.add),
                         reads=[kb, hk("NR", h, 1)], writes=[hk("Rfin", h)])
                for h in range(NH):
                    QT, KT, VT = qc[:, h, cols], qc[:, 4 + h, cols], qc[:, 8 + h, cols]
                    pa, ka = slotA(h)
                    T.op('pe', lambda e: e.transpose(out=pa, in_=KT, identity=ident), reads=["qc%d" % par], writes=[ka])
                    T.op('act', lambda e: e.activation(out=VK[h][0][:, 128:256], in_=pa, func=AF.Copy, scale=sm["bgn"][:, h:h + 1]),
                         reads=[ka, "bgn%d" % bp], writes=[hk("VK", h)])
                    T.op('dve', lambda e: e.tensor_scalar(out=kdec[h][bp], in0=pa, scalar1=sm["dlt"][:, h:h + 1], scalar2=None, op0=ALU.mult),
                         reads=[ka, "dlt%d" % bp], writes=[hk("kdec", h, bp)])
                    pb, kb = slotB(h)
                    T.op('pe', lambda e: e.transpose(out=pb, in_=VT, identity=ident), reads=["qc%d" % par], writes=[kb])
                    T.op('act', lambda e: e.activation(out=VK[h][0][:, 0:128], in_=pb, func=AF.Copy, scale=sm["beta"][:, h:h + 1]),
                         reads=[kb, "beta%d" % bp], writes=[hk("VK", h)])
                    pc, kc = slotC(h)
                    T.op('pe', lambda e: e.transpose(out=pc, in_=QT, identity=ident), reads=["qc%d" % par], writes=[kc])
                    T.op('dve', lambda e: e.tensor_scalar(out=Qg[h][0], in0=pc, scalar1=sm["gam"][:, h:h + 1], scalar2=None, op0=ALU.mult),
                         reads=[kc, "gam%d" % bp], writes=[hk("Qg", h)])
                for h in range(NH):
                    pb, kb = ps[2 + h], "ps%d" % (2 + h)
                    T.op('pe', lambda e: e.matmul(pb[:, 0:256], lhsT=Rfin[h][0], rhs=VK[h][0], start=True, stop=True),
                         reads=[hk("Rfin", h), hk("VK", h)], writes=[kb])
                    T.op('act', lambda e: e.activation(out=uv[h][bp], in_=pb[:, 0:128], func=AF.Copy), reads=[kb], writes=[hk("uv", h, bp)])
                    T.op('dve', lambda e: e.tensor_copy(out=wkn[h][0], in_=pb[:, 128:256]), reads=[kb], writes=[hk("wkn", h)])
                if SUB < 6:
                    return
                for h in range(NH):
                    pa, ka = slotA(h)
                    T.op('pe', lambda e: e.matmul(pa, lhsT=Qg[h][0], rhs=ident, start=True, stop=False), reads=[hk("Qg", h)], writes=[ka])
                    T.op('pe', lambda e: e.matmul(pa, lhsT=wkn[h][0], rhs=PT[h][bp], start=False, stop=True),
                         reads=[hk("wkn", h), hk("PT", h, bp)], writes=[ka])
                    T.op('act', lambda e: e.activation(out=qeT[h][bp], in_=pa, func=AF.Copy), reads=[ka], writes=[hk("qeT", h, bp)])
                    for c in range(2):
                        pp, kp = (slotB(h) if c == 0 else slotC(h))
                        rows = slice(c * 64, (c + 1) * 64)
                        T.op('pe', lambda e: e.matmul(pp, lhsT=wkn[h][0][rows, :], rhs=kdec[h][bp][rows, :], start=True, stop=True),
                             reads=[hk("wkn", h), hk("kdec", h, bp)], writes=[kp])
                        T.op('dve', lambda e: e.scalar_tensor_tensor(out=FT[h][bp * 2 + c], in0=ident,
                                                                     scalar=sm["egl"][:, h * 2 + c:h * 2 + c + 1], in1=pp,
                                                                     op0=ALU.mult, op1=ALU.add),
                             reads=[kp, "egl%d" % bp], writes=[hk("FT", h, bp * 2 + c)])
                for c in range(2 if STAGE >= 4 else 0):
                    rows = slice(c * 64, (c + 1) * 64)
                    if kind == "s":
                        seq = blk * 2 + c
                        T.dma('sp', Sst[s_cur], A["S0"][seq].rearrange("h k v -> k h v"), writes=["S%dbuf" % s_cur])
                    Sc = Sst[s_cur]
                    skey = "S%dbuf" % s_cur
                    op_ = ostage[chunk_ctr % 2]
                    okey = "ost%d" % (chunk_ctr % 2)
                    for h in range(NH):
                        oo = ps[7][0:64, h * 128:(h + 1) * 128]
                        T.op('pe', lambda e: e.matmul(oo, lhsT=qeT[h][bp][:, rows], rhs=Sc[:, h, :], start=True, stop=False),
                             reads=[hk("qeT", h, bp), skey], writes=["ps7"])
                        T.op('pe', lambda e: e.matmul(oo, lhsT=PT[h][bp][rows, rows], rhs=uv[h][bp][rows, :], start=False, stop=True),
                             reads=[hk("PT", h, bp), hk("uv", h, bp)], writes=["ps7"])
                    T.op('act', lambda e: e.activation(out=op_, in_=ps[7][0:64, :], func=AF.Copy), reads=["ps7"], writes=[okey])
                    if kind == "p":
                        row0 = g * 256 + blk * 128 + c * 64
                    else:
                        row0 = SEQ + blk * 128 + c * 64
                    T.dma('sp', A["o_all"][row0:row0 + 64, :], op_, reads=[okey])
                    s_nxt = 1 - s_cur
                    for h in range(NH):
                        so = ps[6][:, h * 128:(h + 1) * 128]
                        T.op('pe', lambda e: e.matmul(so, lhsT=FT[h][bp * 2 + c], rhs=Sc[:, h, :], start=True, stop=False),
                             reads=[hk("FT", h, bp * 2 + c), skey], writes=["ps6"])
                        T.op('pe', lambda e: e.matmul(so, lhsT=kdec[h][bp][rows, :], rhs=uv[h][bp][rows, :], start=False, stop=True),
                             reads=[hk("kdec", h, bp), hk("uv", h, bp)], writes=["ps6"])
                    T.op('dve', lambda e: e.tensor_copy(out=Sst[s_nxt].rearrange("p h v -> p (h v)"), in_=ps[6]),
                         reads=["ps6"], writes=["S%dbuf" % s_nxt])
                    if kind == "s":
                        seq = blk * 2 + c
                        T.dma('sp', A["dn_s"][seq].rearrange("h k v -> k h v"), Sst[s_nxt], reads=["S%dbuf" % s_nxt])
                    s_cur = s_nxt
                    chunk_ctr += 1
            if kind == "p" and g == min(32, NGROUP_DBG) - 1 and blk == 1:
                T.dma('sp', A["dn_p"].rearrange("h k v -> k h v"), Sst[s_cur], reads=["S%dbuf" % s_cur])

        ng = len(groups)
        s1a(groups[0][0], groups[0][1], 0)
        s1b(groups[0][0], groups[0][1], 0)
        for i in range(ng):
            kind, g = groups[i]
            if i + 1 < ng:
                s1a(groups[i + 1][0], groups[i + 1][1], (i + 1) % 2)
            s2(kind, g, i % 2, 0)
            if i + 1 < ng:
                s1b(groups[i + 1][0], groups[i + 1][1], (i + 1) % 2)
            s2(kind, g, i % 2, 1)


_CACHE = {}


def kernel(**inp):
    f = lambda a: np.ascontiguousarray(np.asarray(a, dtype=np.float32))
    x_prompt, x_sample = f(inp["x_prompt"]), f(inp["x_sample"])
    if "prog" not in _CACHE:
        _CACHE["prog"] = build()
    P = _CACHE["prog"]
    consts = _consts()
    pu = f(inp["peer_u"])[0]
    shared = {
        "gmix_col": np.ascontiguousarray(f(inp["g_mix"])[0].reshape(8, 128).T),
        "gffn_col": np.ascontiguousarray(f(inp["g_ffn"])[0].reshape(8, 128).T),
        "gfin_bc": rep128(f(inp["g_final"])),
        "gdn_bc": rep128(np.tile(f(inp["g_dn_out"])[0], 4)),
        "pscale_col": np.ascontiguousarray(f(inp["pool_scale"])[0].reshape(4, 128).T),
        "w_grp": f(inp["w_pool_grp"])[0],
        "keysT": np.ascontiguousarray(f(inp["peer_sub_keys"])[0].transpose(0, 1, 3, 2).reshape(16, 128, 128)),
        "w_up_pool": f(inp["w_up_pool"])[0],
        "w_up_dn": f(inp["w_up_dn"])[0],
        "w_out": f(inp["w_out"])[0],
        "w_q": np.ascontiguousarray(f(inp["w_peer_q"])[0].reshape(D_MODEL, 2048)),
        "UTb": np.ascontiguousarray(pu.reshape(128, 128, 8, 128).transpose(0, 3, 2, 1)),
        "peer_v": f(inp["peer_v"])[0],
    }
    in_maps = []
    for c in range(NCORE):
        b, j = c // 4, c % 4
        m = {
            "xseq": x_prompt[b],
            "xs": x_sample[4 * c:4 * c + 4].reshape(STOK, D_MODEL),
            "S0": f(inp["state_dn"])[0, 4 * c:4 * c + 4],
            "conv0T": np.ascontiguousarray(f(inp["state_dn_conv"])[0, 4 * c:4 * c + 4].transpose(0, 2, 1)),
            "w_in": f(inp["w_in"])[0],
            "gmix_bc": rep128(f(inp["g_mix"])[0]),
            "wconvT": np.ascontiguousarray(f(inp["w_conv"])[0].T),
            "alog_bc": rep128(f(inp["a_log"])[0]),
            "dtb_bc": rep128(f(inp["dt_bias"])[0]),
        }
        for k, v in consts.items():
            m["c_" + k] = v
        m.update(shared)
        m["xown"] = np.ascontiguousarray(np.concatenate([x_prompt[b, j * QTOK:(j + 1) * QTOK], m["xs"]], axis=0))
        m["xhalo"] = np.ascontiguousarray(x_prompt[b, j * QTOK - 16:j * QTOK]) if j > 0 else np.zeros((16, D_MODEL), np.float32)
        m["pool0"] = f(inp["cache_pool"])[0, 4 * c:4 * c + 4]
        selv = np.zeros((128, 4), np.float32)
        selv[:, j] = 1.0
        m["sel"] = selv
        m.update(_bands(j))
        in_maps.append(m)
    res = run_bass_kernel_spmd(P.nc, in_maps, core_ids=list(range(NCORE)))
    R = res.results
    _CACHE["last"] = R
    new_dn_p = np.stack([R[0]["dn_p"], R[4]["dn_p"]])[None]
    new_dn_s = np.concatenate([R[c]["dn_s"] for c in range(NCORE)])[None]
    new_conv_p = np.stack([R[0]["conv_pT"].T, R[4]["conv_pT"].T])[None]
    new_conv_s = np.concatenate([R[c]["conv_sT"].transpose(0, 2, 1) for c in range(NCORE)])[None]
    y_prompt = np.stack([np.concatenate([R[4 * b + j]["y"][:QTOK] for j in range(4)]) for b in range(2)])
    y_sample = np.concatenate([R[c]["y"][QTOK:].reshape(4, 64, D_MODEL) for c in range(NCORE)])
    new_pool_p = np.stack([R[3]["pool_p"], R[7]["pool_p"]])[None]
    new_pool_s = np.concatenate([R[c]["pool_s"] for c in range(NCORE)])[None]
    f32 = lambda a: np.ascontiguousarray(a, dtype=np.float32)
    return (f32(y_prompt), f32(y_sample), f32(new_pool_p), f32(new_conv_p), f32(new_dn_p),
            f32(new_pool_s), f32(new_conv_s), f32(new_dn_s))


TT = 256
NTILE = OWN // TT
POOL_W = (2, 4, 8, 16)


def _bands(j):
    t = np.arange(128)
    out = {}
    main, prev, first, samp, sprev = [], [], [], [], []
    for w in POOL_W:
        src, dst = t[:, None], t[None, :]
        inwin = (src <= dst) & (src > dst - w)
        m = inwin / float(w) - np.eye(128)
        main.append(m)
        pw = ((src - 128) > dst - w) & (src >= 112)
        prev.append(pw / float(w))
        if j == 0:
            cnt = np.minimum(dst + 1, w).astype(np.float64)
            first.append(inwin / cnt - np.eye(128))
        else:
            first.append(m)
        same = (src // 64) == (dst // 64)
        samp.append((inwin & same) / float(w) - np.eye(128))
        sp = np.zeros((128, 128))
        for (r0, c0) in ((97, 0), (113, 64)):
            for i in range(15):
                for d in range(64):
                    if (-15 + i) > d - w:
                        sp[r0 + i, c0 + d] = 1.0 / w
        sprev.append(sp)
    f = lambda l: np.ascontiguousarray(np.stack(l).transpose(1, 0, 2).astype(np.float32))
    return dict(bmain=f(main), bprev=f(prev), bfirst=f(first), bsamp=f(samp), bsprev=f(sprev))


def phase_b(P, C, ps, B):
    nc, T = P.nc, P.T
    ident, iota = C["ident"], C["iota"]
    with ExitStack() as es:
        big = sb(nc, es, "big", [128, 16384])
        def reg(off, shape, dt=F32):
            n = int(np.prod(shape[1:]))
            v = big[:, off:off + n]
            if dt != F32:
                v = v.bitcast(dt)
            if len(shape) == 3:
                v = v.rearrange("p (a b) -> p a b", a=shape[1])
            return v
        xnT = reg(0, [128, 8, TT], F32R)
        wb = [reg(2048 + 2048 * i, [128, 8, 256], F32R) for i in range(3)]
        yaT = reg(8192, [128, 4, TT], F32R)
        ztok, ztok_w = reg(9216, [128, 2, 512]), reg(9216, [128, 2, 512], F32R)
        ybT = reg(10240, [128, 4, TT], F32R)
        mtok, mtok_w = reg(11264, [128, 2, 1024]), reg(11264, [128, 2, 1024], F32R)
        mT = reg(13312, [128, 8, TT], F32R)
        xnTh = reg(15360, [128, 8, 128], F32R)
        utok, utok_w = reg(11264, [128, 512]), reg(11264, [128, 512], F32R)
        otok, otok_w = reg(11776, [128, 512]), reg(11776, [128, 512], F32R)
        ocand, ocand_w = reg(12288, [128, 4, 512]), reg(12288, [128, 4, 512], F32R)
        cand, cand_w = reg(11264, [128, 2048]), reg(11264, [128, 2048], F32R)
        eqb, eqb_w = reg(13312, [128, 2048]), reg(13312, [128, 2048], F32R)
        Wsb = big.rearrange("p (e t) -> p e t", e=64)
        Wsb_w = big.bitcast(F32R).rearrange("p (e t) -> p e t", e=64)
        xres = sb(nc, es, "xres", [128, 2, 1024])
        xn2T = sb(nc, es, "xn2T", [128, 8, TT], F32R)
        ublk = [sb(nc, es, "ublk%d" % i, [128, 8, 128], F32R) for i in range(3)]
        vblk = [sb(nc, es, "vblk%d" % i, [128, 1024], F32R) for i in range(3)]
        actb = [sb(nc, es, "actb%d" % i, [128, TT]) for i in range(2)]
        gab = [sb(nc, es, "gab%d" % i, [128, TT], F32R) for i in range(2)]
        e1T = sb(nc, es, "e1T", [128, TT])
        e2T = sb(nc, es, "e2T", [128, TT])
        gT = sb(nc, es, "gT", [128, TT])
        uprev = sb(nc, es, "uprev", [128, 512])
        xh = sb(nc, es, "xh", [128, 1024])
        xnb = sb(nc, es, "xnb", [128, 1024])
        ss = sb(nc, es, "ssb", [128, 1])
        rs = sb(nc, es, "rsb", [128, 1])
        rstd = sb(nc, es, "rstdb", [128, 1])
        pooledT = sb(nc, es, "pooledT", [128, 128])
        qTsb = sb(nc, es, "qTsb", [128, 2, TT])
        sga = sb(nc, es, "sga", [128, 256])
        sgb = sb(nc, es, "sgb", [128, 256])
        sq4 = sb(nc, es, "sq4", [128, 4])
        B_ = {"v1": [sb(nc, es, "v1_%d" % i, [128, 8, 16]) for i in range(2)],
              "v2": [sb(nc, es, "v2_%d" % i, [128, 8, 16]) for i in range(2)],
              "i1u": [sb(nc, es, "i1u_%d" % i, [128, 8, 16], U32) for i in range(2)],
              "i2u": [sb(nc, es, "i2u_%d" % i, [128, 8, 16], U32) for i in range(2)]}
        i1f = sb(nc, es, "i1f", [128, 8, 16])
        i2f = sb(nc, es, "i2f", [128, 8, 16])
        tmpk = sb(nc, es, "tmpk", [128, 128])
        tmp2 = sb(nc, es, "tmp2", [128, 256])
        cv = sb(nc, es, "cv", [128, 8, 16])
        ciu = sb(nc, es, "ciu", [128, 8, 16], U32)
        abu = sb(nc, es, "abu", [128, 8, 16], U32)
        af = sb(nc, es, "af", [128, 8, 16])
        bf = sb(nc, es, "bf", [128, 8, 16])
        ex = sb(nc, es, "ex", [128, 8, 16])
        s8 = sb(nc, es, "s8", [128, 8])
        e1f = sb(nc, es, "e1f", [128, 128])
        e2f = sb(nc, es, "e2f", [128, 128])
        gsl = sb(nc, es, "gsl", [128, 128])
        oha = [sb(nc, es, "oha%d" % i, [128, 64], F32R) for i in range(4)]
        ohb = [sb(nc, es, "ohb%d" % i, [128, 128], F32R) for i in range(4)]
        gmixc = sb(nc, es, "gmixc", [128, 8]); T.dma('sp', gmixc, B["gmix_col"])
        gffnc = sb(nc, es, "gffnc", [128, 8]); T.dma('sp', gffnc, B["gffn_col"])
        gfin = sb(nc, es, "gfin", [128, 1024]); T.dma('sp', gfin, B["gfin_bc"])
        gdn = sb(nc, es, "gdn", [128, 512]); T.dma('sp', gdn, B["gdn_bc"])
        pscale = sb(nc, es, "pscale", [128, 4]); T.dma('sp', pscale, B["pscale_col"])
        selb = sb(nc, es, "selb", [128, 4]); T.dma('sp', selb, B["sel"])
        wgrp = sb(nc, es, "wgrp", [128, 4, 128]); T.dma('sp', wgrp, B["w_grp"].rearrange("g c d -> c g d"))
        keysT = sb(nc, es, "keysT", [128, 16, 128]); T.dma('sp', keysT, B["keysT"].rearrange("q d k -> d q k"))
        bands = {}
        for k in ("bmain", "bprev", "bfirst", "bsamp", "bsprev"):
            bands[k] = sb(nc, es, k, [128, 4, 128]); T.dma('sp', bands[k], B[k])
        T.barrier()

        w_in, w_upp, w_upd, w_out, w_q = B["w_in"], B["w_up_pool"], B["w_up_dn"], B["w_out"], B["w_q"]
        wrr = [0]

        def load_w(src2d, c0, ncol, kdim):
            i = wrr[0] % 3
            wrr[0] += 1
            dst = wb[i][:, 0:kdim, 0:ncol]
            T.dma('pool', dst, src2d[:, c0:c0 + ncol].rearrange("(k p) c -> p k c", p=128), writes=["wb%d" % i])
            return wb[i], "wb%d" % i

        def norm_block(xsrc, xkey, gcol, dstT, dkey, tcols):
            T.op('act', lambda e: e.activation(out=xnb, in_=xsrc, func=AF.Square, accum_out=ss), reads=[xkey], writes=["xnb", "ssb"])
            T.op('act', lambda e: e.activation(out=rs, in_=ss, func=AF.Sqrt, bias=EPS, scale=1.0 / D_MODEL), reads=["ssb"], writes=["rsb"])
            T.op('dve', lambda e: e.reciprocal(out=rstd, in_=rs), reads=["rsb"], writes=["rstdb"])
            T.op('dve', lambda e: e.tensor_scalar(out=xnb, in0=xsrc, scalar1=rstd, scalar2=None, op0=ALU.mult),
                 reads=[xkey, "rstdb"], writes=["xnb"])
            for half in range(2):
                for q in range(4):
                    dc = half * 4 + q
                    T.op('pe', lambda e: e.transpose(out=ps[half][:, q * 128:(q + 1) * 128], in_=xnb[:, dc * 128:(dc + 1) * 128], identity=ident),
                         reads=["xnb"], writes=["ps%d" % half])
                for q in range(4):
                    dc = half * 4 + q
                    T.op('act' if q % 2 else 'dve',
                         (lambda e: e.activation(out=dstT[:, dc, tcols], in_=ps[half][:, q * 128:(q + 1) * 128], func=AF.Copy, scale=gcol[:, dc:dc + 1]))
                         if q % 2 else
                         (lambda e: e.tensor_scalar(out=dstT[:, dc, tcols], in0=ps[half][:, q * 128:(q + 1) * 128], scalar1=gcol[:, dc:dc + 1], scalar2=None, op0=ALU.mult)),
                         reads=["ps%d" % half], writes=[dkey])

        for ti in (TILES if STAGE >= 5 else []):
            samp = (ti == NTILE - 1)
            blks = [0, 1]
            if ti == 0:
                T.op('pool', lambda e: e.memset(xh, 0.0), writes=["xh"])
                T.dma('sp', xh[112:128, :], B["xhalo"], writes=["xh"])
                norm_block(xh, "xh", gmixc, xnTh, "xnTh", slice(0, 128))
            for blk in blks:
                row0 = ti * TT + blk * 128
                T.dma('sp', xres[:, blk, :], B["xown"][row0:row0 + 128, :], writes=["xres%d" % blk])
                norm_block(xres[:, blk, :], "xres%d" % blk, gmixc, xnT, "xnT", slice(blk * 128, (blk + 1) * 128))
            wu = [load_w(w_in, c * 256, 256, 8) for c in range(2)]
            if ti == 0:
                for c in range(2):
                    for dc in range(8):
                        T.op('pe', lambda e: e.matmul(ps[4][:, c * 256:(c + 1) * 256], lhsT=xnTh[:, dc, :], rhs=wu[c][0][:, dc, :],
                                                      start=(dc == 0), stop=(dc == 7)), reads=["xnTh", wu[c][1]], writes=["ps4"])
                T.op('act', lambda e: e.activation(out=uprev, in_=ps[4], func=AF.Copy), reads=["ps4"], writes=["uprev"])
            if samp:
                T.op('pool', lambda e: e.memset(uprev, 0.0), writes=["uprev"])
            for c in range(2):
                for blk in blks:
                    for dc in range(8):
                        T.op('pe', lambda e: e.matmul(ps[2 + blk][:, c * 256:(c + 1) * 256], lhsT=xnT[:, dc, blk * 128:(blk + 1) * 128],
                                                      rhs=wu[c][0][:, dc, :], start=(dc == 0), stop=(dc == 7)),
                             reads=["xnT", wu[c][1]], writes=["ps%d" % (2 + blk)])
            for blk in blks:
                bi = ti * 2 + blk
                T.op('act', lambda e: e.activation(out=utok_w, in_=ps[2 + blk], func=AF.Copy), reads=["ps%d" % (2 + blk)], writes=["utok"])
                if samp:
                    for sq_ in range(2):
                        seq = blk * 2 + sq_
                        r0 = 97 if sq_ == 0 else 113
                        T.dma('sp', uprev[r0:r0 + 15, :], B["pool0"][seq], writes=["uprev"])
                        T.dma('sp', B["pool_s"][seq], utok[sq_ * 64 + 49: sq_ * 64 + 64, :], reads=["utok"])
                    bm, bp = bands["bsamp"], bands["bsprev"]
                else:
                    bm, bp = (bands["bfirst"] if bi == 0 else bands["bmain"]), bands["bprev"]
                    if bi == 15:
                        T.dma('sp', B["pool_p"], utok[113:128, :], reads=["utok"])
                for gi in range(4):
                    gc_ = slice(gi * 128, (gi + 1) * 128)
                    T.op('pe', lambda e: e.matmul(ps[4][:, 0:128], lhsT=uprev[:, gc_], rhs=bp[:, gi, :], start=True, stop=False),
                         reads=["uprev"], writes=["ps4"])
                    T.op('pe', lambda e: e.matmul(ps[4][:, 0:128], lhsT=utok[:, gc_], rhs=bm[:, gi, :], start=False, stop=True),
                         reads=["utok"], writes=["ps4"])
                    T.op('dve', lambda e: e.tensor_copy(out=pooledT, in_=ps[4][:, 0:128]), reads=["ps4"], writes=["pooledT"])
                    T.op('pe', lambda e: e.matmul(ps[5][:, 0:128], lhsT=wgrp[:, gi, :], rhs=pooledT, start=True, stop=True),
                         reads=["pooledT"], writes=["ps5"])
                    T.op('act', lambda e: e.activation(out=yaT[:, gi, blk * 128:(blk + 1) * 128], in_=ps[5][:, 0:128], func=AF.Copy,
                                                       scale=pscale[:, gi:gi + 1]), reads=["ps5"], writes=["yaT"])
                if not samp:
                    T.op('pool', lambda e: e.tensor_copy(out=uprev, in_=utok), reads=["utok"], writes=["uprev"])
            wz = [load_w(w_in, OFF_Z + c * 256, 256, 8) for c in range(2)]
            for c in range(2):
                for blk in blks:
                    for dc in range(8):
                        T.op('pe', lambda e: e.matmul(ps[2 + blk][:, c * 256:(c + 1) * 256], lhsT=xnT[:, dc, blk * 128:(blk + 1) * 128],
                                                      rhs=wz[c][0][:, dc, :], start=(dc == 0), stop=(dc == 7)),
                             reads=["xnT", wz[c][1]], writes=["ps%d" % (2 + blk)])
            for blk in blks:
                bi = ti * 2 + blk
                T.op('act', lambda e: e.activation(out=ztok_w[:, blk, :], in_=ps[2 + blk], func=AF.Silu), reads=["ps%d" % (2 + blk)], writes=["ztok"])
                if samp:
                    T.dma('pool', otok_w, B["o_all"][SEQ + blk * 128: SEQ + (blk + 1) * 128, :], writes=["otok"])
                elif NGROUP_DBG < 32:
                    T.dma('pool', otok_w, B["o_all"][bi * 128:(bi + 1) * 128, :], writes=["otok"])
                else:
                    T.dma('pool', ocand_w, B["o_all"][0:SEQ, :].rearrange("(q r) c -> r q c", q=4)[bi * 128:(bi + 1) * 128], writes=["ocand"])
                    T.op('dve', lambda e: e.tensor_scalar(out=otok_w, in0=ocand[:, 0, :], scalar1=selb[:, 0:1], scalar2=None, op0=ALU.mult),
                         reads=["ocand"], writes=["otok"])
                    for q in range(1, 4):
                        T.op('dve', lambda e: e.scalar_tensor_tensor(out=otok_w, in0=ocand[:, q, :], scalar=selb[:, q:q + 1], in1=otok,
                                                                     op0=ALU.mult, op1=ALU.add), reads=["ocand", "otok"], writes=["otok"])
                T.op('act', lambda e: e.activation(out=xnb[:, 0:512], in_=otok, func=AF.Square), reads=["otok"], writes=["xnb"])
                T.op('dve', lambda e: e.tensor_reduce(out=sq4, in_=xnb[:, 0:512].rearrange("p (h v) -> p h v", h=4), axis=AX.X, op=ALU.add),
                     reads=["xnb"], writes=["sq4"])
                T.op('act', lambda e: e.activation(out=sq4, in_=sq4, func=AF.Sqrt, bias=EPS, scale=1.0 / 128), reads=["sq4"], writes=["sq4"])
                T.op('dve', lambda e: e.reciprocal(out=sq4, in_=sq4), reads=["sq4"], writes=["sq4"])
                o3 = otok.rearrange("p (h v) -> p h v", h=4)
                o3w = otok_w.rearrange("p (h v) -> p h v", h=4)
                T.op('dve', lambda e: e.tensor_tensor(out=o3w, in0=o3, in1=sq4.unsqueeze(2).to_broadcast([128, 4, 128]), op=ALU.mult),
                     reads=["otok", "sq4"], writes=["otok"])
                T.op('dve', lambda e: e.tensor_tensor(out=otok_w, in0=otok, in1=gdn, op=ALU.mult), reads=["otok"], writes=["otok"])
                T.op('dve', lambda e: e.tensor_tensor(out=otok_w, in0=otok, in1=ztok[:, blk, :], op=ALU.mult), reads=["otok", "ztok"], writes=["otok"])
                for cc in range(4):
                    T.op('pe', lambda e: e.transpose(out=ps[4][:, cc * 128:(cc + 1) * 128], in_=otok[:, cc * 128:(cc + 1) * 128], identity=ident),
                         reads=["otok"], writes=["ps4"])
                T.op('act', lambda e: e.activation(out=ybT[:, :, blk * 128:(blk + 1) * 128], in_=ps[4].rearrange("p (c t) -> p c t", c=4), func=AF.Copy),
                     reads=["ps4"], writes=["ybT"])
            for n in range(4):
                wga = load_w(w_in, OFF_G + n * 256, 256, 8)
                wgb = load_w(w_in, OFF_G + 1024 + n * 256, 256, 8)
                i = wrr[0] % 3
                wrr[0] += 1
                T.dma('pool', wb[i][:, 0:4, :], w_upp[:, n * 256:(n + 1) * 256].rearrange("(k p) c -> p k c", p=128), writes=["wb%d" % i])
                T.dma('pool', wb[i][:, 4:8, :], w_upd[:, n * 256:(n + 1) * 256].rearrange("(k p) c -> p k c", p=128), writes=["wb%d" % i])
                wup, wupk = wb[i], "wb%d" % i
                for blk in blks:
                    tcs = slice(blk * 128, (blk + 1) * 128)
                    pg, pu = ps[blk * 2], ps[blk * 2 + 1]
                    kg, ku = "ps%d" % (blk * 2), "ps%d" % (blk * 2 + 1)
                    for dc in range(8):
                        T.op('pe', lambda e: e.matmul(pg[:, 0:256], lhsT=xnT[:, dc, tcs], rhs=wga[0][:, dc, :], start=(dc == 0), stop=(dc == 7)),
                             reads=["xnT", wga[1]], writes=[kg])
                    for dc in range(8):
                        T.op('pe', lambda e: e.matmul(pg[:, 256:512], lhsT=xnT[:, dc, tcs], rhs=wgb[0][:, dc, :], start=(dc == 0), stop=(dc == 7)),
                             reads=["xnT", wgb[1]], writes=[kg])
                    for cc in range(4):
                        T.op('pe', lambda e: e.matmul(pu[:, 0:256], lhsT=yaT[:, cc, tcs], rhs=wup[:, cc, :], start=(cc == 0), stop=(cc == 3)),
                             reads=["yaT", wupk], writes=[ku])
                    for cc in range(4):
                        T.op('pe', lambda e: e.matmul(pu[:, 256:512], lhsT=ybT[:, cc, tcs], rhs=wup[:, 4 + cc, :], start=(cc == 0), stop=(cc == 3)),
                             reads=["ybT", wupk], writes=[ku])
                    T.op('act', lambda e: e.activation(out=sga, in_=pg[:, 0:256], func=AF.Sigmoid), reads=[kg], writes=["sga"])
                    T.op('act', lambda e: e.activation(out=sgb, in_=pg[:, 256:512], func=AF.Sigmoid), reads=[kg], writes=["sgb"])
                    T.op('dve', lambda e: e.tensor_tensor(out=sga, in0=pu[:, 0:256], in1=sga, op=ALU.mult), reads=[ku, "sga"], writes=["sga"])
                    T.op('dve', lambda e: e.tensor_tensor(out=sgb, in0=pu[:, 256:512], in1=sgb, op=ALU.mult), reads=[ku, "sgb"], writes=["sgb"])
                    T.op('pool', lambda e: e.tensor_tensor(out=mtok_w[:, blk, n * 256:(n + 1) * 256], in0=sga, in1=sgb, op=ALU.add),
                         reads=["sga", "sgb"], writes=["mtok"])
            for blk in blks:
                for half in range(2):
                    for q in range(4):
                        dc = half * 4 + q
                        T.op('pe', lambda e: e.transpose(out=ps[4 + half][:, q * 128:(q + 1) * 128], in_=mtok[:, blk, dc * 128:(dc + 1) * 128], identity=ident),
                             reads=["mtok"], writes=["ps%d" % (4 + half)])
                    T.op('act' if half else 'dve',
                         (lambda e: e.activation(out=mT[:, half * 4:(half + 1) * 4, blk * 128:(blk + 1) * 128],
                                                 in_=ps[4 + half].rearrange("p (q t) -> p q t", q=4), func=AF.Copy)) if half else
                         (lambda e: e.tensor_copy(out=mT[:, half * 4:(half + 1) * 4, blk * 128:(blk + 1) * 128],
                                                  in_=ps[4 + half].rearrange("p (q t) -> p q t", q=4))),
                         reads=["ps%d" % (4 + half)], writes=["mT"])
            for n in range(4):
                wo = load_w(w_out, n * 256, 256, 8)
                for blk in blks:
                    pk = 6 + blk
                    for dc in range(8):
                        T.op('pe', lambda e: e.matmul(ps[pk][:, 0:256], lhsT=mT[:, dc, blk * 128:(blk + 1) * 128], rhs=wo[0][:, dc, :],
                                                      start=(dc == 0), stop=(dc == 7)), reads=["mT", wo[1]], writes=["ps%d" % pk])
                    T.op('dve', lambda e: e.tensor_tensor(out=xres[:, blk, n * 256:(n + 1) * 256], in0=ps[pk][:, 0:256],
                                                          in1=xres[:, blk, n * 256:(n + 1) * 256], op=ALU.add),
                         reads=["ps%d" % pk, "xres%d" % blk], writes=["xres%d" % blk])
            for blk in blks:
                norm_block(xres[:, blk, :], "xres%d" % blk, gffnc, xn2T, "xn2T", slice(blk * 128, (blk + 1) * 128))
            if STAGE < 6:
                for blk in blks:
                    row0 = ti * TT + blk * 128
                    T.dma('sp', B["y"][row0:row0 + 128, :], xres[:, blk, :], reads=["xres%d" % blk])
                T.barrier()
                continue
            for qc_ in range(8):
                wq = load_w(w_q, qc_ * 256, 256, 8)
                for gq in range(2):
                    for dc in range(8):
                        T.op('pe', lambda e: e.matmul(ps[2][:, gq * 256:(gq + 1) * 256], lhsT=wq[0][:, dc, gq * 128:(gq + 1) * 128],
                                                      rhs=xn2T[:, dc, :], start=(dc == 0), stop=(dc == 7)),
                             reads=["xn2T", wq[1]], writes=["ps2"])
                T.op('act', lambda e: e.activation(out=qTsb, in_=ps[2].rearrange("p (g t) -> p g t", g=2), func=AF.Copy), reads=["ps2"], writes=["qTsb"])
                for gq in range(2):
                    cq = qc_ * 2 + gq
                    hh, half = cq // 2, cq % 2
                    for blk in blks:
                        bk = "B%d_" % blk
                        pk = 3 + blk
                        T.op('pe', lambda e: e.matmul(ps[pk][:, 0:128], lhsT=qTsb[:, gq, blk * 128:(blk + 1) * 128],
                                                      rhs=keysT[:, half * 8 + hh, :], start=True, stop=True),
                             reads=["qTsb"], writes=["ps%d" % pk])
                        vvb, iub = B_["v%d" % (half + 1)][blk], B_["i%du" % (half + 1)][blk]
                        T.op('dve', lambda e: e.max(out=vvb[:, hh, 0:8], in_=ps[pk][:, 0:128]), reads=["ps%d" % pk], writes=[bk + "v"])
                        T.op('dve', lambda e: e.match_replace(out=tmpk, in_to_replace=vvb[:, hh, 0:8], in_values=ps[pk][:, 0:128], imm_value=-1e30),
                             reads=["ps%d" % pk, bk + "v"], writes=["tmpk"])
                        T.op('dve', lambda e: e.max(out=vvb[:, hh, 8:16], in_=tmpk), reads=["tmpk"], writes=[bk + "v"])
                        T.op('dve', lambda e: e.max_index(out=iub[:, hh, 0:8], in_max=vvb[:, hh, 0:8], in_values=ps[pk][:, 0:128]),
                             reads=["ps%d" % pk, bk + "v"], writes=[bk + "i"])
                        T.op('dve', lambda e: e.max_index(out=iub[:, hh, 8:16], in_max=vvb[:, hh, 8:16], in_values=ps[pk][:, 0:128]),
                             reads=["ps%d" % pk, bk + "v"], writes=[bk + "i"])
            for blk in blks:
                bk = "B%d_" % blk
                v1b, v2b, i1b, i2b = B_["v1"][blk], B_["v2"][blk], B_["i1u"][blk], B_["i2u"][blk]
                tcs = slice(blk * 128, (blk + 1) * 128)
                T.op('dve', lambda e: e.tensor_copy(out=i1f, in_=i1b), reads=[bk + "i"], writes=["i1f"])
                T.op('dve', lambda e: e.tensor_copy(out=i2f, in_=i2b), reads=[bk + "i"], writes=["i2f"])
                c4 = cand_w.rearrange("p (h a b) -> p h a b", h=8, a=16)
                T.op('dve', lambda e: e.tensor_tensor(out=c4, in0=v1b.unsqueeze(3).to_broadcast([128, 8, 16, 16]),
                                                      in1=v2b.unsqueeze(2).to_broadcast([128, 8, 16, 16]), op=ALU.add),
                     reads=[bk + "v"], writes=["cand"])
                c3 = cand.rearrange("p (h x) -> p h x", h=8)
                for hh in range(8):
                    T.op('dve', lambda e: e.max(out=cv[:, hh, 0:8], in_=c3[:, hh, :]), reads=["cand"], writes=["cv"])
                    T.op('dve', lambda e: e.match_replace(out=tmp2, in_to_replace=cv[:, hh, 0:8], in_values=c3[:, hh, :], imm_value=-1e30),
                         reads=["cand", "cv"], writes=["tmp2"])
                    T.op('dve', lambda e: e.max(out=cv[:, hh, 8:16], in_=tmp2), reads=["tmp2"], writes=["cv"])
                    T.op('dve', lambda e: e.max_index(out=ciu[:, hh, 0:8], in_max=cv[:, hh, 0:8], in_values=c3[:, hh, :]), reads=["cand", "cv"], writes=["ciu"])
                    T.op('dve', lambda e: e.max_index(out=ciu[:, hh, 8:16], in_max=cv[:, hh, 8:16], in_values=c3[:, hh, :]), reads=["cand", "cv"], writes=["ciu"])
                T.op('dve', lambda e: e.tensor_tensor(out=ex, in0=cv, in1=cv[:, :, 0:1].to_broadcast([128, 8, 16]), op=ALU.subtract),
                     reads=["cv"], writes=["ex"])
                T.op('act', lambda e: e.activation(out=ex, in_=ex, func=AF.Exp), reads=["ex"], writes=["ex"])
                T.op('dve', lambda e: e.tensor_reduce(out=s8, in_=ex, axis=AX.X, op=ALU.add), reads=["ex"], writes=["s8"])
                T.op('dve', lambda e: e.reciprocal(out=s8, in_=s8), reads=["s8"], writes=["s8"])
                T.op('dve', lambda e: e.tensor_tensor(out=gsl.rearrange("p (h k) -> p h k", h=8), in0=ex,
                                                      in1=s8.unsqueeze(2).to_broadcast([128, 8, 16]), op=ALU.mult),
                     reads=["ex", "s8"], writes=["gsl"])
                T.op('dve', lambda e: e.tensor_scalar(out=abu, in0=ciu, scalar1=4, scalar2=None, op0=ALU.logical_shift_right), reads=["ciu"], writes=["abu"])
                T.op('dve', lambda e: e.tensor_copy(out=af, in_=abu), reads=["abu"], writes=["af"])
                T.op('dve', lambda e: e.tensor_scalar(out=abu, in0=ciu, scalar1=15, scalar2=None, op0=ALU.bitwise_and), reads=["ciu", "af"], writes=["abu"])
                T.op('dve', lambda e: e.tensor_copy(out=bf, in_=abu), reads=["abu"], writes=["bf"])
                for (sel_f, idx_f, dst) in ((af, i1f, e1f), (bf, i2f, e2f)):
                    e3 = eqb.rearrange("p (s a) -> p s a", a=16)
                    e3w = eqb_w.rearrange("p (s a) -> p s a", a=16)
                    T.op('dve', lambda e: e.tensor_tensor(out=e3w, in0=sel_f.rearrange("p h k -> p (h k)").unsqueeze(2).to_broadcast([128, 128, 16]),
                                                          in1=iota[:, 0:16].unsqueeze(1).to_broadcast([128, 128, 16]), op=ALU.is_equal),
                         reads=["af", "bf"], writes=["eqb"])
                    e4 = eqb.rearrange("p (h k a) -> p h k a", h=8, k=16)
                    e4w = eqb_w.rearrange("p (h k a) -> p h k a", h=8, k=16)
                    T.op('dve', lambda e: e.tensor_tensor(out=e4w, in0=e4, in1=idx_f.unsqueeze(2).to_broadcast([128, 8, 16, 16]), op=ALU.mult),
                         reads=["eqb", "i1f", "i2f"], writes=["eqb"])
                    T.op('dve', lambda e: e.tensor_reduce(out=dst, in_=e3, axis=AX.X, op=ALU.add), reads=["eqb"], writes=["e12f"])
                for (src, dstT, nm) in ((e1f, e1T, "e1T"), (e2f, e2T, "e2T"), (gsl, gT, "gT")):
                    T.op('pe', lambda e: e.transpose(out=ps[5][:, 0:128], in_=src, identity=ident), reads=["e12f", "gsl"], writes=["ps5"])
                    T.op('act', lambda e: e.activation(out=dstT[:, tcs], in_=ps[5][:, 0:128], func=AF.Copy), reads=["ps5"], writes=[nm])
            T.barrier()
            for hf in range(2):
                for t in range(TT):
                    a_ = oha[t % 4]
                    b_ = ohb[t % 4]
                    T.op('dve', lambda e: e.tensor_scalar(out=a_, in0=iota[:, hf * 64:(hf + 1) * 64], scalar1=e1T[:, t:t + 1], scalar2=gT[:, t:t + 1],
                                                          op0=ALU.is_equal, op1=ALU.mult), reads=["e1T", "gT"], writes=["oha%d" % (t % 4)])
                    T.op('dve', lambda e: e.tensor_scalar(out=b_, in0=iota, scalar1=e2T[:, t:t + 1], scalar2=None, op0=ALU.is_equal),
                         reads=["e2T"], writes=["ohb%d" % (t % 4)])
                    pk = 6 + (t // 8) % 2
                    T.op('pe', lambda e: e.matmul(ps[pk][:, (t % 8) * 64:(t % 8 + 1) * 64], lhsT=b_, rhs=a_, start=True, stop=True),
                         reads=["oha%d" % (t % 4), "ohb%d" % (t % 4)], writes=["ps%d" % pk])
                    if t % 8 == 7:
                        t0 = t - 7
                        T.op('act', lambda e: e.activation(out=Wsb_w[:, :, t0:t0 + 8].rearrange("p e t -> p t e"),
                                                           in_=ps[pk].rearrange("p (t e) -> p t e", t=8), func=AF.Copy),
                             reads=["ps%d" % pk], writes=["Wsb"])
                for i in range(64):
                    e1 = hf * 64 + i
                    ub, vb = ublk[e1 % 3], vblk[e1 % 3]
                    uk, vk = "ublk%d" % (e1 % 3), "vblk%d" % (e1 % 3)
                    T.dma('pool', ub, B["UTb"][e1], writes=[uk])
                    T.dma('pool', vb, B["peer_v"][e1 * 128:(e1 + 1) * 128, :], writes=[vk])
                    pk = 4 + e1 % 2
                    for dc in range(8):
                        T.op('pe', lambda e: e.matmul(ps[pk][:, 0:TT], lhsT=ub[:, dc, :], rhs=xn2T[:, dc, :], start=(dc == 0), stop=(dc == 7)),
                             reads=[uk, "xn2T"], writes=["ps%d" % pk])
                    ab, gb_ = actb[e1 % 2], gab[e1 % 2]
                    T.op('act', lambda e: e.activation(out=ab, in_=ps[pk][:, 0:TT], func=AF.Gelu), reads=["ps%d" % pk], writes=["actb%d" % (e1 % 2)])
                    T.op('dve', lambda e: e.tensor_tensor(out=gb_, in0=ab, in1=Wsb[:, i, :], op=ALU.mult),
                         reads=["actb%d" % (e1 % 2), "Wsb"], writes=["gab%d" % (e1 % 2)])
                    for blk in blks:
                        for half in range(2):
                            T.op('pe', lambda e: e.matmul(ps[blk * 2 + half], lhsT=gb_[:, blk * 128:(blk + 1) * 128], rhs=vb[:, half * 512:(half + 1) * 512],
                                                          start=(e1 == 0), stop=(e1 == 127)),
                                 reads=["gab%d" % (e1 % 2), vk], writes=["psy%d" % (blk * 2 + half)])
                T.barrier()
            for blk in blks:
                row0 = ti * TT + blk * 128
                for half in range(2):
                    T.op('dve', lambda e: e.tensor_tensor(out=xres[:, blk, half * 512:(half + 1) * 512], in0=ps[blk * 2 + half],
                                                          in1=xres[:, blk, half * 512:(half + 1) * 512], op=ALU.add),
                         reads=["psy%d" % (blk * 2 + half), "xres%d" % blk], writes=["xres%d" % blk])
                T.op('act', lambda e: e.activation(out=xnb, in_=xres[:, blk, :], func=AF.Square, accum_out=ss), reads=["xres%d" % blk], writes=["xnb", "ssb"])
                T.op('act', lambda e: e.activation(out=rs, in_=ss, func=AF.Sqrt, bias=EPS, scale=1.0 / D_MODEL), reads=["ssb"], writes=["rsb"])
                T.op('dve', lambda e: e.reciprocal(out=rstd, in_=rs), reads=["rsb"], writes=["rstdb"])
                T.op('dve', lambda e: e.scalar_tensor_tensor(out=xnb, in0=xres[:, blk, :], scalar=rstd, in1=gfin, op0=ALU.mult, op1=ALU.mult),
                     reads=["xres%d" % blk, "rstdb"], writes=["xnb"])
                T.dma('sp', B["y"][row0:row0 + 128, :], xnb, reads=["xnb"])
            T.barrier()
```

```python
import os
import numpy as np
from contextlib import ExitStack
import concourse.bass as bass
import concourse.mybir as mybir
from concourse.bass_utils import run_bass_kernel_spmd

F32 = mybir.dt.float32
F32R = mybir.dt.float32r
U32 = mybir.dt.uint32
I32 = mybir.dt.int32
BF16 = mybir.dt.bfloat16
AF = mybir.ActivationFunctionType
ALU = mybir.AluOpType
AX = mybir.AxisListType

D_MODEL = 1024
SEQ = 8192
NCORE = 8
QTOK = 2048
STOK = 256
OWN = QTOK + STOK
D_IN = 4616
OFF_QKV, OFF_Z, OFF_B, OFF_A, OFF_G = 512, 2048, 2560, 2564, 2568
EPS = 1e-6
NEG = -30000.0

STAGE = int(os.environ.get("MK_STAGE", "99"))
NGROUP_DBG = int(os.environ.get("MK_NGROUP", "32"))
SUB = int(os.environ.get("MK_SUB", "99"))
DEBUG_A = int(os.environ.get("MK_DEBUG_A", "0"))
TILES = [int(t) for t in os.environ.get("MK_TILES", "0,1,2,3,4,5,6,7,8").split(",")]


class Trk:
    ROT = int(os.environ.get("MK_ROT", "8000"))
    NDMA = 24

    def __init__(self, nc):
        self.nc = nc
        self.eng = dict(pe=nc.tensor, act=nc.scalar, dve=nc.vector, pool=nc.gpsimd, sp=nc.sync)
        self.sem = {k: nc.alloc_semaphore("prog_%s_0" % k) for k in self.eng}
        self.semgen = {k: 0 for k in self.eng}
        self.cnt = {k: 0 for k in self.eng}
        self.waited = {k: {} for k in self.eng}
        self.lastw = {}
        self.readers = {}
        self.dma_sems = [nc.alloc_semaphore("dmas_%d" % i) for i in range(self.NDMA)]
        self.dma_cnt = [0] * self.NDMA
        self.dma_rr = {'hw': 0, 'sw': 0}
        self.NSW = 8
        self.n_inst = 0

    def _wait(self, e, tok):
        owner, sem, val, semkey = tok
        if owner == e and e == 'pe':
            return
        w = self.waited[e]
        if w.get(semkey, 0) >= val:
            return
        self.eng[e].wait_ge(sem, val)
        w[semkey] = val

    def _deps(self, e, reads, writes):
        for b in reads:
            lw = self.lastw.get(b)
            if lw is not None:
                self._wait(e, lw)
            if b.startswith("ps"):
                for tok in self.readers.get(b, {}).values():
                    if tok[0] != e:
                        self._wait(e, tok)
        for b in writes:
            lw = self.lastw.get(b)
            if lw is not None:
                self._wait(e, lw)
            for tok in self.readers.get(b, {}).values():
                self._wait(e, tok)

    def _record(self, tok, reads, writes):
        for b in reads:
            self.readers.setdefault(b, {})[tok[0]] = tok
        for b in writes:
            self.lastw[b] = tok
            self.readers[b] = {}

    def op(self, e, fn, reads=(), writes=()):
        self._deps(e, reads, writes)
        inst = fn(self.eng[e])
        if self.cnt[e] >= self.ROT:
            self.semgen[e] += 1
            self.sem[e] = self.nc.alloc_semaphore("prog_%s_%d" % (e, self.semgen[e]))
            self.cnt[e] = 0
        self.cnt[e] += 1
        sem = self.sem[e]
        inst.then_inc(sem, 1)
        tok = (e, sem, self.cnt[e], (e, self.semgen[e]))
        self._record(tok, reads, writes)
        self.n_inst += 1
        return inst

    def dma(self, q, out, in_, reads=(), writes=()):
        self._deps(q, reads, writes)
        if q == 'pool':
            k = self.dma_rr['sw']
            self.dma_rr['sw'] = (k + 1) % self.NSW
        else:
            k = self.NSW + self.dma_rr['hw']
            self.dma_rr['hw'] = (self.dma_rr['hw'] + 1) % (self.NDMA - self.NSW)
        sem = self.dma_sems[k]
        if self.dma_cnt[k] > 0:
            self._wait(q, (('dma', k), sem, 16 * self.dma_cnt[k], ('dma', k)))
        inst = self.eng[q].dma_start(out=out, in_=in_)
        self.dma_cnt[k] += 1
        inst.then_inc(sem, 16)
        tok = (('dma', k), sem, 16 * self.dma_cnt[k], ('dma', k))
        self._record(tok, reads, writes)
        self.n_inst += 1
        return inst

    def wait_all_dma(self, e):
        for k in range(self.NDMA):
            if self.dma_cnt[k] > 0:
                self._wait(e, (('dma', k), self.dma_sems[k], 16 * self.dma_cnt[k], ('dma', k)))

    def barrier(self):
        toks = []
        for e2 in self.eng:
            if self.cnt[e2] > 0:
                toks.append((e2, self.sem[e2], self.cnt[e2], (e2, self.semgen[e2])))
        for e in self.eng:
            for tok in toks:
                if tok[0] != e:
                    self._wait(e, tok)
            self.wait_all_dma(e)

    def barrier_consts(self):
        for e in self.eng:
            self.wait_all_dma(e)


def _consts():
    idx = np.arange(128)
    ch = idx // 64
    same = (ch[:, None] == ch[None, :])
    c = {}
    c["ident"] = np.eye(128, dtype=np.float32)
    c["ones"] = np.ones((128, 128), np.float32)
    c["umask"] = (same & (idx[:, None] <= idx[None, :])).astype(np.float32)
    c["bd"] = same.astype(np.float32)
    low_incl = same & (idx[None, :] <= idx[:, None])
    c["lstrict"] = (same & (idx[None, :] < idx[:, None])).astype(np.float32)
    cs = np.zeros((128, 2), np.float32)
    cs[:64, 0] = 1.0
    cs[64:, 1] = 1.0
    c["chunksel"] = cs
    c["iota"] = np.tile(np.arange(128, dtype=np.float32)[None, :], (128, 1))
    return c


def rep128(v):
    v = np.asarray(v, np.float32).reshape(1, -1)
    return np.ascontiguousarray(np.repeat(v, 128, axis=0))


class Prog:
    def __init__(self):
        self.nc = bass.Bass("TRN2", target_bir_lowering=False)
        self.T = Trk(self.nc)
        self.din = {}
        self.dout = {}

    def inp(self, name, shape, dt=F32):
        ap = self.nc.dram_tensor(name, list(shape), dt, kind="ExternalInput").ap()
        self.din[name] = ap
        return ap

    def outp(self, name, shape, dt=F32):
        ap = self.nc.dram_tensor(name, list(shape), dt, kind="ExternalOutput").ap()
        self.dout[name] = ap
        return ap

    def scratch(self, name, shape, dt=F32):
        return self.nc.dram_tensor(name, list(shape), dt).ap()


def sb(nc, es, name, shape, dt=F32):
    h = es.enter_context(nc.sbuf_tensor("s_" + name, list(shape), dt))
    return h.ap()


def build():
    P = Prog()
    nc, T = P.nc, P.T
    xseq = P.inp("xseq", [SEQ, D_MODEL])
    xs = P.inp("xs", [STOK, D_MODEL])
    S0 = P.inp("S0", [4, 4, 128, 128])
    conv0T = P.inp("conv0T", [4, 1536, 3])
    w_in = P.inp("w_in", [D_MODEL, D_IN])
    gmix = P.inp("gmix_bc", [128, D_MODEL])
    wconvT = P.inp("wconvT", [1536, 4])
    alog = P.inp("alog_bc", [128, 4])
    dtb = P.inp("dtb_bc", [128, 4])
    cn = {k: P.inp("c_" + k, list(v.shape)) for k, v in _consts().items()}

    dn_p = P.outp("dn_p", [4, 128, 128])
    dn_s = P.outp("dn_s", [4, 4, 128, 128])
    conv_pT = P.outp("conv_pT", [1536, 3])
    conv_sT = P.outp("conv_sT", [4, 1536, 3])
    o_all = P.outp("o_all", [SEQ + STOK, 512]) if DEBUG_A else P.scratch("o_all", [SEQ + STOK, 512])

    Bd = {}
    for nm, shp in [("xown", [OWN, D_MODEL]), ("xhalo", [16, D_MODEL]), ("pool0", [4, 15, 512]),
                    ("gmix_col", [128, 8]), ("gffn_col", [128, 8]), ("gfin_bc", [128, D_MODEL]), ("gdn_bc", [128, 512]),
                    ("pscale_col", [128, 4]), ("sel", [128, 4]), ("w_grp", [4, 128, 128]), ("keysT", [16, 128, 128]),
                    ("bmain", [128, 4, 128]), ("bprev", [128, 4, 128]), ("bfirst", [128, 4, 128]), ("bsamp", [128, 4, 128]),
                    ("bsprev", [128, 4, 128]), ("w_up_pool", [512, D_MODEL]), ("w_up_dn", [512, D_MODEL]),
                    ("w_out", [D_MODEL, D_MODEL]), ("w_q", [D_MODEL, 2048]), ("UTb", [128, 128, 8, 128]),
                    ("peer_v", [16384, D_MODEL])]:
        Bd[nm] = P.inp(nm, shp)
    Bd["w_in"] = w_in
    Bd["o_all"] = o_all
    Bd["Ubf"] = P.scratch("Ubf", [128, 128, 1024], BF16)
    Bd["Vbf"] = P.scratch("Vbf", [16384, D_MODEL], BF16)
    Bd["y"] = P.outp("y", [OWN, D_MODEL])
    Bd["pool_p"] = P.outp("pool_p", [15, 512])
    Bd["pool_s"] = P.outp("pool_s", [4, 15, 512])

    ps = [nc.alloc_psum_tensor("psb%d" % i, [128, 512], F32).ap() for i in range(8)]

    with ExitStack() as es:
        C = {}
        for k, v in cn.items():
            C[k] = sb(nc, es, "k_" + k, list(v.shape))
            T.dma('sp', C[k], v)
        gmix_sb = sb(nc, es, "gmix", [128, D_MODEL])
        T.dma('sp', gmix_sb, gmix)
        wc_sb = sb(nc, es, "wconv", [128, 12, 4])
        T.dma('sp', wc_sb, wconvT.rearrange("(c p) j -> p c j", p=128))
        alog_sb = sb(nc, es, "alog", [128, 4])
        dtb_sb = sb(nc, es, "dtb", [128, 4])
        T.dma('sp', alog_sb, alog)
        T.dma('sp', dtb_sb, dtb)
        onesR = sb(nc, es, "onesR", [128, 128], F32R)
        T.dma('pool', onesR, cn["ones"])
        T.barrier_consts()
        nexpA = sb(nc, es, "nexpA", [128, 4])
        T.op('act', lambda e: e.activation(out=nexpA, in_=alog_sb, func=AF.Exp), writes=["nexpA"])
        T.op('dve', lambda e: e.tensor_scalar(out=nexpA, in0=nexpA, scalar1=-1.0, scalar2=None, op0=ALU.mult),
             reads=["nexpA"], writes=["nexpA"])

        phase_a(P, es, C, ps, dict(xseq=xseq, xs=xs, S0=S0, conv0T=conv0T, w_in=w_in, gmix=gmix_sb,
                                   wc=wc_sb, dtb=dtb_sb, nexpA=nexpA, onesR=onesR,
                                   dn_p=dn_p, dn_s=dn_s, conv_pT=conv_pT, conv_sT=conv_sT, o_all=o_all,
                                   UTb=Bd["UTb"], peer_v=Bd["peer_v"], Ubf=Bd["Ubf"], Vbf=Bd["Vbf"]))
        T.barrier()
        phase_b(P, C, ps, Bd)
        T.barrier()
    return P


def phase_a(P, es0, C, ps, A):
    nc, T = P.nc, P.T
    ident, onesF = C["ident"], C["ones"]
    with ExitStack() as es:
        w_a = sb(nc, es, "w_a", [128, 8, 1544], F32R)
        for dc in range(8):
            T.dma('pool', w_a[:, dc, 0:1536], A["w_in"][dc * 128:(dc + 1) * 128, OFF_QKV:OFF_Z], writes=["w_a"])
            T.dma('pool', w_a[:, dc, 1536:1544], A["w_in"][dc * 128:(dc + 1) * 128, OFF_B:OFF_G], writes=["w_a"])
        xt = [sb(nc, es, "xt%d" % i, [128, D_MODEL]) for i in range(2)]
        xn = sb(nc, es, "xn", [128, D_MODEL])
        ss = sb(nc, es, "ss", [128, 1])
        rs = sb(nc, es, "rs", [128, 1])
        rstd = sb(nc, es, "rstd", [128, 1])
        xnT = sb(nc, es, "xnT", [128, 8, 256], F32R)
        ext_w = sb(nc, es, "ext", [128, 12, 268], F32R)
        ext = ext_w.bitcast(F32)
        diagW = sb(nc, es, "diagW", [128, 48, 128], F32R)
        for cc in range(12):
            for j in range(4):
                T.op('pool', lambda e: e.tensor_scalar(out=diagW[:, cc * 4 + j, :], in0=ident, scalar1=A["wc"][:, cc, j:j + 1],
                                                       scalar2=None, op0=ALU.mult), writes=["diagW"])
        carry = sb(nc, es, "carry", [128, 12, 3])
        qcs = [sb(nc, es, "qc%d" % i, [128, 12, 256], F32R) for i in range(2)]
        sq = sb(nc, es, "sq", [128, 8, 256], F32R)
        rinv = sb(nc, es, "rinv", [128, 256])
        bas = [[sb(nc, es, "ba%d_%d" % (p_, i), [128, 8]) for i in range(2)] for p_ in range(2)]
        small = {}
        for nm, w in [("beta", 4), ("negbeta", 4), ("g", 4), ("sp", 4), ("gc", 4), ("gam", 4), ("dlt", 4),
                      ("bgn", 4), ("gsel", 8), ("egl", 8), ("tmp4", 4)]:
            small[nm] = [sb(nc, es, "%s%d" % (nm, i), [128, w]) for i in range(2)]
        NH = 4
        def tiles(nm, n=1, shape=(128, 128)):
            return [[sb(nc, es, "%s_h%d_%d" % (nm, h, i), list(shape)) for i in range(n)] for h in range(NH)]
        Lg_all = sb(nc, es, "Lg_all", [128, 4, 128])
        dec = tiles("dec")
        decT = tiles("decT")
        def tilesR(nm, n=1, shape=(128, 128)):
            return [[sb(nc, es, "%s_h%d_%d" % (nm, h, i), list(shape), F32R) for i in range(n)] for h in range(NH)]
        Mp = tilesR("Mp", 2)
        NR = tilesR("NR", 2, (128, 256))
        Rfin = tilesR("Rfin")
        VK = tilesR("VK", 1, (128, 256))
        Qg = tiles("Qg")
        wkn = tiles("wkn")
        kdec = tiles("kdec", 2)
        uv = tiles("uv", 2)
        PT = tiles("PT", 2)
        qeT = tiles("qeT", 2)
        FT = tiles("FT", 4)
        Sst = [sb(nc, es, "Sst%d" % i, [128, 4, 128]) for i in range(2)]
        ostage = [sb(nc, es, "ost%d" % i, [64, 512]) for i in range(2)]

        T.op('dve', lambda e: e.memset(Sst[0], 0.0), writes=["S0buf"])
        T.op('dve', lambda e: e.memset(carry, 0.0), writes=["carry"])
        s_cur = 0
        chunk_ctr = 0

        groups = [("p", g) for g in range(min(32, NGROUP_DBG))] + [("s", 0)]
        def s1a(kind, g, par):
            nseq, L = (1, 256) if kind == "p" else (4, 64)
            ba = bas[par]
            if kind == "p":
                last = (g == min(32, NGROUP_DBG) - 1)
                for e1 in range(g * 4, 128 if last else g * 4 + 4):
                    T.dma('pool', A["Ubf"][e1], A["UTb"][e1].rearrange("p k e -> p (k e)"))
                    T.dma('pool', A["Vbf"][e1 * 128:(e1 + 1) * 128, :], A["peer_v"][e1 * 128:(e1 + 1) * 128, :])
            extv = ext[:, :, 0:nseq * (L + 3)].rearrange("p c (s l) -> p c s l", s=nseq)
            extvw = ext_w[:, :, 0:nseq * (L + 3)].rearrange("p c (s l) -> p c s l", s=nseq)
            for blk in range(2):
                xb = xt[blk]
                src = A["xseq"][g * 256 + blk * 128: g * 256 + (blk + 1) * 128, :] if kind == "p" \
                    else A["xs"][blk * 128:(blk + 1) * 128, :]
                T.dma('sp', xb, src, writes=["xt%d" % blk])
                T.op('act', lambda e: e.activation(out=xn, in_=xb, func=AF.Square, accum_out=ss),
                     reads=["xt%d" % blk], writes=["xn", "ss"])
                T.op('act', lambda e: e.activation(out=rs, in_=ss, func=AF.Sqrt, bias=EPS, scale=1.0 / D_MODEL),
                     reads=["ss"], writes=["rs"])
                T.op('dve', lambda e: e.reciprocal(out=rstd, in_=rs), reads=["rs"], writes=["rstd"])
                T.op('dve', lambda e: e.scalar_tensor_tensor(out=xn, in0=xb, scalar=rstd, in1=A["gmix"],
                                                             op0=ALU.mult, op1=ALU.mult),
                     reads=["xt%d" % blk, "rstd"], writes=["xn"])
                for half in range(2):
                    for q in range(4):
                        dc = half * 4 + q
                        T.op('pe', lambda e: e.transpose(out=ps[half][:, q * 128:(q + 1) * 128],
                                                         in_=xn[:, dc * 128:(dc + 1) * 128], identity=ident),
                             reads=["xn"], writes=["ps%d" % half])
                    T.op('act', lambda e: e.activation(
                        out=xnT[:, half * 4:(half + 1) * 4, blk * 128:(blk + 1) * 128],
                        in_=ps[half].rearrange("p (q t) -> p q t", q=4), func=AF.Copy),
                        reads=["ps%d" % half], writes=["xnT"])
                for dc in range(8):
                    T.op('pe', lambda e: e.matmul(ps[1][:, 504:512], lhsT=xnT[:, dc, blk * 128:(blk + 1) * 128],
                                                  rhs=w_a[:, dc, 1536:1544], start=(dc == 0), stop=(dc == 7)),
                         reads=["xnT", "w_a"], writes=["ps1"])
                T.op('dve', lambda e: e.tensor_copy(out=ba[blk], in_=ps[1][:, 504:512]),
                     reads=["ps1"], writes=["ba%d_%d" % (par, blk)])
        def s1b(kind, g, par):
            if STAGE < 2:
                return
            nseq, L = (1, 256) if kind == "p" else (4, 64)
            qc_w = qcs[par]
            qc = qc_w.bitcast(F32)
            extv = ext[:, :, 0:nseq * (L + 3)].rearrange("p c (s l) -> p c s l", s=nseq)
            extvw = ext_w[:, :, 0:nseq * (L + 3)].rearrange("p c (s l) -> p c s l", s=nseq)
            if kind == "p":
                T.op('pool', lambda e: e.tensor_copy(out=extvw[:, :, 0, 0:3], in_=carry), reads=["carry"], writes=["ext"])
            else:
                for s in range(4):
                    T.dma('pool', extvw[:, :, s, 0:3], A["conv0T"][s].rearrange("(c p) j -> p c j", p=128), writes=["ext"])
            for cc in range(12):
                hf = cc % 2
                pq = ps[hf][:, 0:256]
                for dc in range(8):
                    T.op('pe', lambda e: e.matmul(pq, lhsT=w_a[:, dc, cc * 128:(cc + 1) * 128],
                                                  rhs=xnT[:, dc, :], start=(dc == 0), stop=(dc == 7)),
                         reads=["xnT", "w_a"], writes=["ps%d" % hf])
                T.op('act', lambda e: e.activation(out=extvw[:, cc, :, 3:3 + L],
                                                   in_=pq.rearrange("p (s l) -> p s l", s=nseq),
                                                   func=AF.Copy),
                     reads=["ps%d" % hf], writes=["ext"])
            if kind == "p":
                T.op('pool', lambda e: e.tensor_copy(out=carry, in_=extv[:, :, 0, L:L + 3]), reads=["ext"], writes=["carry"])
                if g == 31:
                    T.dma('sp', A["conv_pT"].rearrange("(c p) j -> p c j", p=128), carry, reads=["carry"])
            else:
                for s in range(4):
                    T.dma('sp', A["conv_sT"][s].rearrange("(c p) j -> p c j", p=128), extv[:, :, s, L:L + 3], reads=["ext"])
            for cc in range(12):
                hf = cc % 2
                pq = ps[hf][:, 0:256]
                for j in range(4):
                    T.op('pe', lambda e: e.matmul(pq, lhsT=diagW[:, cc * 4 + j, :], rhs=extvw[:, cc, :, j:j + L], start=(j == 0), stop=(j == 3)),
                         reads=["ext"], writes=["ps%d" % hf])
                T.op('act', lambda e: e.activation(out=qc_w[:, cc, :], in_=pq, func=AF.Silu), reads=["ps%d" % hf], writes=["qc%d" % par])
            T.op('act', lambda e: e.activation(out=sq, in_=qc[:, 0:8, :], func=AF.Square), reads=["qc%d" % par], writes=["sq"])
            for cc in range(8):
                hf = cc % 2
                pq = ps[hf][:, 0:256]
                T.op('pe', lambda e: e.matmul(pq, lhsT=A["onesR"], rhs=sq[:, cc, :], start=True, stop=True),
                     reads=["sq"], writes=["ps%d" % hf])
                sc = 128.0 if cc < 4 else 1.0
                T.op('act', lambda e: e.activation(out=rinv, in_=pq, func=AF.Sqrt, bias=EPS * sc, scale=sc),
                     reads=["ps%d" % hf], writes=["rinv"])
                T.op('dve', lambda e: e.reciprocal(out=rinv, in_=rinv), reads=["rinv"], writes=["rinv"])
                T.op('dve', lambda e: e.tensor_tensor(out=qc_w[:, cc, :], in0=qc[:, cc, :], in1=rinv, op=ALU.mult),
                     reads=["qc%d" % par, "rinv"], writes=["qc%d" % par])
        def s2(kind, g, par, blk):
            nonlocal s_cur, chunk_ctr
            if STAGE < 3:
                return
            qc_w = qcs[par]
            qc = qc_w.bitcast(F32)
            ba = bas[par]
            if True:
                bp = blk
                cols = slice(blk * 128, (blk + 1) * 128)
                sm = {k: v[bp] for k, v in small.items()}
                bab = ba[blk]
                T.op('act', lambda e: e.activation(out=sm["beta"], in_=bab[:, 0:4], func=AF.Sigmoid),
                     reads=["ba%d_%d" % (par, blk)], writes=["beta%d" % bp])
                T.op('dve', lambda e: e.tensor_tensor(out=sm["sp"], in0=bab[:, 4:8], in1=A["dtb"], op=ALU.add),
                     reads=["ba%d_%d" % (par, blk)], writes=["sp%d" % bp])
                T.op('act', lambda e: e.activation(out=sm["sp"], in_=sm["sp"], func=AF.Exp), reads=["sp%d" % bp], writes=["sp%d" % bp])
                T.op('act', lambda e: e.activation(out=sm["sp"], in_=sm["sp"], func=AF.Ln, bias=1.0),
                     reads=["sp%d" % bp], writes=["sp%d" % bp])
                T.op('dve', lambda e: e.tensor_tensor(out=sm["g"], in0=sm["sp"], in1=A["nexpA"], op=ALU.mult),
                     reads=["sp%d" % bp, "nexpA"], writes=["g%d" % bp])
                T.op('dve', lambda e: e.tensor_scalar(out=sm["negbeta"], in0=sm["beta"], scalar1=-1.0, scalar2=None, op0=ALU.mult),
                     reads=["beta%d" % bp], writes=["negbeta%d" % bp])
                T.op('pe', lambda e: e.matmul(ps[7][:, 0:4], lhsT=C["umask"], rhs=sm["g"], start=True, stop=True),
                     reads=["g%d" % bp], writes=["ps7"])
                T.op('pe', lambda e: e.matmul(ps[7][:, 8:12], lhsT=C["bd"], rhs=sm["g"], start=True, stop=True),
                     reads=["g%d" % bp], writes=["ps7"])
                T.op('act', lambda e: e.activation(out=sm["gc"], in_=ps[7][:, 0:4], func=AF.Copy), reads=["ps7"], writes=["gc%d" % bp])
                T.op('act', lambda e: e.activation(out=sm["gam"], in_=ps[7][:, 0:4], func=AF.Exp), reads=["ps7"], writes=["gam%d" % bp])
                T.op('dve', lambda e: e.tensor_tensor(out=sm["tmp4"], in0=ps[7][:, 8:12], in1=sm["gc"], op=ALU.subtract),
                     reads=["ps7", "gc%d" % bp], writes=["tmp4%d" % bp])
                T.op('act', lambda e: e.activation(out=sm["dlt"], in_=sm["tmp4"], func=AF.Exp), reads=["tmp4%d" % bp], writes=["dlt%d" % bp])
                T.op('dve', lambda e: e.tensor_tensor(out=sm["bgn"], in0=sm["negbeta"], in1=sm["gam"], op=ALU.mult),
                     reads=["negbeta%d" % bp, "gam%d" % bp], writes=["bgn%d" % bp])
                T.op('dve', lambda e: e.tensor_tensor(
                    out=sm["gsel"].rearrange("p (h c) -> p h c", c=2),
                    in0=sm["g"].unsqueeze(2).to_broadcast([128, 4, 2]),
                    in1=C["chunksel"].unsqueeze(1).to_broadcast([128, 4, 2]), op=ALU.mult),
                    reads=["g%d" % bp], writes=["gsel%d" % bp])
                T.op('pe', lambda e: e.matmul(ps[7][:, 16:24], lhsT=onesF, rhs=sm["gsel"], start=True, stop=True),
                     reads=["gsel%d" % bp], writes=["ps7"])
                T.op('act', lambda e: e.activation(out=sm["egl"], in_=ps[7][:, 16:24], func=AF.Exp), reads=["ps7"], writes=["egl%d" % bp])

                if SUB < 1:
                    return

                def hk(nm, h, i=0):
                    return "%s_%d_%d" % (nm, h, i)

                def pslot(h, i):
                    if i == 0:
                        return ps[5][:, h * 128:(h + 1) * 128], "ps5_%d" % h
                    return ps[4][:, 128 + 0:128 + 0], None

                def slotA(h):
                    return ps[2 + h][:, 0:128], "ps%d" % (2 + h)

                def slotB(h):
                    return ps[2 + h][:, 128:256], "ps%d" % (2 + h)

                def slotC(h):
                    return ps[2 + h][:, 256:384], "ps%d" % (2 + h)

                T.op('dve', lambda e: e.tensor_tensor(out=Lg_all, in0=C["umask"].unsqueeze(1).to_broadcast([128, 4, 128]),
                                                      in1=sm["g"].unsqueeze(2).to_broadcast([128, 4, 128]), op=ALU.mult),
                     reads=["g%d" % bp], writes=["Lg_all"])
                T.op('pe', lambda e: e.matmul(ps[7], lhsT=onesF, rhs=Lg_all.rearrange("p h t -> p (h t)"), start=True, stop=True),
                     reads=["Lg_all"], writes=["ps7"])
                for h in range(NH):
                    gcb = ps[7][:, h * 128:(h + 1) * 128]
                    T.op('dve', lambda e: e.tensor_scalar(out=dec[h][0], in0=gcb, scalar1=sm["gc"][:, h:h + 1], scalar2=0.0,
                                                          op0=ALU.subtract, op1=ALU.max), reads=["ps7", "gc%d" % bp], writes=[hk("dec", h)])
                    T.op('dve', lambda e: e.tensor_scalar(out=decT[h][0], in0=gcb, scalar1=sm["gc"][:, h:h + 1], scalar2=0.0,
                                                          op0=ALU.subtract, op1=ALU.min), reads=["ps7", "gc%d" % bp], writes=[hk("decT", h)])
                    T.op('act', lambda e: e.activation(out=dec[h][0], in_=dec[h][0], func=AF.Exp, scale=-1.0), reads=[hk("dec", h)], writes=[hk("dec", h)])
                    T.op('act', lambda e: e.activation(out=decT[h][0], in_=decT[h][0], func=AF.Exp), reads=[hk("decT", h)], writes=[hk("decT", h)])
                    T.op('pool', lambda e: e.tensor_tensor(out=dec[h][0], in0=dec[h][0], in1=C["lstrict"], op=ALU.mult),
                         reads=[hk("dec", h)], writes=[hk("dec", h)])
                    T.op('pool', lambda e: e.tensor_tensor(out=decT[h][0], in0=decT[h][0], in1=C["umask"], op=ALU.mult),
                         reads=[hk("decT", h)], writes=[hk("decT", h)])
                for h in range(NH):
                    pc, kc = ps[2 + h][:, 0:256], "ps%d" % (2 + h)
                    T.op('pe', lambda e: e.matmul(pc, lhsT=qc_w[:, 4 + h, cols], rhs=qc_w[:, h:h + 5:4, cols], start=True, stop=True),
                         reads=["qc%d" % par], writes=[kc])
                    T.op('dve', lambda e: e.scalar_tensor_tensor(out=Mp[h][0], in0=pc[:, 128:256], scalar=sm["negbeta"][:, h:h + 1],
                                                                 in1=dec[h][0], op0=ALU.mult, op1=ALU.mult),
                         reads=[kc, hk("dec", h), "negbeta%d" % bp], writes=[hk("Mp", h, 0)])
                    T.op('dve', lambda e: e.tensor_tensor(out=PT[h][bp], in0=pc[:, 0:128], in1=decT[h][0], op=ALU.mult),
                         reads=[kc, hk("decT", h)], writes=[hk("PT", h, bp)])
                for h in range(NH):
                    pa, ka = ps[2 + h][:, 256:384], "ps%d" % (2 + h)
                    T.op('pe', lambda e: e.transpose(out=pa, in_=Mp[h][0].bitcast(F32), identity=ident), reads=[hk("Mp", h, 0)], writes=[ka])
                    T.op('act', lambda e: e.activation(out=NR[h][0][:, 0:128], in_=pa, func=AF.Copy), reads=[ka], writes=[hk("NR", h, 0)])
                    T.op('pool', lambda e: e.tensor_tensor(out=NR[h][1][:, 128:256], in0=NR[h][0][:, 0:128].bitcast(F32), in1=ident, op=ALU.add),
                         reads=[hk("NR", h, 0)], writes=[hk("NR", h, 1)])
                for h in range(NH):
                    pb, kb = ps[2 + h], "ps%d" % (2 + h)
                    T.op('pe', lambda e: e.matmul(pb[:, 0:128], lhsT=NR[h][0][:, 0:128], rhs=Mp[h][0], start=True, stop=True),
                         reads=[hk("NR", h, 0), hk("Mp", h, 0)], writes=[kb])
                    T.op('pe', lambda e: e.matmul(pb[:, 128:256], lhsT=Mp[h][0], rhs=NR[h][0][:, 0:128], start=True, stop=True),
                         reads=[hk("NR", h, 0), hk("Mp", h, 0)], writes=[kb])
                for h in range(NH):
                    pb, kb = ps[2 + h], "ps%d" % (2 + h)
                    T.op('dve', lambda e: e.tensor_copy(out=Mp[h][1], in_=pb[:, 0:128]), reads=[kb], writes=[hk("Mp", h, 1)])
                    T.op('act', lambda e: e.activation(out=NR[h][1][:, 0:128], in_=pb[:, 128:256], func=AF.Copy), reads=[kb], writes=[hk("NR", h, 1)])
                for lev in range(1, 5):
                    a, b2 = lev % 2, (lev + 1) % 2
                    for h in range(NH):
                        pb, kb = ps[2 + h], "ps%d" % (2 + h)
                        T.op('pe', lambda e: e.matmul(pb[:, 0:256], lhsT=Mp[h][a], rhs=NR[h][a], start=True, stop=True),
                             reads=[hk("NR", h, a), hk("Mp", h, a)], writes=[kb])
                        T.op('pe', lambda e: e.matmul(pb[:, 256:384], lhsT=NR[h][a][:, 0:128], rhs=Mp[h][a], start=True, stop=True),
                             reads=[hk("NR", h, a), hk("Mp", h, a)], writes=[kb])
                    for h in range(NH):
                        pb, kb = ps[2 + h], "ps%d" % (2 + h)
                        T.op('act', lambda e: e.activation(out=NR[h][b2][:, 0:128], in_=pb[:, 0:128], func=AF.Copy), reads=[kb], writes=[hk("NR", h, b2)])
                        T.op('dve', lambda e: e.tensor_tensor(out=NR[h][b2][:, 128:256], in0=pb[:, 128:256], in1=NR[h][a][:, 128:256].bitcast(F32), op=ALU.add),
                             reads=[kb, hk("NR", h, a)], writes=[hk("NR", h, b2)])
                        T.op('dve', lambda e: e.tensor_copy(out=Mp[h][b2], in_=pb[:, 256:384]), reads=[kb], writes=[hk("Mp", h, b2)])
                for h in range(NH):
                    pb, kb = ps[2 + h], "ps%d" % (2 + h)
                    T.op('pe', lambda e: e.matmul(pb[:, 0:128], lhsT=Mp[h][1], rhs=NR[h][1][:, 128:256], start=True, stop=True),
                         reads=[hk("NR", h, 1), hk("Mp", h, 1)], writes=[kb])
                    T.op('dve', lambda e: e.tensor_tensor(out=Rfin[h][0], in0=pb[:, 0:128], in1=NR[h][1][:, 128:256].bitcast(F32), op=ALU.add),
                         reads=[kb, hk("NR", h, 1)], writes=[hk("Rfin", h)])
                for h in range(NH):
                    QT, KT, VT = qc[:, h, cols], qc[:, 4 + h, cols], qc[:, 8 + h, cols]
                    pa, ka = slotA(h)
                    T.op('pe', lambda e: e.transpose(out=pa, in_=KT, identity=ident), reads=["qc%d" % par], writes=[ka])
                    T.op('act', lambda e: e.activation(out=VK[h][0][:, 128:256], in_=pa, func=AF.Copy, scale=sm["bgn"][:, h:h + 1]),
                         reads=[ka, "bgn%d" % bp], writes=[hk("VK", h)])
                    T.op('dve', lambda e: e.tensor_scalar(out=kdec[h][bp], in0=pa, scalar1=sm["dlt"][:, h:h + 1], scalar2=None, op0=ALU.mult),
                         reads=[ka, "dlt%d" % bp], writes=[hk("kdec", h, bp)])
                    pb, kb = slotB(h)
                    T.op('pe', lambda e: e.transpose(out=pb, in_=VT, identity=ident), reads=["qc%d" % par], writes=[kb])
                    T.op('act', lambda e: e.activation(out=VK[h][0][:, 0:128], in_=pb, func=AF.Copy, scale=sm["beta"][:, h:h + 1]),
                         reads=[kb, "beta%d" % bp], writes=[hk("VK", h)])
                    pc, kc = slotC(h)
                    T.op('pe', lambda e: e.transpose(out=pc, in_=QT, identity=ident), reads=["qc%d" % par], writes=[kc])
                    T.op('dve', lambda e: e.tensor_scalar(out=Qg[h][0], in0=pc, scalar1=sm["gam"][:, h:h + 1], scalar2=None, op0=ALU.mult),
                         reads=[kc, "gam%d" % bp], writes=[hk("Qg", h)])
                for h in range(NH):
                    pb, kb = ps[2 + h], "ps%d" % (2 + h)
                    T.op('pe', lambda e: e.matmul(pb[:, 0:256], lhsT=Rfin[h][0], rhs=VK[h][0], start=True, stop=True),
                         reads=[hk("Rfin", h), hk("VK", h)], writes=[kb])
                    T.op('act', lambda e: e.activation(out=uv[h][bp], in_=pb[:, 0:128], func=AF.Copy), reads=[kb], writes=[hk("uv", h, bp)])
                    T.op('dve', lambda e: e.tensor_copy(out=wkn[h][0], in_=pb[:, 128:256]), reads=[kb], writes=[hk("wkn", h)])
                if SUB < 6:
                    return
                for h in range(NH):
                    pa, ka = slotA(h)
                    T.op('pe', lambda e: e.matmul(pa, lhsT=Qg[h][0], rhs=ident, start=True, stop=False), reads=[hk("Qg", h)], writes=[ka])
                    T.op('pe', lambda e: e.matmul(pa, lhsT=wkn[h][0], rhs=PT[h][bp], start=False, stop=True),
                         reads=[hk("wkn", h), hk("PT", h, bp)], writes=[ka])
                    T.op('act', lambda e: e.activation(out=qeT[h][bp], in_=pa, func=AF.Copy), reads=[ka], writes=[hk("qeT", h, bp)])
                    for c in range(2):
                        pp, kp = (slotB(h) if c == 0 else slotC(h))
                        rows = slice(c * 64, (c + 1) * 64)
                        T.op('pe', lambda e: e.matmul(pp, lhsT=wkn[h][0][rows, :], rhs=kdec[h][bp][rows, :], start=True, stop=True),
                             reads=[hk("wkn", h), hk("kdec", h, bp)], writes=[kp])
                        T.op('dve', lambda e: e.scalar_tensor_tensor(out=FT[h][bp * 2 + c], in0=ident,
                                                                     scalar=sm["egl"][:, h * 2 + c:h * 2 + c + 1], in1=pp,
                                                                     op0=ALU.mult, op1=ALU.add),
                             reads=[kp, "egl%d" % bp], writes=[hk("FT", h, bp * 2 + c)])
                for c in range(2 if STAGE >= 4 else 0):
                    rows = slice(c * 64, (c + 1) * 64)
                    if kind == "s":
                        seq = blk * 2 + c
                        T.dma('sp', Sst[s_cur], A["S0"][seq].rearrange("h k v -> k h v"), writes=["S%dbuf" % s_cur])
                    Sc = Sst[s_cur]
                    skey = "S%dbuf" % s_cur
                    op_ = ostage[chunk_ctr % 2]
                    okey = "ost%d" % (chunk_ctr % 2)
                    for h in range(NH):
                        oo = ps[7][0:64, h * 128:(h + 1) * 128]
                        T.op('pe', lambda e: e.matmul(oo, lhsT=qeT[h][bp][:, rows], rhs=Sc[:, h, :], start=True, stop=False),
                             reads=[hk("qeT", h, bp), skey], writes=["ps7"])
                        T.op('pe', lambda e: e.matmul(oo, lhsT=PT[h][bp][rows, rows], rhs=uv[h][bp][rows, :], start=False, stop=True),
                             reads=[hk("PT", h, bp), hk("uv", h, bp)], writes=["ps7"])
                    T.op('act', lambda e: e.activation(out=op_, in_=ps[7][0:64, :], func=AF.Copy), reads=["ps7"], writes=[okey])
                    if kind == "p":
                        row0 = g * 256 + blk * 128 + c * 64
                    else:
                        row0 = SEQ + blk * 128 + c * 64
                    T.dma('sp', A["o_all"][row0:row0 + 64, :], op_, reads=[okey])
                    s_nxt = 1 - s_cur
                    for h in range(NH):
                        so = ps[6][:, h * 128:(h + 1) * 128]
                        T.op('pe', lambda e: e.matmul(so, lhsT=FT[h][bp * 2 + c], rhs=Sc[:, h, :], start=True, stop=False),
                             reads=[hk("FT", h, bp * 2 + c), skey], writes=["ps6"])
                        T.op('pe', lambda e: e.matmul(so, lhsT=kdec[h][bp][rows, :], rhs=uv[h][bp][rows, :], start=False, stop=True),
                             reads=[hk("kdec", h, bp), hk("uv", h, bp)], writes=["ps6"])
                    T.op('dve', lambda e: e.tensor_copy(out=Sst[s_nxt].rearrange("p h v -> p (h v)"), in_=ps[6]),
                         reads=["ps6"], writes=["S%dbuf" % s_nxt])
                    if kind == "s":
                        seq = blk * 2 + c
                        T.dma('sp', A["dn_s"][seq].rearrange("h k v -> k h v"), Sst[s_nxt], reads=["S%dbuf" % s_nxt])
                    s_cur = s_nxt
                    chunk_ctr += 1
            if kind == "p" and g == min(32, NGROUP_DBG) - 1 and blk == 1:
                T.dma('sp', A["dn_p"].rearrange("h k v -> k h v"), Sst[s_cur], reads=["S%dbuf" % s_cur])

        ng = len(groups)
        s1a(groups[0][0], groups[0][1], 0)
        s1b(groups[0][0], groups[0][1], 0)
        for i in range(ng):
            kind, g = groups[i]
            if i + 1 < ng:
                s1a(groups[i + 1][0], groups[i + 1][1], (i + 1) % 2)
            s2(kind, g, i % 2, 0)
            if i + 1 < ng:
                s1b(groups[i + 1][0], groups[i + 1][1], (i + 1) % 2)
            s2(kind, g, i % 2, 1)


_CACHE = {}


def kernel(**inp):
    f = lambda a: np.ascontiguousarray(np.asarray(a, dtype=np.float32))
    x_prompt, x_sample = f(inp["x_prompt"]), f(inp["x_sample"])
    if "prog" not in _CACHE:
        _CACHE["prog"] = build()
    P = _CACHE["prog"]
    consts = _consts()
    pu = f(inp["peer_u"])[0]
    shared = {
        "gmix_col": np.ascontiguousarray(f(inp["g_mix"])[0].reshape(8, 128).T),
        "gffn_col": np.ascontiguousarray(f(inp["g_ffn"])[0].reshape(8, 128).T),
        "gfin_bc": rep128(f(inp["g_final"])),
        "gdn_bc": rep128(np.tile(f(inp["g_dn_out"])[0], 4)),
        "pscale_col": np.ascontiguousarray(f(inp["pool_scale"])[0].reshape(4, 128).T),
        "w_grp": f(inp["w_pool_grp"])[0],
        "keysT": np.ascontiguousarray(f(inp["peer_sub_keys"])[0].transpose(0, 1, 3, 2).reshape(16, 128, 128)),
        "w_up_pool": f(inp["w_up_pool"])[0],
        "w_up_dn": f(inp["w_up_dn"])[0],
        "w_out": f(inp["w_out"])[0],
        "w_q": np.ascontiguousarray(f(inp["w_peer_q"])[0].reshape(D_MODEL, 2048)),
        "UTb": np.ascontiguousarray(pu.reshape(128, 128, 8, 128).transpose(0, 3, 2, 1)),
        "peer_v": f(inp["peer_v"])[0],
    }
    in_maps = []
    for c in range(NCORE):
        b, j = c // 4, c % 4
        m = {
            "xseq": x_prompt[b],
            "xs": x_sample[4 * c:4 * c + 4].reshape(STOK, D_MODEL),
            "S0": f(inp["state_dn"])[0, 4 * c:4 * c + 4],
            "conv0T": np.ascontiguousarray(f(inp["state_dn_conv"])[0, 4 * c:4 * c + 4].transpose(0, 2, 1)),
            "w_in": f(inp["w_in"])[0],
            "gmix_bc": rep128(f(inp["g_mix"])[0]),
            "wconvT": np.ascontiguousarray(f(inp["w_conv"])[0].T),
            "alog_bc": rep128(f(inp["a_log"])[0]),
            "dtb_bc": rep128(f(inp["dt_bias"])[0]),
        }
        for k, v in consts.items():
            m["c_" + k] = v
        m.update(shared)
        m["xown"] = np.ascontiguousarray(np.concatenate([x_prompt[b, j * QTOK:(j + 1) * QTOK], m["xs"]], axis=0))
        m["xhalo"] = np.ascontiguousarray(x_prompt[b, j * QTOK - 16:j * QTOK]) if j > 0 else np.zeros((16, D_MODEL), np.float32)
        m["pool0"] = f(inp["cache_pool"])[0, 4 * c:4 * c + 4]
        selv = np.zeros((128, 4), np.float32)
        selv[:, j] = 1.0
        m["sel"] = selv
        m.update(_bands(j))
        in_maps.append(m)
    res = run_bass_kernel_spmd(P.nc, in_maps, core_ids=list(range(NCORE)))
    R = res.results
    _CACHE["last"] = R
    new_dn_p = np.stack([R[0]["dn_p"], R[4]["dn_p"]])[None]
    new_dn_s = np.concatenate([R[c]["dn_s"] for c in range(NCORE)])[None]
    new_conv_p = np.stack([R[0]["conv_pT"].T, R[4]["conv_pT"].T])[None]
    new_conv_s = np.concatenate([R[c]["conv_sT"].transpose(0, 2, 1) for c in range(NCORE)])[None]
    y_prompt = np.stack([np.concatenate([R[4 * b + j]["y"][:QTOK] for j in range(4)]) for b in range(2)])
    y_sample = np.concatenate([R[c]["y"][QTOK:].reshape(4, 64, D_MODEL) for c in range(NCORE)])
    new_pool_p = np.stack([R[3]["pool_p"], R[7]["pool_p"]])[None]
    new_pool_s = np.concatenate([R[c]["pool_s"] for c in range(NCORE)])[None]
    f32 = lambda a: np.ascontiguousarray(a, dtype=np.float32)
    return (f32(y_prompt), f32(y_sample), f32(new_pool_p), f32(new_conv_p), f32(new_dn_p),
            f32(new_pool_s), f32(new_conv_s), f32(new_dn_s))


TT = 256
NTILE = OWN // TT
POOL_W = (2, 4, 8, 16)


def _bands(j):
    t = np.arange(128)
    out = {}
    main, prev, first, samp, sprev = [], [], [], [], []
    for w in POOL_W:
        src, dst = t[:, None], t[None, :]
        inwin = (src <= dst) & (src > dst - w)
        m = inwin / float(w) - np.eye(128)
        main.append(m)
        pw = ((src - 128) > dst - w) & (src >= 112)
        prev.append(pw / float(w))
        if j == 0:
            cnt = np.minimum(dst + 1, w).astype(np.float64)
            first.append(inwin / cnt - np.eye(128))
        else:
            first.append(m)
        same = (src // 64) == (dst // 64)
        samp.append((inwin & same) / float(w) - np.eye(128))
        sp = np.zeros((128, 128))
        for (r0, c0) in ((97, 0), (113, 64)):
            for i in range(15):
                for d in range(64):
                    if (-15 + i) > d - w:
                        sp[r0 + i, c0 + d] = 1.0 / w
        sprev.append(sp)
    f = lambda l: np.ascontiguousarray(np.stack(l).transpose(1, 0, 2).astype(np.float32))
    return dict(bmain=f(main), bprev=f(prev), bfirst=f(first), bsamp=f(samp), bsprev=f(sprev))


def phase_b(P, C, ps, B):
    nc, T = P.nc, P.T
    ident, iota = C["ident"], C["iota"]
    with ExitStack() as es:
        big = sb(nc, es, "big", [128, 16384])
        def reg(off, shape, dt=F32):
            n = int(np.prod(shape[1:]))
            v = big[:, off:off + n]
            if dt != F32:
                v = v.bitcast(dt)
            if len(shape) == 3:
                v = v.rearrange("p (a b) -> p a b", a=shape[1])
            return v
        xnT = reg(0, [128, 8, TT], F32R)
        wb = [reg(2048 + 2048 * i, [128, 8, 256], F32R) for i in range(3)]
        yaT = reg(8192, [128, 4, TT], F32R)
        ztok, ztok_w = reg(9216, [128, 2, 512]), reg(9216, [128, 2, 512], F32R)
        ybT = reg(10240, [128, 4, TT], F32R)
        mtok, mtok_w = reg(11264, [128, 2, 1024]), reg(11264, [128, 2, 1024], F32R)
        mT = reg(13312, [128, 8, TT], F32R)
        xnTh = reg(15360, [128, 8, 128], F32R)
        utok, utok_w = reg(11264, [128, 512]), reg(11264, [128, 512], F32R)
        otok, otok_w = reg(11776, [128, 512]), reg(11776, [128, 512], F32R)
        ocand, ocand_w = reg(12288, [128, 4, 512]), reg(12288, [128, 4, 512], F32R)
        cand, cand_w = reg(11264, [128, 2048]), reg(11264, [128, 2048], F32R)
        eqb, eqb_w = reg(13312, [128, 2048]), reg(13312, [128, 2048], F32R)
        Wsb = big.rearrange("p (e t) -> p e t", e=64)
        Wsb_w = big.bitcast(F32R).rearrange("p (e t) -> p e t", e=64)
        xres = sb(nc, es, "xres", [128, 2, 1024])
        xn2T = sb(nc, es, "xn2T", [128, 8, TT], F32R)
        ublk = [sb(nc, es, "ublk%d" % i, [128, 8, 128], BF16) for i in range(4)]
        vblk = [sb(nc, es, "vblk%d" % i, [128, 1024], BF16) for i in range(4)]
        xn2Tb = sb(nc, es, "xn2Tb", [128, 8, TT], BF16)
        actb = [sb(nc, es, "actb%d" % i, [128, TT]) for i in range(2)]
        gab = [sb(nc, es, "gab%d" % i, [128, TT], BF16) for i in range(2)]
        e1T = sb(nc, es, "e1T", [128, TT])
        e2T = sb(nc, es, "e2T", [128, TT])
        gT = sb(nc, es, "gT", [128, TT])
        uprev = sb(nc, es, "uprev", [128, 512])
        xh = sb(nc, es, "xh", [128, 1024])
        xnb = sb(nc, es, "xnb", [128, 1024])
        ss = sb(nc, es, "ssb", [128, 1])
        rs = sb(nc, es, "rsb", [128, 1])
        rstd = sb(nc, es, "rstdb", [128, 1])
        pooledT = sb(nc, es, "pooledT", [128, 128])
        qTsb = sb(nc, es, "qTsb", [128, 2, TT])
        sga = sb(nc, es, "sga", [128, 256])
        sgb = sb(nc, es, "sgb", [128, 256])
        sq4 = sb(nc, es, "sq4", [128, 4])
        B_ = {"v1": [sb(nc, es, "v1_%d" % i, [128, 8, 16]) for i in range(2)],
              "v2": [sb(nc, es, "v2_%d" % i, [128, 8, 16]) for i in range(2)],
              "i1u": [sb(nc, es, "i1u_%d" % i, [128, 8, 16], U32) for i in range(2)],
              "i2u": [sb(nc, es, "i2u_%d" % i, [128, 8, 16], U32) for i in range(2)]}
        i1f = sb(nc, es, "i1f", [128, 8, 16])
        i2f = sb(nc, es, "i2f", [128, 8, 16])
        tmpk = sb(nc, es, "tmpk", [128, 128])
        tmp2 = sb(nc, es, "tmp2", [128, 256])
        cv = sb(nc, es, "cv", [128, 8, 16])
        ciu = sb(nc, es, "ciu", [128, 8, 16], U32)
        abu = sb(nc, es, "abu", [128, 8, 16], U32)
        af = sb(nc, es, "af", [128, 8, 16])
        bf = sb(nc, es, "bf", [128, 8, 16])
        ex = sb(nc, es, "ex", [128, 8, 16])
        s8 = sb(nc, es, "s8", [128, 8])
        e1f = sb(nc, es, "e1f", [128, 128])
        e2f = sb(nc, es, "e2f", [128, 128])
        gsl = sb(nc, es, "gsl", [128, 128])
        oha = [sb(nc, es, "oha%d" % i, [128, 64], F32R) for i in range(4)]
        ohb = [sb(nc, es, "ohb%d" % i, [128, 128], F32R) for i in range(4)]
        gmixc = sb(nc, es, "gmixc", [128, 8]); T.dma('sp', gmixc, B["gmix_col"])
        gffnc = sb(nc, es, "gffnc", [128, 8]); T.dma('sp', gffnc, B["gffn_col"])
        gfin = sb(nc, es, "gfin", [128, 1024]); T.dma('sp', gfin, B["gfin_bc"])
        gdn = sb(nc, es, "gdn", [128, 512]); T.dma('sp', gdn, B["gdn_bc"])
        pscale = sb(nc, es, "pscale", [128, 4]); T.dma('sp', pscale, B["pscale_col"])
        selb = sb(nc, es, "selb", [128, 4]); T.dma('sp', selb, B["sel"])
        wgrp = sb(nc, es, "wgrp", [128, 4, 128]); T.dma('sp', wgrp, B["w_grp"].rearrange("g c d -> c g d"))
        keysT = sb(nc, es, "keysT", [128, 16, 128]); T.dma('sp', keysT, B["keysT"].rearrange("q d k -> d q k"))
        bands = {}
        for k in ("bmain", "bprev", "bfirst", "bsamp", "bsprev"):
            bands[k] = sb(nc, es, k, [128, 4, 128]); T.dma('sp', bands[k], B[k])
        T.barrier()

        w_in, w_upp, w_upd, w_out, w_q = B["w_in"], B["w_up_pool"], B["w_up_dn"], B["w_out"], B["w_q"]
        wrr = [0]

        def load_w(src2d, c0, ncol, kdim):
            i = wrr[0] % 3
            wrr[0] += 1
            dst = wb[i][:, 0:kdim, 0:ncol]
            T.dma('pool', dst, src2d[:, c0:c0 + ncol].rearrange("(k p) c -> p k c", p=128), writes=["wb%d" % i])
            return wb[i], "wb%d" % i

        def norm_block(xsrc, xkey, gcol, dstT, dkey, tcols):
            T.op('act', lambda e: e.activation(out=xnb, in_=xsrc, func=AF.Square, accum_out=ss), reads=[xkey], writes=["xnb", "ssb"])
            T.op('act', lambda e: e.activation(out=rs, in_=ss, func=AF.Sqrt, bias=EPS, scale=1.0 / D_MODEL), reads=["ssb"], writes=["rsb"])
            T.op('dve', lambda e: e.reciprocal(out=rstd, in_=rs), reads=["rsb"], writes=["rstdb"])
            T.op('dve', lambda e: e.tensor_scalar(out=xnb, in0=xsrc, scalar1=rstd, scalar2=None, op0=ALU.mult),
                 reads=[xkey, "rstdb"], writes=["xnb"])
            for half in range(2):
                for q in range(4):
                    dc = half * 4 + q
                    T.op('pe', lambda e: e.transpose(out=ps[half][:, q * 128:(q + 1) * 128], in_=xnb[:, dc * 128:(dc + 1) * 128], identity=ident),
                         reads=["xnb"], writes=["ps%d" % half])
                for q in range(4):
                    dc = half * 4 + q
                    T.op('act' if q % 2 else 'dve',
                         (lambda e: e.activation(out=dstT[:, dc, tcols], in_=ps[half][:, q * 128:(q + 1) * 128], func=AF.Copy, scale=gcol[:, dc:dc + 1]))
                         if q % 2 else
                         (lambda e: e.tensor_scalar(out=dstT[:, dc, tcols], in0=ps[half][:, q * 128:(q + 1) * 128], scalar1=gcol[:, dc:dc + 1], scalar2=None, op0=ALU.mult)),
                         reads=["ps%d" % half], writes=[dkey])

        for ti in (TILES if STAGE >= 5 else []):
            samp = (ti == NTILE - 1)
            blks = [0, 1]
            if ti == 0:
                T.op('pool', lambda e: e.memset(xh, 0.0), writes=["xh"])
                T.dma('sp', xh[112:128, :], B["xhalo"], writes=["xh"])
                norm_block(xh, "xh", gmixc, xnTh, "xnTh", slice(0, 128))
            for blk in blks:
                row0 = ti * TT + blk * 128
                T.dma('sp', xres[:, blk, :], B["xown"][row0:row0 + 128, :], writes=["xres%d" % blk])
                norm_block(xres[:, blk, :], "xres%d" % blk, gmixc, xnT, "xnT", slice(blk * 128, (blk + 1) * 128))
            wu = [load_w(w_in, c * 256, 256, 8) for c in range(2)]
            if ti == 0:
                for c in range(2):
                    for dc in range(8):
                        T.op('pe', lambda e: e.matmul(ps[4][:, c * 256:(c + 1) * 256], lhsT=xnTh[:, dc, :], rhs=wu[c][0][:, dc, :],
                                                      start=(dc == 0), stop=(dc == 7)), reads=["xnTh", wu[c][1]], writes=["ps4"])
                T.op('act', lambda e: e.activation(out=uprev, in_=ps[4], func=AF.Copy), reads=["ps4"], writes=["uprev"])
            if samp:
                T.op('pool', lambda e: e.memset(uprev, 0.0), writes=["uprev"])
            for c in range(2):
                for blk in blks:
                    for dc in range(8):
                        T.op('pe', lambda e: e.matmul(ps[2 + blk][:, c * 256:(c + 1) * 256], lhsT=xnT[:, dc, blk * 128:(blk + 1) * 128],
                                                      rhs=wu[c][0][:, dc, :], start=(dc == 0), stop=(dc == 7)),
                             reads=["xnT", wu[c][1]], writes=["ps%d" % (2 + blk)])
            for blk in blks:
                bi = ti * 2 + blk
                T.op('act', lambda e: e.activation(out=utok_w, in_=ps[2 + blk], func=AF.Copy), reads=["ps%d" % (2 + blk)], writes=["utok"])
                if samp:
                    for sq_ in range(2):
                        seq = blk * 2 + sq_
                        r0 = 97 if sq_ == 0 else 113
                        T.dma('sp', uprev[r0:r0 + 15, :], B["pool0"][seq], writes=["uprev"])
                        T.dma('sp', B["pool_s"][seq], utok[sq_ * 64 + 49: sq_ * 64 + 64, :], reads=["utok"])
                    bm, bp = bands["bsamp"], bands["bsprev"]
                else:
                    bm, bp = (bands["bfirst"] if bi == 0 else bands["bmain"]), bands["bprev"]
                    if bi == 15:
                        T.dma('sp', B["pool_p"], utok[113:128, :], reads=["utok"])
                for gi in range(4):
                    gc_ = slice(gi * 128, (gi + 1) * 128)
                    T.op('pe', lambda e: e.matmul(ps[4][:, 0:128], lhsT=uprev[:, gc_], rhs=bp[:, gi, :], start=True, stop=False),
                         reads=["uprev"], writes=["ps4"])
                    T.op('pe', lambda e: e.matmul(ps[4][:, 0:128], lhsT=utok[:, gc_], rhs=bm[:, gi, :], start=False, stop=True),
                         reads=["utok"], writes=["ps4"])
                    T.op('dve', lambda e: e.tensor_copy(out=pooledT, in_=ps[4][:, 0:128]), reads=["ps4"], writes=["pooledT"])
                    T.op('pe', lambda e: e.matmul(ps[5][:, 0:128], lhsT=wgrp[:, gi, :], rhs=pooledT, start=True, stop=True),
                         reads=["pooledT"], writes=["ps5"])
                    T.op('act', lambda e: e.activation(out=yaT[:, gi, blk * 128:(blk + 1) * 128], in_=ps[5][:, 0:128], func=AF.Copy,
                                                       scale=pscale[:, gi:gi + 1]), reads=["ps5"], writes=["yaT"])
                if not samp:
                    T.op('pool', lambda e: e.tensor_copy(out=uprev, in_=utok), reads=["utok"], writes=["uprev"])
            wz = [load_w(w_in, OFF_Z + c * 256, 256, 8) for c in range(2)]
            for c in range(2):
                for blk in blks:
                    for dc in range(8):
                        T.op('pe', lambda e: e.matmul(ps[2 + blk][:, c * 256:(c + 1) * 256], lhsT=xnT[:, dc, blk * 128:(blk + 1) * 128],
                                                      rhs=wz[c][0][:, dc, :], start=(dc == 0), stop=(dc == 7)),
                             reads=["xnT", wz[c][1]], writes=["ps%d" % (2 + blk)])
            for blk in blks:
                bi = ti * 2 + blk
                T.op('act', lambda e: e.activation(out=ztok_w[:, blk, :], in_=ps[2 + blk], func=AF.Silu), reads=["ps%d" % (2 + blk)], writes=["ztok"])
                if samp:
                    T.dma('pool', otok_w, B["o_all"][SEQ + blk * 128: SEQ + (blk + 1) * 128, :], writes=["otok"])
                elif NGROUP_DBG < 32:
                    T.dma('pool', otok_w, B["o_all"][bi * 128:(bi + 1) * 128, :], writes=["otok"])
                else:
                    T.dma('pool', ocand_w, B["o_all"][0:SEQ, :].rearrange("(q r) c -> r q c", q=4)[bi * 128:(bi + 1) * 128], writes=["ocand"])
                    T.op('dve', lambda e: e.tensor_scalar(out=otok_w, in0=ocand[:, 0, :], scalar1=selb[:, 0:1], scalar2=None, op0=ALU.mult),
                         reads=["ocand"], writes=["otok"])
                    for q in range(1, 4):
                        T.op('dve', lambda e: e.scalar_tensor_tensor(out=otok_w, in0=ocand[:, q, :], scalar=selb[:, q:q + 1], in1=otok,
                                                                     op0=ALU.mult, op1=ALU.add), reads=["ocand", "otok"], writes=["otok"])
                T.op('act', lambda e: e.activation(out=xnb[:, 0:512], in_=otok, func=AF.Square), reads=["otok"], writes=["xnb"])
                T.op('dve', lambda e: e.tensor_reduce(out=sq4, in_=xnb[:, 0:512].rearrange("p (h v) -> p h v", h=4), axis=AX.X, op=ALU.add),
                     reads=["xnb"], writes=["sq4"])
                T.op('act', lambda e: e.activation(out=sq4, in_=sq4, func=AF.Sqrt, bias=EPS, scale=1.0 / 128), reads=["sq4"], writes=["sq4"])
                T.op('dve', lambda e: e.reciprocal(out=sq4, in_=sq4), reads=["sq4"], writes=["sq4"])
                o3 = otok.rearrange("p (h v) -> p h v", h=4)
                o3w = otok_w.rearrange("p (h v) -> p h v", h=4)
                T.op('dve', lambda e: e.tensor_tensor(out=o3w, in0=o3, in1=sq4.unsqueeze(2).to_broadcast([128, 4, 128]), op=ALU.mult),
                     reads=["otok", "sq4"], writes=["otok"])
                T.op('dve', lambda e: e.tensor_tensor(out=otok_w, in0=otok, in1=gdn, op=ALU.mult), reads=["otok"], writes=["otok"])
                T.op('dve', lambda e: e.tensor_tensor(out=otok_w, in0=otok, in1=ztok[:, blk, :], op=ALU.mult), reads=["otok", "ztok"], writes=["otok"])
                for cc in range(4):
                    T.op('pe', lambda e: e.transpose(out=ps[4][:, cc * 128:(cc + 1) * 128], in_=otok[:, cc * 128:(cc + 1) * 128], identity=ident),
                         reads=["otok"], writes=["ps4"])
                T.op('act', lambda e: e.activation(out=ybT[:, :, blk * 128:(blk + 1) * 128], in_=ps[4].rearrange("p (c t) -> p c t", c=4), func=AF.Copy),
                     reads=["ps4"], writes=["ybT"])
            for n in range(4):
                wga = load_w(w_in, OFF_G + n * 256, 256, 8)
                wgb = load_w(w_in, OFF_G + 1024 + n * 256, 256, 8)
                i = wrr[0] % 3
                wrr[0] += 1
                T.dma('pool', wb[i][:, 0:4, :], w_upp[:, n * 256:(n + 1) * 256].rearrange("(k p) c -> p k c", p=128), writes=["wb%d" % i])
                T.dma('pool', wb[i][:, 4:8, :], w_upd[:, n * 256:(n + 1) * 256].rearrange("(k p) c -> p k c", p=128), writes=["wb%d" % i])
                wup, wupk = wb[i], "wb%d" % i
                for blk in blks:
                    tcs = slice(blk * 128, (blk + 1) * 128)
                    pg, pu = ps[blk * 2], ps[blk * 2 + 1]
                    kg, ku = "ps%d" % (blk * 2), "ps%d" % (blk * 2 + 1)
                    for dc in range(8):
                        T.op('pe', lambda e: e.matmul(pg[:, 0:256], lhsT=xnT[:, dc, tcs], rhs=wga[0][:, dc, :], start=(dc == 0), stop=(dc == 7)),
                             reads=["xnT", wga[1]], writes=[kg])
                    for dc in range(8):
                        T.op('pe', lambda e: e.matmul(pg[:, 256:512], lhsT=xnT[:, dc, tcs], rhs=wgb[0][:, dc, :], start=(dc == 0), stop=(dc == 7)),
                             reads=["xnT", wgb[1]], writes=[kg])
                    for cc in range(4):
                        T.op('pe', lambda e: e.matmul(pu[:, 0:256], lhsT=yaT[:, cc, tcs], rhs=wup[:, cc, :], start=(cc == 0), stop=(cc == 3)),
                             reads=["yaT", wupk], writes=[ku])
                    for cc in range(4):
                        T.op('pe', lambda e: e.matmul(pu[:, 256:512], lhsT=ybT[:, cc, tcs], rhs=wup[:, 4 + cc, :], start=(cc == 0), stop=(cc == 3)),
                             reads=["ybT", wupk], writes=[ku])
                    T.op('act', lambda e: e.activation(out=sga, in_=pg[:, 0:256], func=AF.Sigmoid), reads=[kg], writes=["sga"])
                    T.op('act', lambda e: e.activation(out=sgb, in_=pg[:, 256:512], func=AF.Sigmoid), reads=[kg], writes=["sgb"])
                    T.op('dve', lambda e: e.tensor_tensor(out=sga, in0=pu[:, 0:256], in1=sga, op=ALU.mult), reads=[ku, "sga"], writes=["sga"])
                    T.op('dve', lambda e: e.tensor_tensor(out=sgb, in0=pu[:, 256:512], in1=sgb, op=ALU.mult), reads=[ku, "sgb"], writes=["sgb"])
                    T.op('pool', lambda e: e.tensor_tensor(out=mtok_w[:, blk, n * 256:(n + 1) * 256], in0=sga, in1=sgb, op=ALU.add),
                         reads=["sga", "sgb"], writes=["mtok"])
            for blk in blks:
                for half in range(2):
                    for q in range(4):
                        dc = half * 4 + q
                        T.op('pe', lambda e: e.transpose(out=ps[4 + half][:, q * 128:(q + 1) * 128], in_=mtok[:, blk, dc * 128:(dc + 1) * 128], identity=ident),
                             reads=["mtok"], writes=["ps%d" % (4 + half)])
                    T.op('act' if half else 'dve',
                         (lambda e: e.activation(out=mT[:, half * 4:(half + 1) * 4, blk * 128:(blk + 1) * 128],
                                                 in_=ps[4 + half].rearrange("p (q t) -> p q t", q=4), func=AF.Copy)) if half else
                         (lambda e: e.tensor_copy(out=mT[:, half * 4:(half + 1) * 4, blk * 128:(blk + 1) * 128],
                                                  in_=ps[4 + half].rearrange("p (q t) -> p q t", q=4))),
                         reads=["ps%d" % (4 + half)], writes=["mT"])
            for n in range(4):
                wo = load_w(w_out, n * 256, 256, 8)
                for blk in blks:
                    pk = 6 + blk
                    for dc in range(8):
                        T.op('pe', lambda e: e.matmul(ps[pk][:, 0:256], lhsT=mT[:, dc, blk * 128:(blk + 1) * 128], rhs=wo[0][:, dc, :],
                                                      start=(dc == 0), stop=(dc == 7)), reads=["mT", wo[1]], writes=["ps%d" % pk])
                    T.op('dve', lambda e: e.tensor_tensor(out=xres[:, blk, n * 256:(n + 1) * 256], in0=ps[pk][:, 0:256],
                                                          in1=xres[:, blk, n * 256:(n + 1) * 256], op=ALU.add),
                         reads=["ps%d" % pk, "xres%d" % blk], writes=["xres%d" % blk])
            for blk in blks:
                norm_block(xres[:, blk, :], "xres%d" % blk, gffnc, xn2T, "xn2T", slice(blk * 128, (blk + 1) * 128))
            T.op('pool', lambda e: e.tensor_copy(out=xn2Tb, in_=xn2T.bitcast(F32)), reads=["xn2T"], writes=["xn2Tb"])
            if STAGE < 6:
                for blk in blks:
                    row0 = ti * TT + blk * 128
                    T.dma('sp', B["y"][row0:row0 + 128, :], xres[:, blk, :], reads=["xres%d" % blk])
                T.barrier()
                continue
            for qc_ in range(8):
                wq = load_w(w_q, qc_ * 256, 256, 8)
                for gq in range(2):
                    for dc in range(8):
                        T.op('pe', lambda e: e.matmul(ps[2][:, gq * 256:(gq + 1) * 256], lhsT=wq[0][:, dc, gq * 128:(gq + 1) * 128],
                                                      rhs=xn2T[:, dc, :], start=(dc == 0), stop=(dc == 7)),
                             reads=["xn2T", wq[1]], writes=["ps2"])
                T.op('act', lambda e: e.activation(out=qTsb, in_=ps[2].rearrange("p (g t) -> p g t", g=2), func=AF.Copy), reads=["ps2"], writes=["qTsb"])
                for gq in range(2):
                    cq = qc_ * 2 + gq
                    hh, half = cq // 2, cq % 2
                    for blk in blks:
                        bk = "B%d_" % blk
                        pk = 3 + blk
                        T.op('pe', lambda e: e.matmul(ps[pk][:, 0:128], lhsT=qTsb[:, gq, blk * 128:(blk + 1) * 128],
                                                      rhs=keysT[:, half * 8 + hh, :], start=True, stop=True),
                             reads=["qTsb"], writes=["ps%d" % pk])
                        vvb, iub = B_["v%d" % (half + 1)][blk], B_["i%du" % (half + 1)][blk]
                        T.op('dve', lambda e: e.max(out=vvb[:, hh, 0:8], in_=ps[pk][:, 0:128]), reads=["ps%d" % pk], writes=[bk + "v"])
                        T.op('dve', lambda e: e.match_replace(out=tmpk, in_to_replace=vvb[:, hh, 0:8], in_values=ps[pk][:, 0:128], imm_value=-1e30),
                             reads=["ps%d" % pk, bk + "v"], writes=["tmpk"])
                        T.op('dve', lambda e: e.max(out=vvb[:, hh, 8:16], in_=tmpk), reads=["tmpk"], writes=[bk + "v"])
                        T.op('dve', lambda e: e.max_index(out=iub[:, hh, 0:8], in_max=vvb[:, hh, 0:8], in_values=ps[pk][:, 0:128]),
                             reads=["ps%d" % pk, bk + "v"], writes=[bk + "i"])
                        T.op('dve', lambda e: e.max_index(out=iub[:, hh, 8:16], in_max=vvb[:, hh, 8:16], in_values=ps[pk][:, 0:128]),
                             reads=["ps%d" % pk, bk + "v"], writes=[bk + "i"])
            for blk in blks:
                bk = "B%d_" % blk
                v1b, v2b, i1b, i2b = B_["v1"][blk], B_["v2"][blk], B_["i1u"][blk], B_["i2u"][blk]
                tcs = slice(blk * 128, (blk + 1) * 128)
                T.op('dve', lambda e: e.tensor_copy(out=i1f, in_=i1b), reads=[bk + "i"], writes=["i1f"])
                T.op('dve', lambda e: e.tensor_copy(out=i2f, in_=i2b), reads=[bk + "i"], writes=["i2f"])
                c4 = cand_w.rearrange("p (h a b) -> p h a b", h=8, a=16)
                T.op('dve', lambda e: e.tensor_tensor(out=c4, in0=v1b.unsqueeze(3).to_broadcast([128, 8, 16, 16]),
                                                      in1=v2b.unsqueeze(2).to_broadcast([128, 8, 16, 16]), op=ALU.add),
                     reads=[bk + "v"], writes=["cand"])
                c3 = cand.rearrange("p (h x) -> p h x", h=8)
                for hh in range(8):
                    T.op('dve', lambda e: e.max(out=cv[:, hh, 0:8], in_=c3[:, hh, :]), reads=["cand"], writes=["cv"])
                    T.op('dve', lambda e: e.match_replace(out=tmp2, in_to_replace=cv[:, hh, 0:8], in_values=c3[:, hh, :], imm_value=-1e30),
                         reads=["cand", "cv"], writes=["tmp2"])
                    T.op('dve', lambda e: e.max(out=cv[:, hh, 8:16], in_=tmp2), reads=["tmp2"], writes=["cv"])
                    T.op('dve', lambda e: e.max_index(out=ciu[:, hh, 0:8], in_max=cv[:, hh, 0:8], in_values=c3[:, hh, :]), reads=["cand", "cv"], writes=["ciu"])
                    T.op('dve', lambda e: e.max_index(out=ciu[:, hh, 8:16], in_max=cv[:, hh, 8:16], in_values=c3[:, hh, :]), reads=["cand", "cv"], writes=["ciu"])
                T.op('dve', lambda e: e.tensor_tensor(out=ex, in0=cv, in1=cv[:, :, 0:1].to_broadcast([128, 8, 16]), op=ALU.subtract),
                     reads=["cv"], writes=["ex"])
                T.op('act', lambda e: e.activation(out=ex, in_=ex, func=AF.Exp), reads=["ex"], writes=["ex"])
                T.op('dve', lambda e: e.tensor_reduce(out=s8, in_=ex, axis=AX.X, op=ALU.add), reads=["ex"], writes=["s8"])
                T.op('dve', lambda e: e.reciprocal(out=s8, in_=s8), reads=["s8"], writes=["s8"])
                T.op('dve', lambda e: e.tensor_tensor(out=gsl.rearrange("p (h k) -> p h k", h=8), in0=ex,
                                                      in1=s8.unsqueeze(2).to_broadcast([128, 8, 16]), op=ALU.mult),
                     reads=["ex", "s8"], writes=["gsl"])
                T.op('dve', lambda e: e.tensor_scalar(out=abu, in0=ciu, scalar1=4, scalar2=None, op0=ALU.logical_shift_right), reads=["ciu"], writes=["abu"])
                T.op('dve', lambda e: e.tensor_copy(out=af, in_=abu), reads=["abu"], writes=["af"])
                T.op('dve', lambda e: e.tensor_scalar(out=abu, in0=ciu, scalar1=15, scalar2=None, op0=ALU.bitwise_and), reads=["ciu", "af"], writes=["abu"])
                T.op('dve', lambda e: e.tensor_copy(out=bf, in_=abu), reads=["abu"], writes=["bf"])
                for (sel_f, idx_f, dst) in ((af, i1f, e1f), (bf, i2f, e2f)):
                    e3 = eqb.rearrange("p (s a) -> p s a", a=16)
                    e3w = eqb_w.rearrange("p (s a) -> p s a", a=16)
                    T.op('dve', lambda e: e.tensor_tensor(out=e3w, in0=sel_f.rearrange("p h k -> p (h k)").unsqueeze(2).to_broadcast([128, 128, 16]),
                                                          in1=iota[:, 0:16].unsqueeze(1).to_broadcast([128, 128, 16]), op=ALU.is_equal),
                         reads=["af", "bf"], writes=["eqb"])
                    e4 = eqb.rearrange("p (h k a) -> p h k a", h=8, k=16)
                    e4w = eqb_w.rearrange("p (h k a) -> p h k a", h=8, k=16)
                    T.op('dve', lambda e: e.tensor_tensor(out=e4w, in0=e4, in1=idx_f.unsqueeze(2).to_broadcast([128, 8, 16, 16]), op=ALU.mult),
                         reads=["eqb", "i1f", "i2f"], writes=["eqb"])
                    T.op('dve', lambda e: e.tensor_reduce(out=dst, in_=e3, axis=AX.X, op=ALU.add), reads=["eqb"], writes=["e12f"])
                for (src, dstT, nm) in ((e1f, e1T, "e1T"), (e2f, e2T, "e2T"), (gsl, gT, "gT")):
                    T.op('pe', lambda e: e.transpose(out=ps[5][:, 0:128], in_=src, identity=ident), reads=["e12f", "gsl"], writes=["ps5"])
                    T.op('act', lambda e: e.activation(out=dstT[:, tcs], in_=ps[5][:, 0:128], func=AF.Copy), reads=["ps5"], writes=[nm])
            T.barrier()
            for hf in range(2):
                for t in range(TT):
                    a_ = oha[t % 4]
                    b_ = ohb[t % 4]
                    T.op('dve', lambda e: e.tensor_scalar(out=a_, in0=iota[:, hf * 64:(hf + 1) * 64], scalar1=e1T[:, t:t + 1], scalar2=gT[:, t:t + 1],
                                                          op0=ALU.is_equal, op1=ALU.mult), reads=["e1T", "gT"], writes=["oha%d" % (t % 4)])
                    T.op('dve', lambda e: e.tensor_scalar(out=b_, in0=iota, scalar1=e2T[:, t:t + 1], scalar2=None, op0=ALU.is_equal),
                         reads=["e2T"], writes=["ohb%d" % (t % 4)])
                    pk = 6 + (t // 8) % 2
                    T.op('pe', lambda e: e.matmul(ps[pk][:, (t % 8) * 64:(t % 8 + 1) * 64], lhsT=b_, rhs=a_, start=True, stop=True),
                         reads=["oha%d" % (t % 4), "ohb%d" % (t % 4)], writes=["ps%d" % pk])
                    if t % 8 == 7:
                        t0 = t - 7
                        T.op('act', lambda e: e.activation(out=Wsb_w[:, :, t0:t0 + 8].rearrange("p e t -> p t e"),
                                                           in_=ps[pk].rearrange("p (t e) -> p t e", t=8), func=AF.Copy),
                             reads=["ps%d" % pk], writes=["Wsb"])
                def d_load(e1):
                    T.dma('sp', ublk[e1 % 4], B["Ubf"][e1].rearrange("p (k e) -> p k e", k=8), writes=["ublk%d" % (e1 % 4)])
                    T.dma('sp', vblk[e1 % 4], B["Vbf"][e1 * 128:(e1 + 1) * 128, :], writes=["vblk%d" % (e1 % 4)])

                def d_scores(e1):
                    pk = 4 + e1 % 2
                    for dc in range(8):
                        T.op('pe', lambda e: e.matmul(ps[pk][:, 0:TT], lhsT=ublk[e1 % 4][:, dc, :], rhs=xn2Tb[:, dc, :], start=(dc == 0), stop=(dc == 7)),
                             reads=["ublk%d" % (e1 % 4), "xn2Tb"], writes=["ps%d" % pk])

                def d_gate(e1):
                    pk = 4 + e1 % 2
                    ab, gb_ = actb[e1 % 2], gab[e1 % 2]
                    T.op('act', lambda e: e.activation(out=ab, in_=ps[pk][:, 0:TT], func=AF.Gelu), reads=["ps%d" % pk], writes=["actb%d" % (e1 % 2)])
                    T.op('dve', lambda e: e.tensor_tensor(out=gb_, in0=ab, in1=Wsb[:, e1 - hf * 64, :], op=ALU.mult),
                         reads=["actb%d" % (e1 % 2), "Wsb"], writes=["gab%d" % (e1 % 2)])

                def d_acc(e1):
                    gb_, vb = gab[e1 % 2], vblk[e1 % 4]
                    for blk in blks:
                        for half in range(2):
                            T.op('pe', lambda e: e.matmul(ps[blk * 2 + half], lhsT=gb_[:, blk * 128:(blk + 1) * 128], rhs=vb[:, half * 512:(half + 1) * 512],
                                                          start=(e1 == 0), stop=(e1 == 127)),
                                 reads=["gab%d" % (e1 % 2), "vblk%d" % (e1 % 4)], writes=["psy%d" % (blk * 2 + half)])

                e0 = hf * 64
                d_load(e0)
                d_load(e0 + 1)
                d_scores(e0)
                d_gate(e0)
                for i in range(64):
                    e1 = e0 + i
                    if i + 2 < 64:
                        d_load(e1 + 2)
                    if i + 1 < 64:
                        d_scores(e1 + 1)
                    d_acc(e1)
                    if i + 1 < 64:
                        d_gate(e1 + 1)
                T.barrier()
            for blk in blks:
                row0 = ti * TT + blk * 128
                for half in range(2):
                    T.op('dve', lambda e: e.tensor_tensor(out=xres[:, blk, half * 512:(half + 1) * 512], in0=ps[blk * 2 + half],
                                                          in1=xres[:, blk, half * 512:(half + 1) * 512], op=ALU.add),
                         reads=["psy%d" % (blk * 2 + half), "xres%d" % blk], writes=["xres%d" % blk])
                T.op('act', lambda e: e.activation(out=xnb, in_=xres[:, blk, :], func=AF.Square, accum_out=ss), reads=["xres%d" % blk], writes=["xnb", "ssb"])
                T.op('act', lambda e: e.activation(out=rs, in_=ss, func=AF.Sqrt, bias=EPS, scale=1.0 / D_MODEL), reads=["ssb"], writes=["rsb"])
                T.op('dve', lambda e: e.reciprocal(out=rstd, in_=rs), reads=["rsb"], writes=["rstdb"])
                T.op('dve', lambda e: e.scalar_tensor_tensor(out=xnb, in0=xres[:, blk, :], scalar=rstd, in1=gfin, op0=ALU.mult, op1=ALU.mult),
                     reads=["xres%d" % blk, "rstdb"], writes=["xnb"])
                T.dma('sp', B["y"][row0:row0 + 128, :], xnb, reads=["xnb"])
            T.barrier()
```

```python
import os
import numpy as np
from contextlib import ExitStack
import concourse.bass as bass
import concourse.mybir as mybir
from concourse.bass_utils import run_bass_kernel_spmd

F32 = mybir.dt.float32
F32R = mybir.dt.float32r
U32 = mybir.dt.uint32
I32 = mybir.dt.int32
BF16 = mybir.dt.bfloat16
AF = mybir.ActivationFunctionType
ALU = mybir.AluOpType
AX = mybir.AxisListType

D_MODEL = 1024
SEQ = 8192
NCORE = 8
QTOK = 2048
STOK = 256
OWN = QTOK + STOK
D_IN = 4616
OFF_QKV, OFF_Z, OFF_B, OFF_A, OFF_G = 512, 2048, 2560, 2564, 2568
EPS = 1e-6
NEG = -30000.0

STAGE = int(os.environ.get("MK_STAGE", "99"))
NGROUP_DBG = int(os.environ.get("MK_NGROUP", "32"))
SUB = int(os.environ.get("MK_SUB", "99"))
DEBUG_A = int(os.environ.get("MK_DEBUG_A", "0"))
TILES = [int(t) for t in os.environ.get("MK_TILES", "0,1,2,3,4,5,6,7,8").split(",")]


class Trk:
    ROT = int(os.environ.get("MK_ROT", "8000"))
    NDMA = 24

    def __init__(self, nc):
        self.nc = nc
        self.eng = dict(pe=nc.tensor, act=nc.scalar, dve=nc.vector, pool=nc.gpsimd, sp=nc.sync)
        self.sem = {k: nc.alloc_semaphore("prog_%s_0" % k) for k in self.eng}
        self.semgen = {k: 0 for k in self.eng}
        self.cnt = {k: 0 for k in self.eng}
        self.waited = {k: {} for k in self.eng}
        self.lastw = {}
        self.readers = {}
        self.dma_sems = [nc.alloc_semaphore("dmas_%d" % i) for i in range(self.NDMA)]
        self.dma_cnt = [0] * self.NDMA
        self.dma_rr = {'hw': 0, 'sw': 0}
        self.NSW = 8
        self.n_inst = 0

    def _wait(self, e, tok):
        owner, sem, val, semkey = tok
        if owner == e and e == 'pe':
            return
        w = self.waited[e]
        if w.get(semkey, 0) >= val:
            return
        self.eng[e].wait_ge(sem, val)
        w[semkey] = val

    def _deps(self, e, reads, writes):
        for b in reads:
            lw = self.lastw.get(b)
            if lw is not None:
                self._wait(e, lw)
            if b.startswith("ps"):
                for tok in self.readers.get(b, {}).values():
                    if tok[0] != e:
                        self._wait(e, tok)
        for b in writes:
            lw = self.lastw.get(b)
            if lw is not None:
                self._wait(e, lw)
            for tok in self.readers.get(b, {}).values():
                self._wait(e, tok)

    def _record(self, tok, reads, writes):
        for b in reads:
            self.readers.setdefault(b, {})[tok[0]] = tok
        for b in writes:
            self.lastw[b] = tok
            self.readers[b] = {}

    def op(self, e, fn, reads=(), writes=()):
        self._deps(e, reads, writes)
        inst = fn(self.eng[e])
        if self.cnt[e] >= self.ROT:
            self.semgen[e] += 1
            self.sem[e] = self.nc.alloc_semaphore("prog_%s_%d" % (e, self.semgen[e]))
            self.cnt[e] = 0
        self.cnt[e] += 1
        sem = self.sem[e]
        inst.then_inc(sem, 1)
        tok = (e, sem, self.cnt[e], (e, self.semgen[e]))
        self._record(tok, reads, writes)
        self.n_inst += 1
        return inst

    def dma(self, q, out, in_, reads=(), writes=()):
        self._deps(q, reads, writes)
        if q == 'pool':
            k = self.dma_rr['sw']
            self.dma_rr['sw'] = (k + 1) % self.NSW
        else:
            k = self.NSW + self.dma_rr['hw']
            self.dma_rr['hw'] = (self.dma_rr['hw'] + 1) % (self.NDMA - self.NSW)
        sem = self.dma_sems[k]
        if self.dma_cnt[k] > 0:
            self._wait(q, (('dma', k), sem, 16 * self.dma_cnt[k], ('dma', k)))
        inst = self.eng[q].dma_start(out=out, in_=in_)
        self.dma_cnt[k] += 1
        inst.then_inc(sem, 16)
        tok = (('dma', k), sem, 16 * self.dma_cnt[k], ('dma', k))
        self._record(tok, reads, writes)
        self.n_inst += 1
        return inst

    def wait_all_dma(self, e):
        for k in range(self.NDMA):
            if self.dma_cnt[k] > 0:
                self._wait(e, (('dma', k), self.dma_sems[k], 16 * self.dma_cnt[k], ('dma', k)))

    def barrier(self):
        toks = []
        for e2 in self.eng:
            if self.cnt[e2] > 0:
                toks.append((e2, self.sem[e2], self.cnt[e2], (e2, self.semgen[e2])))
        for e in self.eng:
            for tok in toks:
                if tok[0] != e:
                    self._wait(e, tok)
            self.wait_all_dma(e)

    def barrier_consts(self):
        for e in self.eng:
            self.wait_all_dma(e)


def _consts():
    idx = np.arange(128)
    ch = idx // 64
    same = (ch[:, None] == ch[None, :])
    c = {}
    c["ident"] = np.eye(128, dtype=np.float32)
    c["ones"] = np.ones((128, 128), np.float32)
    c["umask"] = (same & (idx[:, None] <= idx[None, :])).astype(np.float32)
    c["bd"] = same.astype(np.float32)
    low_incl = same & (idx[None, :] <= idx[:, None])
    c["lstrict"] = (same & (idx[None, :] < idx[:, None])).astype(np.float32)
    cs = np.zeros((128, 2), np.float32)
    cs[:64, 0] = 1.0
    cs[64:, 1] = 1.0
    c["chunksel"] = cs
    c["iota"] = np.tile(np.arange(128, dtype=np.float32)[None, :], (128, 1))
    return c


def rep128(v):
    v = np.asarray(v, np.float32).reshape(1, -1)
    return np.ascontiguousarray(np.repeat(v, 128, axis=0))


class Prog:
    def __init__(self):
        self.nc = bass.Bass("TRN2", target_bir_lowering=False)
        self.T = Trk(self.nc)
        self.din = {}
        self.dout = {}

    def inp(self, name, shape, dt=F32):
        ap = self.nc.dram_tensor(name, list(shape), dt, kind="ExternalInput").ap()
        self.din[name] = ap
        return ap

    def outp(self, name, shape, dt=F32):
        ap = self.nc.dram_tensor(name, list(shape), dt, kind="ExternalOutput").ap()
        self.dout[name] = ap
        return ap

    def scratch(self, name, shape, dt=F32):
        return self.nc.dram_tensor(name, list(shape), dt).ap()


def sb(nc, es, name, shape, dt=F32):
    h = es.enter_context(nc.sbuf_tensor("s_" + name, list(shape), dt))
    return h.ap()


def build():
    P = Prog()
    nc, T = P.nc, P.T
    xseq = P.inp("xseq", [SEQ, D_MODEL])
    xs = P.inp("xs", [STOK, D_MODEL])
    S0 = P.inp("S0", [4, 4, 128, 128])
    conv0T = P.inp("conv0T", [4, 1536, 3])
    w_in = P.inp("w_in", [D_MODEL, D_IN])
    gmix = P.inp("gmix_bc", [128, D_MODEL])
    wconvT = P.inp("wconvT", [1536, 4])
    alog = P.inp("alog_bc", [128, 4])
    dtb = P.inp("dtb_bc", [128, 4])
    cn = {k: P.inp("c_" + k, list(v.shape)) for k, v in _consts().items()}

    dn_p = P.outp("dn_p", [4, 128, 128])
    dn_s = P.outp("dn_s", [4, 4, 128, 128])
    conv_pT = P.outp("conv_pT", [1536, 3])
    conv_sT = P.outp("conv_sT", [4, 1536, 3])
    o_all = P.outp("o_all", [SEQ + STOK, 512]) if DEBUG_A else P.scratch("o_all", [SEQ + STOK, 512])

    Bd = {}
    for nm, shp in [("xown", [OWN, D_MODEL]), ("xhalo", [16, D_MODEL]), ("pool0", [4, 15, 512]),
                    ("gmix_col", [128, 8]), ("gffn_col", [128, 8]), ("gfin_bc", [128, D_MODEL]), ("gdn_bc", [128, 512]),
                    ("pscale_col", [128, 4]), ("sel", [128, 4]), ("w_grp", [4, 128, 128]), ("keysT", [16, 128, 128]),
                    ("bmain", [128, 4, 128]), ("bprev", [128, 4, 128]), ("bfirst", [128, 4, 128]), ("bsamp", [128, 4, 128]),
                    ("bsprev", [128, 4, 128]), ("w_up_pool", [512, D_MODEL]), ("w_up_dn", [512, D_MODEL]),
                    ("w_out", [D_MODEL, D_MODEL]), ("w_q", [D_MODEL, 2048]), ("UTb", [128, 128, 8, 128]),
                    ("peer_v", [16384, D_MODEL])]:
        Bd[nm] = P.inp(nm, shp)
    Bd["w_in"] = w_in
    Bd["o_all"] = o_all
    Bd["Ubf"] = P.scratch("Ubf", [128, 128, 1024], BF16)
    Bd["Vbf"] = P.scratch("Vbf", [16384, D_MODEL], BF16)
    Bd["y"] = P.outp("y", [OWN, D_MODEL])
    Bd["pool_p"] = P.outp("pool_p", [15, 512])
    Bd["pool_s"] = P.outp("pool_s", [4, 15, 512])

    ps = [nc.alloc_psum_tensor("psb%d" % i, [128, 512], F32).ap() for i in range(8)]

    with ExitStack() as es:
        C = {}
        for k, v in cn.items():
            C[k] = sb(nc, es, "k_" + k, list(v.shape))
            T.dma('sp', C[k], v)
        gmix_sb = sb(nc, es, "gmix", [128, D_MODEL])
        T.dma('sp', gmix_sb, gmix)
        wc_sb = sb(nc, es, "wconv", [128, 12, 4])
        T.dma('sp', wc_sb, wconvT.rearrange("(c p) j -> p c j", p=128))
        alog_sb = sb(nc, es, "alog", [128, 4])
        dtb_sb = sb(nc, es, "dtb", [128, 4])
        T.dma('sp', alog_sb, alog)
        T.dma('sp', dtb_sb, dtb)
        onesR = sb(nc, es, "onesR", [128, 128], F32R)
        T.dma('pool', onesR, cn["ones"])
        T.barrier_consts()
        nexpA = sb(nc, es, "nexpA", [128, 4])
        T.op('act', lambda e: e.activation(out=nexpA, in_=alog_sb, func=AF.Exp), writes=["nexpA"])
        T.op('dve', lambda e: e.tensor_scalar(out=nexpA, in0=nexpA, scalar1=-1.0, scalar2=None, op0=ALU.mult),
             reads=["nexpA"], writes=["nexpA"])

        phase_a(P, es, C, ps, dict(xseq=xseq, xs=xs, S0=S0, conv0T=conv0T, w_in=w_in, gmix=gmix_sb,
                                   wc=wc_sb, dtb=dtb_sb, nexpA=nexpA, onesR=onesR,
                                   dn_p=dn_p, dn_s=dn_s, conv_pT=conv_pT, conv_sT=conv_sT, o_all=o_all,
                                   UTb=Bd["UTb"], peer_v=Bd["peer_v"], Ubf=Bd["Ubf"], Vbf=Bd["Vbf"]))
        T.barrier()
        phase_b(P, C, ps, Bd)
        T.barrier()
    return P


def phase_a(P, es0, C, ps, A):
    nc, T = P.nc, P.T
    ident, onesF = C["ident"], C["ones"]
    with ExitStack() as es:
        w_a = sb(nc, es, "w_a", [128, 8, 1544], F32R)
        for dc in range(8):
            T.dma('pool', w_a[:, dc, 0:1536], A["w_in"][dc * 128:(dc + 1) * 128, OFF_QKV:OFF_Z], writes=["w_a"])
            T.dma('pool', w_a[:, dc, 1536:1544], A["w_in"][dc * 128:(dc + 1) * 128, OFF_B:OFF_G], writes=["w_a"])
        xt = [sb(nc, es, "xt%d" % i, [128, D_MODEL]) for i in range(2)]
        xn = sb(nc, es, "xn", [128, D_MODEL])
        ss = sb(nc, es, "ss", [128, 1])
        rs = sb(nc, es, "rs", [128, 1])
        rstd = sb(nc, es, "rstd", [128, 1])
        xnT = sb(nc, es, "xnT", [128, 8, 256], F32R)
        ext_w = sb(nc, es, "ext", [128, 12, 268], F32R)
        ext = ext_w.bitcast(F32)
        diagW = sb(nc, es, "diagW", [128, 48, 128], F32R)
        for cc in range(12):
            for j in range(4):
                T.op('pool', lambda e: e.tensor_scalar(out=diagW[:, cc * 4 + j, :], in0=ident, scalar1=A["wc"][:, cc, j:j + 1],
                                                       scalar2=None, op0=ALU.mult), writes=["diagW"])
        carry = sb(nc, es, "carry", [128, 12, 3])
        qcs = [sb(nc, es, "qc%d" % i, [128, 12, 256], F32R) for i in range(2)]
        sq = sb(nc, es, "sq", [128, 8, 256], F32R)
        rinvs = [sb(nc, es, "rinv%d" % i, [128, 256]) for i in range(2)]
        bas = [[sb(nc, es, "ba%d_%d" % (p_, i), [128, 8]) for i in range(2)] for p_ in range(2)]
        small = {}
        for nm, w in [("beta", 4), ("negbeta", 4), ("g", 4), ("sp", 4), ("gc", 4), ("gam", 4), ("dlt", 4),
                      ("bgn", 4), ("gsel", 8), ("egl", 8), ("tmp4", 4)]:
            small[nm] = [sb(nc, es, "%s%d" % (nm, i), [128, w]) for i in range(2)]
        NH = 4
        def tiles(nm, n=1, shape=(128, 128)):
            return [[sb(nc, es, "%s_h%d_%d" % (nm, h, i), list(shape)) for i in range(n)] for h in range(NH)]
        Lg_all = sb(nc, es, "Lg_all", [128, 4, 128])
        dec = tiles("dec")
        decT = tiles("decT")
        def tilesR(nm, n=1, shape=(128, 128)):
            return [[sb(nc, es, "%s_h%d_%d" % (nm, h, i), list(shape), F32R) for i in range(n)] for h in range(NH)]
        Mp = tilesR("Mp", 2)
        NR = tilesR("NR", 2, (128, 256))
        Rfin = tilesR("Rfin")
        VK = tilesR("VK", 1, (128, 256))
        Qg = tiles("Qg")
        wkn = tiles("wkn")
        kdec = tiles("kdec", 2)
        uv = tiles("uv", 2)
        PT = tiles("PT", 2)
        qeT = tiles("qeT", 2)
        FT = tiles("FT", 4)
        Sst = [sb(nc, es, "Sst%d" % i, [128, 4, 128]) for i in range(2)]
        ostage = [sb(nc, es, "ost%d" % i, [64, 512]) for i in range(2)]

        T.op('dve', lambda e: e.memset(Sst[0], 0.0), writes=["S0buf"])
        T.op('dve', lambda e: e.memset(carry, 0.0), writes=["carry"])
        s_cur = 0
        chunk_ctr = 0

        groups = [("p", g) for g in range(min(32, NGROUP_DBG))] + [("s", 0)]
        def s1a(kind, g, par):
            nseq, L = (1, 256) if kind == "p" else (4, 64)
            ba = bas[par]
            if kind == "p":
                last = (g == min(32, NGROUP_DBG) - 1)
                for e1 in range(g * 4, 128 if last else g * 4 + 4):
                    T.dma('pool', A["Ubf"][e1], A["UTb"][e1].rearrange("p k e -> p (k e)"))
                    T.dma('pool', A["Vbf"][e1 * 128:(e1 + 1) * 128, :], A["peer_v"][e1 * 128:(e1 + 1) * 128, :])
            extv = ext[:, :, 0:nseq * (L + 3)].rearrange("p c (s l) -> p c s l", s=nseq)
            extvw = ext_w[:, :, 0:nseq * (L + 3)].rearrange("p c (s l) -> p c s l", s=nseq)
            for blk in range(2):
                xb = xt[blk]
                src = A["xseq"][g * 256 + blk * 128: g * 256 + (blk + 1) * 128, :] if kind == "p" \
                    else A["xs"][blk * 128:(blk + 1) * 128, :]
                T.dma('sp', xb, src, writes=["xt%d" % blk])
                T.op('act', lambda e: e.activation(out=xn, in_=xb, func=AF.Square, accum_out=ss),
                     reads=["xt%d" % blk], writes=["xn", "ss"])
                T.op('act', lambda e: e.activation(out=rs, in_=ss, func=AF.Sqrt, bias=EPS, scale=1.0 / D_MODEL),
                     reads=["ss"], writes=["rs"])
                T.op('dve', lambda e: e.reciprocal(out=rstd, in_=rs), reads=["rs"], writes=["rstd"])
                T.op('dve', lambda e: e.scalar_tensor_tensor(out=xn, in0=xb, scalar=rstd, in1=A["gmix"],
                                                             op0=ALU.mult, op1=ALU.mult),
                     reads=["xt%d" % blk, "rstd"], writes=["xn"])
                for half in range(2):
                    for q in range(4):
                        dc = half * 4 + q
                        T.op('pe', lambda e: e.transpose(out=ps[half][:, q * 128:(q + 1) * 128],
                                                         in_=xn[:, dc * 128:(dc + 1) * 128], identity=ident),
                             reads=["xn"], writes=["ps%d" % half])
                    T.op('act', lambda e: e.activation(
                        out=xnT[:, half * 4:(half + 1) * 4, blk * 128:(blk + 1) * 128],
                        in_=ps[half].rearrange("p (q t) -> p q t", q=4), func=AF.Copy),
                        reads=["ps%d" % half], writes=["xnT"])
                for dc in range(8):
                    T.op('pe', lambda e: e.matmul(ps[1][:, 504:512], lhsT=xnT[:, dc, blk * 128:(blk + 1) * 128],
                                                  rhs=w_a[:, dc, 1536:1544], start=(dc == 0), stop=(dc == 7)),
                         reads=["xnT", "w_a"], writes=["ps1"])
                T.op('dve', lambda e: e.tensor_copy(out=ba[blk], in_=ps[1][:, 504:512]),
                     reads=["ps1"], writes=["ba%d_%d" % (par, blk)])
        def s1b(kind, g, par):
            if STAGE < 2:
                return
            nseq, L = (1, 256) if kind == "p" else (4, 64)
            qc_w = qcs[par]
            qc = qc_w.bitcast(F32)
            extv = ext[:, :, 0:nseq * (L + 3)].rearrange("p c (s l) -> p c s l", s=nseq)
            extvw = ext_w[:, :, 0:nseq * (L + 3)].rearrange("p c (s l) -> p c s l", s=nseq)
            if kind == "p":
                T.op('pool', lambda e: e.tensor_copy(out=extvw[:, :, 0, 0:3], in_=carry), reads=["carry"], writes=["ext"])
            else:
                for s in range(4):
                    T.dma('pool', extvw[:, :, s, 0:3], A["conv0T"][s].rearrange("(c p) j -> p c j", p=128), writes=["ext"])
            for cc in range(12):
                hf = cc % 2
                pq = ps[hf][:, 0:256]
                for dc in range(8):
                    T.op('pe', lambda e: e.matmul(pq, lhsT=w_a[:, dc, cc * 128:(cc + 1) * 128],
                                                  rhs=xnT[:, dc, :], start=(dc == 0), stop=(dc == 7)),
                         reads=["xnT", "w_a"], writes=["ps%d" % hf])
                T.op('act', lambda e: e.activation(out=extvw[:, cc, :, 3:3 + L],
                                                   in_=pq.rearrange("p (s l) -> p s l", s=nseq),
                                                   func=AF.Copy),
                     reads=["ps%d" % hf], writes=["ext"])
            if kind == "p":
                T.op('pool', lambda e: e.tensor_copy(out=carry, in_=extv[:, :, 0, L:L + 3]), reads=["ext"], writes=["carry"])
                if g == 31:
                    T.dma('sp', A["conv_pT"].rearrange("(c p) j -> p c j", p=128), carry, reads=["carry"])
            else:
                for s in range(4):
                    T.dma('sp', A["conv_sT"][s].rearrange("(c p) j -> p c j", p=128), extv[:, :, s, L:L + 3], reads=["ext"])
            for cc in range(12):
                hf = cc % 2
                pq = ps[hf][:, 0:256]
                for j in range(4):
                    T.op('pe', lambda e: e.matmul(pq, lhsT=diagW[:, cc * 4 + j, :], rhs=extvw[:, cc, :, j:j + L], start=(j == 0), stop=(j == 3)),
                         reads=["ext"], writes=["ps%d" % hf])
                T.op('act', lambda e: e.activation(out=qc_w[:, cc, :], in_=pq, func=AF.Silu), reads=["ps%d" % hf], writes=["qc%d" % par])
            T.op('act', lambda e: e.activation(out=sq, in_=qc[:, 0:8, :], func=AF.Square), reads=["qc%d" % par], writes=["sq"])
            for cc in range(8):
                hf = cc % 2
                pq = ps[hf][:, 0:256]
                T.op('pe', lambda e: e.matmul(pq, lhsT=A["onesR"], rhs=sq[:, cc, :], start=True, stop=True),
                     reads=["sq"], writes=["ps%d" % hf])
                sc = 128.0 if cc < 4 else 1.0
                rinv, rk_ = rinvs[cc % 2], "rinv%d" % (cc % 2)
                T.op('act', lambda e: e.activation(out=rinv, in_=pq, func=AF.Sqrt, bias=EPS * sc, scale=sc),
                     reads=["ps%d" % hf], writes=[rk_])
                T.op('dve', lambda e: e.reciprocal(out=rinv, in_=rinv), reads=[rk_], writes=[rk_])
                T.op('dve', lambda e: e.tensor_tensor(out=qc_w[:, cc, :], in0=qc[:, cc, :], in1=rinv, op=ALU.mult),
                     reads=["qc%d" % par, rk_], writes=["qc%d" % par])
        def s2(kind, g, par, blk):
            nonlocal s_cur, chunk_ctr
            if STAGE < 3:
                return
            qc_w = qcs[par]
            qc = qc_w.bitcast(F32)
            ba = bas[par]
            if True:
                bp = blk
                cols = slice(blk * 128, (blk + 1) * 128)
                sm = {k: v[bp] for k, v in small.items()}
                bab = ba[blk]
                T.op('act', lambda e: e.activation(out=sm["beta"], in_=bab[:, 0:4], func=AF.Sigmoid),
                     reads=["ba%d_%d" % (par, blk)], writes=["beta%d" % bp])
                T.op('dve', lambda e: e.tensor_tensor(out=sm["sp"], in0=bab[:, 4:8], in1=A["dtb"], op=ALU.add),
                     reads=["ba%d_%d" % (par, blk)], writes=["sp%d" % bp])
                T.op('act', lambda e: e.activation(out=sm["sp"], in_=sm["sp"], func=AF.Exp), reads=["sp%d" % bp], writes=["sp%d" % bp])
                T.op('act', lambda e: e.activation(out=sm["sp"], in_=sm["sp"], func=AF.Ln, bias=1.0),
                     reads=["sp%d" % bp], writes=["sp%d" % bp])
                T.op('dve', lambda e: e.tensor_tensor(out=sm["g"], in0=sm["sp"], in1=A["nexpA"], op=ALU.mult),
                     reads=["sp%d" % bp, "nexpA"], writes=["g%d" % bp])
                T.op('dve', lambda e: e.tensor_scalar(out=sm["negbeta"], in0=sm["beta"], scalar1=-1.0, scalar2=None, op0=ALU.mult),
                     reads=["beta%d" % bp], writes=["negbeta%d" % bp])
                T.op('pe', lambda e: e.matmul(ps[7][:, 0:4], lhsT=C["umask"], rhs=sm["g"], start=True, stop=True),
                     reads=["g%d" % bp], writes=["ps7"])
                T.op('pe', lambda e: e.matmul(ps[7][:, 8:12], lhsT=C["bd"], rhs=sm["g"], start=True, stop=True),
                     reads=["g%d" % bp], writes=["ps7"])
                T.op('act', lambda e: e.activation(out=sm["gc"], in_=ps[7][:, 0:4], func=AF.Copy), reads=["ps7"], writes=["gc%d" % bp])
                T.op('act', lambda e: e.activation(out=sm["gam"], in_=ps[7][:, 0:4], func=AF.Exp), reads=["ps7"], writes=["gam%d" % bp])
                T.op('dve', lambda e: e.tensor_tensor(out=sm["tmp4"], in0=ps[7][:, 8:12], in1=sm["gc"], op=ALU.subtract),
                     reads=["ps7", "gc%d" % bp], writes=["tmp4%d" % bp])
                T.op('act', lambda e: e.activation(out=sm["dlt"], in_=sm["tmp4"], func=AF.Exp), reads=["tmp4%d" % bp], writes=["dlt%d" % bp])
                T.op('dve', lambda e: e.tensor_tensor(out=sm["bgn"], in0=sm["negbeta"], in1=sm["gam"], op=ALU.mult),
                     reads=["negbeta%d" % bp, "gam%d" % bp], writes=["bgn%d" % bp])
                T.op('dve', lambda e: e.tensor_tensor(
                    out=sm["gsel"].rearrange("p (h c) -> p h c", c=2),
                    in0=sm["g"].unsqueeze(2).to_broadcast([128, 4, 2]),
                    in1=C["chunksel"].unsqueeze(1).to_broadcast([128, 4, 2]), op=ALU.mult),
                    reads=["g%d" % bp], writes=["gsel%d" % bp])
                T.op('pe', lambda e: e.matmul(ps[7][:, 16:24], lhsT=onesF, rhs=sm["gsel"], start=True, stop=True),
                     reads=["gsel%d" % bp], writes=["ps7"])
                T.op('act', lambda e: e.activation(out=sm["egl"], in_=ps[7][:, 16:24], func=AF.Exp), reads=["ps7"], writes=["egl%d" % bp])

                if SUB < 1:
                    return

                def hk(nm, h, i=0):
                    return "%s_%d_%d" % (nm, h, i)

                def pslot(h, i):
                    if i == 0:
                        return ps[5][:, h * 128:(h + 1) * 128], "ps5_%d" % h
                    return ps[4][:, 128 + 0:128 + 0], None

                def slotA(h):
                    return ps[2 + h][:, 0:128], "ps%d" % (2 + h)

                def slotB(h):
                    return ps[2 + h][:, 128:256], "ps%d" % (2 + h)

                def slotC(h):
                    return ps[2 + h][:, 256:384], "ps%d" % (2 + h)

                T.op('dve', lambda e: e.tensor_tensor(out=Lg_all, in0=C["umask"].unsqueeze(1).to_broadcast([128, 4, 128]),
                                                      in1=sm["g"].unsqueeze(2).to_broadcast([128, 4, 128]), op=ALU.mult),
                     reads=["g%d" % bp], writes=["Lg_all"])
                T.op('pe', lambda e: e.matmul(ps[7], lhsT=onesF, rhs=Lg_all.rearrange("p h t -> p (h t)"), start=True, stop=True),
                     reads=["Lg_all"], writes=["ps7"])
                for h in range(NH):
                    gcb = ps[7][:, h * 128:(h + 1) * 128]
                    T.op('dve', lambda e: e.tensor_scalar(out=dec[h][0], in0=gcb, scalar1=sm["gc"][:, h:h + 1], scalar2=0.0,
                                                          op0=ALU.subtract, op1=ALU.max), reads=["ps7", "gc%d" % bp], writes=[hk("dec", h)])
                    T.op('dve', lambda e: e.tensor_scalar(out=decT[h][0], in0=gcb, scalar1=sm["gc"][:, h:h + 1], scalar2=0.0,
                                                          op0=ALU.subtract, op1=ALU.min), reads=["ps7", "gc%d" % bp], writes=[hk("decT", h)])
                    T.op('act', lambda e: e.activation(out=dec[h][0], in_=dec[h][0], func=AF.Exp, scale=-1.0), reads=[hk("dec", h)], writes=[hk("dec", h)])
                    T.op('act', lambda e: e.activation(out=decT[h][0], in_=decT[h][0], func=AF.Exp), reads=[hk("decT", h)], writes=[hk("decT", h)])
                    T.op('pool', lambda e: e.tensor_tensor(out=dec[h][0], in0=dec[h][0], in1=C["lstrict"], op=ALU.mult),
                         reads=[hk("dec", h)], writes=[hk("dec", h)])
                    T.op('pool', lambda e: e.tensor_tensor(out=decT[h][0], in0=decT[h][0], in1=C["umask"], op=ALU.mult),
                         reads=[hk("decT", h)], writes=[hk("decT", h)])
                for h in range(NH):
                    pc, kc = ps[2 + h][:, 0:256], "ps%d" % (2 + h)
                    T.op('pe', lambda e: e.matmul(pc, lhsT=qc_w[:, 4 + h, cols], rhs=qc_w[:, h:h + 5:4, cols], start=True, stop=True),
                         reads=["qc%d" % par], writes=[kc])
                    T.op('dve', lambda e: e.scalar_tensor_tensor(out=Mp[h][0], in0=pc[:, 128:256], scalar=sm["negbeta"][:, h:h + 1],
                                                                 in1=dec[h][0], op0=ALU.mult, op1=ALU.mult),
                         reads=[kc, hk("dec", h), "negbeta%d" % bp], writes=[hk("Mp", h, 0)])
                    T.op('dve', lambda e: e.tensor_tensor(out=PT[h][bp], in0=pc[:, 0:128], in1=decT[h][0], op=ALU.mult),
                         reads=[kc, hk("decT", h)], writes=[hk("PT", h, bp)])
                for h in range(NH):
                    pa, ka = ps[2 + h][:, 256:384], "ps%d" % (2 + h)
                    T.op('pe', lambda e: e.transpose(out=pa, in_=Mp[h][0].bitcast(F32), identity=ident), reads=[hk("Mp", h, 0)], writes=[ka])
                    T.op('act', lambda e: e.activation(out=NR[h][0][:, 0:128], in_=pa, func=AF.Copy), reads=[ka], writes=[hk("NR", h, 0)])
                    T.op('pool', lambda e: e.tensor_tensor(out=NR[h][1][:, 128:256], in0=NR[h][0][:, 0:128].bitcast(F32), in1=ident, op=ALU.add),
                         reads=[hk("NR", h, 0)], writes=[hk("NR", h, 1)])
                for h in range(NH):
                    pb, kb = ps[2 + h], "ps%d" % (2 + h)
                    T.op('pe', lambda e: e.matmul(pb[:, 0:128], lhsT=NR[h][0][:, 0:128], rhs=Mp[h][0], start=True, stop=True),
                         reads=[hk("NR", h, 0), hk("Mp", h, 0)], writes=[kb])
                    T.op('pe', lambda e: e.matmul(pb[:, 128:256], lhsT=Mp[h][0], rhs=NR[h][0][:, 0:128], start=True, stop=True),
                         reads=[hk("NR", h, 0), hk("Mp", h, 0)], writes=[kb])
                for h in range(NH):
                    pb, kb = ps[2 + h], "ps%d" % (2 + h)
                    T.op('dve', lambda e: e.tensor_copy(out=Mp[h][1], in_=pb[:, 0:128]), reads=[kb], writes=[hk("Mp", h, 1)])
                    T.op('act', lambda e: e.activation(out=NR[h][1][:, 0:128], in_=pb[:, 128:256], func=AF.Copy), reads=[kb], writes=[hk("NR", h, 1)])
                for lev in range(1, 5):
                    a, b2 = lev % 2, (lev + 1) % 2
                    for h in range(NH):
                        pb, kb = ps[2 + h], "ps%d" % (2 + h)
                        T.op('pe', lambda e: e.matmul(pb[:, 0:256], lhsT=Mp[h][a], rhs=NR[h][a], start=True, stop=True),
                             reads=[hk("NR", h, a), hk("Mp", h, a)], writes=[kb])
                        T.op('pe', lambda e: e.matmul(pb[:, 256:384], lhsT=NR[h][a][:, 0:128], rhs=Mp[h][a], start=True, stop=True),
                             reads=[hk("NR", h, a), hk("Mp", h, a)], writes=[kb])
                    for h in range(NH):
                        pb, kb = ps[2 + h], "ps%d" % (2 + h)
                        T.op('act', lambda e: e.activation(out=NR[h][b2][:, 0:128], in_=pb[:, 0:128], func=AF.Copy), reads=[kb], writes=[hk("NR", h, b2)])
                        T.op('dve', lambda e: e.tensor_tensor(out=NR[h][b2][:, 128:256], in0=pb[:, 128:256], in1=NR[h][a][:, 128:256].bitcast(F32), op=ALU.add),
                             reads=[kb, hk("NR", h, a)], writes=[hk("NR", h, b2)])
                        T.op('dve', lambda e: e.tensor_copy(out=Mp[h][b2], in_=pb[:, 256:384]), reads=[kb], writes=[hk("Mp", h, b2)])
                for h in range(NH):
                    pb, kb = ps[2 + h], "ps%d" % (2 + h)
                    T.op('pe', lambda e: e.matmul(pb[:, 0:128], lhsT=Mp[h][1], rhs=NR[h][1][:, 128:256], start=True, stop=True),
                         reads=[hk("NR", h, 1), hk("Mp", h, 1)], writes=[kb])
                    T.op('dve', lambda e: e.tensor_tensor(out=Rfin[h][0], in0=pb[:, 0:128], in1=NR[h][1][:, 128:256].bitcast(F32), op=ALU.add),
                         reads=[kb, hk("NR", h, 1)], writes=[hk("Rfin", h)])
                for h in range(NH):
                    QT, KT, VT = qc[:, h, cols], qc[:, 4 + h, cols], qc[:, 8 + h, cols]
                    pa, ka = slotA(h)
                    T.op('pe', lambda e: e.transpose(out=pa, in_=KT, identity=ident), reads=["qc%d" % par], writes=[ka])
                    T.op('act', lambda e: e.activation(out=VK[h][0][:, 128:256], in_=pa, func=AF.Copy, scale=sm["bgn"][:, h:h + 1]),
                         reads=[ka, "bgn%d" % bp], writes=[hk("VK", h)])
                    T.op('dve', lambda e: e.tensor_scalar(out=kdec[h][bp], in0=pa, scalar1=sm["dlt"][:, h:h + 1], scalar2=None, op0=ALU.mult),
                         reads=[ka, "dlt%d" % bp], writes=[hk("kdec", h, bp)])
                    pb, kb = slotB(h)
                    T.op('pe', lambda e: e.transpose(out=pb, in_=VT, identity=ident), reads=["qc%d" % par], writes=[kb])
                    T.op('act', lambda e: e.activation(out=VK[h][0][:, 0:128], in_=pb, func=AF.Copy, scale=sm["beta"][:, h:h + 1]),
                         reads=[kb, "beta%d" % bp], writes=[hk("VK", h)])
                    pc, kc = slotC(h)
                    T.op('pe', lambda e: e.transpose(out=pc, in_=QT, identity=ident), reads=["qc%d" % par], writes=[kc])
                    T.op('dve', lambda e: e.tensor_scalar(out=Qg[h][0], in0=pc, scalar1=sm["gam"][:, h:h + 1], scalar2=None, op0=ALU.mult),
                         reads=[kc, "gam%d" % bp], writes=[hk("Qg", h)])
                for h in range(NH):
                    pb, kb = ps[2 + h], "ps%d" % (2 + h)
                    T.op('pe', lambda e: e.matmul(pb[:, 0:256], lhsT=Rfin[h][0], rhs=VK[h][0], start=True, stop=True),
                         reads=[hk("Rfin", h), hk("VK", h)], writes=[kb])
                    T.op('act', lambda e: e.activation(out=uv[h][bp], in_=pb[:, 0:128], func=AF.Copy), reads=[kb], writes=[hk("uv", h, bp)])
                    T.op('dve', lambda e: e.tensor_copy(out=wkn[h][0], in_=pb[:, 128:256]), reads=[kb], writes=[hk("wkn", h)])
                if SUB < 6:
                    return
                for h in range(NH):
                    pa, ka = slotA(h)
                    T.op('pe', lambda e: e.matmul(pa, lhsT=Qg[h][0], rhs=ident, start=True, stop=False), reads=[hk("Qg", h)], writes=[ka])
                    T.op('pe', lambda e: e.matmul(pa, lhsT=wkn[h][0], rhs=PT[h][bp], start=False, stop=True),
                         reads=[hk("wkn", h), hk("PT", h, bp)], writes=[ka])
                    T.op('act', lambda e: e.activation(out=qeT[h][bp], in_=pa, func=AF.Copy), reads=[ka], writes=[hk("qeT", h, bp)])
                    for c in range(2):
                        pp, kp = (slotB(h) if c == 0 else slotC(h))
                        rows = slice(c * 64, (c + 1) * 64)
                        T.op('pe', lambda e: e.matmul(pp, lhsT=wkn[h][0][rows, :], rhs=kdec[h][bp][rows, :], start=True, stop=True),
                             reads=[hk("wkn", h), hk("kdec", h, bp)], writes=[kp])
                        T.op('dve', lambda e: e.scalar_tensor_tensor(out=FT[h][bp * 2 + c], in0=ident,
                                                                     scalar=sm["egl"][:, h * 2 + c:h * 2 + c + 1], in1=pp,
                                                                     op0=ALU.mult, op1=ALU.add),
                             reads=[kp, "egl%d" % bp], writes=[hk("FT", h, bp * 2 + c)])
                for c in range(2 if STAGE >= 4 else 0):
                    rows = slice(c * 64, (c + 1) * 64)
                    if kind == "s":
                        seq = blk * 2 + c
                        T.dma('sp', Sst[s_cur], A["S0"][seq].rearrange("h k v -> k h v"), writes=["S%dbuf" % s_cur])
                    Sc = Sst[s_cur]
                    skey = "S%dbuf" % s_cur
                    op_ = ostage[chunk_ctr % 2]
                    okey = "ost%d" % (chunk_ctr % 2)
                    for h in range(NH):
                        oo = ps[7][0:64, h * 128:(h + 1) * 128]
                        T.op('pe', lambda e: e.matmul(oo, lhsT=qeT[h][bp][:, rows], rhs=Sc[:, h, :], start=True, stop=False),
                             reads=[hk("qeT", h, bp), skey], writes=["ps7"])
                        T.op('pe', lambda e: e.matmul(oo, lhsT=PT[h][bp][rows, rows], rhs=uv[h][bp][rows, :], start=False, stop=True),
                             reads=[hk("PT", h, bp), hk("uv", h, bp)], writes=["ps7"])
                    T.op('act', lambda e: e.activation(out=op_, in_=ps[7][0:64, :], func=AF.Copy), reads=["ps7"], writes=[okey])
                    if kind == "p":
                        row0 = g * 256 + blk * 128 + c * 64
                    else:
                        row0 = SEQ + blk * 128 + c * 64
                    T.dma('sp', A["o_all"][row0:row0 + 64, :], op_, reads=[okey])
                    s_nxt = 1 - s_cur
                    for h in range(NH):
                        so = ps[6][:, h * 128:(h + 1) * 128]
                        T.op('pe', lambda e: e.matmul(so, lhsT=FT[h][bp * 2 + c], rhs=Sc[:, h, :], start=True, stop=False),
                             reads=[hk("FT", h, bp * 2 + c), skey], writes=["ps6"])
                        T.op('pe', lambda e: e.matmul(so, lhsT=kdec[h][bp][rows, :], rhs=uv[h][bp][rows, :], start=False, stop=True),
                             reads=[hk("kdec", h, bp), hk("uv", h, bp)], writes=["ps6"])
                    T.op('dve', lambda e: e.tensor_copy(out=Sst[s_nxt].rearrange("p h v -> p (h v)"), in_=ps[6]),
                         reads=["ps6"], writes=["S%dbuf" % s_nxt])
                    if kind == "s":
                        seq = blk * 2 + c
                        T.dma('sp', A["dn_s"][seq].rearrange("h k v -> k h v"), Sst[s_nxt], reads=["S%dbuf" % s_nxt])
                    s_cur = s_nxt
                    chunk_ctr += 1
            if kind == "p" and g == min(32, NGROUP_DBG) - 1 and blk == 1:
                T.dma('sp', A["dn_p"].rearrange("h k v -> k h v"), Sst[s_cur], reads=["S%dbuf" % s_cur])

        ng = len(groups)
        s1a(groups[0][0], groups[0][1], 0)
        s1b(groups[0][0], groups[0][1], 0)
        for i in range(ng):
            kind, g = groups[i]
            if i + 1 < ng:
                s1a(groups[i + 1][0], groups[i + 1][1], (i + 1) % 2)
            s2(kind, g, i % 2, 0)
            if i + 1 < ng:
                s1b(groups[i + 1][0], groups[i + 1][1], (i + 1) % 2)
            s2(kind, g, i % 2, 1)


_CACHE = {}


def kernel(**inp):
    f = lambda a: np.ascontiguousarray(np.asarray(a, dtype=np.float32))
    x_prompt, x_sample = f(inp["x_prompt"]), f(inp["x_sample"])
    if "prog" not in _CACHE:
        _CACHE["prog"] = build()
    P = _CACHE["prog"]
    consts = _consts()
    pu = f(inp["peer_u"])[0]
    shared = {
        "gmix_col": np.ascontiguousarray(f(inp["g_mix"])[0].reshape(8, 128).T),
        "gffn_col": np.ascontiguousarray(f(inp["g_ffn"])[0].reshape(8, 128).T),
        "gfin_bc": rep128(f(inp["g_final"])),
        "gdn_bc": rep128(np.tile(f(inp["g_dn_out"])[0], 4)),
        "pscale_col": np.ascontiguousarray(f(inp["pool_scale"])[0].reshape(4, 128).T),
        "w_grp": f(inp["w_pool_grp"])[0],
        "keysT": np.ascontiguousarray(f(inp["peer_sub_keys"])[0].transpose(0, 1, 3, 2).reshape(16, 128, 128)),
        "w_up_pool": f(inp["w_up_pool"])[0],
        "w_up_dn": f(inp["w_up_dn"])[0],
        "w_out": f(inp["w_out"])[0],
        "w_q": np.ascontiguousarray(f(inp["w_peer_q"])[0].reshape(D_MODEL, 2048)),
        "UTb": np.ascontiguousarray(pu.reshape(128, 128, 8, 128).transpose(0, 3, 2, 1)),
        "peer_v": f(inp["peer_v"])[0],
    }
    in_maps = []
    for c in range(NCORE):
        b, j = c // 4, c % 4
        m = {
            "xseq": x_prompt[b],
            "xs": x_sample[4 * c:4 * c + 4].reshape(STOK, D_MODEL),
            "S0": f(inp["state_dn"])[0, 4 * c:4 * c + 4],
            "conv0T": np.ascontiguousarray(f(inp["state_dn_conv"])[0, 4 * c:4 * c + 4].transpose(0, 2, 1)),
            "w_in": f(inp["w_in"])[0],
            "gmix_bc": rep128(f(inp["g_mix"])[0]),
            "wconvT": np.ascontiguousarray(f(inp["w_conv"])[0].T),
            "alog_bc": rep128(f(inp["a_log"])[0]),
            "dtb_bc": rep128(f(inp["dt_bias"])[0]),
        }
        for k, v in consts.items():
            m["c_" + k] = v
        m.update(shared)
        m["xown"] = np.ascontiguousarray(np.concatenate([x_prompt[b, j * QTOK:(j + 1) * QTOK], m["xs"]], axis=0))
        m["xhalo"] = np.ascontiguousarray(x_prompt[b, j * QTOK - 16:j * QTOK]) if j > 0 else np.zeros((16, D_MODEL), np.float32)
        m["pool0"] = f(inp["cache_pool"])[0, 4 * c:4 * c + 4]
        selv = np.zeros((128, 4), np.float32)
        selv[:, j] = 1.0
        m["sel"] = selv
        m.update(_bands(j))
        in_maps.append(m)
    res = run_bass_kernel_spmd(P.nc, in_maps, core_ids=list(range(NCORE)))
    R = res.results
    _CACHE["last"] = R
    new_dn_p = np.stack([R[0]["dn_p"], R[4]["dn_p"]])[None]
    new_dn_s = np.concatenate([R[c]["dn_s"] for c in range(NCORE)])[None]
    new_conv_p = np.stack([R[0]["conv_pT"].T, R[4]["conv_pT"].T])[None]
    new_conv_s = np.concatenate([R[c]["conv_sT"].transpose(0, 2, 1) for c in range(NCORE)])[None]
    y_prompt = np.stack([np.concatenate([R[4 * b + j]["y"][:QTOK] for j in range(4)]) for b in range(2)])
    y_sample = np.concatenate([R[c]["y"][QTOK:].reshape(4, 64, D_MODEL) for c in range(NCORE)])
    new_pool_p = np.stack([R[3]["pool_p"], R[7]["pool_p"]])[None]
    new_pool_s = np.concatenate([R[c]["pool_s"] for c in range(NCORE)])[None]
    f32 = lambda a: np.ascontiguousarray(a, dtype=np.float32)
    return (f32(y_prompt), f32(y_sample), f32(new_pool_p), f32(new_conv_p), f32(new_dn_p),
            f32(new_pool_s), f32(new_conv_s), f32(new_dn_s))


TT = 256
NTILE = OWN // TT
POOL_W = (2, 4, 8, 16)


def _bands(j):
    t = np.arange(128)
    out = {}
    main, prev, first, samp, sprev = [], [], [], [], []
    for w in POOL_W:
        src, dst = t[:, None], t[None, :]
        inwin = (src <= dst) & (src > dst - w)
        m = inwin / float(w) - np.eye(128)
        main.append(m)
        pw = ((src - 128) > dst - w) & (src >= 112)
        prev.append(pw / float(w))
        if j == 0:
            cnt = np.minimum(dst + 1, w).astype(np.float64)
            first.append(inwin / cnt - np.eye(128))
        else:
            first.append(m)
        same = (src // 64) == (dst // 64)
        samp.append((inwin & same) / float(w) - np.eye(128))
        sp = np.zeros((128, 128))
        for (r0, c0) in ((97, 0), (113, 64)):
            for i in range(15):
                for d in range(64):
                    if (-15 + i) > d - w:
                        sp[r0 + i, c0 + d] = 1.0 / w
        sprev.append(sp)
    f = lambda l: np.ascontiguousarray(np.stack(l).transpose(1, 0, 2).astype(np.float32))
    return dict(bmain=f(main), bprev=f(prev), bfirst=f(first), bsamp=f(samp), bsprev=f(sprev))


def phase_b(P, C, ps, B):
    nc, T = P.nc, P.T
    ident, iota = C["ident"], C["iota"]
    with ExitStack() as es:
        big = sb(nc, es, "big", [128, 16384])
        def reg(off, shape, dt=F32):
            n = int(np.prod(shape[1:]))
            v = big[:, off:off + n]
            if dt != F32:
                v = v.bitcast(dt)
            if len(shape) == 3:
                v = v.rearrange("p (a b) -> p a b", a=shape[1])
            return v
        xnT = reg(0, [128, 8, TT], F32R)
        wb = [reg(2048 + 2048 * i, [128, 8, 256], F32R) for i in range(3)]
        yaT = reg(8192, [128, 4, TT], F32R)
        ztok, ztok_w = reg(9216, [128, 2, 512]), reg(9216, [128, 2, 512], F32R)
        ybT = reg(10240, [128, 4, TT], F32R)
        mtok, mtok_w = reg(11264, [128, 2, 1024]), reg(11264, [128, 2, 1024], F32R)
        mT = reg(13312, [128, 8, TT], F32R)
        xnTh = reg(15360, [128, 8, 128], F32R)
        utok, utok_w = reg(11264, [128, 512]), reg(11264, [128, 512], F32R)
        otok, otok_w = reg(11776, [128, 512]), reg(11776, [128, 512], F32R)
        ocand, ocand_w = reg(12288, [128, 4, 512]), reg(12288, [128, 4, 512], F32R)
        cand, cand_w = reg(11264, [128, 2048]), reg(11264, [128, 2048], F32R)
        eqb, eqb_w = reg(13312, [128, 2048]), reg(13312, [128, 2048], F32R)
        Wsb = big.rearrange("p (e t) -> p e t", e=64)
        Wsb_w = big.bitcast(F32R).rearrange("p (e t) -> p e t", e=64)
        xres = sb(nc, es, "xres", [128, 2, 1024])
        xn2T = sb(nc, es, "xn2T", [128, 8, TT], F32R)
        ublk = [sb(nc, es, "ublk%d" % i, [128, 8, 128], BF16) for i in range(4)]
        vblk = [sb(nc, es, "vblk%d" % i, [128, 1024], BF16) for i in range(4)]
        xn2Tb = sb(nc, es, "xn2Tb", [128, 8, TT], BF16)
        actb = [sb(nc, es, "actb%d" % i, [128, TT]) for i in range(2)]
        gab = [sb(nc, es, "gab%d" % i, [128, TT], BF16) for i in range(2)]
        e1T = sb(nc, es, "e1T", [128, TT])
        e2T = sb(nc, es, "e2T", [128, TT])
        gT = sb(nc, es, "gT", [128, TT])
        uprev = sb(nc, es, "uprev", [128, 512])
        xh = sb(nc, es, "xh", [128, 1024])
        xnb = sb(nc, es, "xnb", [128, 1024])
        ss = sb(nc, es, "ssb", [128, 1])
        rs = sb(nc, es, "rsb", [128, 1])
        rstd = sb(nc, es, "rstdb", [128, 1])
        pooledT = sb(nc, es, "pooledT", [128, 128])
        qTsb = sb(nc, es, "qTsb", [128, 2, TT])
        sga = sb(nc, es, "sga", [128, 256])
        sgb = sb(nc, es, "sgb", [128, 256])
        sq4 = sb(nc, es, "sq4", [128, 4])
        B_ = {"v1": [sb(nc, es, "v1_%d" % i, [128, 8, 16]) for i in range(2)],
              "v2": [sb(nc, es, "v2_%d" % i, [128, 8, 16]) for i in range(2)],
              "i1u": [sb(nc, es, "i1u_%d" % i, [128, 8, 16], U32) for i in range(2)],
              "i2u": [sb(nc, es, "i2u_%d" % i, [128, 8, 16], U32) for i in range(2)]}
        i1f = sb(nc, es, "i1f", [128, 8, 16])
        i2f = sb(nc, es, "i2f", [128, 8, 16])
        tmpk = sb(nc, es, "tmpk", [128, 128])
        tmp2 = sb(nc, es, "tmp2", [128, 256])
        cv = sb(nc, es, "cv", [128, 8, 16])
        ciu = sb(nc, es, "ciu", [128, 8, 16], U32)
        abu = sb(nc, es, "abu", [128, 8, 16], U32)
        af = sb(nc, es, "af", [128, 8, 16])
        bf = sb(nc, es, "bf", [128, 8, 16])
        ex = sb(nc, es, "ex", [128, 8, 16])
        s8 = sb(nc, es, "s8", [128, 8])
        e1f = sb(nc, es, "e1f", [128, 128])
        e2f = sb(nc, es, "e2f", [128, 128])
        gsl = sb(nc, es, "gsl", [128, 128])
        oha = [sb(nc, es, "oha%d" % i, [128, 64], F32R) for i in range(4)]
        ohb = [sb(nc, es, "ohb%d" % i, [128, 128], F32R) for i in range(4)]
        gmixc = sb(nc, es, "gmixc", [128, 8]); T.dma('sp', gmixc, B["gmix_col"])
        gffnc = sb(nc, es, "gffnc", [128, 8]); T.dma('sp', gffnc, B["gffn_col"])
        gfin = sb(nc, es, "gfin", [128, 1024]); T.dma('sp', gfin, B["gfin_bc"])
        gdn = sb(nc, es, "gdn", [128, 512]); T.dma('sp', gdn, B["gdn_bc"])
        pscale = sb(nc, es, "pscale", [128, 4]); T.dma('sp', pscale, B["pscale_col"])
        selb = sb(nc, es, "selb", [128, 4]); T.dma('sp', selb, B["sel"])
        wgrp = sb(nc, es, "wgrp", [128, 4, 128]); T.dma('sp', wgrp, B["w_grp"].rearrange("g c d -> c g d"))
        keysT = sb(nc, es, "keysT", [128, 16, 128]); T.dma('sp', keysT, B["keysT"].rearrange("q d k -> d q k"))
        bands = {}
        for k in ("bmain", "bprev", "bfirst", "bsamp", "bsprev"):
            bands[k] = sb(nc, es, k, [128, 4, 128]); T.dma('sp', bands[k], B[k])
        T.barrier()

        w_in, w_upp, w_upd, w_out, w_q = B["w_in"], B["w_up_pool"], B["w_up_dn"], B["w_out"], B["w_q"]
        wrr = [0]

        def load_w(src2d, c0, ncol, kdim):
            i = wrr[0] % 3
            wrr[0] += 1
            dst = wb[i][:, 0:kdim, 0:ncol]
            T.dma('pool', dst, src2d[:, c0:c0 + ncol].rearrange("(k p) c -> p k c", p=128), writes=["wb%d" % i])
            return wb[i], "wb%d" % i

        def norm_block(xsrc, xkey, gcol, dstT, dkey, tcols):
            T.op('act', lambda e: e.activation(out=xnb, in_=xsrc, func=AF.Square, accum_out=ss), reads=[xkey], writes=["xnb", "ssb"])
            T.op('act', lambda e: e.activation(out=rs, in_=ss, func=AF.Sqrt, bias=EPS, scale=1.0 / D_MODEL), reads=["ssb"], writes=["rsb"])
            T.op('dve', lambda e: e.reciprocal(out=rstd, in_=rs), reads=["rsb"], writes=["rstdb"])
            T.op('dve', lambda e: e.tensor_scalar(out=xnb, in0=xsrc, scalar1=rstd, scalar2=None, op0=ALU.mult),
                 reads=[xkey, "rstdb"], writes=["xnb"])
            for half in range(2):
                for q in range(4):
                    dc = half * 4 + q
                    T.op('pe', lambda e: e.transpose(out=ps[half][:, q * 128:(q + 1) * 128], in_=xnb[:, dc * 128:(dc + 1) * 128], identity=ident),
                         reads=["xnb"], writes=["ps%d" % half])
                for q in range(4):
                    dc = half * 4 + q
                    T.op('act' if q % 2 else 'dve',
                         (lambda e: e.activation(out=dstT[:, dc, tcols], in_=ps[half][:, q * 128:(q + 1) * 128], func=AF.Copy, scale=gcol[:, dc:dc + 1]))
                         if q % 2 else
                         (lambda e: e.tensor_scalar(out=dstT[:, dc, tcols], in0=ps[half][:, q * 128:(q + 1) * 128], scalar1=gcol[:, dc:dc + 1], scalar2=None, op0=ALU.mult)),
                         reads=["ps%d" % half], writes=[dkey])

        for ti in (TILES if STAGE >= 5 else []):
            samp = (ti == NTILE - 1)
            blks = [0, 1]
            if ti == 0:
                T.op('pool', lambda e: e.memset(xh, 0.0), writes=["xh"])
                T.dma('sp', xh[112:128, :], B["xhalo"], writes=["xh"])
                norm_block(xh, "xh", gmixc, xnTh, "xnTh", slice(0, 128))
            for blk in blks:
                row0 = ti * TT + blk * 128
                T.dma('sp', xres[:, blk, :], B["xown"][row0:row0 + 128, :], writes=["xres%d" % blk])
                norm_block(xres[:, blk, :], "xres%d" % blk, gmixc, xnT, "xnT", slice(blk * 128, (blk + 1) * 128))
            wu = [load_w(w_in, c * 256, 256, 8) for c in range(2)]
            if ti == 0:
                for c in range(2):
                    for dc in range(8):
                        T.op('pe', lambda e: e.matmul(ps[4][:, c * 256:(c + 1) * 256], lhsT=xnTh[:, dc, :], rhs=wu[c][0][:, dc, :],
                                                      start=(dc == 0), stop=(dc == 7)), reads=["xnTh", wu[c][1]], writes=["ps4"])
                T.op('act', lambda e: e.activation(out=uprev, in_=ps[4], func=AF.Copy), reads=["ps4"], writes=["uprev"])
            if samp:
                T.op('pool', lambda e: e.memset(uprev, 0.0), writes=["uprev"])
            for c in range(2):
                for blk in blks:
                    for dc in range(8):
                        T.op('pe', lambda e: e.matmul(ps[2 + blk][:, c * 256:(c + 1) * 256], lhsT=xnT[:, dc, blk * 128:(blk + 1) * 128],
                                                      rhs=wu[c][0][:, dc, :], start=(dc == 0), stop=(dc == 7)),
                             reads=["xnT", wu[c][1]], writes=["ps%d" % (2 + blk)])
            for blk in blks:
                bi = ti * 2 + blk
                T.op('act', lambda e: e.activation(out=utok_w, in_=ps[2 + blk], func=AF.Copy), reads=["ps%d" % (2 + blk)], writes=["utok"])
                if samp:
                    for sq_ in range(2):
                        seq = blk * 2 + sq_
                        r0 = 97 if sq_ == 0 else 113
                        T.dma('sp', uprev[r0:r0 + 15, :], B["pool0"][seq], writes=["uprev"])
                        T.dma('sp', B["pool_s"][seq], utok[sq_ * 64 + 49: sq_ * 64 + 64, :], reads=["utok"])
                    bm, bp = bands["bsamp"], bands["bsprev"]
                else:
                    bm, bp = (bands["bfirst"] if bi == 0 else bands["bmain"]), bands["bprev"]
                    if bi == 15:
                        T.dma('sp', B["pool_p"], utok[113:128, :], reads=["utok"])
                for gi in range(4):
                    gc_ = slice(gi * 128, (gi + 1) * 128)
                    T.op('pe', lambda e: e.matmul(ps[4][:, 0:128], lhsT=uprev[:, gc_], rhs=bp[:, gi, :], start=True, stop=False),
                         reads=["uprev"], writes=["ps4"])
                    T.op('pe', lambda e: e.matmul(ps[4][:, 0:128], lhsT=utok[:, gc_], rhs=bm[:, gi, :], start=False, stop=True),
                         reads=["utok"], writes=["ps4"])
                    T.op('dve', lambda e: e.tensor_copy(out=pooledT, in_=ps[4][:, 0:128]), reads=["ps4"], writes=["pooledT"])
                    T.op('pe', lambda e: e.matmul(ps[5][:, 0:128], lhsT=wgrp[:, gi, :], rhs=pooledT, start=True, stop=True),
                         reads=["pooledT"], writes=["ps5"])
                    T.op('act', lambda e: e.activation(out=yaT[:, gi, blk * 128:(blk + 1) * 128], in_=ps[5][:, 0:128], func=AF.Copy,
                                                       scale=pscale[:, gi:gi + 1]), reads=["ps5"], writes=["yaT"])
                if not samp:
                    T.op('pool', lambda e: e.tensor_copy(out=uprev, in_=utok), reads=["utok"], writes=["uprev"])
            wz = [load_w(w_in, OFF_Z + c * 256, 256, 8) for c in range(2)]
            for c in range(2):
                for blk in blks:
                    for dc in range(8):
                        T.op('pe', lambda e: e.matmul(ps[2 + blk][:, c * 256:(c + 1) * 256], lhsT=xnT[:, dc, blk * 128:(blk + 1) * 128],
                                                      rhs=wz[c][0][:, dc, :], start=(dc == 0), stop=(dc == 7)),
                             reads=["xnT", wz[c][1]], writes=["ps%d" % (2 + blk)])
            for blk in blks:
                bi = ti * 2 + blk
                T.op('act', lambda e: e.activation(out=ztok_w[:, blk, :], in_=ps[2 + blk], func=AF.Silu), reads=["ps%d" % (2 + blk)], writes=["ztok"])
                if samp:
                    T.dma('pool', otok_w, B["o_all"][SEQ + blk * 128: SEQ + (blk + 1) * 128, :], writes=["otok"])
                elif NGROUP_DBG < 32:
                    T.dma('pool', otok_w, B["o_all"][bi * 128:(bi + 1) * 128, :], writes=["otok"])
                else:
                    T.dma('pool', ocand_w, B["o_all"][0:SEQ, :].rearrange("(q r) c -> r q c", q=4)[bi * 128:(bi + 1) * 128], writes=["ocand"])
                    T.op('dve', lambda e: e.tensor_scalar(out=otok_w, in0=ocand[:, 0, :], scalar1=selb[:, 0:1], scalar2=None, op0=ALU.mult),
                         reads=["ocand"], writes=["otok"])
                    for q in range(1, 4):
                        T.op('dve', lambda e: e.scalar_tensor_tensor(out=otok_w, in0=ocand[:, q, :], scalar=selb[:, q:q + 1], in1=otok,
                                                                     op0=ALU.mult, op1=ALU.add), reads=["ocand", "otok"], writes=["otok"])
                T.op('act', lambda e: e.activation(out=xnb[:, 0:512], in_=otok, func=AF.Square), reads=["otok"], writes=["xnb"])
                T.op('dve', lambda e: e.tensor_reduce(out=sq4, in_=xnb[:, 0:512].rearrange("p (h v) -> p h v", h=4), axis=AX.X, op=ALU.add),
                     reads=["xnb"], writes=["sq4"])
                T.op('act', lambda e: e.activation(out=sq4, in_=sq4, func=AF.Sqrt, bias=EPS, scale=1.0 / 128), reads=["sq4"], writes=["sq4"])
                T.op('dve', lambda e: e.reciprocal(out=sq4, in_=sq4), reads=["sq4"], writes=["sq4"])
                o3 = otok.rearrange("p (h v) -> p h v", h=4)
                o3w = otok_w.rearrange("p (h v) -> p h v", h=4)
                T.op('dve', lambda e: e.tensor_tensor(out=o3w, in0=o3, in1=sq4.unsqueeze(2).to_broadcast([128, 4, 128]), op=ALU.mult),
                     reads=["otok", "sq4"], writes=["otok"])
                T.op('dve', lambda e: e.tensor_tensor(out=otok_w, in0=otok, in1=gdn, op=ALU.mult), reads=["otok"], writes=["otok"])
                T.op('dve', lambda e: e.tensor_tensor(out=otok_w, in0=otok, in1=ztok[:, blk, :], op=ALU.mult), reads=["otok", "ztok"], writes=["otok"])
                for cc in range(4):
                    T.op('pe', lambda e: e.transpose(out=ps[4][:, cc * 128:(cc + 1) * 128], in_=otok[:, cc * 128:(cc + 1) * 128], identity=ident),
                         reads=["otok"], writes=["ps4"])
                T.op('act', lambda e: e.activation(out=ybT[:, :, blk * 128:(blk + 1) * 128], in_=ps[4].rearrange("p (c t) -> p c t", c=4), func=AF.Copy),
                     reads=["ps4"], writes=["ybT"])
            for n in range(4):
                wga = load_w(w_in, OFF_G + n * 256, 256, 8)
                wgb = load_w(w_in, OFF_G + 1024 + n * 256, 256, 8)
                i = wrr[0] % 3
                wrr[0] += 1
                T.dma('pool', wb[i][:, 0:4, :], w_upp[:, n * 256:(n + 1) * 256].rearrange("(k p) c -> p k c", p=128), writes=["wb%d" % i])
                T.dma('pool', wb[i][:, 4:8, :], w_upd[:, n * 256:(n + 1) * 256].rearrange("(k p) c -> p k c", p=128), writes=["wb%d" % i])
                wup, wupk = wb[i], "wb%d" % i
                for blk in blks:
                    tcs = slice(blk * 128, (blk + 1) * 128)
                    pg, pu = ps[blk * 2], ps[blk * 2 + 1]
                    kg, ku = "ps%d" % (blk * 2), "ps%d" % (blk * 2 + 1)
                    for dc in range(8):
                        T.op('pe', lambda e: e.matmul(pg[:, 0:256], lhsT=xnT[:, dc, tcs], rhs=wga[0][:, dc, :], start=(dc == 0), stop=(dc == 7)),
                             reads=["xnT", wga[1]], writes=[kg])
                    for dc in range(8):
                        T.op('pe', lambda e: e.matmul(pg[:, 256:512], lhsT=xnT[:, dc, tcs], rhs=wgb[0][:, dc, :], start=(dc == 0), stop=(dc == 7)),
                             reads=["xnT", wgb[1]], writes=[kg])
                    for cc in range(4):
                        T.op('pe', lambda e: e.matmul(pu[:, 0:256], lhsT=yaT[:, cc, tcs], rhs=wup[:, cc, :], start=(cc == 0), stop=(cc == 3)),
                             reads=["yaT", wupk], writes=[ku])
                    for cc in range(4):
                        T.op('pe', lambda e: e.matmul(pu[:, 256:512], lhsT=ybT[:, cc, tcs], rhs=wup[:, 4 + cc, :], start=(cc == 0), stop=(cc == 3)),
                             reads=["ybT", wupk], writes=[ku])
                    T.op('act', lambda e: e.activation(out=sga, in_=pg[:, 0:256], func=AF.Sigmoid), reads=[kg], writes=["sga"])
                    T.op('act', lambda e: e.activation(out=sgb, in_=pg[:, 256:512], func=AF.Sigmoid), reads=[kg], writes=["sgb"])
                    T.op('dve', lambda e: e.tensor_tensor(out=sga, in0=pu[:, 0:256], in1=sga, op=ALU.mult), reads=[ku, "sga"], writes=["sga"])
                    T.op('dve', lambda e: e.tensor_tensor(out=sgb, in0=pu[:, 256:512], in1=sgb, op=ALU.mult), reads=[ku, "sgb"], writes=["sgb"])
                    T.op('pool', lambda e: e.tensor_tensor(out=mtok_w[:, blk, n * 256:(n + 1) * 256], in0=sga, in1=sgb, op=ALU.add),
                         reads=["sga", "sgb"], writes=["mtok"])
            for blk in blks:
                for half in range(2):
                    for q in range(4):
                        dc = half * 4 + q
                        T.op('pe', lambda e: e.transpose(out=ps[4 + half][:, q * 128:(q + 1) * 128], in_=mtok[:, blk, dc * 128:(dc + 1) * 128], identity=ident),
                             reads=["mtok"], writes=["ps%d" % (4 + half)])
                    T.op('act' if half else 'dve',
                         (lambda e: e.activation(out=mT[:, half * 4:(half + 1) * 4, blk * 128:(blk + 1) * 128],
                                                 in_=ps[4 + half].rearrange("p (q t) -> p q t", q=4), func=AF.Copy)) if half else
                         (lambda e: e.tensor_copy(out=mT[:, half * 4:(half + 1) * 4, blk * 128:(blk + 1) * 128],
                                                  in_=ps[4 + half].rearrange("p (q t) -> p q t", q=4))),
                         reads=["ps%d" % (4 + half)], writes=["mT"])
            for n in range(4):
                wo = load_w(w_out, n * 256, 256, 8)
                for blk in blks:
                    pk = 6 + blk
                    for dc in range(8):
                        T.op('pe', lambda e: e.matmul(ps[pk][:, 0:256], lhsT=mT[:, dc, blk * 128:(blk + 1) * 128], rhs=wo[0][:, dc, :],
                                                      start=(dc == 0), stop=(dc == 7)), reads=["mT", wo[1]], writes=["ps%d" % pk])
                    T.op('dve', lambda e: e.tensor_tensor(out=xres[:, blk, n * 256:(n + 1) * 256], in0=ps[pk][:, 0:256],
                                                          in1=xres[:, blk, n * 256:(n + 1) * 256], op=ALU.add),
                         reads=["ps%d" % pk, "xres%d" % blk], writes=["xres%d" % blk])
            for blk in blks:
                norm_block(xres[:, blk, :], "xres%d" % blk, gffnc, xn2T, "xn2T", slice(blk * 128, (blk + 1) * 128))
            T.op('pool', lambda e: e.tensor_copy(out=xn2Tb, in_=xn2T.bitcast(F32)), reads=["xn2T"], writes=["xn2Tb"])
            if STAGE < 6:
                for blk in blks:
                    row0 = ti * TT + blk * 128
                    T.dma('sp', B["y"][row0:row0 + 128, :], xres[:, blk, :], reads=["xres%d" % blk])
                T.barrier()
                continue
            for qc_ in range(8):
                wq = load_w(w_q, qc_ * 256, 256, 8)
                for gq in range(2):
                    for dc in range(8):
                        T.op('pe', lambda e: e.matmul(ps[2][:, gq * 256:(gq + 1) * 256], lhsT=wq[0][:, dc, gq * 128:(gq + 1) * 128],
                                                      rhs=xn2T[:, dc, :], start=(dc == 0), stop=(dc == 7)),
                             reads=["xn2T", wq[1]], writes=["ps2"])
                T.op('act', lambda e: e.activation(out=qTsb, in_=ps[2].rearrange("p (g t) -> p g t", g=2), func=AF.Copy), reads=["ps2"], writes=["qTsb"])
                for gq in range(2):
                    cq = qc_ * 2 + gq
                    hh, half = cq // 2, cq % 2
                    for blk in blks:
                        bk = "B%d_" % blk
                        pk = 3 + blk
                        T.op('pe', lambda e: e.matmul(ps[pk][:, 0:128], lhsT=qTsb[:, gq, blk * 128:(blk + 1) * 128],
                                                      rhs=keysT[:, half * 8 + hh, :], start=True, stop=True),
                             reads=["qTsb"], writes=["ps%d" % pk])
                        vvb, iub = B_["v%d" % (half + 1)][blk], B_["i%du" % (half + 1)][blk]
                        T.op('dve', lambda e: e.max(out=vvb[:, hh, 0:8], in_=ps[pk][:, 0:128]), reads=["ps%d" % pk], writes=[bk + "v"])
                        T.op('dve', lambda e: e.match_replace(out=tmpk, in_to_replace=vvb[:, hh, 0:8], in_values=ps[pk][:, 0:128], imm_value=-1e30),
                             reads=["ps%d" % pk, bk + "v"], writes=["tmpk"])
                        T.op('dve', lambda e: e.max(out=vvb[:, hh, 8:16], in_=tmpk), reads=["tmpk"], writes=[bk + "v"])
                        T.op('dve', lambda e: e.max_index(out=iub[:, hh, 0:8], in_max=vvb[:, hh, 0:8], in_values=ps[pk][:, 0:128]),
                             reads=["ps%d" % pk, bk + "v"], writes=[bk + "i"])
                        T.op('dve', lambda e: e.max_index(out=iub[:, hh, 8:16], in_max=vvb[:, hh, 8:16], in_values=ps[pk][:, 0:128]),
                             reads=["ps%d" % pk, bk + "v"], writes=[bk + "i"])
            for blk in blks:
                bk = "B%d_" % blk
                v1b, v2b, i1b, i2b = B_["v1"][blk], B_["v2"][blk], B_["i1u"][blk], B_["i2u"][blk]
                tcs = slice(blk * 128, (blk + 1) * 128)
                T.op('dve', lambda e: e.tensor_copy(out=i1f, in_=i1b), reads=[bk + "i"], writes=["i1f"])
                T.op('dve', lambda e: e.tensor_copy(out=i2f, in_=i2b), reads=[bk + "i"], writes=["i2f"])
                c4 = cand_w.rearrange("p (h a b) -> p h a b", h=8, a=16)
                T.op('dve', lambda e: e.tensor_tensor(out=c4, in0=v1b.unsqueeze(3).to_broadcast([128, 8, 16, 16]),
                                                      in1=v2b.unsqueeze(2).to_broadcast([128, 8, 16, 16]), op=ALU.add),
                     reads=[bk + "v"], writes=["cand"])
                c3 = cand.rearrange("p (h x) -> p h x", h=8)
                for hh in range(8):
                    T.op('dve', lambda e: e.max(out=cv[:, hh, 0:8], in_=c3[:, hh, :]), reads=["cand"], writes=["cv"])
                    T.op('dve', lambda e: e.match_replace(out=tmp2, in_to_replace=cv[:, hh, 0:8], in_values=c3[:, hh, :], imm_value=-1e30),
                         reads=["cand", "cv"], writes=["tmp2"])
                    T.op('dve', lambda e: e.max(out=cv[:, hh, 8:16], in_=tmp2), reads=["tmp2"], writes=["cv"])
                    T.op('dve', lambda e: e.max_index(out=ciu[:, hh, 0:8], in_max=cv[:, hh, 0:8], in_values=c3[:, hh, :]), reads=["cand", "cv"], writes=["ciu"])
                    T.op('dve', lambda e: e.max_index(out=ciu[:, hh, 8:16], in_max=cv[:, hh, 8:16], in_values=c3[:, hh, :]), reads=["cand", "cv"], writes=["ciu"])
                T.op('dve', lambda e: e.tensor_tensor(out=ex, in0=cv, in1=cv[:, :, 0:1].to_broadcast([128, 8, 16]), op=ALU.subtract),
                     reads=["cv"], writes=["ex"])
                T.op('act', lambda e: e.activation(out=ex, in_=ex, func=AF.Exp), reads=["ex"], writes=["ex"])
                T.op('dve', lambda e: e.tensor_reduce(out=s8, in_=ex, axis=AX.X, op=ALU.add), reads=["ex"], writes=["s8"])
                T.op('dve', lambda e: e.reciprocal(out=s8, in_=s8), reads=["s8"], writes=["s8"])
                T.op('dve', lambda e: e.tensor_tensor(out=gsl.rearrange("p (h k) -> p h k", h=8), in0=ex,
                                                      in1=s8.unsqueeze(2).to_broadcast([128, 8, 16]), op=ALU.mult),
                     reads=["ex", "s8"], writes=["gsl"])
                T.op('dve', lambda e: e.tensor_scalar(out=abu, in0=ciu, scalar1=4, scalar2=None, op0=ALU.logical_shift_right), reads=["ciu"], writes=["abu"])
                T.op('dve', lambda e: e.tensor_copy(out=af, in_=abu), reads=["abu"], writes=["af"])
                T.op('dve', lambda e: e.tensor_scalar(out=abu, in0=ciu, scalar1=15, scalar2=None, op0=ALU.bitwise_and), reads=["ciu", "af"], writes=["abu"])
                T.op('dve', lambda e: e.tensor_copy(out=bf, in_=abu), reads=["abu"], writes=["bf"])
                for (sel_f, idx_f, dst) in ((af, i1f, e1f), (bf, i2f, e2f)):
                    e3 = eqb.rearrange("p (s a) -> p s a", a=16)
                    e3w = eqb_w.rearrange("p (s a) -> p s a", a=16)
                    T.op('dve', lambda e: e.tensor_tensor(out=e3w, in0=sel_f.rearrange("p h k -> p (h k)").unsqueeze(2).to_broadcast([128, 128, 16]),
                                                          in1=iota[:, 0:16].unsqueeze(1).to_broadcast([128, 128, 16]), op=ALU.is_equal),
                         reads=["af", "bf"], writes=["eqb"])
                    e4 = eqb.rearrange("p (h k a) -> p h k a", h=8, k=16)
                    e4w = eqb_w.rearrange("p (h k a) -> p h k a", h=8, k=16)
                    T.op('dve', lambda e: e.tensor_tensor(out=e4w, in0=e4, in1=idx_f.unsqueeze(2).to_broadcast([128, 8, 16, 16]), op=ALU.mult),
                         reads=["eqb", "i1f", "i2f"], writes=["eqb"])
                    T.op('dve', lambda e: e.tensor_reduce(out=dst, in_=e3, axis=AX.X, op=ALU.add), reads=["eqb"], writes=["e12f"])
                for (src, dstT, nm) in ((e1f, e1T, "e1T"), (e2f, e2T, "e2T"), (gsl, gT, "gT")):
                    T.op('pe', lambda e: e.transpose(out=ps[5][:, 0:128], in_=src, identity=ident), reads=["e12f", "gsl"], writes=["ps5"])
                    T.op('act', lambda e: e.activation(out=dstT[:, tcs], in_=ps[5][:, 0:128], func=AF.Copy), reads=["ps5"], writes=[nm])
            T.barrier()
            for hf in range(2):
                for t in range(TT):
                    a_ = oha[t % 4]
                    b_ = ohb[t % 4]
                    T.op('dve', lambda e: e.tensor_scalar(out=a_, in0=iota[:, hf * 64:(hf + 1) * 64], scalar1=e1T[:, t:t + 1], scalar2=gT[:, t:t + 1],
                                                          op0=ALU.is_equal, op1=ALU.mult), reads=["e1T", "gT"], writes=["oha%d" % (t % 4)])
                    T.op('dve', lambda e: e.tensor_scalar(out=b_, in0=iota, scalar1=e2T[:, t:t + 1], scalar2=None, op0=ALU.is_equal),
                         reads=["e2T"], writes=["ohb%d" % (t % 4)])
                    pk = 6 + (t // 8) % 2
                    T.op('pe', lambda e: e.matmul(ps[pk][:, (t % 8) * 64:(t % 8 + 1) * 64], lhsT=b_, rhs=a_, start=True, stop=True),
                         reads=["oha%d" % (t % 4), "ohb%d" % (t % 4)], writes=["ps%d" % pk])
                    if t % 8 == 7:
                        t0 = t - 7
                        T.op('act', lambda e: e.activation(out=Wsb_w[:, :, t0:t0 + 8].rearrange("p e t -> p t e"),
                                                           in_=ps[pk].rearrange("p (t e) -> p t e", t=8), func=AF.Copy),
                             reads=["ps%d" % pk], writes=["Wsb"])
                def d_load(e1):
                    T.dma('sp', ublk[e1 % 4], B["Ubf"][e1].rearrange("p (k e) -> p k e", k=8), writes=["ublk%d" % (e1 % 4)])
                    T.dma('sp', vblk[e1 % 4], B["Vbf"][e1 * 128:(e1 + 1) * 128, :], writes=["vblk%d" % (e1 % 4)])

                def d_scores(e1):
                    pk = 4 + e1 % 2
                    for dc in range(8):
                        T.op('pe', lambda e: e.matmul(ps[pk][:, 0:TT], lhsT=ublk[e1 % 4][:, dc, :], rhs=xn2Tb[:, dc, :], start=(dc == 0), stop=(dc == 7)),
                             reads=["ublk%d" % (e1 % 4), "xn2Tb"], writes=["ps%d" % pk])

                def d_gate(e1):
                    pk = 4 + e1 % 2
                    ab, gb_ = actb[e1 % 2], gab[e1 % 2]
                    T.op('act', lambda e: e.activation(out=ab, in_=ps[pk][:, 0:TT], func=AF.Gelu), reads=["ps%d" % pk], writes=["actb%d" % (e1 % 2)])
                    T.op('dve', lambda e: e.tensor_tensor(out=gb_, in0=ab, in1=Wsb[:, e1 - hf * 64, :], op=ALU.mult),
                         reads=["actb%d" % (e1 % 2), "Wsb"], writes=["gab%d" % (e1 % 2)])

                def d_acc(e1):
                    gb_, vb = gab[e1 % 2], vblk[e1 % 4]
                    for blk in blks:
                        for half in range(2):
                            T.op('pe', lambda e: e.matmul(ps[blk * 2 + half], lhsT=gb_[:, blk * 128:(blk + 1) * 128], rhs=vb[:, half * 512:(half + 1) * 512],
                                                          start=(e1 == 0), stop=(e1 == 127)),
                                 reads=["gab%d" % (e1 % 2), "vblk%d" % (e1 % 4)], writes=["psy%d" % (blk * 2 + half)])

                e0 = hf * 64
                d_load(e0)
                d_load(e0 + 1)
                d_scores(e0)
                d_gate(e0)
                for i in range(64):
                    e1 = e0 + i
                    if i + 2 < 64:
                        d_load(e1 + 2)
                    if i + 1 < 64:
                        d_scores(e1 + 1)
                    d_acc(e1)
                    if i + 1 < 64:
                        d_gate(e1 + 1)
            T.barrier()
            for blk in blks:
                row0 = ti * TT + blk * 128
                for half in range(2):
                    T.op('dve', lambda e: e.tensor_tensor(out=xres[:, blk, half * 512:(half + 1) * 512], in0=ps[blk * 2 + half],
                                                          in1=xres[:, blk, half * 512:(half + 1) * 512], op=ALU.add),
                         reads=["psy%d" % (blk * 2 + half), "xres%d" % blk], writes=["xres%d" % blk])
                T.op('act', lambda e: e.activation(out=xnb, in_=xres[:, blk, :], func=AF.Square, accum_out=ss), reads=["xres%d" % blk], writes=["xnb", "ssb"])
                T.op('act', lambda e: e.activation(out=rs, in_=ss, func=AF.Sqrt, bias=EPS, scale=1.0 / D_MODEL), reads=["ssb"], writes=["rsb"])
                T.op('dve', lambda e: e.reciprocal(out=rstd, in_=rs), reads=["rsb"], writes=["rstdb"])
                T.op('dve', lambda e: e.scalar_tensor_tensor(out=xnb, in0=xres[:, blk, :], scalar=rstd, in1=gfin, op0=ALU.mult, op1=ALU.mult),
                     reads=["xres%d" % blk, "rstdb"], writes=["xnb"])
                T.dma('sp', B["y"][row0:row0 + 128, :], xnb, reads=["xnb"])
            T.barrier()
```

```python
import os
import numpy as np
from contextlib import ExitStack
import concourse.bass as bass
import concourse.mybir as mybir
from concourse.bass_utils import run_bass_kernel_spmd

F32 = mybir.dt.float32
F32R = mybir.dt.float32r
U32 = mybir.dt.uint32
I32 = mybir.dt.int32
BF16 = mybir.dt.bfloat16
AF = mybir.ActivationFunctionType
ALU = mybir.AluOpType
AX = mybir.AxisListType

D_MODEL = 1024
SEQ = 8192
NCORE = 8
QTOK = 2048
STOK = 256
OWN = QTOK + STOK
D_IN = 4616
OFF_QKV, OFF_Z, OFF_B, OFF_A, OFF_G = 512, 2048, 2560, 2564, 2568
EPS = 1e-6
NEG = -30000.0

STAGE = int(os.environ.get("MK_STAGE", "99"))
NGROUP_DBG = int(os.environ.get("MK_NGROUP", "32"))
SUB = int(os.environ.get("MK_SUB", "99"))
DEBUG_A = int(os.environ.get("MK_DEBUG_A", "0"))
TILES = [int(t) for t in os.environ.get("MK_TILES", "0,1,2,3,4,5,6,7,8").split(",")]


class Trk:
    ROT = int(os.environ.get("MK_ROT", "8000"))
    NDMA = 24

    def __init__(self, nc):
        self.nc = nc
        self.eng = dict(pe=nc.tensor, act=nc.scalar, dve=nc.vector, pool=nc.gpsimd, sp=nc.sync)
        self.sem = {k: nc.alloc_semaphore("prog_%s_0" % k) for k in self.eng}
        self.semgen = {k: 0 for k in self.eng}
        self.cnt = {k: 0 for k in self.eng}
        self.waited = {k: {} for k in self.eng}
        self.lastw = {}
        self.readers = {}
        self.dma_sems = [nc.alloc_semaphore("dmas_%d" % i) for i in range(self.NDMA)]
        self.dma_cnt = [0] * self.NDMA
        self.dma_rr = {'hw': 0, 'sw': 0}
        self.NSW = 8
        self.n_inst = 0

    def _wait(self, e, tok):
        owner, sem, val, semkey = tok
        if owner == e and e == 'pe':
            return
        w = self.waited[e]
        if w.get(semkey, 0) >= val:
            return
        self.eng[e].wait_ge(sem, val)
        w[semkey] = val

    def _deps(self, e, reads, writes):
        for b in reads:
            lw = self.lastw.get(b)
            if lw is not None:
                self._wait(e, lw)
            if b.startswith("ps"):
                for tok in self.readers.get(b, {}).values():
                    if tok[0] != e:
                        self._wait(e, tok)
        for b in writes:
            lw = self.lastw.get(b)
            if lw is not None:
                self._wait(e, lw)
            for tok in self.readers.get(b, {}).values():
                self._wait(e, tok)

    def _record(self, tok, reads, writes):
        for b in reads:
            self.readers.setdefault(b, {})[tok[0]] = tok
        for b in writes:
            self.lastw[b] = tok
            self.readers[b] = {}

    def op(self, e, fn, reads=(), writes=()):
        self._deps(e, reads, writes)
        inst = fn(self.eng[e])
        if self.cnt[e] >= self.ROT:
            self.semgen[e] += 1
            self.sem[e] = self.nc.alloc_semaphore("prog_%s_%d" % (e, self.semgen[e]))
            self.cnt[e] = 0
        self.cnt[e] += 1
        sem = self.sem[e]
        inst.then_inc(sem, 1)
        tok = (e, sem, self.cnt[e], (e, self.semgen[e]))
        self._record(tok, reads, writes)
        self.n_inst += 1
        return inst

    def dma(self, q, out, in_, reads=(), writes=()):
        self._deps(q, reads, writes)
        if q == 'pool':
            k = self.dma_rr['sw']
            self.dma_rr['sw'] = (k + 1) % self.NSW
        else:
            k = self.NSW + self.dma_rr['hw']
            self.dma_rr['hw'] = (self.dma_rr['hw'] + 1) % (self.NDMA - self.NSW)
        sem = self.dma_sems[k]
        if self.dma_cnt[k] > 0:
            self._wait(q, (('dma', k), sem, 16 * self.dma_cnt[k], ('dma', k)))
        inst = self.eng[q].dma_start(out=out, in_=in_)
        self.dma_cnt[k] += 1
        inst.then_inc(sem, 16)
        tok = (('dma', k), sem, 16 * self.dma_cnt[k], ('dma', k))
        self._record(tok, reads, writes)
        self.n_inst += 1
        return inst

    def wait_all_dma(self, e):
        for k in range(self.NDMA):
            if self.dma_cnt[k] > 0:
                self._wait(e, (('dma', k), self.dma_sems[k], 16 * self.dma_cnt[k], ('dma', k)))

    def barrier(self):
        toks = []
        for e2 in self.eng:
            if self.cnt[e2] > 0:
                toks.append((e2, self.sem[e2], self.cnt[e2], (e2, self.semgen[e2])))
        for e in self.eng:
            for tok in toks:
                if tok[0] != e:
                    self._wait(e, tok)
            self.wait_all_dma(e)

    def barrier_consts(self):
        for e in self.eng:
            self.wait_all_dma(e)


def _consts():
    idx = np.arange(128)
    ch = idx // 64
    same = (ch[:, None] == ch[None, :])
    c = {}
    c["ident"] = np.eye(128, dtype=np.float32)
    c["ones"] = np.ones((128, 128), np.float32)
    c["umask"] = (same & (idx[:, None] <= idx[None, :])).astype(np.float32)
    c["bd"] = same.astype(np.float32)
    low_incl = same & (idx[None, :] <= idx[:, None])
    c["lstrict"] = (same & (idx[None, :] < idx[:, None])).astype(np.float32)
    cs = np.zeros((128, 2), np.float32)
    cs[:64, 0] = 1.0
    cs[64:, 1] = 1.0
    c["chunksel"] = cs
    c["iota"] = np.tile(np.arange(128, dtype=np.float32)[None, :], (128, 1))
    return c


def _block_weights(w_in, w_upp, w_upd, w_out, w_q):
    def blk(w, c0):
        k = w.shape[0] // 128
        return w[:, c0:c0 + 256].reshape(k, 128, 256).transpose(1, 0, 2)
    chunks = []
    for c in range(2):
        chunks.append(blk(w_in, c * 256))
    for c in range(2):
        chunks.append(blk(w_in, OFF_Z + c * 256))
    for n in range(4):
        chunks.append(blk(w_in, OFF_G + n * 256))
    for n in range(4):
        chunks.append(blk(w_in, OFF_G + 1024 + n * 256))
    for n in range(4):
        chunks.append(np.concatenate([blk(w_upp, n * 256), blk(w_upd, n * 256)], axis=1))
    for n in range(4):
        chunks.append(blk(w_out, n * 256))
    for n in range(8):
        chunks.append(blk(w_q, n * 256))
    return np.ascontiguousarray(np.stack(chunks).reshape(28, 128, 2048).astype(np.float32))


def rep128(v):
    v = np.asarray(v, np.float32).reshape(1, -1)
    return np.ascontiguousarray(np.repeat(v, 128, axis=0))


class Prog:
    def __init__(self):
        self.nc = bass.Bass("TRN2", target_bir_lowering=False)
        self.T = Trk(self.nc)
        self.din = {}
        self.dout = {}

    def inp(self, name, shape, dt=F32):
        ap = self.nc.dram_tensor(name, list(shape), dt, kind="ExternalInput").ap()
        self.din[name] = ap
        return ap

    def outp(self, name, shape, dt=F32):
        ap = self.nc.dram_tensor(name, list(shape), dt, kind="ExternalOutput").ap()
        self.dout[name] = ap
        return ap

    def scratch(self, name, shape, dt=F32):
        return self.nc.dram_tensor(name, list(shape), dt).ap()


def sb(nc, es, name, shape, dt=F32):
    h = es.enter_context(nc.sbuf_tensor("s_" + name, list(shape), dt))
    return h.ap()


def build():
    P = Prog()
    nc, T = P.nc, P.T
    xseq = P.inp("xseq", [SEQ, D_MODEL])
    xs = P.inp("xs", [STOK, D_MODEL])
    S0 = P.inp("S0", [4, 4, 128, 128])
    conv0T = P.inp("conv0T", [4, 1536, 3])
    w_in = P.inp("w_in", [D_MODEL, D_IN])
    gmix = P.inp("gmix_bc", [128, D_MODEL])
    wconvT = P.inp("wconvT", [1536, 4])
    alog = P.inp("alog_bc", [128, 4])
    dtb = P.inp("dtb_bc", [128, 4])
    cn = {k: P.inp("c_" + k, list(v.shape)) for k, v in _consts().items()}

    dn_p = P.outp("dn_p", [4, 128, 128])
    dn_s = P.outp("dn_s", [4, 4, 128, 128])
    conv_pT = P.outp("conv_pT", [1536, 3])
    conv_sT = P.outp("conv_sT", [4, 1536, 3])
    o_all = P.outp("o_all", [SEQ + STOK, 512]) if DEBUG_A else P.scratch("o_all", [SEQ + STOK, 512])

    Bd = {}
    for nm, shp in [("xown", [OWN, D_MODEL]), ("xhalo", [16, D_MODEL]), ("pool0", [4, 15, 512]),
                    ("gmix_col", [128, 8]), ("gffn_col", [128, 8]), ("gfin_bc", [128, D_MODEL]), ("gdn_bc", [128, 512]),
                    ("pscale_col", [128, 4]), ("sel", [128, 4]), ("w_grp", [4, 128, 128]), ("keysT", [16, 128, 128]),
                    ("bmain", [128, 4, 128]), ("bprev", [128, 4, 128]), ("bfirst", [128, 4, 128]), ("bsamp", [128, 4, 128]),
                    ("bsprev", [128, 4, 128]), ("wblk", [28, 128, 2048]), ("UTb", [128, 128, 8, 128]),
                    ("peer_v", [16384, D_MODEL])]:
        Bd[nm] = P.inp(nm, shp)
    Bd["w_in"] = w_in
    Bd["o_all"] = o_all
    Bd["Ubf"] = P.scratch("Ubf", [128, 128, 1024], BF16)
    Bd["Vbf"] = P.scratch("Vbf", [16384, D_MODEL], BF16)
    Bd["y"] = P.outp("y", [OWN, D_MODEL])
    Bd["pool_p"] = P.outp("pool_p", [15, 512])
    Bd["pool_s"] = P.outp("pool_s", [4, 15, 512])

    ps = [nc.alloc_psum_tensor("psb%d" % i, [128, 512], F32).ap() for i in range(8)]

    with ExitStack() as es:
        C = {}
        for k, v in cn.items():
            C[k] = sb(nc, es, "k_" + k, list(v.shape))
            T.dma('sp', C[k], v)
        gmix_sb = sb(nc, es, "gmix", [128, D_MODEL])
        T.dma('sp', gmix_sb, gmix)
        wc_sb = sb(nc, es, "wconv", [128, 12, 4])
        T.dma('sp', wc_sb, wconvT.rearrange("(c p) j -> p c j", p=128))
        alog_sb = sb(nc, es, "alog", [128, 4])
        dtb_sb = sb(nc, es, "dtb", [128, 4])
        T.dma('sp', alog_sb, alog)
        T.dma('sp', dtb_sb, dtb)
        onesR = sb(nc, es, "onesR", [128, 128], F32R)
        T.dma('pool', onesR, cn["ones"])
        T.barrier_consts()
        nexpA = sb(nc, es, "nexpA", [128, 4])
        T.op('act', lambda e: e.activation(out=nexpA, in_=alog_sb, func=AF.Exp), writes=["nexpA"])
        T.op('dve', lambda e: e.tensor_scalar(out=nexpA, in0=nexpA, scalar1=-1.0, scalar2=None, op0=ALU.mult),
             reads=["nexpA"], writes=["nexpA"])

        phase_a(P, es, C, ps, dict(xseq=xseq, xs=xs, S0=S0, conv0T=conv0T, w_in=w_in, gmix=gmix_sb,
                                   wc=wc_sb, dtb=dtb_sb, nexpA=nexpA, onesR=onesR,
                                   dn_p=dn_p, dn_s=dn_s, conv_pT=conv_pT, conv_sT=conv_sT, o_all=o_all,
                                   UTb=Bd["UTb"], peer_v=Bd["peer_v"], Ubf=Bd["Ubf"], Vbf=Bd["Vbf"]))
        T.barrier()
        phase_b(P, C, ps, Bd)
        T.barrier()
    return P


def phase_a(P, es0, C, ps, A):
    nc, T = P.nc, P.T
    ident, onesF = C["ident"], C["ones"]
    with ExitStack() as es:
        w_a = sb(nc, es, "w_a", [128, 8, 1544], F32R)
        for dc in range(8):
            T.dma('pool', w_a[:, dc, 0:1536], A["w_in"][dc * 128:(dc + 1) * 128, OFF_QKV:OFF_Z], writes=["w_a"])
            T.dma('pool', w_a[:, dc, 1536:1544], A["w_in"][dc * 128:(dc + 1) * 128, OFF_B:OFF_G], writes=["w_a"])
        xt = [sb(nc, es, "xt%d" % i, [128, D_MODEL]) for i in range(2)]
        xn = sb(nc, es, "xn", [128, D_MODEL])
        ss = sb(nc, es, "ss", [128, 1])
        rs = sb(nc, es, "rs", [128, 1])
        rstd = sb(nc, es, "rstd", [128, 1])
        xnT = sb(nc, es, "xnT", [128, 8, 256], F32R)
        ext_w = sb(nc, es, "ext", [128, 12, 268], F32R)
        ext = ext_w.bitcast(F32)
        diagW = sb(nc, es, "diagW", [128, 48, 128], F32R)
        for cc in range(12):
            for j in range(4):
                T.op('pool', lambda e: e.tensor_scalar(out=diagW[:, cc * 4 + j, :], in0=ident, scalar1=A["wc"][:, cc, j:j + 1],
                                                       scalar2=None, op0=ALU.mult), writes=["diagW"])
        carry = sb(nc, es, "carry", [128, 12, 3])
        qcs = [sb(nc, es, "qc%d" % i, [128, 12, 256], F32R) for i in range(2)]
        sq = sb(nc, es, "sq", [128, 8, 256], F32R)
        rinvs = [sb(nc, es, "rinv%d" % i, [128, 256]) for i in range(2)]
        bas = [[sb(nc, es, "ba%d_%d" % (p_, i), [128, 8]) for i in range(2)] for p_ in range(2)]
        small = {}
        for nm, w in [("beta", 4), ("negbeta", 4), ("g", 4), ("sp", 4), ("gc", 4), ("gam", 4), ("dlt", 4),
                      ("bgn", 4), ("gsel", 8), ("egl", 8), ("tmp4", 4)]:
            small[nm] = [sb(nc, es, "%s%d" % (nm, i), [128, w]) for i in range(2)]
        NH = 4
        def tiles(nm, n=1, shape=(128, 128)):
            return [[sb(nc, es, "%s_h%d_%d" % (nm, h, i), list(shape)) for i in range(n)] for h in range(NH)]
        Lg_all = sb(nc, es, "Lg_all", [128, 4, 128])
        dec = tiles("dec")
        decT = tiles("decT")
        def tilesR(nm, n=1, shape=(128, 128)):
            return [[sb(nc, es, "%s_h%d_%d" % (nm, h, i), list(shape), F32R) for i in range(n)] for h in range(NH)]
        Mp = tilesR("Mp", 2)
        NR = tilesR("NR", 2, (128, 256))
        Rfin = tilesR("Rfin")
        VK = tilesR("VK", 1, (128, 256))
        Qg = tiles("Qg")
        wkn = tiles("wkn")
        kdec = tiles("kdec", 2)
        uv = tiles("uv", 2)
        PT = tiles("PT", 2)
        qeT = tiles("qeT", 2)
        FT = tiles("FT", 4)
        Sst = [sb(nc, es, "Sst%d" % i, [128, 4, 128]) for i in range(2)]
        ostage = [sb(nc, es, "ost%d" % i, [64, 512]) for i in range(2)]

        T.op('dve', lambda e: e.memset(Sst[0], 0.0), writes=["S0buf"])
        T.op('dve', lambda e: e.memset(carry, 0.0), writes=["carry"])
        s_cur = 0
        chunk_ctr = 0

        groups = [("p", g) for g in range(min(32, NGROUP_DBG))] + [("s", 0)]
        def s1a(kind, g, par):
            nseq, L = (1, 256) if kind == "p" else (4, 64)
            ba = bas[par]
            if kind == "p":
                last = (g == min(32, NGROUP_DBG) - 1)
                for e1 in range(g * 4, 128 if last else g * 4 + 4):
                    T.dma('pool', A["Ubf"][e1], A["UTb"][e1].rearrange("p k e -> p (k e)"))
                    T.dma('pool', A["Vbf"][e1 * 128:(e1 + 1) * 128, :], A["peer_v"][e1 * 128:(e1 + 1) * 128, :])
            extv = ext[:, :, 0:nseq * (L + 3)].rearrange("p c (s l) -> p c s l", s=nseq)
            extvw = ext_w[:, :, 0:nseq * (L + 3)].rearrange("p c (s l) -> p c s l", s=nseq)
            for blk in range(2):
                xb = xt[blk]
                src = A["xseq"][g * 256 + blk * 128: g * 256 + (blk + 1) * 128, :] if kind == "p" \
                    else A["xs"][blk * 128:(blk + 1) * 128, :]
                T.dma('sp', xb, src, writes=["xt%d" % blk])
                T.op('act', lambda e: e.activation(out=xn, in_=xb, func=AF.Square, accum_out=ss),
                     reads=["xt%d" % blk], writes=["xn", "ss"])
                T.op('act', lambda e: e.activation(out=rs, in_=ss, func=AF.Sqrt, bias=EPS, scale=1.0 / D_MODEL),
                     reads=["ss"], writes=["rs"])
                T.op('dve', lambda e: e.reciprocal(out=rstd, in_=rs), reads=["rs"], writes=["rstd"])
                T.op('dve', lambda e: e.scalar_tensor_tensor(out=xn, in0=xb, scalar=rstd, in1=A["gmix"],
                                                             op0=ALU.mult, op1=ALU.mult),
                     reads=["xt%d" % blk, "rstd"], writes=["xn"])
                for half in range(2):
                    for q in range(4):
                        dc = half * 4 + q
                        T.op('pe', lambda e: e.transpose(out=ps[half][:, q * 128:(q + 1) * 128],
                                                         in_=xn[:, dc * 128:(dc + 1) * 128], identity=ident),
                             reads=["xn"], writes=["ps%d" % half])
                    T.op('act', lambda e: e.activation(
                        out=xnT[:, half * 4:(half + 1) * 4, blk * 128:(blk + 1) * 128],
                        in_=ps[half].rearrange("p (q t) -> p q t", q=4), func=AF.Copy),
                        reads=["ps%d" % half], writes=["xnT"])
                for dc in range(8):
                    T.op('pe', lambda e: e.matmul(ps[1][:, 504:512], lhsT=xnT[:, dc, blk * 128:(blk + 1) * 128],
                                                  rhs=w_a[:, dc, 1536:1544], start=(dc == 0), stop=(dc == 7)),
                         reads=["xnT", "w_a"], writes=["ps1"])
                T.op('dve', lambda e: e.tensor_copy(out=ba[blk], in_=ps[1][:, 504:512]),
                     reads=["ps1"], writes=["ba%d_%d" % (par, blk)])
        def s1b(kind, g, par):
            if STAGE < 2:
                return
            nseq, L = (1, 256) if kind == "p" else (4, 64)
            qc_w = qcs[par]
            qc = qc_w.bitcast(F32)
            extv = ext[:, :, 0:nseq * (L + 3)].rearrange("p c (s l) -> p c s l", s=nseq)
            extvw = ext_w[:, :, 0:nseq * (L + 3)].rearrange("p c (s l) -> p c s l", s=nseq)
            if kind == "p":
                T.op('pool', lambda e: e.tensor_copy(out=extvw[:, :, 0, 0:3], in_=carry), reads=["carry"], writes=["ext"])
            else:
                for s in range(4):
                    T.dma('pool', extvw[:, :, s, 0:3], A["conv0T"][s].rearrange("(c p) j -> p c j", p=128), writes=["ext"])
            for cc in range(12):
                hf = cc % 2
                pq = ps[hf][:, 0:256]
                for dc in range(8):
                    T.op('pe', lambda e: e.matmul(pq, lhsT=w_a[:, dc, cc * 128:(cc + 1) * 128],
                                                  rhs=xnT[:, dc, :], start=(dc == 0), stop=(dc == 7)),
                         reads=["xnT", "w_a"], writes=["ps%d" % hf])
                T.op('act', lambda e: e.activation(out=extvw[:, cc, :, 3:3 + L],
                                                   in_=pq.rearrange("p (s l) -> p s l", s=nseq),
                                                   func=AF.Copy),
                     reads=["ps%d" % hf], writes=["ext"])
            if kind == "p":
                T.op('pool', lambda e: e.tensor_copy(out=carry, in_=extv[:, :, 0, L:L + 3]), reads=["ext"], writes=["carry"])
                if g == 31:
                    T.dma('sp', A["conv_pT"].rearrange("(c p) j -> p c j", p=128), carry, reads=["carry"])
            else:
                for s in range(4):
                    T.dma('sp', A["conv_sT"][s].rearrange("(c p) j -> p c j", p=128), extv[:, :, s, L:L + 3], reads=["ext"])
            for cc in range(12):
                hf = cc % 2
                pq = ps[hf][:, 0:256]
                for j in range(4):
                    T.op('pe', lambda e: e.matmul(pq, lhsT=diagW[:, cc * 4 + j, :], rhs=extvw[:, cc, :, j:j + L], start=(j == 0), stop=(j == 3)),
                         reads=["ext"], writes=["ps%d" % hf])
                T.op('act', lambda e: e.activation(out=qc_w[:, cc, :], in_=pq, func=AF.Silu), reads=["ps%d" % hf], writes=["qc%d" % par])
            T.op('act', lambda e: e.activation(out=sq, in_=qc[:, 0:8, :], func=AF.Square), reads=["qc%d" % par], writes=["sq"])
            for cc in range(8):
                hf = cc % 2
                pq = ps[hf][:, 0:256]
                T.op('pe', lambda e: e.matmul(pq, lhsT=A["onesR"], rhs=sq[:, cc, :], start=True, stop=True),
                     reads=["sq"], writes=["ps%d" % hf])
                sc = 128.0 if cc < 4 else 1.0
                rinv, rk_ = rinvs[cc % 2], "rinv%d" % (cc % 2)
                T.op('act', lambda e: e.activation(out=rinv, in_=pq, func=AF.Sqrt, bias=EPS * sc, scale=sc),
                     reads=["ps%d" % hf], writes=[rk_])
                T.op('dve', lambda e: e.reciprocal(out=rinv, in_=rinv), reads=[rk_], writes=[rk_])
                T.op('dve', lambda e: e.tensor_tensor(out=qc_w[:, cc, :], in0=qc[:, cc, :], in1=rinv, op=ALU.mult),
                     reads=["qc%d" % par, rk_], writes=["qc%d" % par])
        def s2(kind, g, par, blk):
            nonlocal s_cur, chunk_ctr
            if STAGE < 3:
                return
            qc_w = qcs[par]
            qc = qc_w.bitcast(F32)
            ba = bas[par]
            if True:
                bp = blk
                cols = slice(blk * 128, (blk + 1) * 128)
                sm = {k: v[bp] for k, v in small.items()}
                bab = ba[blk]
                T.op('act', lambda e: e.activation(out=sm["beta"], in_=bab[:, 0:4], func=AF.Sigmoid),
                     reads=["ba%d_%d" % (par, blk)], writes=["beta%d" % bp])
                T.op('dve', lambda e: e.tensor_tensor(out=sm["sp"], in0=bab[:, 4:8], in1=A["dtb"], op=ALU.add),
                     reads=["ba%d_%d" % (par, blk)], writes=["sp%d" % bp])
                T.op('act', lambda e: e.activation(out=sm["sp"], in_=sm["sp"], func=AF.Exp), reads=["sp%d" % bp], writes=["sp%d" % bp])
                T.op('act', lambda e: e.activation(out=sm["sp"], in_=sm["sp"], func=AF.Ln, bias=1.0),
                     reads=["sp%d" % bp], writes=["sp%d" % bp])
                T.op('dve', lambda e: e.tensor_tensor(out=sm["g"], in0=sm["sp"], in1=A["nexpA"], op=ALU.mult),
                     reads=["sp%d" % bp, "nexpA"], writes=["g%d" % bp])
                T.op('dve', lambda e: e.tensor_scalar(out=sm["negbeta"], in0=sm["beta"], scalar1=-1.0, scalar2=None, op0=ALU.mult),
                     reads=["beta%d" % bp], writes=["negbeta%d" % bp])
                T.op('pe', lambda e: e.matmul(ps[7][:, 0:4], lhsT=C["umask"], rhs=sm["g"], start=True, stop=True),
                     reads=["g%d" % bp], writes=["ps7"])
                T.op('pe', lambda e: e.matmul(ps[7][:, 8:12], lhsT=C["bd"], rhs=sm["g"], start=True, stop=True),
                     reads=["g%d" % bp], writes=["ps7"])
                T.op('act', lambda e: e.activation(out=sm["gc"], in_=ps[7][:, 0:4], func=AF.Copy), reads=["ps7"], writes=["gc%d" % bp])
                T.op('act', lambda e: e.activation(out=sm["gam"], in_=ps[7][:, 0:4], func=AF.Exp), reads=["ps7"], writes=["gam%d" % bp])
                T.op('dve', lambda e: e.tensor_tensor(out=sm["tmp4"], in0=ps[7][:, 8:12], in1=sm["gc"], op=ALU.subtract),
                     reads=["ps7", "gc%d" % bp], writes=["tmp4%d" % bp])
                T.op('act', lambda e: e.activation(out=sm["dlt"], in_=sm["tmp4"], func=AF.Exp), reads=["tmp4%d" % bp], writes=["dlt%d" % bp])
                T.op('dve', lambda e: e.tensor_tensor(out=sm["bgn"], in0=sm["negbeta"], in1=sm["gam"], op=ALU.mult),
                     reads=["negbeta%d" % bp, "gam%d" % bp], writes=["bgn%d" % bp])
                T.op('dve', lambda e: e.tensor_tensor(
                    out=sm["gsel"].rearrange("p (h c) -> p h c", c=2),
                    in0=sm["g"].unsqueeze(2).to_broadcast([128, 4, 2]),
                    in1=C["chunksel"].unsqueeze(1).to_broadcast([128, 4, 2]), op=ALU.mult),
                    reads=["g%d" % bp], writes=["gsel%d" % bp])
                T.op('pe', lambda e: e.matmul(ps[7][:, 16:24], lhsT=onesF, rhs=sm["gsel"], start=True, stop=True),
                     reads=["gsel%d" % bp], writes=["ps7"])
                T.op('act', lambda e: e.activation(out=sm["egl"], in_=ps[7][:, 16:24], func=AF.Exp), reads=["ps7"], writes=["egl%d" % bp])

                if SUB < 1:
                    return

                def hk(nm, h, i=0):
                    return "%s_%d_%d" % (nm, h, i)

                def pslot(h, i):
                    if i == 0:
                        return ps[5][:, h * 128:(h + 1) * 128], "ps5_%d" % h
                    return ps[4][:, 128 + 0:128 + 0], None

                def slotA(h):
                    return ps[2 + h][:, 0:128], "ps%d" % (2 + h)

                def slotB(h):
                    return ps[2 + h][:, 128:256], "ps%d" % (2 + h)

                def slotC(h):
                    return ps[2 + h][:, 256:384], "ps%d" % (2 + h)

                T.op('dve', lambda e: e.tensor_tensor(out=Lg_all, in0=C["umask"].unsqueeze(1).to_broadcast([128, 4, 128]),
                                                      in1=sm["g"].unsqueeze(2).to_broadcast([128, 4, 128]), op=ALU.mult),
                     reads=["g%d" % bp], writes=["Lg_all"])
                T.op('pe', lambda e: e.matmul(ps[7], lhsT=onesF, rhs=Lg_all.rearrange("p h t -> p (h t)"), start=True, stop=True),
                     reads=["Lg_all"], writes=["ps7"])
                for h in range(NH):
                    gcb = ps[7][:, h * 128:(h + 1) * 128]
                    T.op('dve', lambda e: e.tensor_scalar(out=dec[h][0], in0=gcb, scalar1=sm["gc"][:, h:h + 1], scalar2=0.0,
                                                          op0=ALU.subtract, op1=ALU.max), reads=["ps7", "gc%d" % bp], writes=[hk("dec", h)])
                    T.op('dve', lambda e: e.tensor_scalar(out=decT[h][0], in0=gcb, scalar1=sm["gc"][:, h:h + 1], scalar2=0.0,
                                                          op0=ALU.subtract, op1=ALU.min), reads=["ps7", "gc%d" % bp], writes=[hk("decT", h)])
                    T.op('act', lambda e: e.activation(out=dec[h][0], in_=dec[h][0], func=AF.Exp, scale=-1.0), reads=[hk("dec", h)], writes=[hk("dec", h)])
                    T.op('act', lambda e: e.activation(out=decT[h][0], in_=decT[h][0], func=AF.Exp), reads=[hk("decT", h)], writes=[hk("decT", h)])
                    T.op('pool', lambda e: e.tensor_tensor(out=dec[h][0], in0=dec[h][0], in1=C["lstrict"], op=ALU.mult),
                         reads=[hk("dec", h)], writes=[hk("dec", h)])
                    T.op('pool', lambda e: e.tensor_tensor(out=decT[h][0], in0=decT[h][0], in1=C["umask"], op=ALU.mult),
                         reads=[hk("decT", h)], writes=[hk("decT", h)])
                for h in range(NH):
                    pc, kc = ps[2 + h][:, 0:256], "ps%d" % (2 + h)
                    T.op('pe', lambda e: e.matmul(pc, lhsT=qc_w[:, 4 + h, cols], rhs=qc_w[:, h:h + 5:4, cols], start=True, stop=True),
                         reads=["qc%d" % par], writes=[kc])
                    T.op('dve', lambda e: e.scalar_tensor_tensor(out=Mp[h][0], in0=pc[:, 128:256], scalar=sm["negbeta"][:, h:h + 1],
                                                                 in1=dec[h][0], op0=ALU.mult, op1=ALU.mult),
                         reads=[kc, hk("dec", h), "negbeta%d" % bp], writes=[hk("Mp", h, 0)])
                    T.op('dve', lambda e: e.tensor_tensor(out=PT[h][bp], in0=pc[:, 0:128], in1=decT[h][0], op=ALU.mult),
                         reads=[kc, hk("decT", h)], writes=[hk("PT", h, bp)])
                for h in range(NH):
                    pa, ka = ps[2 + h][:, 256:384], "ps%d" % (2 + h)
                    T.op('pe', lambda e: e.transpose(out=pa, in_=Mp[h][0].bitcast(F32), identity=ident), reads=[hk("Mp", h, 0)], writes=[ka])
                    T.op('act', lambda e: e.activation(out=NR[h][0][:, 0:128], in_=pa, func=AF.Copy), reads=[ka], writes=[hk("NR", h, 0)])
                    T.op('pool', lambda e: e.tensor_tensor(out=NR[h][1][:, 128:256], in0=NR[h][0][:, 0:128].bitcast(F32), in1=ident, op=ALU.add),
                         reads=[hk("NR", h, 0)], writes=[hk("NR", h, 1)])
                for h in range(NH):
                    pb, kb = ps[2 + h], "ps%d" % (2 + h)
                    T.op('pe', lambda e: e.matmul(pb[:, 0:128], lhsT=NR[h][0][:, 0:128], rhs=Mp[h][0], start=True, stop=True),
                         reads=[hk("NR", h, 0), hk("Mp", h, 0)], writes=[kb])
                    T.op('pe', lambda e: e.matmul(pb[:, 128:256], lhsT=Mp[h][0], rhs=NR[h][0][:, 0:128], start=True, stop=True),
                         reads=[hk("NR", h, 0), hk("Mp", h, 0)], writes=[kb])
                for h in range(NH):
                    pb, kb = ps[2 + h], "ps%d" % (2 + h)
                    T.op('dve', lambda e: e.tensor_copy(out=Mp[h][1], in_=pb[:, 0:128]), reads=[kb], writes=[hk("Mp", h, 1)])
                    T.op('act', lambda e: e.activation(out=NR[h][1][:, 0:128], in_=pb[:, 128:256], func=AF.Copy), reads=[kb], writes=[hk("NR", h, 1)])
                for lev in range(1, 5):
                    a, b2 = lev % 2, (lev + 1) % 2
                    for h in range(NH):
                        pb, kb = ps[2 + h], "ps%d" % (2 + h)
                        T.op('pe', lambda e: e.matmul(pb[:, 0:256], lhsT=Mp[h][a], rhs=NR[h][a], start=True, stop=True),
                             reads=[hk("NR", h, a), hk("Mp", h, a)], writes=[kb])
                        T.op('pe', lambda e: e.matmul(pb[:, 256:384], lhsT=NR[h][a][:, 0:128], rhs=Mp[h][a], start=True, stop=True),
                             reads=[hk("NR", h, a), hk("Mp", h, a)], writes=[kb])
                    for h in range(NH):
                        pb, kb = ps[2 + h], "ps%d" % (2 + h)
                        T.op('act', lambda e: e.activation(out=NR[h][b2][:, 0:128], in_=pb[:, 0:128], func=AF.Copy), reads=[kb], writes=[hk("NR", h, b2)])
                        T.op('dve', lambda e: e.tensor_tensor(out=NR[h][b2][:, 128:256], in0=pb[:, 128:256], in1=NR[h][a][:, 128:256].bitcast(F32), op=ALU.add),
                             reads=[kb, hk("NR", h, a)], writes=[hk("NR", h, b2)])
                        T.op('dve', lambda e: e.tensor_copy(out=Mp[h][b2], in_=pb[:, 256:384]), reads=[kb], writes=[hk("Mp", h, b2)])
                for h in range(NH):
                    pb, kb = ps[2 + h], "ps%d" % (2 + h)
                    T.op('pe', lambda e: e.matmul(pb[:, 0:128], lhsT=Mp[h][1], rhs=NR[h][1][:, 128:256], start=True, stop=True),
                         reads=[hk("NR", h, 1), hk("Mp", h, 1)], writes=[kb])
                    T.op('dve', lambda e: e.tensor_tensor(out=Rfin[h][0], in0=pb[:, 0:128], in1=NR[h][1][:, 128:256].bitcast(F32), op=ALU.add),
                         reads=[kb, hk("NR", h, 1)], writes=[hk("Rfin", h)])
                for h in range(NH):
                    QT, KT, VT = qc[:, h, cols], qc[:, 4 + h, cols], qc[:, 8 + h, cols]
                    pa, ka = slotA(h)
                    T.op('pe', lambda e: e.transpose(out=pa, in_=KT, identity=ident), reads=["qc%d" % par], writes=[ka])
                    T.op('act', lambda e: e.activation(out=VK[h][0][:, 128:256], in_=pa, func=AF.Copy, scale=sm["bgn"][:, h:h + 1]),
                         reads=[ka, "bgn%d" % bp], writes=[hk("VK", h)])
                    T.op('dve', lambda e: e.tensor_scalar(out=kdec[h][bp], in0=pa, scalar1=sm["dlt"][:, h:h + 1], scalar2=None, op0=ALU.mult),
                         reads=[ka, "dlt%d" % bp], writes=[hk("kdec", h, bp)])
                    pb, kb = slotB(h)
                    T.op('pe', lambda e: e.transpose(out=pb, in_=VT, identity=ident), reads=["qc%d" % par], writes=[kb])
                    T.op('act', lambda e: e.activation(out=VK[h][0][:, 0:128], in_=pb, func=AF.Copy, scale=sm["beta"][:, h:h + 1]),
                         reads=[kb, "beta%d" % bp], writes=[hk("VK", h)])
                    pc, kc = slotC(h)
                    T.op('pe', lambda e: e.transpose(out=pc, in_=QT, identity=ident), reads=["qc%d" % par], writes=[kc])
                    T.op('dve', lambda e: e.tensor_scalar(out=Qg[h][0], in0=pc, scalar1=sm["gam"][:, h:h + 1], scalar2=None, op0=ALU.mult),
                         reads=[kc, "gam%d" % bp], writes=[hk("Qg", h)])
                for h in range(NH):
                    pb, kb = ps[2 + h], "ps%d" % (2 + h)
                    T.op('pe', lambda e: e.matmul(pb[:, 0:256], lhsT=Rfin[h][0], rhs=VK[h][0], start=True, stop=True),
                         reads=[hk("Rfin", h), hk("VK", h)], writes=[kb])
                    T.op('act', lambda e: e.activation(out=uv[h][bp], in_=pb[:, 0:128], func=AF.Copy), reads=[kb], writes=[hk("uv", h, bp)])
                    T.op('dve', lambda e: e.tensor_copy(out=wkn[h][0], in_=pb[:, 128:256]), reads=[kb], writes=[hk("wkn", h)])
                if SUB < 6:
                    return
                for h in range(NH):
                    pa, ka = slotA(h)
                    T.op('pe', lambda e: e.matmul(pa, lhsT=Qg[h][0], rhs=ident, start=True, stop=False), reads=[hk("Qg", h)], writes=[ka])
                    T.op('pe', lambda e: e.matmul(pa, lhsT=wkn[h][0], rhs=PT[h][bp], start=False, stop=True),
                         reads=[hk("wkn", h), hk("PT", h, bp)], writes=[ka])
                    T.op('act', lambda e: e.activation(out=qeT[h][bp], in_=pa, func=AF.Copy), reads=[ka], writes=[hk("qeT", h, bp)])
                    for c in range(2):
                        pp, kp = (slotB(h) if c == 0 else slotC(h))
                        rows = slice(c * 64, (c + 1) * 64)
                        T.op('pe', lambda e: e.matmul(pp, lhsT=wkn[h][0][rows, :], rhs=kdec[h][bp][rows, :], start=True, stop=True),
                             reads=[hk("wkn", h), hk("kdec", h, bp)], writes=[kp])
                        T.op('dve', lambda e: e.scalar_tensor_tensor(out=FT[h][bp * 2 + c], in0=ident,
                                                                     scalar=sm["egl"][:, h * 2 + c:h * 2 + c + 1], in1=pp,
                                                                     op0=ALU.mult, op1=ALU.add),
                             reads=[kp, "egl%d" % bp], writes=[hk("FT", h, bp * 2 + c)])
                for c in range(2 if STAGE >= 4 else 0):
                    rows = slice(c * 64, (c + 1) * 64)
                    if kind == "s":
                        seq = blk * 2 + c
                        T.dma('sp', Sst[s_cur], A["S0"][seq].rearrange("h k v -> k h v"), writes=["S%dbuf" % s_cur])
                    Sc = Sst[s_cur]
                    skey = "S%dbuf" % s_cur
                    op_ = ostage[chunk_ctr % 2]
                    okey = "ost%d" % (chunk_ctr % 2)
                    for h in range(NH):
                        oo = ps[7][0:64, h * 128:(h + 1) * 128]
                        T.op('pe', lambda e: e.matmul(oo, lhsT=qeT[h][bp][:, rows], rhs=Sc[:, h, :], start=True, stop=False),
                             reads=[hk("qeT", h, bp), skey], writes=["ps7"])
                        T.op('pe', lambda e: e.matmul(oo, lhsT=PT[h][bp][rows, rows], rhs=uv[h][bp][rows, :], start=False, stop=True),
                             reads=[hk("PT", h, bp), hk("uv", h, bp)], writes=["ps7"])
                    T.op('act', lambda e: e.activation(out=op_, in_=ps[7][0:64, :], func=AF.Copy), reads=["ps7"], writes=[okey])
                    if kind == "p":
                        row0 = g * 256 + blk * 128 + c * 64
                    else:
                        row0 = SEQ + blk * 128 + c * 64
                    T.dma('sp', A["o_all"][row0:row0 + 64, :], op_, reads=[okey])
                    s_nxt = 1 - s_cur
                    for h in range(NH):
                        so = ps[6][:, h * 128:(h + 1) * 128]
                        T.op('pe', lambda e: e.matmul(so, lhsT=FT[h][bp * 2 + c], rhs=Sc[:, h, :], start=True, stop=False),
                             reads=[hk("FT", h, bp * 2 + c), skey], writes=["ps6"])
                        T.op('pe', lambda e: e.matmul(so, lhsT=kdec[h][bp][rows, :], rhs=uv[h][bp][rows, :], start=False, stop=True),
                             reads=[hk("kdec", h, bp), hk("uv", h, bp)], writes=["ps6"])
                    T.op('dve', lambda e: e.tensor_copy(out=Sst[s_nxt].rearrange("p h v -> p (h v)"), in_=ps[6]),
                         reads=["ps6"], writes=["S%dbuf" % s_nxt])
                    if kind == "s":
                        seq = blk * 2 + c
                        T.dma('sp', A["dn_s"][seq].rearrange("h k v -> k h v"), Sst[s_nxt], reads=["S%dbuf" % s_nxt])
                    s_cur = s_nxt
                    chunk_ctr += 1
            if kind == "p" and g == min(32, NGROUP_DBG) - 1 and blk == 1:
                T.dma('sp', A["dn_p"].rearrange("h k v -> k h v"), Sst[s_cur], reads=["S%dbuf" % s_cur])

        ng = len(groups)
        s1a(groups[0][0], groups[0][1], 0)
        s1b(groups[0][0], groups[0][1], 0)
        for i in range(ng):
            kind, g = groups[i]
            if i + 1 < ng:
                s1a(groups[i + 1][0], groups[i + 1][1], (i + 1) % 2)
            s2(kind, g, i % 2, 0)
            if i + 1 < ng:
                s1b(groups[i + 1][0], groups[i + 1][1], (i + 1) % 2)
            s2(kind, g, i % 2, 1)


_CACHE = {}


def kernel(**inp):
    f = lambda a: np.ascontiguousarray(np.asarray(a, dtype=np.float32))
    x_prompt, x_sample = f(inp["x_prompt"]), f(inp["x_sample"])
    if "prog" not in _CACHE:
        _CACHE["prog"] = build()
    P = _CACHE["prog"]
    consts = _consts()
    pu = f(inp["peer_u"])[0]
    shared = {
        "gmix_col": np.ascontiguousarray(f(inp["g_mix"])[0].reshape(8, 128).T),
        "gffn_col": np.ascontiguousarray(f(inp["g_ffn"])[0].reshape(8, 128).T),
        "gfin_bc": rep128(f(inp["g_final"])),
        "gdn_bc": rep128(np.tile(f(inp["g_dn_out"])[0], 4)),
        "pscale_col": np.ascontiguousarray(f(inp["pool_scale"])[0].reshape(4, 128).T),
        "w_grp": f(inp["w_pool_grp"])[0],
        "keysT": np.ascontiguousarray(f(inp["peer_sub_keys"])[0].transpose(0, 1, 3, 2).reshape(16, 128, 128)),
        "wblk": _block_weights(f(inp["w_in"])[0], f(inp["w_up_pool"])[0], f(inp["w_up_dn"])[0], f(inp["w_out"])[0],
                               f(inp["w_peer_q"])[0].reshape(D_MODEL, 2048)),
        "UTb": np.ascontiguousarray(pu.reshape(128, 128, 8, 128).transpose(0, 3, 2, 1)),
        "peer_v": f(inp["peer_v"])[0],
    }
    in_maps = []
    for c in range(NCORE):
        b, j = c // 4, c % 4
        m = {
            "xseq": x_prompt[b],
            "xs": x_sample[4 * c:4 * c + 4].reshape(STOK, D_MODEL),
            "S0": f(inp["state_dn"])[0, 4 * c:4 * c + 4],
            "conv0T": np.ascontiguousarray(f(inp["state_dn_conv"])[0, 4 * c:4 * c + 4].transpose(0, 2, 1)),
            "w_in": f(inp["w_in"])[0],
            "gmix_bc": rep128(f(inp["g_mix"])[0]),
            "wconvT": np.ascontiguousarray(f(inp["w_conv"])[0].T),
            "alog_bc": rep128(f(inp["a_log"])[0]),
            "dtb_bc": rep128(f(inp["dt_bias"])[0]),
        }
        for k, v in consts.items():
            m["c_" + k] = v
        m.update(shared)
        m["xown"] = np.ascontiguousarray(np.concatenate([x_prompt[b, j * QTOK:(j + 1) * QTOK], m["xs"]], axis=0))
        m["xhalo"] = np.ascontiguousarray(x_prompt[b, j * QTOK - 16:j * QTOK]) if j > 0 else np.zeros((16, D_MODEL), np.float32)
        m["pool0"] = f(inp["cache_pool"])[0, 4 * c:4 * c + 4]
        selv = np.zeros((128, 4), np.float32)
        selv[:, j] = 1.0
        m["sel"] = selv
        m.update(_bands(j))
        in_maps.append(m)
    res = run_bass_kernel_spmd(P.nc, in_maps, core_ids=list(range(NCORE)))
    R = res.results
    _CACHE["last"] = R
    new_dn_p = np.stack([R[0]["dn_p"], R[4]["dn_p"]])[None]
    new_dn_s = np.concatenate([R[c]["dn_s"] for c in range(NCORE)])[None]
    new_conv_p = np.stack([R[0]["conv_pT"].T, R[4]["conv_pT"].T])[None]
    new_conv_s = np.concatenate([R[c]["conv_sT"].transpose(0, 2, 1) for c in range(NCORE)])[None]
    y_prompt = np.stack([np.concatenate([R[4 * b + j]["y"][:QTOK] for j in range(4)]) for b in range(2)])
    y_sample = np.concatenate([R[c]["y"][QTOK:].reshape(4, 64, D_MODEL) for c in range(NCORE)])
    new_pool_p = np.stack([R[3]["pool_p"], R[7]["pool_p"]])[None]
    new_pool_s = np.concatenate([R[c]["pool_s"] for c in range(NCORE)])[None]
    f32 = lambda a: np.ascontiguousarray(a, dtype=np.float32)
    return (f32(y_prompt), f32(y_sample), f32(new_pool_p), f32(new_conv_p), f32(new_dn_p),
            f32(new_pool_s), f32(new_conv_s), f32(new_dn_s))


TT = 256
NTILE = OWN // TT
POOL_W = (2, 4, 8, 16)


def _bands(j):
    t = np.arange(128)
    out = {}
    main, prev, first, samp, sprev = [], [], [], [], []
    for w in POOL_W:
        src, dst = t[:, None], t[None, :]
        inwin = (src <= dst) & (src > dst - w)
        m = inwin / float(w) - np.eye(128)
        main.append(m)
        pw = ((src - 128) > dst - w) & (src >= 112)
        prev.append(pw / float(w))
        if j == 0:
            cnt = np.minimum(dst + 1, w).astype(np.float64)
            first.append(inwin / cnt - np.eye(128))
        else:
            first.append(m)
        same = (src // 64) == (dst // 64)
        samp.append((inwin & same) / float(w) - np.eye(128))
        sp = np.zeros((128, 128))
        for (r0, c0) in ((97, 0), (113, 64)):
            for i in range(15):
                for d in range(64):
                    if (-15 + i) > d - w:
                        sp[r0 + i, c0 + d] = 1.0 / w
        sprev.append(sp)
    f = lambda l: np.ascontiguousarray(np.stack(l).transpose(1, 0, 2).astype(np.float32))
    return dict(bmain=f(main), bprev=f(prev), bfirst=f(first), bsamp=f(samp), bsprev=f(sprev))


def phase_b(P, C, ps, B):
    nc, T = P.nc, P.T
    ident, iota = C["ident"], C["iota"]
    with ExitStack() as es:
        big = sb(nc, es, "big", [128, 16384])
        def reg(off, shape, dt=F32):
            n = int(np.prod(shape[1:]))
            v = big[:, off:off + n]
            if dt != F32:
                v = v.bitcast(dt)
            if len(shape) == 3:
                v = v.rearrange("p (a b) -> p a b", a=shape[1])
            return v
        xnT = reg(0, [128, 8, TT], F32R)
        wb = [reg(2048 + 2048 * i, [128, 8, 256], F32R) for i in range(3)]
        yaT = reg(8192, [128, 4, TT], F32R)
        ztok, ztok_w = reg(9216, [128, 2, 512]), reg(9216, [128, 2, 512], F32R)
        ybT = reg(10240, [128, 4, TT], F32R)
        mtok, mtok_w = reg(11264, [128, 2, 1024]), reg(11264, [128, 2, 1024], F32R)
        mT = reg(13312, [128, 8, TT], F32R)
        xnTh = reg(15360, [128, 8, 128], F32R)
        utok, utok_w = reg(11264, [128, 512]), reg(11264, [128, 512], F32R)
        otok, otok_w = reg(11776, [128, 512]), reg(11776, [128, 512], F32R)
        ocand, ocand_w = reg(12288, [128, 4, 512]), reg(12288, [128, 4, 512], F32R)
        cand, cand_w = reg(11264, [128, 2048]), reg(11264, [128, 2048], F32R)
        eqb, eqb_w = reg(13312, [128, 2048]), reg(13312, [128, 2048], F32R)
        Wsb = big.rearrange("p (e t) -> p e t", e=64)
        Wsb_w = big.bitcast(F32R).rearrange("p (e t) -> p e t", e=64)
        xres = sb(nc, es, "xres", [128, 2, 1024])
        xn2T = sb(nc, es, "xn2T", [128, 8, TT], F32R)
        ublk = [sb(nc, es, "ublk%d" % i, [128, 8, 128], BF16) for i in range(4)]
        vblk = [sb(nc, es, "vblk%d" % i, [128, 1024], BF16) for i in range(4)]
        xn2Tb = sb(nc, es, "xn2Tb", [128, 8, TT], BF16)
        actb = [sb(nc, es, "actb%d" % i, [128, TT]) for i in range(2)]
        gab = [sb(nc, es, "gab%d" % i, [128, TT], BF16) for i in range(2)]
        e1T = sb(nc, es, "e1T", [128, TT])
        e2T = sb(nc, es, "e2T", [128, TT])
        gT = sb(nc, es, "gT", [128, TT])
        uprev = sb(nc, es, "uprev", [128, 512])
        xh = sb(nc, es, "xh", [128, 1024])
        xnb = sb(nc, es, "xnb", [128, 1024])
        ss = sb(nc, es, "ssb", [128, 1])
        rs = sb(nc, es, "rsb", [128, 1])
        rstd = sb(nc, es, "rstdb", [128, 1])
        pooledT = sb(nc, es, "pooledT", [128, 128])
        qTsb = sb(nc, es, "qTsb", [128, 2, TT])
        sga = sb(nc, es, "sga", [128, 256])
        sgb = sb(nc, es, "sgb", [128, 256])
        sq4 = sb(nc, es, "sq4", [128, 4])
        B_ = {"v1": [sb(nc, es, "v1_%d" % i, [128, 8, 16]) for i in range(2)],
              "v2": [sb(nc, es, "v2_%d" % i, [128, 8, 16]) for i in range(2)],
              "i1u": [sb(nc, es, "i1u_%d" % i, [128, 8, 16], U32) for i in range(2)],
              "i2u": [sb(nc, es, "i2u_%d" % i, [128, 8, 16], U32) for i in range(2)]}
        i1f = sb(nc, es, "i1f", [128, 8, 16])
        i2f = sb(nc, es, "i2f", [128, 8, 16])
        tmpk = sb(nc, es, "tmpk", [128, 128])
        tmp2 = sb(nc, es, "tmp2", [128, 256])
        cv = sb(nc, es, "cv", [128, 8, 16])
        ciu = sb(nc, es, "ciu", [128, 8, 16], U32)
        abu = sb(nc, es, "abu", [128, 8, 16], U32)
        af = sb(nc, es, "af", [128, 8, 16])
        bf = sb(nc, es, "bf", [128, 8, 16])
        ex = sb(nc, es, "ex", [128, 8, 16])
        s8 = sb(nc, es, "s8", [128, 8])
        e1f = sb(nc, es, "e1f", [128, 128])
        e2f = sb(nc, es, "e2f", [128, 128])
        gsl = sb(nc, es, "gsl", [128, 128])
        NOH = 3
        oha = [sb(nc, es, "oha%d" % i, [128, 128], F32R) for i in range(NOH)]
        ohb = [sb(nc, es, "ohb%d" % i, [128, 128], F32R) for i in range(NOH)]
        Whi = sb(nc, es, "Whi", [128, 64, TT], BF16)
        gmixc = sb(nc, es, "gmixc", [128, 8]); T.dma('sp', gmixc, B["gmix_col"])
        gffnc = sb(nc, es, "gffnc", [128, 8]); T.dma('sp', gffnc, B["gffn_col"])
        gfin = sb(nc, es, "gfin", [128, 1024]); T.dma('sp', gfin, B["gfin_bc"])
        gdn = sb(nc, es, "gdn", [128, 512]); T.dma('sp', gdn, B["gdn_bc"])
        pscale = sb(nc, es, "pscale", [128, 4]); T.dma('sp', pscale, B["pscale_col"])
        selb = sb(nc, es, "selb", [128, 4]); T.dma('sp', selb, B["sel"])
        wgrp = sb(nc, es, "wgrp", [128, 4, 128]); T.dma('sp', wgrp, B["w_grp"].rearrange("g c d -> c g d"))
        keysT = sb(nc, es, "keysT", [128, 16, 128]); T.dma('sp', keysT, B["keysT"].rearrange("q d k -> d q k"))
        bands = {}
        for k in ("bmain", "bprev", "bfirst", "bsamp", "bsprev"):
            bands[k] = sb(nc, es, k, [128, 4, 128]); T.dma('sp', bands[k], B[k])
        T.barrier()

        wblk = B["wblk"]
        WB_U, WB_Z, WB_GA, WB_GB, WB_UP, WB_O, WB_Q = 0, 2, 4, 8, 12, 16, 20
        wrr = [0]

        def load_w(ci):
            i = wrr[0] % 3
            wrr[0] += 1
            T.dma('pool', wb[i].rearrange("p k c -> p (k c)"), wblk[ci], writes=["wb%d" % i])
            return wb[i], "wb%d" % i

        def norm_block(xsrc, xkey, gcol, dstT, dkey, tcols):
            T.op('act', lambda e: e.activation(out=xnb, in_=xsrc, func=AF.Square, accum_out=ss), reads=[xkey], writes=["xnb", "ssb"])
            T.op('act', lambda e: e.activation(out=rs, in_=ss, func=AF.Sqrt, bias=EPS, scale=1.0 / D_MODEL), reads=["ssb"], writes=["rsb"])
            T.op('dve', lambda e: e.reciprocal(out=rstd, in_=rs), reads=["rsb"], writes=["rstdb"])
            T.op('dve', lambda e: e.tensor_scalar(out=xnb, in0=xsrc, scalar1=rstd, scalar2=None, op0=ALU.mult),
                 reads=[xkey, "rstdb"], writes=["xnb"])
            for half in range(2):
                for q in range(4):
                    dc = half * 4 + q
                    T.op('pe', lambda e: e.transpose(out=ps[half][:, q * 128:(q + 1) * 128], in_=xnb[:, dc * 128:(dc + 1) * 128], identity=ident),
                         reads=["xnb"], writes=["ps%d" % half])
                for q in range(4):
                    dc = half * 4 + q
                    T.op('act' if q % 2 else 'dve',
                         (lambda e: e.activation(out=dstT[:, dc, tcols], in_=ps[half][:, q * 128:(q + 1) * 128], func=AF.Copy, scale=gcol[:, dc:dc + 1]))
                         if q % 2 else
                         (lambda e: e.tensor_scalar(out=dstT[:, dc, tcols], in0=ps[half][:, q * 128:(q + 1) * 128], scalar1=gcol[:, dc:dc + 1], scalar2=None, op0=ALU.mult)),
                         reads=["ps%d" % half], writes=[dkey])

        for ti in (TILES if STAGE >= 5 else []):
            samp = (ti == NTILE - 1)
            blks = [0, 1]
            if ti == 0:
                T.op('pool', lambda e: e.memset(xh, 0.0), writes=["xh"])
                T.dma('sp', xh[112:128, :], B["xhalo"], writes=["xh"])
                norm_block(xh, "xh", gmixc, xnTh, "xnTh", slice(0, 128))
            for blk in blks:
                row0 = ti * TT + blk * 128
                T.dma('sp', xres[:, blk, :], B["xown"][row0:row0 + 128, :], writes=["xres%d" % blk])
                norm_block(xres[:, blk, :], "xres%d" % blk, gmixc, xnT, "xnT", slice(blk * 128, (blk + 1) * 128))
            wu = [load_w(WB_U + c) for c in range(2)]
            if ti == 0:
                for c in range(2):
                    for dc in range(8):
                        T.op('pe', lambda e: e.matmul(ps[4][:, c * 256:(c + 1) * 256], lhsT=xnTh[:, dc, :], rhs=wu[c][0][:, dc, :],
                                                      start=(dc == 0), stop=(dc == 7)), reads=["xnTh", wu[c][1]], writes=["ps4"])
                T.op('act', lambda e: e.activation(out=uprev, in_=ps[4], func=AF.Copy), reads=["ps4"], writes=["uprev"])
            if samp:
                T.op('pool', lambda e: e.memset(uprev, 0.0), writes=["uprev"])
            for c in range(2):
                for blk in blks:
                    for dc in range(8):
                        T.op('pe', lambda e: e.matmul(ps[2 + blk][:, c * 256:(c + 1) * 256], lhsT=xnT[:, dc, blk * 128:(blk + 1) * 128],
                                                      rhs=wu[c][0][:, dc, :], start=(dc == 0), stop=(dc == 7)),
                             reads=["xnT", wu[c][1]], writes=["ps%d" % (2 + blk)])
            for blk in blks:
                bi = ti * 2 + blk
                T.op('act', lambda e: e.activation(out=utok_w, in_=ps[2 + blk], func=AF.Copy), reads=["ps%d" % (2 + blk)], writes=["utok"])
                if samp:
                    for sq_ in range(2):
                        seq = blk * 2 + sq_
                        r0 = 97 if sq_ == 0 else 113
                        T.dma('sp', uprev[r0:r0 + 15, :], B["pool0"][seq], writes=["uprev"])
                        T.dma('sp', B["pool_s"][seq], utok[sq_ * 64 + 49: sq_ * 64 + 64, :], reads=["utok"])
                    bm, bp = bands["bsamp"], bands["bsprev"]
                else:
                    bm, bp = (bands["bfirst"] if bi == 0 else bands["bmain"]), bands["bprev"]
                    if bi == 15:
                        T.dma('sp', B["pool_p"], utok[113:128, :], reads=["utok"])
                for gi in range(4):
                    gc_ = slice(gi * 128, (gi + 1) * 128)
                    T.op('pe', lambda e: e.matmul(ps[4][:, 0:128], lhsT=uprev[:, gc_], rhs=bp[:, gi, :], start=True, stop=False),
                         reads=["uprev"], writes=["ps4"])
                    T.op('pe', lambda e: e.matmul(ps[4][:, 0:128], lhsT=utok[:, gc_], rhs=bm[:, gi, :], start=False, stop=True),
                         reads=["utok"], writes=["ps4"])
                    T.op('dve', lambda e: e.tensor_copy(out=pooledT, in_=ps[4][:, 0:128]), reads=["ps4"], writes=["pooledT"])
                    T.op('pe', lambda e: e.matmul(ps[5][:, 0:128], lhsT=wgrp[:, gi, :], rhs=pooledT, start=True, stop=True),
                         reads=["pooledT"], writes=["ps5"])
                    T.op('act', lambda e: e.activation(out=yaT[:, gi, blk * 128:(blk + 1) * 128], in_=ps[5][:, 0:128], func=AF.Copy,
                                                       scale=pscale[:, gi:gi + 1]), reads=["ps5"], writes=["yaT"])
                if not samp:
                    T.op('pool', lambda e: e.tensor_copy(out=uprev, in_=utok), reads=["utok"], writes=["uprev"])
            wz = [load_w(WB_Z + c) for c in range(2)]
            for c in range(2):
                for blk in blks:
                    for dc in range(8):
                        T.op('pe', lambda e: e.matmul(ps[2 + blk][:, c * 256:(c + 1) * 256], lhsT=xnT[:, dc, blk * 128:(blk + 1) * 128],
                                                      rhs=wz[c][0][:, dc, :], start=(dc == 0), stop=(dc == 7)),
                             reads=["xnT", wz[c][1]], writes=["ps%d" % (2 + blk)])
            for blk in blks:
                bi = ti * 2 + blk
                T.op('act', lambda e: e.activation(out=ztok_w[:, blk, :], in_=ps[2 + blk], func=AF.Silu), reads=["ps%d" % (2 + blk)], writes=["ztok"])
                if samp:
                    T.dma('pool', otok_w, B["o_all"][SEQ + blk * 128: SEQ + (blk + 1) * 128, :], writes=["otok"])
                elif NGROUP_DBG < 32:
                    T.dma('pool', otok_w, B["o_all"][bi * 128:(bi + 1) * 128, :], writes=["otok"])
                else:
                    T.dma('pool', ocand_w, B["o_all"][0:SEQ, :].rearrange("(q r) c -> r q c", q=4)[bi * 128:(bi + 1) * 128], writes=["ocand"])
                    T.op('dve', lambda e: e.tensor_scalar(out=otok_w, in0=ocand[:, 0, :], scalar1=selb[:, 0:1], scalar2=None, op0=ALU.mult),
                         reads=["ocand"], writes=["otok"])
                    for q in range(1, 4):
                        T.op('dve', lambda e: e.scalar_tensor_tensor(out=otok_w, in0=ocand[:, q, :], scalar=selb[:, q:q + 1], in1=otok,
                                                                     op0=ALU.mult, op1=ALU.add), reads=["ocand", "otok"], writes=["otok"])
                T.op('act', lambda e: e.activation(out=xnb[:, 0:512], in_=otok, func=AF.Square), reads=["otok"], writes=["xnb"])
                T.op('dve', lambda e: e.tensor_reduce(out=sq4, in_=xnb[:, 0:512].rearrange("p (h v) -> p h v", h=4), axis=AX.X, op=ALU.add),
                     reads=["xnb"], writes=["sq4"])
                T.op('act', lambda e: e.activation(out=sq4, in_=sq4, func=AF.Sqrt, bias=EPS, scale=1.0 / 128), reads=["sq4"], writes=["sq4"])
                T.op('dve', lambda e: e.reciprocal(out=sq4, in_=sq4), reads=["sq4"], writes=["sq4"])
                o3 = otok.rearrange("p (h v) -> p h v", h=4)
                o3w = otok_w.rearrange("p (h v) -> p h v", h=4)
                T.op('dve', lambda e: e.tensor_tensor(out=o3w, in0=o3, in1=sq4.unsqueeze(2).to_broadcast([128, 4, 128]), op=ALU.mult),
                     reads=["otok", "sq4"], writes=["otok"])
                T.op('dve', lambda e: e.tensor_tensor(out=otok_w, in0=otok, in1=gdn, op=ALU.mult), reads=["otok"], writes=["otok"])
                T.op('dve', lambda e: e.tensor_tensor(out=otok_w, in0=otok, in1=ztok[:, blk, :], op=ALU.mult), reads=["otok", "ztok"], writes=["otok"])
                for cc in range(4):
                    T.op('pe', lambda e: e.transpose(out=ps[4][:, cc * 128:(cc + 1) * 128], in_=otok[:, cc * 128:(cc + 1) * 128], identity=ident),
                         reads=["otok"], writes=["ps4"])
                T.op('act', lambda e: e.activation(out=ybT[:, :, blk * 128:(blk + 1) * 128], in_=ps[4].rearrange("p (c t) -> p c t", c=4), func=AF.Copy),
                     reads=["ps4"], writes=["ybT"])
            for n in range(4):
                wga = load_w(WB_GA + n)
                wgb = load_w(WB_GB + n)
                wup, wupk = load_w(WB_UP + n)
                for blk in blks:
                    tcs = slice(blk * 128, (blk + 1) * 128)
                    pg, pu = ps[blk * 2], ps[blk * 2 + 1]
                    kg, ku = "ps%d" % (blk * 2), "ps%d" % (blk * 2 + 1)
                    for dc in range(8):
                        T.op('pe', lambda e: e.matmul(pg[:, 0:256], lhsT=xnT[:, dc, tcs], rhs=wga[0][:, dc, :], start=(dc == 0), stop=(dc == 7)),
                             reads=["xnT", wga[1]], writes=[kg])
                    for dc in range(8):
                        T.op('pe', lambda e: e.matmul(pg[:, 256:512], lhsT=xnT[:, dc, tcs], rhs=wgb[0][:, dc, :], start=(dc == 0), stop=(dc == 7)),
                             reads=["xnT", wgb[1]], writes=[kg])
                    for cc in range(4):
                        T.op('pe', lambda e: e.matmul(pu[:, 0:256], lhsT=yaT[:, cc, tcs], rhs=wup[:, cc, :], start=(cc == 0), stop=(cc == 3)),
                             reads=["yaT", wupk], writes=[ku])
                    for cc in range(4):
                        T.op('pe', lambda e: e.matmul(pu[:, 256:512], lhsT=ybT[:, cc, tcs], rhs=wup[:, 4 + cc, :], start=(cc == 0), stop=(cc == 3)),
                             reads=["ybT", wupk], writes=[ku])
                    T.op('act', lambda e: e.activation(out=sga, in_=pg[:, 0:256], func=AF.Sigmoid), reads=[kg], writes=["sga"])
                    T.op('act', lambda e: e.activation(out=sgb, in_=pg[:, 256:512], func=AF.Sigmoid), reads=[kg], writes=["sgb"])
                    T.op('dve', lambda e: e.tensor_tensor(out=sga, in0=pu[:, 0:256], in1=sga, op=ALU.mult), reads=[ku, "sga"], writes=["sga"])
                    T.op('dve', lambda e: e.tensor_tensor(out=sgb, in0=pu[:, 256:512], in1=sgb, op=ALU.mult), reads=[ku, "sgb"], writes=["sgb"])
                    T.op('pool', lambda e: e.tensor_tensor(out=mtok_w[:, blk, n * 256:(n + 1) * 256], in0=sga, in1=sgb, op=ALU.add),
                         reads=["sga", "sgb"], writes=["mtok"])
            for blk in blks:
                for half in range(2):
                    for q in range(4):
                        dc = half * 4 + q
                        T.op('pe', lambda e: e.transpose(out=ps[4 + half][:, q * 128:(q + 1) * 128], in_=mtok[:, blk, dc * 128:(dc + 1) * 128], identity=ident),
                             reads=["mtok"], writes=["ps%d" % (4 + half)])
                    T.op('act' if half else 'dve',
                         (lambda e: e.activation(out=mT[:, half * 4:(half + 1) * 4, blk * 128:(blk + 1) * 128],
                                                 in_=ps[4 + half].rearrange("p (q t) -> p q t", q=4), func=AF.Copy)) if half else
                         (lambda e: e.tensor_copy(out=mT[:, half * 4:(half + 1) * 4, blk * 128:(blk + 1) * 128],
                                                  in_=ps[4 + half].rearrange("p (q t) -> p q t", q=4))),
                         reads=["ps%d" % (4 + half)], writes=["mT"])
            for n in range(4):
                wo = load_w(WB_O + n)
                for blk in blks:
                    pk = 6 + blk
                    for dc in range(8):
                        T.op('pe', lambda e: e.matmul(ps[pk][:, 0:256], lhsT=mT[:, dc, blk * 128:(blk + 1) * 128], rhs=wo[0][:, dc, :],
                                                      start=(dc == 0), stop=(dc == 7)), reads=["mT", wo[1]], writes=["ps%d" % pk])
                    T.op('dve', lambda e: e.tensor_tensor(out=xres[:, blk, n * 256:(n + 1) * 256], in0=ps[pk][:, 0:256],
                                                          in1=xres[:, blk, n * 256:(n + 1) * 256], op=ALU.add),
                         reads=["ps%d" % pk, "xres%d" % blk], writes=["xres%d" % blk])
            for blk in blks:
                norm_block(xres[:, blk, :], "xres%d" % blk, gffnc, xn2T, "xn2T", slice(blk * 128, (blk + 1) * 128))
            T.op('pool', lambda e: e.tensor_copy(out=xn2Tb, in_=xn2T.bitcast(F32)), reads=["xn2T"], writes=["xn2Tb"])
            if STAGE < 6:
                for blk in blks:
                    row0 = ti * TT + blk * 128
                    T.dma('sp', B["y"][row0:row0 + 128, :], xres[:, blk, :], reads=["xres%d" % blk])
                T.barrier()
                continue
            for qc_ in range(8):
                wq = load_w(WB_Q + qc_)
                for gq in range(2):
                    for dc in range(8):
                        T.op('pe', lambda e: e.matmul(ps[2][:, gq * 256:(gq + 1) * 256], lhsT=wq[0][:, dc, gq * 128:(gq + 1) * 128],
                                                      rhs=xn2T[:, dc, :], start=(dc == 0), stop=(dc == 7)),
                             reads=["xn2T", wq[1]], writes=["ps2"])
                T.op('act', lambda e: e.activation(out=qTsb, in_=ps[2].rearrange("p (g t) -> p g t", g=2), func=AF.Copy), reads=["ps2"], writes=["qTsb"])
                for gq in range(2):
                    cq = qc_ * 2 + gq
                    hh, half = cq // 2, cq % 2
                    for blk in blks:
                        bk = "B%d_" % blk
                        pk = 3 + blk
                        T.op('pe', lambda e: e.matmul(ps[pk][:, 0:128], lhsT=qTsb[:, gq, blk * 128:(blk + 1) * 128],
                                                      rhs=keysT[:, half * 8 + hh, :], start=True, stop=True),
                             reads=["qTsb"], writes=["ps%d" % pk])
                        vvb, iub = B_["v%d" % (half + 1)][blk], B_["i%du" % (half + 1)][blk]
                        T.op('dve', lambda e: e.max(out=vvb[:, hh, 0:8], in_=ps[pk][:, 0:128]), reads=["ps%d" % pk], writes=[bk + "v"])
                        T.op('dve', lambda e: e.match_replace(out=tmpk, in_to_replace=vvb[:, hh, 0:8], in_values=ps[pk][:, 0:128], imm_value=-1e30),
                             reads=["ps%d" % pk, bk + "v"], writes=["tmpk"])
                        T.op('dve', lambda e: e.max(out=vvb[:, hh, 8:16], in_=tmpk), reads=["tmpk"], writes=[bk + "v"])
                        T.op('dve', lambda e: e.max_index(out=iub[:, hh, 0:8], in_max=vvb[:, hh, 0:8], in_values=ps[pk][:, 0:128]),
                             reads=["ps%d" % pk, bk + "v"], writes=[bk + "i"])
                        T.op('dve', lambda e: e.max_index(out=iub[:, hh, 8:16], in_max=vvb[:, hh, 8:16], in_values=ps[pk][:, 0:128]),
                             reads=["ps%d" % pk, bk + "v"], writes=[bk + "i"])
            for blk in blks:
                bk = "B%d_" % blk
                v1b, v2b, i1b, i2b = B_["v1"][blk], B_["v2"][blk], B_["i1u"][blk], B_["i2u"][blk]
                tcs = slice(blk * 128, (blk + 1) * 128)
                T.op('pool', lambda e: e.tensor_copy(out=i1f, in_=i1b), reads=[bk + "i"], writes=["i1f"])
                T.op('pool', lambda e: e.tensor_copy(out=i2f, in_=i2b), reads=[bk + "i"], writes=["i2f"])
                c4 = cand_w.rearrange("p (h a b) -> p h a b", h=8, a=16)
                T.op('pool', lambda e: e.tensor_tensor(out=c4, in0=v1b.unsqueeze(3).to_broadcast([128, 8, 16, 16]),
                                                      in1=v2b.unsqueeze(2).to_broadcast([128, 8, 16, 16]), op=ALU.add),
                     reads=[bk + "v"], writes=["cand"])
                c3 = cand.rearrange("p (h x) -> p h x", h=8)
                for hh in range(8):
                    T.op('dve', lambda e: e.max(out=cv[:, hh, 0:8], in_=c3[:, hh, :]), reads=["cand"], writes=["cv"])
                    T.op('dve', lambda e: e.match_replace(out=tmp2, in_to_replace=cv[:, hh, 0:8], in_values=c3[:, hh, :], imm_value=-1e30),
                         reads=["cand", "cv"], writes=["tmp2"])
                    T.op('dve', lambda e: e.max(out=cv[:, hh, 8:16], in_=tmp2), reads=["tmp2"], writes=["cv"])
                    T.op('dve', lambda e: e.max_index(out=ciu[:, hh, 0:8], in_max=cv[:, hh, 0:8], in_values=c3[:, hh, :]), reads=["cand", "cv"], writes=["ciu"])
                    T.op('dve', lambda e: e.max_index(out=ciu[:, hh, 8:16], in_max=cv[:, hh, 8:16], in_values=c3[:, hh, :]), reads=["cand", "cv"], writes=["ciu"])
                T.op('dve', lambda e: e.tensor_tensor(out=ex, in0=cv, in1=cv[:, :, 0:1].to_broadcast([128, 8, 16]), op=ALU.subtract),
                     reads=["cv"], writes=["ex"])
                T.op('act', lambda e: e.activation(out=ex, in_=ex, func=AF.Exp), reads=["ex"], writes=["ex"])
                T.op('dve', lambda e: e.tensor_reduce(out=s8, in_=ex, axis=AX.X, op=ALU.add), reads=["ex"], writes=["s8"])
                T.op('dve', lambda e: e.reciprocal(out=s8, in_=s8), reads=["s8"], writes=["s8"])
                T.op('dve', lambda e: e.tensor_tensor(out=gsl.rearrange("p (h k) -> p h k", h=8), in0=ex,
                                                      in1=s8.unsqueeze(2).to_broadcast([128, 8, 16]), op=ALU.mult),
                     reads=["ex", "s8"], writes=["gsl"])
                T.op('dve', lambda e: e.tensor_scalar(out=abu, in0=ciu, scalar1=4, scalar2=None, op0=ALU.logical_shift_right), reads=["ciu"], writes=["abu"])
                T.op('dve', lambda e: e.tensor_copy(out=af, in_=abu), reads=["abu"], writes=["af"])
                T.op('dve', lambda e: e.tensor_scalar(out=abu, in0=ciu, scalar1=15, scalar2=None, op0=ALU.bitwise_and), reads=["ciu", "af"], writes=["abu"])
                T.op('dve', lambda e: e.tensor_copy(out=bf, in_=abu), reads=["abu"], writes=["bf"])
                for (sel_f, idx_f, dst) in ((af, i1f, e1f), (bf, i2f, e2f)):
                    e3 = eqb.rearrange("p (s a) -> p s a", a=16)
                    e3w = eqb_w.rearrange("p (s a) -> p s a", a=16)
                    T.op('dve', lambda e: e.tensor_tensor(out=e3w, in0=sel_f.rearrange("p h k -> p (h k)").unsqueeze(2).to_broadcast([128, 128, 16]),
                                                          in1=iota[:, 0:16].unsqueeze(1).to_broadcast([128, 128, 16]), op=ALU.is_equal),
                         reads=["af", "bf"], writes=["eqb"])
                    e4 = eqb.rearrange("p (h k a) -> p h k a", h=8, k=16)
                    e4w = eqb_w.rearrange("p (h k a) -> p h k a", h=8, k=16)
                    T.op('pool', lambda e: e.tensor_tensor(out=e4w, in0=e4, in1=idx_f.unsqueeze(2).to_broadcast([128, 8, 16, 16]), op=ALU.mult),
                         reads=["eqb", "i1f", "i2f"], writes=["eqb"])
                    T.op('dve', lambda e: e.tensor_reduce(out=dst, in_=e3, axis=AX.X, op=ALU.add), reads=["eqb"], writes=["e12f"])
                for (src, dstT, nm) in ((e1f, e1T, "e1T"), (e2f, e2T, "e2T"), (gsl, gT, "gT")):
                    T.op('pe', lambda e: e.transpose(out=ps[5][:, 0:128], in_=src, identity=ident), reads=["e12f", "gsl"], writes=["ps5"])
                    T.op('act', lambda e: e.activation(out=dstT[:, tcs], in_=ps[5][:, 0:128], func=AF.Copy), reads=["ps5"], writes=[nm])
            T.barrier()
            for t in range(TT):
                a_ = oha[t % NOH]
                b_ = ohb[t % NOH]
                T.op('dve', lambda e: e.tensor_scalar(out=a_, in0=iota, scalar1=e1T[:, t:t + 1], scalar2=gT[:, t:t + 1],
                                                      op0=ALU.is_equal, op1=ALU.mult), reads=["e1T", "gT"], writes=["oha%d" % (t % NOH)])
                T.op('dve', lambda e: e.tensor_scalar(out=b_, in0=iota, scalar1=e2T[:, t:t + 1], scalar2=None, op0=ALU.is_equal),
                     reads=["e2T"], writes=["ohb%d" % (t % NOH)])
                pk = 6 + (t // 4) % 2
                T.op('pe', lambda e: e.matmul(ps[pk][:, (t % 4) * 128:(t % 4 + 1) * 128], lhsT=b_, rhs=a_, start=True, stop=True),
                     reads=["oha%d" % (t % NOH), "ohb%d" % (t % NOH)], writes=["ps%d" % pk])
                if t % 4 == 3:
                    t0 = t - 3
                    pv = ps[pk].rearrange("p (t e) -> p t e", t=4)
                    T.op('act', lambda e: e.activation(out=Wsb_w[:, :, t0:t0 + 4].rearrange("p e t -> p t e"), in_=pv[:, :, 0:64], func=AF.Copy),
                         reads=["ps%d" % pk], writes=["Wsb"])
                    T.op('act', lambda e: e.activation(out=Whi[:, :, t0:t0 + 4].rearrange("p e t -> p t e"), in_=pv[:, :, 64:128], func=AF.Copy),
                         reads=["ps%d" % pk], writes=["Whi"])

            def d_load(e1):
                T.dma('sp', ublk[e1 % 4], B["Ubf"][e1].rearrange("p (k e) -> p k e", k=8), writes=["ublk%d" % (e1 % 4)])
                T.dma('sp', vblk[e1 % 4], B["Vbf"][e1 * 128:(e1 + 1) * 128, :], writes=["vblk%d" % (e1 % 4)])

            def d_scores(e1):
                pk = 4 + e1 % 2
                for dc in range(8):
                    T.op('pe', lambda e: e.matmul(ps[pk][:, 0:TT], lhsT=ublk[e1 % 4][:, dc, :], rhs=xn2Tb[:, dc, :], start=(dc == 0), stop=(dc == 7)),
                         reads=["ublk%d" % (e1 % 4), "xn2Tb"], writes=["ps%d" % pk])

            def d_gate(e1):
                pk = 4 + e1 % 2
                ab, gb_ = actb[e1 % 2], gab[e1 % 2]
                wrow, wkey = (Wsb[:, e1, :], "Wsb") if e1 < 64 else (Whi[:, e1 - 64, :], "Whi")
                T.op('act', lambda e: e.activation(out=ab, in_=ps[pk][:, 0:TT], func=AF.Gelu), reads=["ps%d" % pk], writes=["actb%d" % (e1 % 2)])
                T.op('dve', lambda e: e.tensor_tensor(out=gb_, in0=ab, in1=wrow, op=ALU.mult),
                     reads=["actb%d" % (e1 % 2), wkey], writes=["gab%d" % (e1 % 2)])

            def d_acc(e1):
                gb_, vb = gab[e1 % 2], vblk[e1 % 4]
                for blk in blks:
                    for half in range(2):
                        T.op('pe', lambda e: e.matmul(ps[blk * 2 + half], lhsT=gb_[:, blk * 128:(blk + 1) * 128], rhs=vb[:, half * 512:(half + 1) * 512],
                                                      start=(e1 == 0), stop=(e1 == 127)),
                             reads=["gab%d" % (e1 % 2), "vblk%d" % (e1 % 4)], writes=["psy%d" % (blk * 2 + half)])

            d_load(0)
            d_load(1)
            d_scores(0)
            d_gate(0)
            for e1 in range(128):
                if e1 + 2 < 128:
                    d_load(e1 + 2)
                if e1 + 1 < 128:
                    d_scores(e1 + 1)
                d_acc(e1)
                if e1 + 1 < 128:
                    d_gate(e1 + 1)
            T.barrier()
            for blk in blks:
                row0 = ti * TT + blk * 128
                for half in range(2):
                    T.op('dve', lambda e: e.tensor_tensor(out=xres[:, blk, half * 512:(half + 1) * 512], in0=ps[blk * 2 + half],
                                                          in1=xres[:, blk, half * 512:(half + 1) * 512], op=ALU.add),
                         reads=["psy%d" % (blk * 2 + half), "xres%d" % blk], writes=["xres%d" % blk])
                T.op('act', lambda e: e.activation(out=xnb, in_=xres[:, blk, :], func=AF.Square, accum_out=ss), reads=["xres%d" % blk], writes=["xnb", "ssb"])
                T.op('act', lambda e: e.activation(out=rs, in_=ss, func=AF.Sqrt, bias=EPS, scale=1.0 / D_MODEL), reads=["ssb"], writes=["rsb"])
                T.op('dve', lambda e: e.reciprocal(out=rstd, in_=rs), reads=["rsb"], writes=["rstdb"])
                T.op('dve', lambda e: e.scalar_tensor_tensor(out=xnb, in0=xres[:, blk, :], scalar=rstd, in1=gfin, op0=ALU.mult, op1=ALU.mult),
                     reads=["xres%d" % blk, "rstdb"], writes=["xnb"])
                T.dma('sp', B["y"][row0:row0 + 128, :], xnb, reads=["xnb"])
            T.barrier()
```

```python
import os
import numpy as np
from contextlib import ExitStack
import concourse.bass as bass
import concourse.mybir as mybir
from concourse.bass_utils import run_bass_kernel_spmd

F32 = mybir.dt.float32
F32R = mybir.dt.float32r
U32 = mybir.dt.uint32
I32 = mybir.dt.int32
BF16 = mybir.dt.bfloat16
AF = mybir.ActivationFunctionType
ALU = mybir.AluOpType
AX = mybir.AxisListType

D_MODEL = 1024
SEQ = 8192
NCORE = 8
QTOK = 2048
STOK = 256
OWN = QTOK + STOK
D_IN = 4616
OFF_QKV, OFF_Z, OFF_B, OFF_A, OFF_G = 512, 2048, 2560, 2564, 2568
EPS = 1e-6
NEG = -30000.0

STAGE = int(os.environ.get("MK_STAGE", "99"))
NGROUP_DBG = int(os.environ.get("MK_NGROUP", "32"))
SUB = int(os.environ.get("MK_SUB", "99"))
DEBUG_A = int(os.environ.get("MK_DEBUG_A", "0"))
TILES = [int(t) for t in os.environ.get("MK_TILES", "0,1,2,3,4,5,6,7,8").split(",")]


class Trk:
    ROT = int(os.environ.get("MK_ROT", "8000"))
    NDMA = 24

    def __init__(self, nc):
        self.nc = nc
        self.eng = dict(pe=nc.tensor, act=nc.scalar, dve=nc.vector, pool=nc.gpsimd, sp=nc.sync)
        self.sem = {k: nc.alloc_semaphore("prog_%s_0" % k) for k in self.eng}
        self.semgen = {k: 0 for k in self.eng}
        self.cnt = {k: 0 for k in self.eng}
        self.waited = {k: {} for k in self.eng}
        self.lastw = {}
        self.readers = {}
        self.dma_sems = [nc.alloc_semaphore("dmas_%d" % i) for i in range(self.NDMA)]
        self.dma_cnt = [0] * self.NDMA
        self.dma_rr = {'hw': 0, 'sw': 0}
        self.NSW = 8
        self.n_inst = 0

    def _wait(self, e, tok):
        owner, sem, val, semkey = tok
        if owner == e and e == 'pe':
            return
        w = self.waited[e]
        if w.get(semkey, 0) >= val:
            return
        self.eng[e].wait_ge(sem, val)
        w[semkey] = val

    def _deps(self, e, reads, writes):
        for b in reads:
            lw = self.lastw.get(b)
            if lw is not None:
                self._wait(e, lw)
            if b.startswith("ps"):
                for tok in self.readers.get(b, {}).values():
                    if tok[0] != e:
                        self._wait(e, tok)
        for b in writes:
            lw = self.lastw.get(b)
            if lw is not None:
                self._wait(e, lw)
            for tok in self.readers.get(b, {}).values():
                self._wait(e, tok)

    def _record(self, tok, reads, writes):
        for b in reads:
            self.readers.setdefault(b, {})[tok[0]] = tok
        for b in writes:
            self.lastw[b] = tok
            self.readers[b] = {}

    def op(self, e, fn, reads=(), writes=()):
        self._deps(e, reads, writes)
        inst = fn(self.eng[e])
        if self.cnt[e] >= self.ROT:
            self.semgen[e] += 1
            self.sem[e] = self.nc.alloc_semaphore("prog_%s_%d" % (e, self.semgen[e]))
            self.cnt[e] = 0
        self.cnt[e] += 1
        sem = self.sem[e]
        inst.then_inc(sem, 1)
        tok = (e, sem, self.cnt[e], (e, self.semgen[e]))
        self._record(tok, reads, writes)
        self.n_inst += 1
        return inst

    def dma(self, q, out, in_, reads=(), writes=()):
        self._deps(q, reads, writes)
        if q == 'pool':
            k = self.dma_rr['sw']
            self.dma_rr['sw'] = (k + 1) % self.NSW
        else:
            k = self.NSW + self.dma_rr['hw']
            self.dma_rr['hw'] = (self.dma_rr['hw'] + 1) % (self.NDMA - self.NSW)
        sem = self.dma_sems[k]
        if self.dma_cnt[k] > 0:
            self._wait(q, (('dma', k), sem, 16 * self.dma_cnt[k], ('dma', k)))
        inst = self.eng[q].dma_start(out=out, in_=in_)
        self.dma_cnt[k] += 1
        inst.then_inc(sem, 16)
        tok = (('dma', k), sem, 16 * self.dma_cnt[k], ('dma', k))
        self._record(tok, reads, writes)
        self.n_inst += 1
        return inst

    def wait_all_dma(self, e):
        for k in range(self.NDMA):
            if self.dma_cnt[k] > 0:
                self._wait(e, (('dma', k), self.dma_sems[k], 16 * self.dma_cnt[k], ('dma', k)))

    def barrier(self):
        toks = []
        for e2 in self.eng:
            if self.cnt[e2] > 0:
                toks.append((e2, self.sem[e2], self.cnt[e2], (e2, self.semgen[e2])))
        for e in self.eng:
            for tok in toks:
                if tok[0] != e:
                    self._wait(e, tok)
            self.wait_all_dma(e)

    def barrier_consts(self):
        for e in self.eng:
            self.wait_all_dma(e)


def _consts():
    idx = np.arange(128)
    ch = idx // 64
    same = (ch[:, None] == ch[None, :])
    c = {}
    c["ident"] = np.eye(128, dtype=np.float32)
    c["ones"] = np.ones((128, 128), np.float32)
    c["umask"] = (same & (idx[:, None] <= idx[None, :])).astype(np.float32)
    c["bd"] = same.astype(np.float32)
    low_incl = same & (idx[None, :] <= idx[:, None])
    c["lstrict"] = (same & (idx[None, :] < idx[:, None])).astype(np.float32)
    cs = np.zeros((128, 2), np.float32)
    cs[:64, 0] = 1.0
    cs[64:, 1] = 1.0
    c["chunksel"] = cs
    c["iota"] = np.tile(np.arange(128, dtype=np.float32)[None, :], (128, 1))
    return c


def _block_weights(w_in, w_upp, w_upd, w_out, w_q):
    def blk(w, c0):
        k = w.shape[0] // 128
        return w[:, c0:c0 + 256].reshape(k, 128, 256).transpose(1, 0, 2)
    chunks = []
    for c in range(2):
        chunks.append(blk(w_in, c * 256))
    for c in range(2):
        chunks.append(blk(w_in, OFF_Z + c * 256))
    for n in range(4):
        chunks.append(blk(w_in, OFF_G + n * 256))
    for n in range(4):
        chunks.append(blk(w_in, OFF_G + 1024 + n * 256))
    for n in range(4):
        chunks.append(np.concatenate([blk(w_upp, n * 256), blk(w_upd, n * 256)], axis=1))
    for n in range(4):
        chunks.append(blk(w_out, n * 256))
    for n in range(8):
        chunks.append(blk(w_q, n * 256))
    return np.ascontiguousarray(np.stack(chunks).reshape(28, 128, 2048).astype(np.float32))


def rep128(v):
    v = np.asarray(v, np.float32).reshape(1, -1)
    return np.ascontiguousarray(np.repeat(v, 128, axis=0))


class Prog:
    def __init__(self):
        self.nc = bass.Bass("TRN2", target_bir_lowering=False)
        self.T = Trk(self.nc)
        self.din = {}
        self.dout = {}

    def inp(self, name, shape, dt=F32):
        ap = self.nc.dram_tensor(name, list(shape), dt, kind="ExternalInput").ap()
        self.din[name] = ap
        return ap

    def outp(self, name, shape, dt=F32):
        ap = self.nc.dram_tensor(name, list(shape), dt, kind="ExternalOutput").ap()
        self.dout[name] = ap
        return ap

    def scratch(self, name, shape, dt=F32):
        return self.nc.dram_tensor(name, list(shape), dt).ap()


def sb(nc, es, name, shape, dt=F32):
    h = es.enter_context(nc.sbuf_tensor("s_" + name, list(shape), dt))
    return h.ap()


def build():
    P = Prog()
    nc, T = P.nc, P.T
    xseq = P.inp("xseq", [SEQ, D_MODEL])
    xs = P.inp("xs", [STOK, D_MODEL])
    S0 = P.inp("S0", [4, 4, 128, 128])
    conv0T = P.inp("conv0T", [4, 1536, 3])
    w_in = P.inp("w_in", [D_MODEL, D_IN])
    gmix = P.inp("gmix_bc", [128, D_MODEL])
    wconvT = P.inp("wconvT", [1536, 4])
    alog = P.inp("alog_bc", [128, 4])
    dtb = P.inp("dtb_bc", [128, 4])
    cn = {k: P.inp("c_" + k, list(v.shape)) for k, v in _consts().items()}

    dn_p = P.outp("dn_p", [4, 128, 128])
    dn_s = P.outp("dn_s", [4, 4, 128, 128])
    conv_pT = P.outp("conv_pT", [1536, 3])
    conv_sT = P.outp("conv_sT", [4, 1536, 3])
    o_all = P.outp("o_all", [SEQ + STOK, 512]) if DEBUG_A else P.scratch("o_all", [SEQ + STOK, 512])

    Bd = {}
    for nm, shp in [("xown", [OWN, D_MODEL]), ("xhalo", [16, D_MODEL]), ("pool0", [4, 15, 512]),
                    ("gmix_col", [128, 8]), ("gffn_col", [128, 8]), ("gfin_bc", [128, D_MODEL]), ("gdn_bc", [128, 512]),
                    ("pscale_col", [128, 4]), ("sel", [128, 4]), ("w_grp", [4, 128, 128]), ("keysT", [16, 128, 128]),
                    ("bmain", [128, 4, 128]), ("bprev", [128, 4, 128]), ("bfirst", [128, 4, 128]), ("bsamp", [128, 4, 128]),
                    ("bsprev", [128, 4, 128]), ("wblk", [28, 128, 2048]), ("UTb", [128, 128, 8, 128]),
                    ("peer_v", [16384, D_MODEL])]:
        Bd[nm] = P.inp(nm, shp)
    Bd["w_in"] = w_in
    Bd["o_all"] = o_all
    Bd["Ubf"] = P.scratch("Ubf", [128, 128, 1024], BF16)
    Bd["Vbf"] = P.scratch("Vbf", [16384, D_MODEL], BF16)
    Bd["y"] = P.outp("y", [OWN, D_MODEL])
    Bd["pool_p"] = P.outp("pool_p", [15, 512])
    Bd["pool_s"] = P.outp("pool_s", [4, 15, 512])

    ps = [nc.alloc_psum_tensor("psb%d" % i, [128, 512], F32).ap() for i in range(8)]

    with ExitStack() as es:
        C = {}
        for k, v in cn.items():
            C[k] = sb(nc, es, "k_" + k, list(v.shape))
            T.dma('sp', C[k], v)
        gmix_sb = sb(nc, es, "gmix", [128, D_MODEL])
        T.dma('sp', gmix_sb, gmix)
        wc_sb = sb(nc, es, "wconv", [128, 12, 4])
        T.dma('sp', wc_sb, wconvT.rearrange("(c p) j -> p c j", p=128))
        alog_sb = sb(nc, es, "alog", [128, 4])
        dtb_sb = sb(nc, es, "dtb", [128, 4])
        T.dma('sp', alog_sb, alog)
        T.dma('sp', dtb_sb, dtb)
        onesR = sb(nc, es, "onesR", [128, 128], F32R)
        T.dma('pool', onesR, cn["ones"])
        T.barrier_consts()
        nexpA = sb(nc, es, "nexpA", [128, 4])
        T.op('act', lambda e: e.activation(out=nexpA, in_=alog_sb, func=AF.Exp), writes=["nexpA"])
        T.op('dve', lambda e: e.tensor_scalar(out=nexpA, in0=nexpA, scalar1=-1.0, scalar2=None, op0=ALU.mult),
             reads=["nexpA"], writes=["nexpA"])

        phase_a(P, es, C, ps, dict(xseq=xseq, xs=xs, S0=S0, conv0T=conv0T, w_in=w_in, gmix=gmix_sb,
                                   wc=wc_sb, dtb=dtb_sb, nexpA=nexpA, onesR=onesR,
                                   dn_p=dn_p, dn_s=dn_s, conv_pT=conv_pT, conv_sT=conv_sT, o_all=o_all,
                                   UTb=Bd["UTb"], peer_v=Bd["peer_v"], Ubf=Bd["Ubf"], Vbf=Bd["Vbf"]))
        T.barrier()
        phase_b(P, C, ps, Bd)
        T.barrier()
    return P


def phase_a(P, es0, C, ps, A):
    nc, T = P.nc, P.T
    ident, onesF = C["ident"], C["ones"]
    with ExitStack() as es:
        w_a = sb(nc, es, "w_a", [128, 8, 1544], F32R)
        for dc in range(8):
            T.dma('pool', w_a[:, dc, 0:1536], A["w_in"][dc * 128:(dc + 1) * 128, OFF_QKV:OFF_Z], writes=["w_a"])
            T.dma('pool', w_a[:, dc, 1536:1544], A["w_in"][dc * 128:(dc + 1) * 128, OFF_B:OFF_G], writes=["w_a"])
        xt = [sb(nc, es, "xt%d" % i, [128, D_MODEL]) for i in range(2)]
        xn = sb(nc, es, "xn", [128, D_MODEL])
        ss = sb(nc, es, "ss", [128, 1])
        rs = sb(nc, es, "rs", [128, 1])
        rstd = sb(nc, es, "rstd", [128, 1])
        xnT = sb(nc, es, "xnT", [128, 8, 256], F32R)
        ext_w = sb(nc, es, "ext", [128, 12, 268], F32R)
        ext = ext_w.bitcast(F32)
        diagW = sb(nc, es, "diagW", [128, 48, 128], F32R)
        for cc in range(12):
            for j in range(4):
                T.op('pool', lambda e: e.tensor_scalar(out=diagW[:, cc * 4 + j, :], in0=ident, scalar1=A["wc"][:, cc, j:j + 1],
                                                       scalar2=None, op0=ALU.mult), writes=["diagW"])
        carry = sb(nc, es, "carry", [128, 12, 3])
        qcs = [sb(nc, es, "qc%d" % i, [128, 12, 256], F32R) for i in range(2)]
        sq = sb(nc, es, "sq", [128, 8, 256], F32R)
        rinvs = [sb(nc, es, "rinv%d" % i, [128, 256]) for i in range(2)]
        bas = [[sb(nc, es, "ba%d_%d" % (p_, i), [128, 8]) for i in range(2)] for p_ in range(2)]
        small = {}
        for nm, w in [("beta", 4), ("negbeta", 4), ("g", 4), ("sp", 4), ("gc", 4), ("gam", 4), ("dlt", 4),
                      ("bgn", 4), ("gsel", 8), ("egl", 8), ("tmp4", 4)]:
            small[nm] = [sb(nc, es, "%s%d" % (nm, i), [128, w]) for i in range(2)]
        NH = 4
        def tiles(nm, n=1, shape=(128, 128)):
            return [[sb(nc, es, "%s_h%d_%d" % (nm, h, i), list(shape)) for i in range(n)] for h in range(NH)]
        Lg_all = sb(nc, es, "Lg_all", [128, 4, 128])
        dec = tiles("dec")
        decT = tiles("decT")
        def tilesR(nm, n=1, shape=(128, 128)):
            return [[sb(nc, es, "%s_h%d_%d" % (nm, h, i), list(shape), F32R) for i in range(n)] for h in range(NH)]
        Mp = tilesR("Mp", 2)
        NR = tilesR("NR", 2, (128, 256))
        Rfin = tilesR("Rfin")
        VK = tilesR("VK", 1, (128, 256))
        Qg = tiles("Qg")
        wkn = tiles("wkn")
        kdec = tiles("kdec", 2)
        uv = tiles("uv", 2)
        PT = tiles("PT", 2)
        qeT = tiles("qeT", 2)
        FT = tiles("FT", 4)
        Sst = [sb(nc, es, "Sst%d" % i, [128, 4, 128]) for i in range(2)]
        ostage = [sb(nc, es, "ost%d" % i, [64, 512]) for i in range(2)]

        T.op('dve', lambda e: e.memset(Sst[0], 0.0), writes=["S0buf"])
        T.op('dve', lambda e: e.memset(carry, 0.0), writes=["carry"])
        s_cur = 0
        chunk_ctr = 0

        groups = [("p", g) for g in range(min(32, NGROUP_DBG))] + [("s", 0)]
        def s1a(kind, g, par):
            nseq, L = (1, 256) if kind == "p" else (4, 64)
            ba = bas[par]
            if kind == "p":
                last = (g == min(32, NGROUP_DBG) - 1)
                for e1 in range(g * 4, 128 if last else g * 4 + 4):
                    T.dma('pool', A["Ubf"][e1], A["UTb"][e1].rearrange("p k e -> p (k e)"))
                    T.dma('pool', A["Vbf"][e1 * 128:(e1 + 1) * 128, :], A["peer_v"][e1 * 128:(e1 + 1) * 128, :])
            extv = ext[:, :, 0:nseq * (L + 3)].rearrange("p c (s l) -> p c s l", s=nseq)
            extvw = ext_w[:, :, 0:nseq * (L + 3)].rearrange("p c (s l) -> p c s l", s=nseq)
            for blk in range(2):
                xb = xt[blk]
                src = A["xseq"][g * 256 + blk * 128: g * 256 + (blk + 1) * 128, :] if kind == "p" \
                    else A["xs"][blk * 128:(blk + 1) * 128, :]
                T.dma('sp', xb, src, writes=["xt%d" % blk])
                T.op('act', lambda e: e.activation(out=xn, in_=xb, func=AF.Square, accum_out=ss),
                     reads=["xt%d" % blk], writes=["xn", "ss"])
                T.op('act', lambda e: e.activation(out=rs, in_=ss, func=AF.Sqrt, bias=EPS, scale=1.0 / D_MODEL),
                     reads=["ss"], writes=["rs"])
                T.op('dve', lambda e: e.reciprocal(out=rstd, in_=rs), reads=["rs"], writes=["rstd"])
                T.op('dve', lambda e: e.scalar_tensor_tensor(out=xn, in0=xb, scalar=rstd, in1=A["gmix"],
                                                             op0=ALU.mult, op1=ALU.mult),
                     reads=["xt%d" % blk, "rstd"], writes=["xn"])
                for half in range(2):
                    for q in range(4):
                        dc = half * 4 + q
                        T.op('pe', lambda e: e.transpose(out=ps[half][:, q * 128:(q + 1) * 128],
                                                         in_=xn[:, dc * 128:(dc + 1) * 128], identity=ident),
                             reads=["xn"], writes=["ps%d" % half])
                    T.op('act', lambda e: e.activation(
                        out=xnT[:, half * 4:(half + 1) * 4, blk * 128:(blk + 1) * 128],
                        in_=ps[half].rearrange("p (q t) -> p q t", q=4), func=AF.Copy),
                        reads=["ps%d" % half], writes=["xnT"])
                for dc in range(8):
                    T.op('pe', lambda e: e.matmul(ps[1][:, 504:512], lhsT=xnT[:, dc, blk * 128:(blk + 1) * 128],
                                                  rhs=w_a[:, dc, 1536:1544], start=(dc == 0), stop=(dc == 7)),
                         reads=["xnT", "w_a"], writes=["ps1"])
                T.op('dve', lambda e: e.tensor_copy(out=ba[blk], in_=ps[1][:, 504:512]),
                     reads=["ps1"], writes=["ba%d_%d" % (par, blk)])
        def s1b(kind, g, par):
            if STAGE < 2:
                return
            nseq, L = (1, 256) if kind == "p" else (4, 64)
            qc_w = qcs[par]
            qc = qc_w.bitcast(F32)
            extv = ext[:, :, 0:nseq * (L + 3)].rearrange("p c (s l) -> p c s l", s=nseq)
            extvw = ext_w[:, :, 0:nseq * (L + 3)].rearrange("p c (s l) -> p c s l", s=nseq)
            if kind == "p":
                T.op('pool', lambda e: e.tensor_copy(out=extvw[:, :, 0, 0:3], in_=carry), reads=["carry"], writes=["ext"])
            else:
                for s in range(4):
                    T.dma('pool', extvw[:, :, s, 0:3], A["conv0T"][s].rearrange("(c p) j -> p c j", p=128), writes=["ext"])
            for cc in range(12):
                hf = cc % 2
                pq = ps[hf][:, 0:256]
                for dc in range(8):
                    T.op('pe', lambda e: e.matmul(pq, lhsT=w_a[:, dc, cc * 128:(cc + 1) * 128],
                                                  rhs=xnT[:, dc, :], start=(dc == 0), stop=(dc == 7)),
                         reads=["xnT", "w_a"], writes=["ps%d" % hf])
                T.op('act', lambda e: e.activation(out=extvw[:, cc, :, 3:3 + L],
                                                   in_=pq.rearrange("p (s l) -> p s l", s=nseq),
                                                   func=AF.Copy),
                     reads=["ps%d" % hf], writes=["ext"])
            if kind == "p":
                T.op('pool', lambda e: e.tensor_copy(out=carry, in_=extv[:, :, 0, L:L + 3]), reads=["ext"], writes=["carry"])
                if g == 31:
                    T.dma('sp', A["conv_pT"].rearrange("(c p) j -> p c j", p=128), carry, reads=["carry"])
            else:
                for s in range(4):
                    T.dma('sp', A["conv_sT"][s].rearrange("(c p) j -> p c j", p=128), extv[:, :, s, L:L + 3], reads=["ext"])
            for cc in range(12):
                hf = cc % 2
                pq = ps[hf][:, 0:256]
                for j in range(4):
                    T.op('pe', lambda e: e.matmul(pq, lhsT=diagW[:, cc * 4 + j, :], rhs=extvw[:, cc, :, j:j + L], start=(j == 0), stop=(j == 3)),
                         reads=["ext"], writes=["ps%d" % hf])
                T.op('act', lambda e: e.activation(out=qc_w[:, cc, :], in_=pq, func=AF.Silu), reads=["ps%d" % hf], writes=["qc%d" % par])
            T.op('act', lambda e: e.activation(out=sq, in_=qc[:, 0:8, :], func=AF.Square), reads=["qc%d" % par], writes=["sq"])
            for cc in range(8):
                hf = cc % 2
                pq = ps[hf][:, 0:256]
                T.op('pe', lambda e: e.matmul(pq, lhsT=A["onesR"], rhs=sq[:, cc, :], start=True, stop=True),
                     reads=["sq"], writes=["ps%d" % hf])
                sc = 128.0 if cc < 4 else 1.0
                rinv, rk_ = rinvs[cc % 2], "rinv%d" % (cc % 2)
                T.op('act', lambda e: e.activation(out=rinv, in_=pq, func=AF.Sqrt, bias=EPS * sc, scale=sc),
                     reads=["ps%d" % hf], writes=[rk_])
                T.op('dve', lambda e: e.reciprocal(out=rinv, in_=rinv), reads=[rk_], writes=[rk_])
                T.op('dve', lambda e: e.tensor_tensor(out=qc_w[:, cc, :], in0=qc[:, cc, :], in1=rinv, op=ALU.mult),
                     reads=["qc%d" % par, rk_], writes=["qc%d" % par])
        def s2(kind, g, par, blk):
            nonlocal s_cur, chunk_ctr
            if STAGE < 3:
                return
            qc_w = qcs[par]
            qc = qc_w.bitcast(F32)
            ba = bas[par]
            if True:
                bp = blk
                cols = slice(blk * 128, (blk + 1) * 128)
                sm = {k: v[bp] for k, v in small.items()}
                bab = ba[blk]
                T.op('act', lambda e: e.activation(out=sm["beta"], in_=bab[:, 0:4], func=AF.Sigmoid),
                     reads=["ba%d_%d" % (par, blk)], writes=["beta%d" % bp])
                T.op('dve', lambda e: e.tensor_tensor(out=sm["sp"], in0=bab[:, 4:8], in1=A["dtb"], op=ALU.add),
                     reads=["ba%d_%d" % (par, blk)], writes=["sp%d" % bp])
                T.op('act', lambda e: e.activation(out=sm["sp"], in_=sm["sp"], func=AF.Exp), reads=["sp%d" % bp], writes=["sp%d" % bp])
                T.op('act', lambda e: e.activation(out=sm["sp"], in_=sm["sp"], func=AF.Ln, bias=1.0),
                     reads=["sp%d" % bp], writes=["sp%d" % bp])
                T.op('dve', lambda e: e.tensor_tensor(out=sm["g"], in0=sm["sp"], in1=A["nexpA"], op=ALU.mult),
                     reads=["sp%d" % bp, "nexpA"], writes=["g%d" % bp])
                T.op('dve', lambda e: e.tensor_scalar(out=sm["negbeta"], in0=sm["beta"], scalar1=-1.0, scalar2=None, op0=ALU.mult),
                     reads=["beta%d" % bp], writes=["negbeta%d" % bp])
                T.op('pe', lambda e: e.matmul(ps[7][:, 0:4], lhsT=C["umask"], rhs=sm["g"], start=True, stop=True),
                     reads=["g%d" % bp], writes=["ps7"])
                T.op('pe', lambda e: e.matmul(ps[7][:, 8:12], lhsT=C["bd"], rhs=sm["g"], start=True, stop=True),
                     reads=["g%d" % bp], writes=["ps7"])
                T.op('act', lambda e: e.activation(out=sm["gc"], in_=ps[7][:, 0:4], func=AF.Copy), reads=["ps7"], writes=["gc%d" % bp])
                T.op('act', lambda e: e.activation(out=sm["gam"], in_=ps[7][:, 0:4], func=AF.Exp), reads=["ps7"], writes=["gam%d" % bp])
                T.op('dve', lambda e: e.tensor_tensor(out=sm["tmp4"], in0=ps[7][:, 8:12], in1=sm["gc"], op=ALU.subtract),
                     reads=["ps7", "gc%d" % bp], writes=["tmp4%d" % bp])
                T.op('act', lambda e: e.activation(out=sm["dlt"], in_=sm["tmp4"], func=AF.Exp), reads=["tmp4%d" % bp], writes=["dlt%d" % bp])
                T.op('dve', lambda e: e.tensor_tensor(out=sm["bgn"], in0=sm["negbeta"], in1=sm["gam"], op=ALU.mult),
                     reads=["negbeta%d" % bp, "gam%d" % bp], writes=["bgn%d" % bp])
                T.op('dve', lambda e: e.tensor_tensor(
                    out=sm["gsel"].rearrange("p (h c) -> p h c", c=2),
                    in0=sm["g"].unsqueeze(2).to_broadcast([128, 4, 2]),
                    in1=C["chunksel"].unsqueeze(1).to_broadcast([128, 4, 2]), op=ALU.mult),
                    reads=["g%d" % bp], writes=["gsel%d" % bp])
                T.op('pe', lambda e: e.matmul(ps[7][:, 16:24], lhsT=onesF, rhs=sm["gsel"], start=True, stop=True),
                     reads=["gsel%d" % bp], writes=["ps7"])
                T.op('act', lambda e: e.activation(out=sm["egl"], in_=ps[7][:, 16:24], func=AF.Exp), reads=["ps7"], writes=["egl%d" % bp])

                if SUB < 1:
                    return

                def hk(nm, h, i=0):
                    return "%s_%d_%d" % (nm, h, i)

                def pslot(h, i):
                    if i == 0:
                        return ps[5][:, h * 128:(h + 1) * 128], "ps5_%d" % h
                    return ps[4][:, 128 + 0:128 + 0], None

                def slotA(h):
                    return ps[2 + h][:, 0:128], "ps%d" % (2 + h)

                def slotB(h):
                    return ps[2 + h][:, 128:256], "ps%d" % (2 + h)

                def slotC(h):
                    return ps[2 + h][:, 256:384], "ps%d" % (2 + h)

                T.op('dve', lambda e: e.tensor_tensor(out=Lg_all, in0=C["umask"].unsqueeze(1).to_broadcast([128, 4, 128]),
                                                      in1=sm["g"].unsqueeze(2).to_broadcast([128, 4, 128]), op=ALU.mult),
                     reads=["g%d" % bp], writes=["Lg_all"])
                T.op('pe', lambda e: e.matmul(ps[7], lhsT=onesF, rhs=Lg_all.rearrange("p h t -> p (h t)"), start=True, stop=True),
                     reads=["Lg_all"], writes=["ps7"])
                for h in range(NH):
                    gcb = ps[7][:, h * 128:(h + 1) * 128]
                    T.op('dve', lambda e: e.tensor_scalar(out=dec[h][0], in0=gcb, scalar1=sm["gc"][:, h:h + 1], scalar2=0.0,
                                                          op0=ALU.subtract, op1=ALU.max), reads=["ps7", "gc%d" % bp], writes=[hk("dec", h)])
                    T.op('dve', lambda e: e.tensor_scalar(out=decT[h][0], in0=gcb, scalar1=sm["gc"][:, h:h + 1], scalar2=0.0,
                                                          op0=ALU.subtract, op1=ALU.min), reads=["ps7", "gc%d" % bp], writes=[hk("decT", h)])
                    T.op('act', lambda e: e.activation(out=dec[h][0], in_=dec[h][0], func=AF.Exp, scale=-1.0), reads=[hk("dec", h)], writes=[hk("dec", h)])
                    T.op('act', lambda e: e.activation(out=decT[h][0], in_=decT[h][0], func=AF.Exp), reads=[hk("decT", h)], writes=[hk("decT", h)])
                    T.op('pool', lambda e: e.tensor_tensor(out=dec[h][0], in0=dec[h][0], in1=C["lstrict"], op=ALU.mult),
                         reads=[hk("dec", h)], writes=[hk("dec", h)])
                    T.op('pool', lambda e: e.tensor_tensor(out=decT[h][0], in0=decT[h][0], in1=C["umask"], op=ALU.mult),
                         reads=[hk("decT", h)], writes=[hk("decT", h)])
                for h in range(NH):
                    pc, kc = ps[2 + h][:, 0:256], "ps%d" % (2 + h)
                    T.op('pe', lambda e: e.matmul(pc, lhsT=qc_w[:, 4 + h, cols], rhs=qc_w[:, h:h + 5:4, cols], start=True, stop=True),
                         reads=["qc%d" % par], writes=[kc])
                    T.op('dve', lambda e: e.scalar_tensor_tensor(out=Mp[h][0], in0=pc[:, 128:256], scalar=sm["negbeta"][:, h:h + 1],
                                                                 in1=dec[h][0], op0=ALU.mult, op1=ALU.mult),
                         reads=[kc, hk("dec", h), "negbeta%d" % bp], writes=[hk("Mp", h, 0)])
                    T.op('dve', lambda e: e.tensor_tensor(out=PT[h][bp], in0=pc[:, 0:128], in1=decT[h][0], op=ALU.mult),
                         reads=[kc, hk("decT", h)], writes=[hk("PT", h, bp)])
                for h in range(NH):
                    pa, ka = ps[2 + h][:, 256:384], "ps%d" % (2 + h)
                    T.op('pe', lambda e: e.transpose(out=pa, in_=Mp[h][0].bitcast(F32), identity=ident), reads=[hk("Mp", h, 0)], writes=[ka])
                    T.op('act', lambda e: e.activation(out=NR[h][0][:, 0:128], in_=pa, func=AF.Copy), reads=[ka], writes=[hk("NR", h, 0)])
                    T.op('pool', lambda e: e.tensor_tensor(out=NR[h][1][:, 128:256], in0=NR[h][0][:, 0:128].bitcast(F32), in1=ident, op=ALU.add),
                         reads=[hk("NR", h, 0)], writes=[hk("NR", h, 1)])
                for h in range(NH):
                    pb, kb = ps[2 + h], "ps%d" % (2 + h)
                    T.op('pe', lambda e: e.matmul(pb[:, 0:128], lhsT=NR[h][0][:, 0:128], rhs=Mp[h][0], start=True, stop=True),
                         reads=[hk("NR", h, 0), hk("Mp", h, 0)], writes=[kb])
                    T.op('pe', lambda e: e.matmul(pb[:, 128:256], lhsT=Mp[h][0], rhs=NR[h][0][:, 0:128], start=True, stop=True),
                         reads=[hk("NR", h, 0), hk("Mp", h, 0)], writes=[kb])
                for h in range(NH):
                    pb, kb = ps[2 + h], "ps%d" % (2 + h)
                    T.op('dve', lambda e: e.tensor_copy(out=Mp[h][1], in_=pb[:, 0:128]), reads=[kb], writes=[hk("Mp", h, 1)])
                    T.op('act', lambda e: e.activation(out=NR[h][1][:, 0:128], in_=pb[:, 128:256], func=AF.Copy), reads=[kb], writes=[hk("NR", h, 1)])
                for lev in range(1, 5):
                    a, b2 = lev % 2, (lev + 1) % 2
                    for h in range(NH):
                        pb, kb = ps[2 + h], "ps%d" % (2 + h)
                        T.op('pe', lambda e: e.matmul(pb[:, 0:256], lhsT=Mp[h][a], rhs=NR[h][a], start=True, stop=True),
                             reads=[hk("NR", h, a), hk("Mp", h, a)], writes=[kb])
                        T.op('pe', lambda e: e.matmul(pb[:, 256:384], lhsT=NR[h][a][:, 0:128], rhs=Mp[h][a], start=True, stop=True),
                             reads=[hk("NR", h, a), hk("Mp", h, a)], writes=[kb])
                    for h in range(NH):
                        pb, kb = ps[2 + h], "ps%d" % (2 + h)
                        T.op('act', lambda e: e.activation(out=NR[h][b2][:, 0:128], in_=pb[:, 0:128], func=AF.Copy), reads=[kb], writes=[hk("NR", h, b2)])
                        T.op('dve', lambda e: e.tensor_tensor(out=NR[h][b2][:, 128:256], in0=pb[:, 128:256], in1=NR[h][a][:, 128:256].bitcast(F32), op=ALU.add),
                             reads=[kb, hk("NR", h, a)], writes=[hk("NR", h, b2)])
                        T.op('dve', lambda e: e.tensor_copy(out=Mp[h][b2], in_=pb[:, 256:384]), reads=[kb], writes=[hk("Mp", h, b2)])
                for h in range(NH):
                    pb, kb = ps[2 + h], "ps%d" % (2 + h)
                    T.op('pe', lambda e: e.matmul(pb[:, 0:128], lhsT=Mp[h][1], rhs=NR[h][1][:, 128:256], start=True, stop=True),
                         reads=[hk("NR", h, 1), hk("Mp", h, 1)], writes=[kb])
                    T.op('dve', lambda e: e.tensor_tensor(out=Rfin[h][0], in0=pb[:, 0:128], in1=NR[h][1][:, 128:256].bitcast(F32), op=ALU.add),
                         reads=[kb, hk("NR", h, 1)], writes=[hk("Rfin", h)])
                for h in range(NH):
                    QT, KT, VT = qc[:, h, cols], qc[:, 4 + h, cols], qc[:, 8 + h, cols]
                    pa, ka = slotA(h)
                    T.op('pe', lambda e: e.transpose(out=pa, in_=KT, identity=ident), reads=["qc%d" % par], writes=[ka])
                    T.op('act', lambda e: e.activation(out=VK[h][0][:, 128:256], in_=pa, func=AF.Copy, scale=sm["bgn"][:, h:h + 1]),
                         reads=[ka, "bgn%d" % bp], writes=[hk("VK", h)])
                    T.op('dve', lambda e: e.tensor_scalar(out=kdec[h][bp], in0=pa, scalar1=sm["dlt"][:, h:h + 1], scalar2=None, op0=ALU.mult),
                         reads=[ka, "dlt%d" % bp], writes=[hk("kdec", h, bp)])
                    pb, kb = slotB(h)
                    T.op('pe', lambda e: e.transpose(out=pb, in_=VT, identity=ident), reads=["qc%d" % par], writes=[kb])
                    T.op('act', lambda e: e.activation(out=VK[h][0][:, 0:128], in_=pb, func=AF.Copy, scale=sm["beta"][:, h:h + 1]),
                         reads=[kb, "beta%d" % bp], writes=[hk("VK", h)])
                    pc, kc = slotC(h)
                    T.op('pe', lambda e: e.transpose(out=pc, in_=QT, identity=ident), reads=["qc%d" % par], writes=[kc])
                    T.op('dve', lambda e: e.tensor_scalar(out=Qg[h][0], in0=pc, scalar1=sm["gam"][:, h:h + 1], scalar2=None, op0=ALU.mult),
                         reads=[kc, "gam%d" % bp], writes=[hk("Qg", h)])
                for h in range(NH):
                    pb, kb = ps[2 + h], "ps%d" % (2 + h)
                    T.op('pe', lambda e: e.matmul(pb[:, 0:256], lhsT=Rfin[h][0], rhs=VK[h][0], start=True, stop=True),
                         reads=[hk("Rfin", h), hk("VK", h)], writes=[kb])
                    T.op('act', lambda e: e.activation(out=uv[h][bp], in_=pb[:, 0:128], func=AF.Copy), reads=[kb], writes=[hk("uv", h, bp)])
                    T.op('dve', lambda e: e.tensor_copy(out=wkn[h][0], in_=pb[:, 128:256]), reads=[kb], writes=[hk("wkn", h)])
                if SUB < 6:
                    return
                for h in range(NH):
                    pa, ka = slotA(h)
                    T.op('pe', lambda e: e.matmul(pa, lhsT=Qg[h][0], rhs=ident, start=True, stop=False), reads=[hk("Qg", h)], writes=[ka])
                    T.op('pe', lambda e: e.matmul(pa, lhsT=wkn[h][0], rhs=PT[h][bp], start=False, stop=True),
                         reads=[hk("wkn", h), hk("PT", h, bp)], writes=[ka])
                    T.op('act', lambda e: e.activation(out=qeT[h][bp], in_=pa, func=AF.Copy), reads=[ka], writes=[hk("qeT", h, bp)])
                    for c in range(2):
                        pp, kp = (slotB(h) if c == 0 else slotC(h))
                        rows = slice(c * 64, (c + 1) * 64)
                        T.op('pe', lambda e: e.matmul(pp, lhsT=wkn[h][0][rows, :], rhs=kdec[h][bp][rows, :], start=True, stop=True),
                             reads=[hk("wkn", h), hk("kdec", h, bp)], writes=[kp])
                        T.op('dve', lambda e: e.scalar_tensor_tensor(out=FT[h][bp * 2 + c], in0=ident,
                                                                     scalar=sm["egl"][:, h * 2 + c:h * 2 + c + 1], in1=pp,
                                                                     op0=ALU.mult, op1=ALU.add),
                             reads=[kp, "egl%d" % bp], writes=[hk("FT", h, bp * 2 + c)])
                for c in range(2 if STAGE >= 4 else 0):
                    rows = slice(c * 64, (c + 1) * 64)
                    if kind == "s":
                        seq = blk * 2 + c
                        T.dma('sp', Sst[s_cur], A["S0"][seq].rearrange("h k v -> k h v"), writes=["S%dbuf" % s_cur])
                    Sc = Sst[s_cur]
                    skey = "S%dbuf" % s_cur
                    op_ = ostage[chunk_ctr % 2]
                    okey = "ost%d" % (chunk_ctr % 2)
                    for h in range(NH):
                        oo = ps[7][0:64, h * 128:(h + 1) * 128]
                        T.op('pe', lambda e: e.matmul(oo, lhsT=qeT[h][bp][:, rows], rhs=Sc[:, h, :], start=True, stop=False),
                             reads=[hk("qeT", h, bp), skey], writes=["ps7"])
                        T.op('pe', lambda e: e.matmul(oo, lhsT=PT[h][bp][rows, rows], rhs=uv[h][bp][rows, :], start=False, stop=True),
                             reads=[hk("PT", h, bp), hk("uv", h, bp)], writes=["ps7"])
                    T.op('act', lambda e: e.activation(out=op_, in_=ps[7][0:64, :], func=AF.Copy), reads=["ps7"], writes=[okey])
                    if kind == "p":
                        row0 = g * 256 + blk * 128 + c * 64
                    else:
                        row0 = SEQ + blk * 128 + c * 64
                    T.dma('sp', A["o_all"][row0:row0 + 64, :], op_, reads=[okey])
                    s_nxt = 1 - s_cur
                    for h in range(NH):
                        so = ps[6][:, h * 128:(h + 1) * 128]
                        T.op('pe', lambda e: e.matmul(so, lhsT=FT[h][bp * 2 + c], rhs=Sc[:, h, :], start=True, stop=False),
                             reads=[hk("FT", h, bp * 2 + c), skey], writes=["ps6"])
                        T.op('pe', lambda e: e.matmul(so, lhsT=kdec[h][bp][rows, :], rhs=uv[h][bp][rows, :], start=False, stop=True),
                             reads=[hk("kdec", h, bp), hk("uv", h, bp)], writes=["ps6"])
                    T.op('dve', lambda e: e.tensor_copy(out=Sst[s_nxt].rearrange("p h v -> p (h v)"), in_=ps[6]),
                         reads=["ps6"], writes=["S%dbuf" % s_nxt])
                    if kind == "s":
                        seq = blk * 2 + c
                        T.dma('sp', A["dn_s"][seq].rearrange("h k v -> k h v"), Sst[s_nxt], reads=["S%dbuf" % s_nxt])
                    s_cur = s_nxt
                    chunk_ctr += 1
            if kind == "p" and g == min(32, NGROUP_DBG) - 1 and blk == 1:
                T.dma('sp', A["dn_p"].rearrange("h k v -> k h v"), Sst[s_cur], reads=["S%dbuf" % s_cur])

        ng = len(groups)
        s1a(groups[0][0], groups[0][1], 0)
        s1b(groups[0][0], groups[0][1], 0)
        for i in range(ng):
            kind, g = groups[i]
            if i + 1 < ng:
                s1a(groups[i + 1][0], groups[i + 1][1], (i + 1) % 2)
            s2(kind, g, i % 2, 0)
            if i + 1 < ng:
                s1b(groups[i + 1][0], groups[i + 1][1], (i + 1) % 2)
            s2(kind, g, i % 2, 1)


_CACHE = {}


def kernel(**inp):
    f = lambda a: np.ascontiguousarray(np.asarray(a, dtype=np.float32))
    x_prompt, x_sample = f(inp["x_prompt"]), f(inp["x_sample"])
    if "prog" not in _CACHE:
        _CACHE["prog"] = build()
    P = _CACHE["prog"]
    consts = _consts()
    pu = f(inp["peer_u"])[0]
    shared = {
        "gmix_col": np.ascontiguousarray(f(inp["g_mix"])[0].reshape(8, 128).T),
        "gffn_col": np.ascontiguousarray(f(inp["g_ffn"])[0].reshape(8, 128).T),
        "gfin_bc": rep128(f(inp["g_final"])),
        "gdn_bc": rep128(np.tile(f(inp["g_dn_out"])[0], 4)),
        "pscale_col": np.ascontiguousarray(f(inp["pool_scale"])[0].reshape(4, 128).T),
        "w_grp": f(inp["w_pool_grp"])[0],
        "keysT": np.ascontiguousarray(f(inp["peer_sub_keys"])[0].transpose(0, 1, 3, 2).reshape(16, 128, 128)),
        "wblk": _block_weights(f(inp["w_in"])[0], f(inp["w_up_pool"])[0], f(inp["w_up_dn"])[0], f(inp["w_out"])[0],
                               f(inp["w_peer_q"])[0].reshape(D_MODEL, 2048)),
        "UTb": np.ascontiguousarray(pu.reshape(128, 128, 8, 128).transpose(0, 3, 2, 1)),
        "peer_v": f(inp["peer_v"])[0],
    }
    in_maps = []
    for c in range(NCORE):
        b, j = c // 4, c % 4
        m = {
            "xseq": x_prompt[b],
            "xs": x_sample[4 * c:4 * c + 4].reshape(STOK, D_MODEL),
            "S0": f(inp["state_dn"])[0, 4 * c:4 * c + 4],
            "conv0T": np.ascontiguousarray(f(inp["state_dn_conv"])[0, 4 * c:4 * c + 4].transpose(0, 2, 1)),
            "w_in": f(inp["w_in"])[0],
            "gmix_bc": rep128(f(inp["g_mix"])[0]),
            "wconvT": np.ascontiguousarray(f(inp["w_conv"])[0].T),
            "alog_bc": rep128(f(inp["a_log"])[0]),
            "dtb_bc": rep128(f(inp["dt_bias"])[0]),
        }
        for k, v in consts.items():
            m["c_" + k] = v
        m.update(shared)
        m["xown"] = np.ascontiguousarray(np.concatenate([x_prompt[b, j * QTOK:(j + 1) * QTOK], m["xs"]], axis=0))
        m["xhalo"] = np.ascontiguousarray(x_prompt[b, j * QTOK - 16:j * QTOK]) if j > 0 else np.zeros((16, D_MODEL), np.float32)
        m["pool0"] = f(inp["cache_pool"])[0, 4 * c:4 * c + 4]
        selv = np.zeros((128, 4), np.float32)
        selv[:, j] = 1.0
        m["sel"] = selv
        m.update(_bands(j))
        in_maps.append(m)
    res = run_bass_kernel_spmd(P.nc, in_maps, core_ids=list(range(NCORE)))
    R = res.results
    _CACHE["last"] = R
    new_dn_p = np.stack([R[0]["dn_p"], R[4]["dn_p"]])[None]
    new_dn_s = np.concatenate([R[c]["dn_s"] for c in range(NCORE)])[None]
    new_conv_p = np.stack([R[0]["conv_pT"].T, R[4]["conv_pT"].T])[None]
    new_conv_s = np.concatenate([R[c]["conv_sT"].transpose(0, 2, 1) for c in range(NCORE)])[None]
    y_prompt = np.stack([np.concatenate([R[4 * b + j]["y"][:QTOK] for j in range(4)]) for b in range(2)])
    y_sample = np.concatenate([R[c]["y"][QTOK:].reshape(4, 64, D_MODEL) for c in range(NCORE)])
    new_pool_p = np.stack([R[3]["pool_p"], R[7]["pool_p"]])[None]
    new_pool_s = np.concatenate([R[c]["pool_s"] for c in range(NCORE)])[None]
    f32 = lambda a: np.ascontiguousarray(a, dtype=np.float32)
    return (f32(y_prompt), f32(y_sample), f32(new_pool_p), f32(new_conv_p), f32(new_dn_p),
            f32(new_pool_s), f32(new_conv_s), f32(new_dn_s))


TT = 256
NTILE = OWN // TT
POOL_W = (2, 4, 8, 16)


def _bands(j):
    t = np.arange(128)
    out = {}
    main, prev, first, samp, sprev = [], [], [], [], []
    for w in POOL_W:
        src, dst = t[:, None], t[None, :]
        inwin = (src <= dst) & (src > dst - w)
        m = inwin / float(w) - np.eye(128)
        main.append(m)
        pw = ((src - 128) > dst - w) & (src >= 112)
        prev.append(pw / float(w))
        if j == 0:
            cnt = np.minimum(dst + 1, w).astype(np.float64)
            first.append(inwin / cnt - np.eye(128))
        else:
            first.append(m)
        same = (src // 64) == (dst // 64)
        samp.append((inwin & same) / float(w) - np.eye(128))
        sp = np.zeros((128, 128))
        for (r0, c0) in ((97, 0), (113, 64)):
            for i in range(15):
                for d in range(64):
                    if (-15 + i) > d - w:
                        sp[r0 + i, c0 + d] = 1.0 / w
        sprev.append(sp)
    f = lambda l: np.ascontiguousarray(np.stack(l).transpose(1, 0, 2).astype(np.float32))
    return dict(bmain=f(main), bprev=f(prev), bfirst=f(first), bsamp=f(samp), bsprev=f(sprev))


def phase_b(P, C, ps, B):
    nc, T = P.nc, P.T
    ident, iota = C["ident"], C["iota"]
    with ExitStack() as es:
        big = sb(nc, es, "big", [128, 16384])
        def reg(off, shape, dt=F32):
            n = int(np.prod(shape[1:]))
            v = big[:, off:off + n]
            if dt != F32:
                v = v.bitcast(dt)
            if len(shape) == 3:
                v = v.rearrange("p (a b) -> p a b", a=shape[1])
            return v
        xnT = reg(0, [128, 8, TT], F32R)
        wb = [reg(2048 + 2048 * i, [128, 8, 256], F32R) for i in range(3)]
        yaT = reg(8192, [128, 4, TT], F32R)
        ztok, ztok_w = reg(9216, [128, 2, 512]), reg(9216, [128, 2, 512], F32R)
        ybT = reg(10240, [128, 4, TT], F32R)
        mtok, mtok_w = reg(11264, [128, 2, 1024]), reg(11264, [128, 2, 1024], F32R)
        mT = reg(13312, [128, 8, TT], F32R)
        xnTh = reg(15360, [128, 8, 128], F32R)
        utok, utok_w = reg(11264, [128, 512]), reg(11264, [128, 512], F32R)
        otok, otok_w = reg(11776, [128, 512]), reg(11776, [128, 512], F32R)
        ocand, ocand_w = reg(12288, [128, 4, 512]), reg(12288, [128, 4, 512], F32R)
        cand, cand_w = reg(11264, [128, 2048]), reg(11264, [128, 2048], F32R)
        eqb, eqb_w = reg(13312, [128, 2048]), reg(13312, [128, 2048], F32R)
        Wsb = big.rearrange("p (t e) -> p t e", e=64)
        Wsb_w = big.bitcast(F32R).rearrange("p (t e) -> p t e", e=64)
        xres = sb(nc, es, "xres", [128, 2, 1024])
        xn2T = sb(nc, es, "xn2T", [128, 8, TT], F32R)
        ublk = [sb(nc, es, "ublk%d" % i, [128, 8, 128], BF16) for i in range(4)]
        vblk = [sb(nc, es, "vblk%d" % i, [128, 1024], BF16) for i in range(4)]
        xn2Tb = sb(nc, es, "xn2Tb", [128, 8, TT], BF16)
        actb = [sb(nc, es, "actb%d" % i, [128, TT]) for i in range(2)]
        gab = [sb(nc, es, "gab%d" % i, [128, TT], BF16) for i in range(2)]
        e1T = sb(nc, es, "e1T", [128, TT])
        e2T = sb(nc, es, "e2T", [128, TT])
        gT = sb(nc, es, "gT", [128, TT])
        uprev = sb(nc, es, "uprev", [128, 512])
        xh = sb(nc, es, "xh", [128, 1024])
        xnb = sb(nc, es, "xnb", [128, 1024])
        ss = sb(nc, es, "ssb", [128, 1])
        rs = sb(nc, es, "rsb", [128, 1])
        rstd = sb(nc, es, "rstdb", [128, 1])
        pooledT = sb(nc, es, "pooledT", [128, 128])
        qTsb = sb(nc, es, "qTsb", [128, 2, TT])
        sga = sb(nc, es, "sga", [128, 256])
        sgb = sb(nc, es, "sgb", [128, 256])
        sq4 = sb(nc, es, "sq4", [128, 4])
        B_ = {"v1": [sb(nc, es, "v1_%d" % i, [128, 8, 16]) for i in range(2)],
              "v2": [sb(nc, es, "v2_%d" % i, [128, 8, 16]) for i in range(2)],
              "i1u": [sb(nc, es, "i1u_%d" % i, [128, 8, 16], U32) for i in range(2)],
              "i2u": [sb(nc, es, "i2u_%d" % i, [128, 8, 16], U32) for i in range(2)]}
        i1f = sb(nc, es, "i1f", [128, 8, 16])
        i2f = sb(nc, es, "i2f", [128, 8, 16])
        tmpk = sb(nc, es, "tmpk", [128, 128])
        tmp2 = sb(nc, es, "tmp2", [128, 256])
        cv = sb(nc, es, "cv", [128, 8, 16])
        ciu = sb(nc, es, "ciu", [128, 8, 16], U32)
        abu = sb(nc, es, "abu", [128, 8, 16], U32)
        af = sb(nc, es, "af", [128, 8, 16])
        bf = sb(nc, es, "bf", [128, 8, 16])
        ex = sb(nc, es, "ex", [128, 8, 16])
        s8 = sb(nc, es, "s8", [128, 8])
        e1f = sb(nc, es, "e1f", [128, 128])
        e2f = sb(nc, es, "e2f", [128, 128])
        gsl = sb(nc, es, "gsl", [128, 128])
        ne2T = sb(nc, es, "ne2T", [128, TT])
        ohtmp = [sb(nc, es, "ohtmp%d" % i, [128, 128]) for i in range(2)]
        NOH = 6
        oha = [sb(nc, es, "oha%d" % i, [128, 128], F32R) for i in range(NOH)]
        ohb = [sb(nc, es, "ohb%d" % i, [128, 128], F32R) for i in range(NOH)]
        Whi = sb(nc, es, "Whi", [128, TT, 64], BF16)
        gmixc = sb(nc, es, "gmixc", [128, 8]); T.dma('sp', gmixc, B["gmix_col"])
        gffnc = sb(nc, es, "gffnc", [128, 8]); T.dma('sp', gffnc, B["gffn_col"])
        gfin = sb(nc, es, "gfin", [128, 1024]); T.dma('sp', gfin, B["gfin_bc"])
        gdn = sb(nc, es, "gdn", [128, 512]); T.dma('sp', gdn, B["gdn_bc"])
        pscale = sb(nc, es, "pscale", [128, 4]); T.dma('sp', pscale, B["pscale_col"])
        selb = sb(nc, es, "selb", [128, 4]); T.dma('sp', selb, B["sel"])
        wgrp = sb(nc, es, "wgrp", [128, 4, 128]); T.dma('sp', wgrp, B["w_grp"].rearrange("g c d -> c g d"))
        keysT = sb(nc, es, "keysT", [128, 16, 128]); T.dma('sp', keysT, B["keysT"].rearrange("q d k -> d q k"))
        bands = {}
        for k in ("bmain", "bprev", "bfirst", "bsamp", "bsprev"):
            bands[k] = sb(nc, es, k, [128, 4, 128]); T.dma('sp', bands[k], B[k])
        T.barrier()

        wblk = B["wblk"]
        WB_U, WB_Z, WB_GA, WB_GB, WB_UP, WB_O, WB_Q = 0, 2, 4, 8, 12, 16, 20
        wrr = [0]

        def load_w(ci):
            i = wrr[0] % 3
            wrr[0] += 1
            T.dma('pool', wb[i].rearrange("p k c -> p (k c)"), wblk[ci], writes=["wb%d" % i])
            return wb[i], "wb%d" % i

        def norm_block(xsrc, xkey, gcol, dstT, dkey, tcols):
            T.op('act', lambda e: e.activation(out=xnb, in_=xsrc, func=AF.Square, accum_out=ss), reads=[xkey], writes=["xnb", "ssb"])
            T.op('act', lambda e: e.activation(out=rs, in_=ss, func=AF.Sqrt, bias=EPS, scale=1.0 / D_MODEL), reads=["ssb"], writes=["rsb"])
            T.op('dve', lambda e: e.reciprocal(out=rstd, in_=rs), reads=["rsb"], writes=["rstdb"])
            T.op('dve', lambda e: e.tensor_scalar(out=xnb, in0=xsrc, scalar1=rstd, scalar2=None, op0=ALU.mult),
                 reads=[xkey, "rstdb"], writes=["xnb"])
            for half in range(2):
                for q in range(4):
                    dc = half * 4 + q
                    T.op('pe', lambda e: e.transpose(out=ps[half][:, q * 128:(q + 1) * 128], in_=xnb[:, dc * 128:(dc + 1) * 128], identity=ident),
                         reads=["xnb"], writes=["ps%d" % half])
                for q in range(4):
                    dc = half * 4 + q
                    T.op('act' if q % 2 else 'dve',
                         (lambda e: e.activation(out=dstT[:, dc, tcols], in_=ps[half][:, q * 128:(q + 1) * 128], func=AF.Copy, scale=gcol[:, dc:dc + 1]))
                         if q % 2 else
                         (lambda e: e.tensor_scalar(out=dstT[:, dc, tcols], in0=ps[half][:, q * 128:(q + 1) * 128], scalar1=gcol[:, dc:dc + 1], scalar2=None, op0=ALU.mult)),
                         reads=["ps%d" % half], writes=[dkey])

        for ti in (TILES if STAGE >= 5 else []):
            samp = (ti == NTILE - 1)
            blks = [0, 1]
            if ti == 0:
                T.op('pool', lambda e: e.memset(xh, 0.0), writes=["xh"])
                T.dma('sp', xh[112:128, :], B["xhalo"], writes=["xh"])
                norm_block(xh, "xh", gmixc, xnTh, "xnTh", slice(0, 128))
            for blk in blks:
                row0 = ti * TT + blk * 128
                T.dma('sp', xres[:, blk, :], B["xown"][row0:row0 + 128, :], writes=["xres%d" % blk])
                norm_block(xres[:, blk, :], "xres%d" % blk, gmixc, xnT, "xnT", slice(blk * 128, (blk + 1) * 128))
            wu = [load_w(WB_U + c) for c in range(2)]
            if ti == 0:
                for c in range(2):
                    for dc in range(8):
                        T.op('pe', lambda e: e.matmul(ps[4][:, c * 256:(c + 1) * 256], lhsT=xnTh[:, dc, :], rhs=wu[c][0][:, dc, :],
                                                      start=(dc == 0), stop=(dc == 7)), reads=["xnTh", wu[c][1]], writes=["ps4"])
                T.op('act', lambda e: e.activation(out=uprev, in_=ps[4], func=AF.Copy), reads=["ps4"], writes=["uprev"])
            if samp:
                T.op('pool', lambda e: e.memset(uprev, 0.0), writes=["uprev"])
            for c in range(2):
                for blk in blks:
                    for dc in range(8):
                        T.op('pe', lambda e: e.matmul(ps[2 + blk][:, c * 256:(c + 1) * 256], lhsT=xnT[:, dc, blk * 128:(blk + 1) * 128],
                                                      rhs=wu[c][0][:, dc, :], start=(dc == 0), stop=(dc == 7)),
                             reads=["xnT", wu[c][1]], writes=["ps%d" % (2 + blk)])
            for blk in blks:
                bi = ti * 2 + blk
                T.op('act', lambda e: e.activation(out=utok_w, in_=ps[2 + blk], func=AF.Copy), reads=["ps%d" % (2 + blk)], writes=["utok"])
                if samp:
                    for sq_ in range(2):
                        seq = blk * 2 + sq_
                        r0 = 97 if sq_ == 0 else 113
                        T.dma('sp', uprev[r0:r0 + 15, :], B["pool0"][seq], writes=["uprev"])
                        T.dma('sp', B["pool_s"][seq], utok[sq_ * 64 + 49: sq_ * 64 + 64, :], reads=["utok"])
                    bm, bp = bands["bsamp"], bands["bsprev"]
                else:
                    bm, bp = (bands["bfirst"] if bi == 0 else bands["bmain"]), bands["bprev"]
                    if bi == 15:
                        T.dma('sp', B["pool_p"], utok[113:128, :], reads=["utok"])
                for gi in range(4):
                    gc_ = slice(gi * 128, (gi + 1) * 128)
                    T.op('pe', lambda e: e.matmul(ps[4][:, 0:128], lhsT=uprev[:, gc_], rhs=bp[:, gi, :], start=True, stop=False),
                         reads=["uprev"], writes=["ps4"])
                    T.op('pe', lambda e: e.matmul(ps[4][:, 0:128], lhsT=utok[:, gc_], rhs=bm[:, gi, :], start=False, stop=True),
                         reads=["utok"], writes=["ps4"])
                    T.op('dve', lambda e: e.tensor_copy(out=pooledT, in_=ps[4][:, 0:128]), reads=["ps4"], writes=["pooledT"])
                    T.op('pe', lambda e: e.matmul(ps[5][:, 0:128], lhsT=wgrp[:, gi, :], rhs=pooledT, start=True, stop=True),
                         reads=["pooledT"], writes=["ps5"])
                    T.op('act', lambda e: e.activation(out=yaT[:, gi, blk * 128:(blk + 1) * 128], in_=ps[5][:, 0:128], func=AF.Copy,
                                                       scale=pscale[:, gi:gi + 1]), reads=["ps5"], writes=["yaT"])
                if not samp:
                    T.op('pool', lambda e: e.tensor_copy(out=uprev, in_=utok), reads=["utok"], writes=["uprev"])
            wz = [load_w(WB_Z + c) for c in range(2)]
            for c in range(2):
                for blk in blks:
                    for dc in range(8):
                        T.op('pe', lambda e: e.matmul(ps[2 + blk][:, c * 256:(c + 1) * 256], lhsT=xnT[:, dc, blk * 128:(blk + 1) * 128],
                                                      rhs=wz[c][0][:, dc, :], start=(dc == 0), stop=(dc == 7)),
                             reads=["xnT", wz[c][1]], writes=["ps%d" % (2 + blk)])
            for blk in blks:
                bi = ti * 2 + blk
                T.op('act', lambda e: e.activation(out=ztok_w[:, blk, :], in_=ps[2 + blk], func=AF.Silu), reads=["ps%d" % (2 + blk)], writes=["ztok"])
                if samp:
                    T.dma('pool', otok_w, B["o_all"][SEQ + blk * 128: SEQ + (blk + 1) * 128, :], writes=["otok"])
                elif NGROUP_DBG < 32:
                    T.dma('pool', otok_w, B["o_all"][bi * 128:(bi + 1) * 128, :], writes=["otok"])
                else:
                    T.dma('pool', ocand_w, B["o_all"][0:SEQ, :].rearrange("(q r) c -> r q c", q=4)[bi * 128:(bi + 1) * 128], writes=["ocand"])
                    T.op('dve', lambda e: e.tensor_scalar(out=otok_w, in0=ocand[:, 0, :], scalar1=selb[:, 0:1], scalar2=None, op0=ALU.mult),
                         reads=["ocand"], writes=["otok"])
                    for q in range(1, 4):
                        T.op('dve', lambda e: e.scalar_tensor_tensor(out=otok_w, in0=ocand[:, q, :], scalar=selb[:, q:q + 1], in1=otok,
                                                                     op0=ALU.mult, op1=ALU.add), reads=["ocand", "otok"], writes=["otok"])
                T.op('act', lambda e: e.activation(out=xnb[:, 0:512], in_=otok, func=AF.Square), reads=["otok"], writes=["xnb"])
                T.op('dve', lambda e: e.tensor_reduce(out=sq4, in_=xnb[:, 0:512].rearrange("p (h v) -> p h v", h=4), axis=AX.X, op=ALU.add),
                     reads=["xnb"], writes=["sq4"])
                T.op('act', lambda e: e.activation(out=sq4, in_=sq4, func=AF.Sqrt, bias=EPS, scale=1.0 / 128), reads=["sq4"], writes=["sq4"])
                T.op('dve', lambda e: e.reciprocal(out=sq4, in_=sq4), reads=["sq4"], writes=["sq4"])
                o3 = otok.rearrange("p (h v) -> p h v", h=4)
                o3w = otok_w.rearrange("p (h v) -> p h v", h=4)
                T.op('dve', lambda e: e.tensor_tensor(out=o3w, in0=o3, in1=sq4.unsqueeze(2).to_broadcast([128, 4, 128]), op=ALU.mult),
                     reads=["otok", "sq4"], writes=["otok"])
                T.op('dve', lambda e: e.tensor_tensor(out=otok_w, in0=otok, in1=gdn, op=ALU.mult), reads=["otok"], writes=["otok"])
                T.op('dve', lambda e: e.tensor_tensor(out=otok_w, in0=otok, in1=ztok[:, blk, :], op=ALU.mult), reads=["otok", "ztok"], writes=["otok"])
                for cc in range(4):
                    T.op('pe', lambda e: e.transpose(out=ps[4][:, cc * 128:(cc + 1) * 128], in_=otok[:, cc * 128:(cc + 1) * 128], identity=ident),
                         reads=["otok"], writes=["ps4"])
                T.op('act', lambda e: e.activation(out=ybT[:, :, blk * 128:(blk + 1) * 128], in_=ps[4].rearrange("p (c t) -> p c t", c=4), func=AF.Copy),
                     reads=["ps4"], writes=["ybT"])
            for n in range(4):
                wga = load_w(WB_GA + n)
                wgb = load_w(WB_GB + n)
                wup, wupk = load_w(WB_UP + n)
                for blk in blks:
                    tcs = slice(blk * 128, (blk + 1) * 128)
                    pg, pu = ps[blk * 2], ps[blk * 2 + 1]
                    kg, ku = "ps%d" % (blk * 2), "ps%d" % (blk * 2 + 1)
                    for dc in range(8):
                        T.op('pe', lambda e: e.matmul(pg[:, 0:256], lhsT=xnT[:, dc, tcs], rhs=wga[0][:, dc, :], start=(dc == 0), stop=(dc == 7)),
                             reads=["xnT", wga[1]], writes=[kg])
                    for dc in range(8):
                        T.op('pe', lambda e: e.matmul(pg[:, 256:512], lhsT=xnT[:, dc, tcs], rhs=wgb[0][:, dc, :], start=(dc == 0), stop=(dc == 7)),
                             reads=["xnT", wgb[1]], writes=[kg])
                    for cc in range(4):
                        T.op('pe', lambda e: e.matmul(pu[:, 0:256], lhsT=yaT[:, cc, tcs], rhs=wup[:, cc, :], start=(cc == 0), stop=(cc == 3)),
                             reads=["yaT", wupk], writes=[ku])
                    for cc in range(4):
                        T.op('pe', lambda e: e.matmul(pu[:, 256:512], lhsT=ybT[:, cc, tcs], rhs=wup[:, 4 + cc, :], start=(cc == 0), stop=(cc == 3)),
                             reads=["ybT", wupk], writes=[ku])
                    T.op('act', lambda e: e.activation(out=sga, in_=pg[:, 0:256], func=AF.Sigmoid), reads=[kg], writes=["sga"])
                    T.op('act', lambda e: e.activation(out=sgb, in_=pg[:, 256:512], func=AF.Sigmoid), reads=[kg], writes=["sgb"])
                    T.op('dve', lambda e: e.tensor_tensor(out=sga, in0=pu[:, 0:256], in1=sga, op=ALU.mult), reads=[ku, "sga"], writes=["sga"])
                    T.op('dve', lambda e: e.tensor_tensor(out=sgb, in0=pu[:, 256:512], in1=sgb, op=ALU.mult), reads=[ku, "sgb"], writes=["sgb"])
                    T.op('pool', lambda e: e.tensor_tensor(out=mtok_w[:, blk, n * 256:(n + 1) * 256], in0=sga, in1=sgb, op=ALU.add),
                         reads=["sga", "sgb"], writes=["mtok"])
            for blk in blks:
                for half in range(2):
                    for q in range(4):
                        dc = half * 4 + q
                        T.op('pe', lambda e: e.transpose(out=ps[4 + half][:, q * 128:(q + 1) * 128], in_=mtok[:, blk, dc * 128:(dc + 1) * 128], identity=ident),
                             reads=["mtok"], writes=["ps%d" % (4 + half)])
                    T.op('act' if half else 'dve',
                         (lambda e: e.activation(out=mT[:, half * 4:(half + 1) * 4, blk * 128:(blk + 1) * 128],
                                                 in_=ps[4 + half].rearrange("p (q t) -> p q t", q=4), func=AF.Copy)) if half else
                         (lambda e: e.tensor_copy(out=mT[:, half * 4:(half + 1) * 4, blk * 128:(blk + 1) * 128],
                                                  in_=ps[4 + half].rearrange("p (q t) -> p q t", q=4))),
                         reads=["ps%d" % (4 + half)], writes=["mT"])
            for n in range(4):
                wo = load_w(WB_O + n)
                for blk in blks:
                    pk = 6 + blk
                    for dc in range(8):
                        T.op('pe', lambda e: e.matmul(ps[pk][:, 0:256], lhsT=mT[:, dc, blk * 128:(blk + 1) * 128], rhs=wo[0][:, dc, :],
                                                      start=(dc == 0), stop=(dc == 7)), reads=["mT", wo[1]], writes=["ps%d" % pk])
                    T.op('dve', lambda e: e.tensor_tensor(out=xres[:, blk, n * 256:(n + 1) * 256], in0=ps[pk][:, 0:256],
                                                          in1=xres[:, blk, n * 256:(n + 1) * 256], op=ALU.add),
                         reads=["ps%d" % pk, "xres%d" % blk], writes=["xres%d" % blk])
            for blk in blks:
                norm_block(xres[:, blk, :], "xres%d" % blk, gffnc, xn2T, "xn2T", slice(blk * 128, (blk + 1) * 128))
            T.op('pool', lambda e: e.tensor_copy(out=xn2Tb, in_=xn2T.bitcast(F32)), reads=["xn2T"], writes=["xn2Tb"])
            if STAGE < 6:
                for blk in blks:
                    row0 = ti * TT + blk * 128
                    T.dma('sp', B["y"][row0:row0 + 128, :], xres[:, blk, :], reads=["xres%d" % blk])
                T.barrier()
                continue
            for qc_ in range(8):
                wq = load_w(WB_Q + qc_)
                for gq in range(2):
                    for dc in range(8):
                        T.op('pe', lambda e: e.matmul(ps[2][:, gq * 256:(gq + 1) * 256], lhsT=wq[0][:, dc, gq * 128:(gq + 1) * 128],
                                                      rhs=xn2T[:, dc, :], start=(dc == 0), stop=(dc == 7)),
                             reads=["xn2T", wq[1]], writes=["ps2"])
                T.op('act', lambda e: e.activation(out=qTsb, in_=ps[2].rearrange("p (g t) -> p g t", g=2), func=AF.Copy), reads=["ps2"], writes=["qTsb"])
                for gq in range(2):
                    cq = qc_ * 2 + gq
                    hh, half = cq // 2, cq % 2
                    for blk in blks:
                        bk = "B%d_" % blk
                        pk = 3 + blk
                        T.op('pe', lambda e: e.matmul(ps[pk][:, 0:128], lhsT=qTsb[:, gq, blk * 128:(blk + 1) * 128],
                                                      rhs=keysT[:, half * 8 + hh, :], start=True, stop=True),
                             reads=["qTsb"], writes=["ps%d" % pk])
                        vvb, iub = B_["v%d" % (half + 1)][blk], B_["i%du" % (half + 1)][blk]
                        T.op('dve', lambda e: e.max(out=vvb[:, hh, 0:8], in_=ps[pk][:, 0:128]), reads=["ps%d" % pk], writes=[bk + "v"])
                        T.op('dve', lambda e: e.match_replace(out=tmpk, in_to_replace=vvb[:, hh, 0:8], in_values=ps[pk][:, 0:128], imm_value=-1e30),
                             reads=["ps%d" % pk, bk + "v"], writes=["tmpk"])
                        T.op('dve', lambda e: e.max(out=vvb[:, hh, 8:16], in_=tmpk), reads=["tmpk"], writes=[bk + "v"])
                        T.op('dve', lambda e: e.max_index(out=iub[:, hh, 0:8], in_max=vvb[:, hh, 0:8], in_values=ps[pk][:, 0:128]),
                             reads=["ps%d" % pk, bk + "v"], writes=[bk + "i"])
                        T.op('dve', lambda e: e.max_index(out=iub[:, hh, 8:16], in_max=vvb[:, hh, 8:16], in_values=ps[pk][:, 0:128]),
                             reads=["ps%d" % pk, bk + "v"], writes=[bk + "i"])
            for blk in blks:
                bk = "B%d_" % blk
                v1b, v2b, i1b, i2b = B_["v1"][blk], B_["v2"][blk], B_["i1u"][blk], B_["i2u"][blk]
                tcs = slice(blk * 128, (blk + 1) * 128)
                T.op('pool', lambda e: e.tensor_copy(out=i1f, in_=i1b), reads=[bk + "i"], writes=["i1f"])
                T.op('pool', lambda e: e.tensor_copy(out=i2f, in_=i2b), reads=[bk + "i"], writes=["i2f"])
                c4 = cand_w.rearrange("p (h a b) -> p h a b", h=8, a=16)
                T.op('pool', lambda e: e.tensor_tensor(out=c4, in0=v1b.unsqueeze(3).to_broadcast([128, 8, 16, 16]),
                                                      in1=v2b.unsqueeze(2).to_broadcast([128, 8, 16, 16]), op=ALU.add),
                     reads=[bk + "v"], writes=["cand"])
                c3 = cand.rearrange("p (h x) -> p h x", h=8)
                for hh in range(8):
                    T.op('dve', lambda e: e.max(out=cv[:, hh, 0:8], in_=c3[:, hh, :]), reads=["cand"], writes=["cv"])
                    T.op('dve', lambda e: e.match_replace(out=tmp2, in_to_replace=cv[:, hh, 0:8], in_values=c3[:, hh, :], imm_value=-1e30),
                         reads=["cand", "cv"], writes=["tmp2"])
                    T.op('dve', lambda e: e.max(out=cv[:, hh, 8:16], in_=tmp2), reads=["tmp2"], writes=["cv"])
                    T.op('dve', lambda e: e.max_index(out=ciu[:, hh, 0:8], in_max=cv[:, hh, 0:8], in_values=c3[:, hh, :]), reads=["cand", "cv"], writes=["ciu"])
                    T.op('dve', lambda e: e.max_index(out=ciu[:, hh, 8:16], in_max=cv[:, hh, 8:16], in_values=c3[:, hh, :]), reads=["cand", "cv"], writes=["ciu"])
                T.op('dve', lambda e: e.tensor_tensor(out=ex, in0=cv, in1=cv[:, :, 0:1].to_broadcast([128, 8, 16]), op=ALU.subtract),
                     reads=["cv"], writes=["ex"])
                T.op('act', lambda e: e.activation(out=ex, in_=ex, func=AF.Exp), reads=["ex"], writes=["ex"])
                T.op('dve', lambda e: e.tensor_reduce(out=s8, in_=ex, axis=AX.X, op=ALU.add), reads=["ex"], writes=["s8"])
                T.op('dve', lambda e: e.reciprocal(out=s8, in_=s8), reads=["s8"], writes=["s8"])
                T.op('dve', lambda e: e.tensor_tensor(out=gsl.rearrange("p (h k) -> p h k", h=8), in0=ex,
                                                      in1=s8.unsqueeze(2).to_broadcast([128, 8, 16]), op=ALU.mult),
                     reads=["ex", "s8"], writes=["gsl"])
                T.op('dve', lambda e: e.tensor_scalar(out=abu, in0=ciu, scalar1=4, scalar2=None, op0=ALU.logical_shift_right), reads=["ciu"], writes=["abu"])
                T.op('dve', lambda e: e.tensor_copy(out=af, in_=abu), reads=["abu"], writes=["af"])
                T.op('dve', lambda e: e.tensor_scalar(out=abu, in0=ciu, scalar1=15, scalar2=None, op0=ALU.bitwise_and), reads=["ciu", "af"], writes=["abu"])
                T.op('dve', lambda e: e.tensor_copy(out=bf, in_=abu), reads=["abu"], writes=["bf"])
                for (sel_f, idx_f, dst) in ((af, i1f, e1f), (bf, i2f, e2f)):
                    e3 = eqb.rearrange("p (s a) -> p s a", a=16)
                    e3w = eqb_w.rearrange("p (s a) -> p s a", a=16)
                    T.op('dve', lambda e: e.tensor_tensor(out=e3w, in0=sel_f.rearrange("p h k -> p (h k)").unsqueeze(2).to_broadcast([128, 128, 16]),
                                                          in1=iota[:, 0:16].unsqueeze(1).to_broadcast([128, 128, 16]), op=ALU.is_equal),
                         reads=["af", "bf"], writes=["eqb"])
                    e4 = eqb.rearrange("p (h k a) -> p h k a", h=8, k=16)
                    e4w = eqb_w.rearrange("p (h k a) -> p h k a", h=8, k=16)
                    T.op('pool', lambda e: e.tensor_tensor(out=e4w, in0=e4, in1=idx_f.unsqueeze(2).to_broadcast([128, 8, 16, 16]), op=ALU.mult),
                         reads=["eqb", "i1f", "i2f"], writes=["eqb"])
                    T.op('dve', lambda e: e.tensor_reduce(out=dst, in_=e3, axis=AX.X, op=ALU.add), reads=["eqb"], writes=["e12f"])
                for (src, dstT, nm) in ((e1f, e1T, "e1T"), (e2f, e2T, "e2T"), (gsl, gT, "gT")):
                    T.op('pe', lambda e: e.transpose(out=ps[5][:, 0:128], in_=src, identity=ident), reads=["e12f", "gsl"], writes=["ps5"])
                    T.op('act', lambda e: e.activation(out=dstT[:, tcs], in_=ps[5][:, 0:128], func=AF.Copy), reads=["ps5"], writes=[nm])
            T.op('pool', lambda e: e.tensor_scalar(out=ne2T, in0=e2T, scalar1=-1.0, scalar2=None, op0=ALU.mult), reads=["e2T"], writes=["ne2T"])
            T.barrier()
            for t in range(TT):
                a_ = oha[t % NOH]
                b_ = ohb[t % NOH]
                T.op('dve', lambda e: e.tensor_scalar(out=a_, in0=iota, scalar1=e1T[:, t:t + 1], scalar2=gT[:, t:t + 1],
                                                      op0=ALU.is_equal, op1=ALU.mult), reads=["e1T", "gT"], writes=["oha%d" % (t % NOH)])
                if t % 4 != 1:
                    T.op('dve', lambda e: e.tensor_scalar(out=b_, in0=iota, scalar1=e2T[:, t:t + 1], scalar2=None, op0=ALU.is_equal),
                         reads=["e2T"], writes=["ohb%d" % (t % NOH)])
                else:
                    d2 = ohtmp[(t // 4) % 2]
                    T.op('act', lambda e: e.activation(out=d2, in_=iota, func=AF.Square, bias=ne2T[:, t:t + 1], scale=1.0),
                         reads=["ne2T"], writes=["ohtmp%d" % ((t // 4) % 2)])
                    T.op('act', lambda e: e.activation(out=b_, in_=d2, func=AF.Relu, bias=1.0, scale=-1.0),
                         reads=["ohtmp%d" % ((t // 4) % 2)], writes=["ohb%d" % (t % NOH)])
                pk = 6 + (t // 4) % 2
                T.op('pe', lambda e: e.matmul(ps[pk][:, (t % 4) * 128:(t % 4 + 1) * 128], lhsT=b_, rhs=a_, start=True, stop=True),
                     reads=["oha%d" % (t % NOH), "ohb%d" % (t % NOH)], writes=["ps%d" % pk])
                def w_evac(g_):
                    pk_ = 6 + g_ % 2
                    t0 = g_ * 4
                    pv = ps[pk_].rearrange("p (t e) -> p t e", t=4)
                    T.op('act', lambda e: e.activation(out=Wsb_w[:, t0:t0 + 4, :], in_=pv[:, :, 0:64], func=AF.Copy),
                         reads=["ps%d" % pk_], writes=["Wsb"])
                    T.op('act', lambda e: e.activation(out=Whi[:, t0:t0 + 4, :], in_=pv[:, :, 64:128], func=AF.Copy),
                         reads=["ps%d" % pk_], writes=["Whi"])
                if t % 4 == 3 and t >= 7:
                    w_evac(t // 4 - 1)
                if t == TT - 1:
                    w_evac(t // 4)

            def d_load(e1):
                T.dma('sp', ublk[e1 % 4], B["Ubf"][e1].rearrange("p (k e) -> p k e", k=8), writes=["ublk%d" % (e1 % 4)])
                T.dma('sp', vblk[e1 % 4], B["Vbf"][e1 * 128:(e1 + 1) * 128, :], writes=["vblk%d" % (e1 % 4)])

            def d_scores(e1):
                pk = 4 + e1 % 2
                for dc in range(8):
                    T.op('pe', lambda e: e.matmul(ps[pk][:, 0:TT], lhsT=ublk[e1 % 4][:, dc, :], rhs=xn2Tb[:, dc, :], start=(dc == 0), stop=(dc == 7)),
                         reads=["ublk%d" % (e1 % 4), "xn2Tb"], writes=["ps%d" % pk])

            def d_gate(e1):
                pk = 4 + e1 % 2
                ab, gb_ = actb[e1 % 2], gab[e1 % 2]
                wrow, wkey = (Wsb[:, :, e1], "Wsb") if e1 < 64 else (Whi[:, :, e1 - 64], "Whi")
                T.op('act', lambda e: e.activation(out=ab, in_=ps[pk][:, 0:TT], func=AF.Gelu), reads=["ps%d" % pk], writes=["actb%d" % (e1 % 2)])
                T.op('dve', lambda e: e.tensor_tensor(out=gb_, in0=ab, in1=wrow, op=ALU.mult),
                     reads=["actb%d" % (e1 % 2), wkey], writes=["gab%d" % (e1 % 2)])

            def d_acc(e1):
                gb_, vb = gab[e1 % 2], vblk[e1 % 4]
                for blk in blks:
                    for half in range(2):
                        T.op('pe', lambda e: e.matmul(ps[blk * 2 + half], lhsT=gb_[:, blk * 128:(blk + 1) * 128], rhs=vb[:, half * 512:(half + 1) * 512],
                                                      start=(e1 == 0), stop=(e1 == 127)),
                             reads=["gab%d" % (e1 % 2), "vblk%d" % (e1 % 4)], writes=["psy%d" % (blk * 2 + half)])

            d_load(0)
            d_load(1)
            d_scores(0)
            d_gate(0)
            for e1 in range(128):
                if e1 + 2 < 128:
                    d_load(e1 + 2)
                if e1 + 1 < 128:
                    d_scores(e1 + 1)
                d_acc(e1)
                if e1 + 1 < 128:
                    d_gate(e1 + 1)
            T.barrier()
            for blk in blks:
                row0 = ti * TT + blk * 128
                for half in range(2):
                    T.op('dve', lambda e: e.tensor_tensor(out=xres[:, blk, half * 512:(half + 1) * 512], in0=ps[blk * 2 + half],
                                                          in1=xres[:, blk, half * 512:(half + 1) * 512], op=ALU.add),
                         reads=["psy%d" % (blk * 2 + half), "xres%d" % blk], writes=["xres%d" % blk])
                T.op('act', lambda e: e.activation(out=xnb, in_=xres[:, blk, :], func=AF.Square, accum_out=ss), reads=["xres%d" % blk], writes=["xnb", "ssb"])
                T.op('act', lambda e: e.activation(out=rs, in_=ss, func=AF.Sqrt, bias=EPS, scale=1.0 / D_MODEL), reads=["ssb"], writes=["rsb"])
                T.op('dve', lambda e: e.reciprocal(out=rstd, in_=rs), reads=["rsb"], writes=["rstdb"])
                T.op('dve', lambda e: e.scalar_tensor_tensor(out=xnb, in0=xres[:, blk, :], scalar=rstd, in1=gfin, op0=ALU.mult, op1=ALU.mult),
                     reads=["xres%d" % blk, "rstdb"], writes=["xnb"])
                T.dma('sp', B["y"][row0:row0 + 128, :], xnb, reads=["xnb"])
            T.barrier()
```
